# Optimizing a Trainium2 kernel written in Bass

```python
import math
import jax, jax.numpy as jnp
from jax import lax
import numpy as np

D_MODEL = 1024
BATCH = 8
SEQ = 2048
DEPTH = 2
DEC_BATCH = 32
DEC_SEQ = 4
PAST_LEN = 8192
PAGE_SIZE = 128

HEAD_DIM = 64
N_MIXERS = 2
N_A_LAYERS = (DEPTH + 1) // 2
N_B_LAYERS = DEPTH // 2
WINDOWS = (128, 512, 2048)
DILATIONS = (1, 4, 16)
N_GROUPS = 3
GROUP_HEADS = 4
A_HEADS = N_GROUPS * GROUP_HEADS
A_KEYS = WINDOWS[0] // DILATIONS[0] + 1
A_QKV = 3 * A_HEADS * HEAD_DIM
A_OUT = GROUP_HEADS * HEAD_DIM
Q_BLOCK = 128
N_MEM = 256
MEM_HEADS = 4
MEM_W = MEM_HEADS * HEAD_DIM
RWKV_W = 3 * D_MODEL // 4
RWKV_HEADS = RWKV_W // HEAD_DIM
LORA_DECAY = 64
LORA_A = 64
C_SHIFT = 3 * RWKV_W + LORA_DECAY + LORA_A
GN_EPS = 64e-5
A_GATE = A_OUT + MEM_W
A_IN = A_QKV + MEM_W + A_GATE
B_GATE = RWKV_W + MEM_W
B_IN = C_SHIFT + MEM_W + B_GATE
N_BUCKETS = 32
MAX_DISTANCE = WINDOWS[-1]
RMS_EPS = 1e-6
NEG_INF = -1e30
SCALE = HEAD_DIM ** -0.5

kernel_name = "hybrid_dilated_rwkv7_memory_decoder_step"


def rms_norm(x, g):
    xf = x.astype(jnp.float32)
    y = xf * lax.rsqrt(jnp.mean(xf * xf, axis=-1, keepdims=True) + RMS_EPS)
    return (y * g.astype(jnp.float32)).astype(x.dtype)


def t5_bucket(dist):
    max_exact = N_BUCKETS // 2
    d = jnp.maximum(dist, 1).astype(jnp.float32)
    large = max_exact + (jnp.log(d / max_exact) / math.log(MAX_DISTANCE / max_exact)
                         * (N_BUCKETS - max_exact)).astype(jnp.int32)
    large = jnp.minimum(large, N_BUCKETS - 1)
    return jnp.where(dist < max_exact, dist, large)


def group_bias(rel_bias, g):
    dist = DILATIONS[g] * jnp.arange(A_KEYS, dtype=jnp.int32)
    b = rel_bias[t5_bucket(dist)]
    return b[:, g * GROUP_HEADS:(g + 1) * GROUP_HEADS].T.astype(jnp.float32)


def memory_kv(mem, g, w):
    kv = rms_norm(mem, g) @ w
    return kv.reshape(mem.shape[0], mem.shape[1], 2, MEM_HEADS, HEAD_DIM)


def cross_attn(q, kv):
    logits = jnp.einsum('bthd,bmhd->bhtm', q.astype(jnp.float32), kv[:, :, 0].astype(jnp.float32)) * SCALE
    p = jax.nn.softmax(logits, axis=-1)
    o = jnp.einsum('bhtm,bmhd->bthd', p, kv[:, :, 1].astype(jnp.float32))
    return o.reshape(q.shape[0], q.shape[1], MEM_W).astype(q.dtype)


def dilated_group(q, ctx, q_idx, dil, bias):
    idx = q_idx[:, None] - dil * jnp.arange(A_KEYS, dtype=jnp.int32)[None, :]
    valid = idx >= 0
    kv = jnp.take(ctx, jnp.maximum(idx, 0), axis=1, mode='clip')
    logits = jnp.einsum('bthd,btkhd->bhtk', q.astype(jnp.float32), kv[:, :, :, 0].astype(jnp.float32)) * SCALE
    logits = jnp.where(valid[None, None], logits + bias[None, :, None, :], NEG_INF)
    m = jnp.max(logits, axis=-1)
    p = jnp.exp(logits - m[..., None])
    s = jnp.sum(p, axis=-1)
    o = jnp.einsum('bhtk,btkhd->bthd', p, kv[:, :, :, 1].astype(jnp.float32))
    o = o / s.transpose(0, 2, 1)[..., None]
    return o, m.transpose(0, 2, 1), s.transpose(0, 2, 1)


def mixer_a(xn, w_in, w_out, rel_bias, mem_kv, past):
    bsz, T, _ = xn.shape
    p = xn @ w_in
    qkv = p[..., :A_QKV].reshape(bsz, T, 3, A_HEADS, HEAD_DIM)
    q_mem = p[..., A_QKV:A_QKV + MEM_W].reshape(bsz, T, MEM_HEADS, HEAD_DIM)
    gate = p[..., A_QKV + MEM_W:]
    q = qkv[:, :, 0]
    kv = qkv[:, :, 1:]
    ctxs, offs, new_rows, biases = [], [], [], []
    for g in range(N_GROUPS):
        kv_g = kv[:, :, :, g * GROUP_HEADS:(g + 1) * GROUP_HEADS]
        if past is None:
            ctxs.append(kv_g)
            offs.append(0)
            new_rows.append(kv_g[:, T - min(WINDOWS[g], T):])
        else:
            ctxs.append(jnp.concatenate([past[g].astype(kv_g.dtype), kv_g], axis=1))
            offs.append(past[g].shape[1])
            new_rows.append(kv_g)
        biases.append(group_bias(rel_bias, g))

    def block(args):
        q_blk, t_blk = args
        os_, ms, ss = [], [], []
        for g in range(N_GROUPS):
            o, m, s = dilated_group(q_blk[:, :, g * GROUP_HEADS:(g + 1) * GROUP_HEADS], ctxs[g],
                                    offs[g] + t_blk, DILATIONS[g], biases[g])
            os_.append(o); ms.append(m); ss.append(s)
        ms = jnp.stack(ms); ss = jnp.stack(ss)
        wts = jnp.exp(ms - jnp.max(ms, axis=0, keepdims=True)) * ss
        return jnp.sum(wts[..., None] * jnp.stack(os_), axis=0) / jnp.sum(wts, axis=0)[..., None]

    nb = T // Q_BLOCK if T % Q_BLOCK == 0 else 1
    qb = T // nb
    q_blocks = q.reshape(bsz, nb, qb, A_HEADS, HEAD_DIM).swapaxes(0, 1)
    t_blocks = jnp.arange(T, dtype=jnp.int32).reshape(nb, qb)
    o = lax.map(block, (q_blocks, t_blocks)).swapaxes(0, 1).reshape(bsz, T, A_OUT).astype(xn.dtype)
    o_mem = cross_attn(q_mem, mem_kv)
    h = jnp.concatenate([o, o_mem], axis=-1) * jax.nn.silu(gate)
    return h @ w_out, new_rows


def wkv_scan(r, w, k, v, a, b, s0):
    seq = tuple(t.astype(jnp.float32).swapaxes(0, 1) for t in (r, w, k, v, a, b))

    def step(S, inp):
        r_t, w_t, k_t, v_t, a_t, b_t = inp
        sa = jnp.einsum('bhvk,bhk->bhv', S, a_t)
        S = S * w_t[:, :, None, :] + sa[..., None] * b_t[:, :, None, :] + v_t[..., None] * k_t[:, :, None, :]
        return S, jnp.einsum('bhvk,bhk->bhv', S, r_t)

    S, ys = lax.scan(step, s0.astype(jnp.float32), seq)
    return ys.swapaxes(0, 1), S


def mixer_b(xn, w_in, w_out, mu, w0, w_up, a0, a_up, k_k, k_a, r_k, ln_w, ln_b, mem_kv, shift_prev, s0):
    bsz, T, _ = xn.shape
    p = xn @ w_in
    cols = p[..., :C_SHIFT]
    q_mem = p[..., C_SHIFT:C_SHIFT + MEM_W].reshape(bsz, T, MEM_HEADS, HEAD_DIM)
    gate = p[..., C_SHIFT + MEM_W:]
    prev = jnp.concatenate([shift_prev[:, None].astype(cols.dtype), cols[:, :-1]], axis=1)
    xs = (cols + mu * (prev - cols)).astype(jnp.float32)
    r = xs[..., :RWKV_W]
    k = xs[..., RWKV_W:2 * RWKV_W]
    v = xs[..., 2 * RWKV_W:3 * RWKV_W]
    wd = xs[..., 3 * RWKV_W:3 * RWKV_W + LORA_DECAY]
    ad = xs[..., 3 * RWKV_W + LORA_DECAY:]
    w_log = -jax.nn.softplus(-(w0 + jnp.tanh(wd) @ w_up)) - 0.5
    decay = jnp.exp(-jnp.exp(w_log))
    a = jax.nn.sigmoid(a0 + ad @ a_up)
    hs = (bsz, T, RWKV_HEADS, HEAD_DIM)
    r, k, v, decay, a = (t.reshape(hs) for t in (r, k, v, decay, a))
    kk = k * k_k.reshape(RWKV_HEADS, HEAD_DIM)
    kk = kk / jnp.maximum(jnp.sqrt(jnp.sum(kk * kk, axis=-1, keepdims=True)), 1e-12)
    k = k * (1.0 + (a - 1.0) * k_a.reshape(RWKV_HEADS, HEAD_DIM))
    y, S = wkv_scan(r, decay, k, v, -kk, kk * a, s0)
    mean = jnp.mean(y, axis=-1, keepdims=True)
    var = jnp.mean(jnp.square(y - mean), axis=-1, keepdims=True)
    y = (y - mean) * lax.rsqrt(var + GN_EPS) * ln_w.reshape(RWKV_HEADS, HEAD_DIM) + ln_b.reshape(RWKV_HEADS, HEAD_DIM)
    y = y + jnp.sum(r * k * r_k, axis=-1, keepdims=True) * v
    y = y.reshape(bsz, T, RWKV_W).astype(xn.dtype)
    o_mem = cross_attn(q_mem, mem_kv)
    h = jnp.concatenate([y, o_mem], axis=-1) * jax.nn.silu(gate)
    return h @ w_out, cols[:, -1], S.astype(xn.dtype)


def setup_inputs(seed: int = 0) -> dict:
    key = jax.random.key(seed)
    ks = list(jax.random.split(key, 40))

    def nrm(shape, scale):
        return scale * jax.random.normal(ks.pop(), shape, jnp.float32)

    def unif(shape, lo, hi):
        return jax.random.uniform(ks.pop(), shape, jnp.float32, lo, hi)

    return {
        "x_prompt": nrm((BATCH, SEQ, D_MODEL), 1.0),
        "x_sample": nrm((DEC_BATCH, DEC_SEQ, D_MODEL), 1.0),
        "mem_prompt": nrm((BATCH, N_MEM, D_MODEL), 1.0),
        "cache_mem_kv": nrm((DEPTH, DEC_BATCH, N_MEM, 2, MEM_HEADS, HEAD_DIM), 1.0),
        "cache_win0": nrm((N_A_LAYERS, DEC_BATCH, min(WINDOWS[0], PAST_LEN), 2, GROUP_HEADS, HEAD_DIM), 1.0),
        "cache_win1": nrm((N_A_LAYERS, DEC_BATCH, min(WINDOWS[1], PAST_LEN), 2, GROUP_HEADS, HEAD_DIM), 1.0),
        "cache_win2": nrm((N_A_LAYERS, DEC_BATCH, min(WINDOWS[2], PAST_LEN), 2, GROUP_HEADS, HEAD_DIM), 1.0),
        "state_wkv": nrm((N_B_LAYERS, DEC_BATCH, RWKV_HEADS, HEAD_DIM, HEAD_DIM), 1.0),
        "state_shift": nrm((N_B_LAYERS, DEC_BATCH, C_SHIFT), 1.0),
        "norm_pre": 1.0 + nrm((DEPTH, D_MODEL), 0.05),
        "norm_post": 1.0 + nrm((DEPTH, D_MODEL), 0.05),
        "norm_mem": 1.0 + nrm((DEPTH, D_MODEL), 0.05),
        "w_mem_kv": nrm((DEPTH, D_MODEL, 2 * MEM_W), D_MODEL ** -0.5),
        "rel_bias": nrm((N_BUCKETS, A_HEADS), 0.5),
        "w_in_a": nrm((N_A_LAYERS, D_MODEL, A_IN), D_MODEL ** -0.5),
        "w_out_a": nrm((N_A_LAYERS, A_GATE, D_MODEL), A_GATE ** -0.5),
        "w_in_b": nrm((N_B_LAYERS, D_MODEL, B_IN), D_MODEL ** -0.5),
        "w_out_b": nrm((N_B_LAYERS, B_GATE, D_MODEL), B_GATE ** -0.5),
        "rwkv_mu": unif((N_B_LAYERS, C_SHIFT), 0.0, 1.0),
        "rwkv_w0": unif((N_B_LAYERS, RWKV_W), -4.0, 1.0),
        "rwkv_w_up": nrm((N_B_LAYERS, LORA_DECAY, RWKV_W), 0.5 * LORA_DECAY ** -0.5),
        "rwkv_a0": nrm((N_B_LAYERS, RWKV_W), 0.5),
        "rwkv_a_up": nrm((N_B_LAYERS, LORA_A, RWKV_W), LORA_A ** -0.5),
        "rwkv_k_k": 0.85 + nrm((N_B_LAYERS, RWKV_W), 0.05),
        "rwkv_k_a": 1.0 + nrm((N_B_LAYERS, RWKV_W), 0.05),
        "rwkv_r_k": nrm((N_B_LAYERS, RWKV_HEADS, HEAD_DIM), 0.1),
        "rwkv_ln_w": 1.0 + nrm((N_B_LAYERS, RWKV_W), 0.05),
        "rwkv_ln_b": nrm((N_B_LAYERS, RWKV_W), 0.02),
    }


def reference(x_prompt, x_sample, mem_prompt, cache_mem_kv, cache_win0, cache_win1, cache_win2, state_wkv,
              state_shift, norm_pre, norm_post, norm_mem, w_mem_kv, rel_bias, w_in_a, w_out_a, w_in_b, w_out_b,
              rwkv_mu, rwkv_w0, rwkv_w_up, rwkv_a0, rwkv_a_up, rwkv_k_k, rwkv_k_a, rwkv_r_k, rwkv_ln_w, rwkv_ln_b):
    xp, xs = x_prompt, x_sample
    mem_new = []
    win_p = [[], [], []]
    win_s = [[], [], []]
    wkv_p, wkv_s, sh_p, sh_s = [], [], [], []
    for i in range(DEPTH):
        j = i // N_MIXERS
        mkv_p = memory_kv(mem_prompt, norm_mem[i], w_mem_kv[i])
        mem_new.append(mkv_p)
        mkv_s = cache_mem_kv[i]
        xn_p = rms_norm(xp, norm_pre[i])
        xn_s = rms_norm(xs, norm_pre[i])
        if i % N_MIXERS == 0:
            hp, rows_p = mixer_a(xn_p, w_in_a[j], w_out_a[j], rel_bias, mkv_p, None)
            hs, rows_s = mixer_a(xn_s, w_in_a[j], w_out_a[j], rel_bias, mkv_s,
                                 (cache_win0[j], cache_win1[j], cache_win2[j]))
            for g in range(N_GROUPS):
                win_p[g].append(rows_p[g])
                win_s[g].append(rows_s[g])
        else:
            rw = (rwkv_mu[j], rwkv_w0[j], rwkv_w_up[j], rwkv_a0[j], rwkv_a_up[j], rwkv_k_k[j], rwkv_k_a[j],
                  rwkv_r_k[j], rwkv_ln_w[j], rwkv_ln_b[j])
            shift0 = jnp.zeros((xp.shape[0], C_SHIFT), xp.dtype)
            s0 = jnp.zeros((xp.shape[0], RWKV_HEADS, HEAD_DIM, HEAD_DIM), jnp.float32)
            hp, shp, Sp = mixer_b(xn_p, w_in_b[j], w_out_b[j], *rw, mkv_p, shift0, s0)
            hs, shs, Ss = mixer_b(xn_s, w_in_b[j], w_out_b[j], *rw, mkv_s, state_shift[j], state_wkv[j])
            wkv_p.append(Sp); wkv_s.append(Ss); sh_p.append(shp); sh_s.append(shs)
        xp = xp + rms_norm(hp, norm_post[i])
        xs = xs + rms_norm(hs, norm_post[i])
    new_mem_kv = jnp.stack(mem_new)
    win0_p, win1_p, win2_p = (jnp.stack(w) for w in win_p)
    win0_s, win1_s, win2_s = (jnp.stack(w) for w in win_s)
    wkv_prompt = jnp.stack(wkv_p)
    wkv_sample = jnp.stack(wkv_s)
    shift_prompt = jnp.stack(sh_p)
    shift_sample = jnp.stack(sh_s)
    return (xp, xs, new_mem_kv, win0_p, win1_p, win2_p, win0_s, win1_s, win2_s,
            wkv_prompt, wkv_sample, shift_prompt, shift_sample)
```

```python
import math
from contextlib import ExitStack
import numpy as np
import concourse.bass as bass
import concourse.mybir as mybir
from concourse.bass_utils import run_bass_kernel_spmd

F32 = mybir.dt.float32
BF16 = mybir.dt.bfloat16
AF = mybir.ActivationFunctionType
ALU = mybir.AluOpType
AX = mybir.AxisListType

ENGS = ("pe", "act", "dve", "pool", "sp")
NCORES = 8
T = 2048
D = 1024
NT = 16
NEG = -30000.0
SCALE = 0.125
DIL = (1, 4, 16)
RMS_EPS = 1e-6
C_SHIFT = 2432
A_IN = 3072
B_IN = 3712


class Tk:
    __slots__ = ("name", "lw", "rd", "dsem", "dcount", "excl")

    def __init__(self, name="", excl=False):
        self.name = name
        self.excl = excl
        self.lw = None
        self.rd = []
        self.dsem = None
        self.dcount = 0


class Op:
    __slots__ = ("eng", "fn", "deps", "is_dma", "pos", "signal", "cnt", "sem_tk", "waits", "line")


class Prog:
    def __init__(self, nc):
        self.nc = nc
        self.ops = []
        self.eng_ops = {e: [] for e in ENGS}
        self.last_dma = {}

    def op(self, eng, fn, reads=(), writes=(), dma=False, sem_tk=None, extra_deps=()):
        if len(self.ops) >= getattr(self, "maxops", 10 ** 9):
            return None
        o = Op()
        import sys as _sys
        o.line = _sys._getframe(1).f_lineno
        o.eng = eng
        o.fn = fn
        o.is_dma = dma
        o.signal = dma
        o.cnt = 0
        o.waits = []
        deps = []
        for r in reads:
            if r.lw is not None:
                deps.append((r.lw, "raw"))
            if r.excl:
                for rr in r.rd:
                    deps.append((rr, "war"))
        for w in writes:
            if w.lw is not None:
                deps.append((w.lw, "waw"))
            for rr in w.rd:
                deps.append((rr, "war"))
        for xd in extra_deps:
            deps.append((xd, "raw"))
        fdeps = []
        seen = set()
        for d, kind in deps:
            if d is o:
                continue
            if (not d.is_dma) and d.eng == eng and (not dma):
                if eng == "pe" or kind != "raw":
                    continue
            if id(d) in seen:
                continue
            seen.add(id(d))
            fdeps.append(d)
        o.deps = fdeps
        for r in (reads if fn is not None else ()):
            if r.excl:
                r.rd = [o]
            else:
                r.rd.append(o)
        for w in (writes if fn is not None else ()):
            w.lw = o
            w.rd = []
        if dma:
            if sem_tk is None:
                sem_tk = (list(writes) + list(reads))[0]
            o.sem_tk = sem_tk
            sem_tk.dcount += 1
            o.cnt = sem_tk.dcount * 16
            self.last_dma[id(sem_tk)] = o
        else:
            o.sem_tk = None
        o.pos = len(self.eng_ops[eng])
        self.eng_ops[eng].append(o)
        self.ops.append(o)
        return o

    def finalize_and_emit(self, stack):
        nc = self.nc
        waited_pos = {e: {p: -1 for p in ENGS} for e in ENGS}
        waited_dma = {e: {} for e in ENGS}
        for o in self.ops:
            e = o.eng
            for d in o.deps:
                if d.is_dma:
                    key = id(d.sem_tk)
                    if waited_dma[e].get(key, 0) >= d.cnt:
                        continue
                    waited_dma[e][key] = d.cnt
                    o.waits.append(d)
                else:
                    if waited_pos[e][d.eng] >= d.pos:
                        continue
                    waited_pos[e][d.eng] = d.pos
                    d.signal = True
                    o.waits.append(d)
        for e in ENGS:
            c = 0
            for o in self.eng_ops[e]:
                if not o.is_dma and o.signal:
                    c += 1
                    o.cnt = c
        esem = {e: stack.enter_context(nc.semaphore("es_" + e)) for e in ENGS}
        nd = 0
        for o in self.ops:
            if o.is_dma and o.sem_tk.dsem is None:
                o.sem_tk.dsem = stack.enter_context(nc.semaphore("ds%d" % nd))
                nd += 1
        self.n_dma_sems = nd
        block = stack.enter_context(nc.Block())

        def emit(engobj, elist):
            for o in elist:
                for d in o.waits:
                    if d.is_dma:
                        engobj.wait_ge(d.sem_tk.dsem, d.cnt)
                    else:
                        engobj.wait_ge(esem[d.eng], d.cnt)
                if o.fn is None:
                    continue
                ins = o.fn(engobj)
                if o.is_dma:
                    ins.then_inc(o.sem_tk.dsem, 16)
                elif o.signal:
                    ins.then_inc(esem[o.eng], 1)

        @block.tensor
        def _(pe):
            emit(pe, self.eng_ops["pe"])

        @block.scalar
        def _(act):
            emit(act, self.eng_ops["act"])

        @block.vector
        def _(dve):
            emit(dve, self.eng_ops["dve"])

        @block.gpsimd
        def _(pool):
            emit(pool, self.eng_ops["pool"])

        @block.sync
        def _(sp):
            emit(sp, self.eng_ops["sp"])
            done = set()
            for o in self.ops:
                if o.is_dma and id(o.sem_tk) not in done:
                    done.add(id(o.sem_tk))
                    sp.wait_ge(o.sem_tk.dsem, 16 * o.sem_tk.dcount)


def _t5_bucket_np(dist):
    dist = np.asarray(dist, dtype=np.int32)
    d = np.maximum(dist, 1).astype(np.float32)
    large = 16 + (np.log(d / np.float32(16.0)) / np.float32(math.log(2048 / 16)) * np.float32(16.0)).astype(np.int32)
    large = np.minimum(large, 31)
    return np.where(dist < 16, dist, large)


def _onehot_const():
    oh = np.zeros((33, 3, 384), np.float32)
    for g in range(3):
        for u in range(383):
            dist = u - 127
            if 0 <= dist <= 128:
                b = int(_t5_bucket_np(DIL[g] * dist))
                oh[b, g, u] = 1.0
            else:
                oh[32, g, u] = NEG
        oh[32, g, 383] = NEG
    return oh.reshape(33, 3 * 384)


STAGE = 2
DBG = set()


def build_program():
    nc = bass.Bass("TRN2", target_bir_lowering=False)
    try:
        nc.allow_low_precision("bf16 matmul operands with fp32 accumulation (matches problem tolerance)")
    except Exception:
        pass
    try:
        nc.allow_non_contiguous_dma("strided window / toeplitz accesses")
    except Exception:
        pass
    P = Prog(nc)
    for _d in DBG:
        if _d.startswith('maxops='):
            P.maxops = int(_d.split('=')[1])

    def din(name, shape):
        return nc.dram_tensor(name, list(shape), F32, kind="ExternalInput").ap()

    def dout(name, shape):
        return nc.dram_tensor(name, list(shape), F32, kind="ExternalOutput").ap()

    x_p = din("x_p", [T, D])
    x_s = din("x_s", [16, D])
    mem_p = din("mem_p", [256, D])
    c_mem = din("c_mem", [2, 4, 256, 512])
    c_w0 = din("c_w0", [4, 128, 512])
    c_w1 = din("c_w1", [4, 512, 512])
    c_w2 = din("c_w2", [4, 2048, 512])
    s_wkv = din("s_wkv", [4, 12, 64, 64])
    s_shift = din("s_shift", [4, C_SHIFT])
    norm_pre = din("norm_pre", [2, D])
    norm_post = din("norm_post", [2, D])
    norm_mem = din("norm_mem", [2, D])
    w_mem = din("w_mem", [2, D, 512])
    rel_bias = din("rel_bias", [32, 12])
    w_in_a = din("w_in_a", [D, A_IN])
    w_out_a = din("w_out_a", [512, D])
    w_in_b = din("w_in_b", [D, B_IN])
    w_out_b = din("w_out_b", [D, D])
    onehot = din("c_onehot", [33, 3 * 384])
    r_mu = din("r_mu", [1, C_SHIFT])
    r_w0 = din("r_w0", [1, 768])
    r_wup = din("r_wup", [64, 768])
    r_a0 = din("r_a0", [1, 768])
    r_aup = din("r_aup", [64, 768])
    r_kk = din("r_kk", [1, 768])
    r_ka = din("r_ka", [1, 768])
    r_rk = din("r_rk", [1, 768])
    r_lnw = din("r_lnw", [1, 768])
    r_lnb = din("r_lnb", [1, 768])
    y_p = dout("y_p", [T, D])
    y_s = dout("y_s", [16, D])
    o_mem = dout("o_mem", [2, 256, 512])
    o_w0p = dout("o_w0p", [128, 512])
    o_w1p = dout("o_w1p", [512, 512])
    o_w2p = dout("o_w2p", [2048, 512])
    o_w0s = dout("o_w0s", [16, 512])
    o_w1s = dout("o_w1s", [16, 512])
    o_w2s = dout("o_w2s", [16, 512])
    o_wkvp = dout("o_wkvp", [12, 64, 64])
    o_wkvs = dout("o_wkvs", [4, 12, 64, 64])
    o_shp = dout("o_shp", [1, C_SHIFT])
    o_shs = dout("o_shs", [4, C_SHIFT])
    x1_d = nc.dram_tensor("x1_d", [T, D], F32, kind="Internal").ap()
    x1s_d = nc.dram_tensor("x1s_d", [16, D], F32, kind="Internal").ap()
    e_d = nc.dram_tensor("e_d", [12, 3 * 384], F32, kind="Internal").ap()
    out_tokens = []

    with ExitStack() as st:
        def sb(name, shape, dt):
            return st.enter_context(nc.sbuf_tensor(name, list(shape), dt))

        def psum(name, shape, dt):
            return st.enter_context(nc.psum_tensor(name, list(shape), dt))

        ps_mm = [psum("ps_mm%d" % i, [128, 512], F32) for i in range(2)]
        ps_s = [psum("ps_s%d" % i, [128, 512], F32) for i in range(2)]
        ps_o = [psum("ps_o%d" % i, [128, 512], F32) for i in range(2)]
        ps_t = psum("ps_t", [128, 1024], BF16)
        ps_x = psum("ps_x", [128, 512], F32)
        t_ps_mm = [Tk("ps_mm0", True), Tk("ps_mm1", True)]
        t_ps_s = [Tk("ps_s0", True), Tk("ps_s1", True)]
        t_ps_o = [Tk("ps_o0", True), Tk("ps_o1", True)]
        t_ps_t = Tk("ps_t", True)
        t_ps_x = Tk("ps_x", True)
        rr = {"mm": 0, "s": 0, "o": 0, "ev": 0}

        def nxt(k):
            rr[k] += 1
            return rr[k] & 1

        def ev_eng():
            rr["ev"] += 1
            return "act" if (rr["ev"] & 1) else "dve"

        def copy_op(eng, out, in_, reads, writes):
            if eng == "act":
                P.op("act", lambda e: e.copy(out=out, in_=in_), reads=reads, writes=writes)
            else:
                P.op(eng, lambda e: e.tensor_copy(out=out, in_=in_), reads=reads, writes=writes)

        identf = sb("identf", [128, 128], F32)
        ident = sb("ident", [128, 128], BF16)
        t_ident = Tk()

        P.op("pool", lambda e: e.memset(identf[:], 0.0), writes=[t_ident])
        P.op("pool", lambda e: e.affine_select(out=identf[:], in_=identf[:], pattern=[[-1, 128]], compare_op=ALU.not_equal,
                                               fill=1.0, base=0, channel_multiplier=1), reads=[t_ident], writes=[t_ident])
        P.op("dve", lambda e: e.tensor_copy(out=ident[:], in_=identf[:]), reads=[t_ident], writes=[t_ident])

        ARENA_WORDS = 49000
        arena = sb("arena", [128, ARENA_WORDS], F32)
        gains4 = sb("gains", [128, 2, 1024], F32)
        gmem = arena[:, 0:2048].rearrange("p (a d) -> p a d", a=2)
        barsb = sb("barsb", [128, 16], F32)
        t_bar = {e: Tk() for e in ("pe", "act", "dve", "pool")}

        def barrier():
            dmas = list(P.last_dma.values())
            P.op("pe", lambda e: e.transpose(out=ps_t[0:16, 0:16], in_=ident[0:16, 0:16], identity=ident[0:16, 0:16]),
                 reads=[t_ident], writes=[t_ps_t, t_bar["pe"]], extra_deps=dmas)
            P.op("act", lambda e: e.copy(out=barsb[:, 0:1], in_=barsb[:, 8:9]), writes=[t_bar["act"]], extra_deps=dmas)
            P.op("dve", lambda e: e.tensor_copy(out=barsb[:, 1:2], in_=barsb[:, 9:10]), writes=[t_bar["dve"]], extra_deps=dmas)
            P.op("pool", lambda e: e.memset(barsb[:, 2:3], 0.0), writes=[t_bar["pool"]], extra_deps=dmas)
            allb = list(t_bar.values())
            P.op("pe", lambda e: e.transpose(out=ps_t[0:16, 0:16], in_=ident[0:16, 0:16], identity=ident[0:16, 0:16]),
                 reads=[t_ident] + allb, writes=[t_ps_t])
            P.op("act", lambda e: e.copy(out=barsb[:, 3:4], in_=barsb[:, 8:9]), reads=allb)
            P.op("dve", lambda e: e.tensor_copy(out=barsb[:, 4:5], in_=barsb[:, 9:10]), reads=allb)
            P.op("pool", lambda e: e.memset(barsb[:, 5:6], 0.0), reads=allb)
            P.op("sp", None, reads=allb)

        class _G:
            def __getitem__(self, key):
                p, gi, c = key
                return gains4[p, gi, c] if gi < 2 else gmem[p, gi - 4, c]
        gains = _G()
        t_gains = Tk()
        def load_gains(items):
            for (i, src, l) in items:
                ap_b = bass.AP(tensor=src.tensor, offset=l * D, ap=[[0, 128], [1, D]])
                P.op("sp", (lambda e, i=i, ap_b=ap_b: e.dma_start(out=gains[:, i, :], in_=ap_b)), writes=[t_gains], dma=True)
        load_gains([(0, norm_pre, 0), (1, norm_post, 0), (4, norm_mem, 0), (5, norm_mem, 1)])
        G_PRE, G_POST, G_MEM = (0, 0), (1, 1), (4, 5)

        ss = sb("ss", [128, 8], F32)
        junk = sb("junk", [128, 1024], BF16)
        t_junk = Tk()

        def rmsnorm(src_ap, np_, gi, dst_ap, t_src, t_dst, extra_reads=()):
            t_ss = Tk()
            P.op("act", lambda e: e.activation(out=junk[:np_, :], in_=src_ap, func=AF.Square),
                 reads=[t_src], writes=[t_junk])
            P.op("dve", lambda e: e.reduce_sum(out=ss[:np_, 0:1], in_=junk[:np_, :], axis=AX.X),
                 reads=[t_junk], writes=[t_ss])
            P.op("dve", lambda e: e.tensor_scalar(out=ss[:np_, 1:2], in0=ss[:np_, 0:1], scalar1=1.0 / D, scalar2=RMS_EPS,
                                                  op0=ALU.mult, op1=ALU.add), reads=[t_ss], writes=[t_ss])
            P.op("act", lambda e: e.activation(out=ss[:np_, 2:3], in_=ss[:np_, 1:2], func=AF.Sqrt), reads=[t_ss], writes=[t_ss])
            P.op("dve", lambda e: e.reciprocal(out=ss[:np_, 3:4], in_=ss[:np_, 2:3]), reads=[t_ss], writes=[t_ss])
            P.op("dve", lambda e: e.scalar_tensor_tensor(out=dst_ap, in0=src_ap, scalar=ss[:np_, 3:4], in1=gains[:np_, gi, :],
                                                         op0=ALU.mult, op1=ALU.mult),
                 reads=[t_src, t_ss, t_gains] + list(extra_reads), writes=[t_dst])

        apos = [0]
        atop = [ARENA_WORDS]

        def carve_top(nwords):
            atop[0] -= nwords
            assert atop[0] >= apos[0]
            return arena[:, atop[0]:atop[0] + nwords]

        def carve(nbytes):
            w0 = apos[0]
            nw = (nbytes + 3) // 4
            apos[0] += nw
            assert apos[0] <= atop[0], ("arena overflow", apos[0], atop[0])
            return arena[:, w0:w0 + nw]

        def carve_bf(n):
            return carve(2 * n).bitcast(BF16)

        def carve_f(n):
            return carve(4 * n)

        KmT = sb("KmT", [128, 2, 2, 256], BF16)
        Vm = sb("Vm", [128, 2, 2, 4, 80], BF16)
        t_KmT = [Tk(), Tk()]
        t_Vm = [Tk(), Tk()]
        mark = apos[0]
        carve_f(2048)
        memx = carve_f(2 * 1024).rearrange("p (b d) -> p b d", b=2)
        memn = carve_bf(2 * 1024).rearrange("p (b d) -> p b d", b=2)
        memnT = carve_bf(8 * 256).rearrange("p (k m) -> p k m", k=8)
        wm = carve_bf(8 * 512).rearrange("p (k n) -> p k n", k=8)
        kvst = carve_f(2 * 512).rearrange("p (b n) -> p b n", b=2)
        t_memx, t_memn, t_memnT, t_wm, t_kvst = Tk(), Tk(), Tk(), Tk(), [Tk(), Tk()]
        P.op("sp", lambda e: e.dma_start(out=memx, in_=mem_p.rearrange("(b p) d -> p b d", p=128)), writes=[t_memx], dma=True)
        if "novm" not in DBG:
            P.op("pool", lambda e: e.memset(Vm[:, :, :, :, 64:65], 1.0), writes=t_Vm)
        for l in range(0 if "nomem" in DBG else 2):
            P.op("pool", (lambda e, l=l: e.dma_start(out=wm, in_=w_mem[l].rearrange("(k p) n -> p k n", p=128))),
                 writes=[t_wm], dma=True)
            for b in range(2):
                rmsnorm(memx[:, b, :], 128, G_MEM[l], memn[:, b, :], t_memx, t_memn)

                def tr(e, b=b):
                    for k in range(8):
                        ins = e.transpose(out=ps_t[:, k * 128:(k + 1) * 128], in_=memn[:, b, k * 128:(k + 1) * 128], identity=ident[:])
                    return ins
                P.op("pe", tr, reads=[t_memn, t_ident], writes=[t_ps_t])
                copy_op(ev_eng(), memnT[:, :, b * 128:(b + 1) * 128], ps_t[:].rearrange("p (k m) -> p k m", k=8), [t_ps_t], [t_memnT])
            for b in range(2):
                i = nxt("mm")

                def mm(e, b=b, i=i):
                    for k in range(8):
                        ins = e.matmul(ps_mm[i][:, :], lhsT=memnT[:, k, b * 128:(b + 1) * 128], rhs=wm[:, k, :], start=(k == 0), stop=(k == 7))
                    return ins
                P.op("pe", mm, reads=[t_memnT, t_wm], writes=[t_ps_mm[i]])
                P.op("act", (lambda e, b=b, i=i: e.copy(out=kvst[:, b, :], in_=ps_mm[i][:, :])), reads=[t_ps_mm[i]], writes=[t_kvst[b]])
                P.op("dve", (lambda e, b=b, i=i, l=l: e.tensor_copy(out=Vm[:, l, b, :, 0:64],
                                                                    in_=ps_mm[i][:, 256:512].rearrange("p (h d) -> p h d", h=4))),
                     reads=[t_ps_mm[i]], writes=[t_Vm[l]])
                P.op("sp", (lambda e, b=b, l=l: e.dma_start(out=o_mem[l, b * 128:(b + 1) * 128, :], in_=kvst[:, b, :])),
                     reads=[t_kvst[b]], dma=True)
                out_tokens.append(t_kvst[b])
            for pr in range(2):
                i = nxt("mm")

                def mmk(e, pr=pr, i=i):
                    for k in range(8):
                        ins = e.matmul(ps_mm[i][:, 0:256], lhsT=wm[:, k, pr * 128:(pr + 1) * 128], rhs=memnT[:, k, :], start=(k == 0), stop=(k == 7))
                    return ins
                P.op("pe", mmk, reads=[t_memnT, t_wm], writes=[t_ps_mm[i]])
                copy_op(ev_eng(), KmT[:, l, pr, :], ps_mm[i][:, 0:256], [t_ps_mm[i]], [t_KmT[l]])


        biasT = carve_top(12 * 2 * 128).rearrange("p (a b q) -> p a b q", a=12, b=2)
        t_biasT = Tk()
        M1e = carve_top(264).bitcast(BF16)
        M1o = carve_top(264).bitcast(BF16)
        M2e = carve_top(1032).bitcast(BF16)
        M2o = carve_top(1032).bitcast(BF16)
        t_M = Tk()
        mark = apos[0]
        rb33 = carve_f(12)
        oh33 = carve_f(1152)
        esb = carve_f(1152)
        mtmp = carve_f(2064)
        t_rb, t_oh, t_esb, t_mtmp, t_ed = Tk(), Tk(), Tk(), Tk(), Tk()
        P.op("pool", lambda e: e.memset(rb33[32:33, :], 1.0), writes=[t_rb])
        P.op("sp", lambda e: e.dma_start(out=rb33[0:32, :], in_=rel_bias), writes=[t_rb], dma=True)
        P.op("sp", lambda e: e.dma_start(out=oh33[0:33, :], in_=onehot), writes=[t_oh], dma=True)
        for g in range(3):
            P.op("pe", (lambda e, g=g: e.matmul(ps_x[0:12, 0:384], lhsT=rb33[0:33, :], rhs=oh33[0:33, g * 384:(g + 1) * 384], start=True, stop=True)),
                 reads=[t_rb, t_oh], writes=[t_ps_x])
            P.op("act", (lambda e, g=g: e.copy(out=esb[0:12, g * 384:(g + 1) * 384], in_=ps_x[0:12, 0:384])), reads=[t_ps_x], writes=[t_esb])
        P.op("sp", lambda e: e.dma_start(out=e_d, in_=esb[0:12, :]), reads=[t_esb], writes=[t_ed], dma=True, sem_tk=t_esb)
        bstage = carve_f(24 * 128).rearrange("p (a q) -> p a q", a=24)
        Jrev = carve_f(128)
        t_bst, t_J = Tk(), Tk()

        P.op("pool", lambda e: e.memset(Jrev[:, :], 0.0), writes=[t_J])
        P.op("pool", lambda e: e.affine_select(out=Jrev[:, :], in_=Jrev[:, :], pattern=[[1, 128]], compare_op=ALU.not_equal, fill=1.0, base=-127,
                                               channel_multiplier=1), reads=[t_J], writes=[t_J])
        for g in range(3):
            for h in range(4):
                for blk in range(2):
                    off = (4 * g + h) * 1152 + g * 384 + (128 if blk == 0 else 0)
                    src = bass.AP(tensor=e_d.tensor, offset=off, ap=[[1, 128], [1, 128]])
                    P.op("sp", (lambda e, g=g, h=h, blk=blk, src=src: e.dma_start(out=bstage[:, (g * 4 + h) * 2 + blk, :], in_=src)),
                         reads=[t_ed], writes=[t_bst], dma=True)
        for a4 in range(6):
            i = nxt("mm")
            P.op("pe", (lambda e, a4=a4, i=i: e.matmul(ps_mm[i][:, :], lhsT=Jrev[:, :], rhs=bstage[:, 4 * a4:4 * a4 + 4, :].rearrange("p a q -> p (a q)"),
                                                         start=True, stop=True)), reads=[t_J, t_bst], writes=[t_ps_mm[i]])
            copy_op(ev_eng(), biasT.rearrange("p a b q -> p (a b q)")[:, 512 * a4:512 * (a4 + 1)], ps_mm[i][:, :], [t_ps_mm[i]], [t_biasT])
        for (Mt, ncol, base, cm) in ((M1e, 528, 16, 4), (M1o, 528, 15, 4), (M2e, 2064, 16, 16), (M2o, 2064, 15, 16)):
            P.op("pool", (lambda e, ncol=ncol: e.memset(mtmp[:, 0:ncol], 0.0)), writes=[t_mtmp])
            P.op("pool", (lambda e, ncol=ncol, base=base, cm=cm: e.affine_select(out=mtmp[:, 0:ncol], in_=mtmp[:, 0:ncol], pattern=[[-1, ncol]],
                                                                                compare_op=ALU.not_equal, fill=1.0, base=base, channel_multiplier=cm)),
                 reads=[t_mtmp], writes=[t_mtmp])
            P.op("dve", (lambda e, Mt=Mt, ncol=ncol: e.tensor_copy(out=Mt[:, :], in_=mtmp[:, 0:ncol])), reads=[t_mtmp], writes=[t_M])

        def perm_lhsT(g, r, Tq):
            if g == 1:
                Tp = Tq % 4
                if r % 2 == 0:
                    s0 = 16 + 128 * Tp - r
                    return M1e[:, s0:s0 + 128]
                s0 = 15 + 128 * Tp - r
                return M1o[:, s0:s0 + 128]
            if r % 2 == 0:
                s0 = 16 + 128 * Tq - r
                return M2e[:, s0:s0 + 128]
            s0 = 15 + 128 * Tq - r
            return M2o[:, s0:s0 + 128]

        barrier()
        apos[0] = 0
        L0 = apos[0]
        xnT = carve_bf(8 * 2048).rearrange("p (k t) -> p k t", k=8)
        xnTs = carve_bf(8 * 16).rearrange("p (k t) -> p k t", k=8)
        NW = 4
        wsl = [carve_bf(8 * 256).rearrange("p (k n) -> p k n", k=8) for _ in range(NW)]
        t_wsl = [Tk() for _ in range(NW)]
        xld = [carve_f(1024) for _ in range(2)]
        t_xld = [Tk(), Tk()]
        xnb = [carve_bf(1024) for _ in range(2)]
        t_xnb = [Tk(), Tk()]
        t_xnT = [Tk() for _ in range(NT)]
        t_xnTs = Tk()
        QT = carve_bf(2 * 2048).rearrange("p (m t) -> p m t", m=2)
        KT = carve_bf(2 * 2048).rearrange("p (m t) -> p m t", m=2)
        QmT = carve_bf(2 * 2048).rearrange("p (m t) -> p m t", m=2)
        t_QT = [[Tk() for _ in range(4)] for _ in range(2)]
        t_KT = [[Tk() for _ in range(4)] for _ in range(2)]
        t_QmT = [[Tk() for _ in range(4)] for _ in range(2)]
        QTs = carve_bf(8 * 16).rearrange("p (m t) -> p m t", m=8)
        KTs = carve_bf(6 * 16).rearrange("p (m t) -> p m t", m=6)
        t_QTs, t_KTs = Tk(), Tk()
        gate = carve_bf(16 * 512).rearrange("p (i n) -> p i n", i=16)
        t_gate = [Tk() for _ in range(NT)]
        gate_s = carve_bf(4 * 512).rearrange("p (b n) -> p b n", b=4)
        t_gate_s = Tk()
        Vg = carve_bf(16 * 4 * 80).rearrange("p (i h d) -> p i h d", i=16, h=4)
        t_Vg = [Tk() for _ in range(NT)]
        Vns = carve_bf(4 * 3 * 4 * 80).rearrange("p (b g h d) -> p b g h d", b=4, g=3, h=4)
        t_Vns = Tk()
        Og = carve_bf(48 * 260).rearrange("p (i n) -> p i n", i=48)
        t_Og = [Tk() for _ in range(48)]
        wst = [carve_f(512) for _ in range(2)]
        t_wst = [Tk(), Tk()]
        wins = carve_f(3 * 512).rearrange("p (g n) -> p g n", g=3)
        t_wins = Tk()
        sbf = [carve_f(256) for _ in range(2)]
        t_sbf = [Tk(), Tk()]
        PT = [carve_bf(256) for _ in range(2)]
        t_PT = [Tk(), Tk()]
        rr.update({"w": 0, "sb": 0, "pt": 0, "wst": 0})

        P.op("pool", lambda e: e.memset(Vg[:, :, :, 64:65], 1.0), writes=t_Vg)
        P.op("pool", lambda e: e.memset(Vns[0:4, :, :, :, 64:65], 1.0), writes=[t_Vns])

        def phaseA(src_dram, gi, lname):
            for i in range(NT + 1):
                s_ = i & 1
                np_ = 128 if i < NT else 16
                if i < NT:
                    P.op("sp", (lambda e, i=i, s_=s_: e.dma_start(out=xld[s_][:, :], in_=src_dram[0][i * 128:(i + 1) * 128, :])),
                         writes=[t_xld[s_]], dma=True)
                else:
                    P.op("sp", (lambda e, s_=s_: e.dma_start(out=xld[s_][0:16, :], in_=src_dram[1])), writes=[t_xld[s_]], dma=True)
                rmsnorm(xld[s_][0:np_, :], np_, gi, xnb[s_][0:np_, :], t_xld[s_], t_xnb[s_])

                def tr(e, s_=s_, np_=np_):
                    for k in range(8):
                        ins = e.transpose(out=ps_t[:, k * 128:k * 128 + np_], in_=xnb[s_][0:np_, k * 128:(k + 1) * 128], identity=ident[0:np_, 0:np_])
                    return ins
                P.op("pe", tr, reads=[t_xnb[s_], t_ident], writes=[t_ps_t])
                if i < NT:
                    copy_op(ev_eng(), xnT[:, :, i * 128:(i + 1) * 128], ps_t[:].rearrange("p (k m) -> p k m", k=8), [t_ps_t], [t_xnT[i]])
                else:
                    copy_op(ev_eng(), xnTs[:, :, :], ps_t[:].rearrange("p (k m) -> p k m", k=8)[:, :, 0:16], [t_ps_t], [t_xnTs])

        phaseA((x_p, x_s), G_PRE[0], "l0")

        chunk_cols = []
        for g in range(3):
            chunk_cols += [("q", g, 256 * g), ("k", g, 768 + 256 * g), ("v", g, 1536 + 256 * g)]
        chunk_cols += [("qm", 0, 2304), ("gate", 0, 2560), ("gate", 1, 2816)]
        wstate = {"loaded": 0}

        def load_w(n):
            if n >= len(chunk_cols) or n < wstate["loaded"]:
                return
            assert n == wstate["loaded"]
            wstate["loaded"] += 1
            c0 = chunk_cols[n][2]
            sl_ = n % NW
            P.op("pool", (lambda e, c0=c0, sl_=sl_: e.dma_start(out=wsl[sl_], in_=w_in_a[:, c0:c0 + 256].rearrange("(k p) n -> p k n", p=128))),
                 writes=[t_wsl[sl_]], dma=True)

        def proj_fm(sl_, dst, t_dst, sdst, t_sdst, smb0):
            for mb in range(2):
                for tg in range(4):
                    i = nxt("mm")

                    def mm(e, mb=mb, tg=tg, i=i):
                        for k in range(8):
                            ins = e.matmul(ps_mm[i][:, :], lhsT=wsl[sl_][:, k, mb * 128:(mb + 1) * 128], rhs=xnT[:, k, tg * 512:(tg + 1) * 512],
                                           start=(k == 0), stop=(k == 7))
                        return ins
                    P.op("pe", mm, reads=[t_wsl[sl_]] + t_xnT[4 * tg:4 * tg + 4], writes=[t_ps_mm[i]])
                    copy_op(ev_eng(), dst[:, mb, tg * 512:(tg + 1) * 512], ps_mm[i][:, :], [t_ps_mm[i]], [t_dst[mb][tg]])
                def mms(e, mb=mb):
                    for k in range(8):
                        ins = e.matmul(ps_x[:, 0:16], lhsT=wsl[sl_][:, k, mb * 128:(mb + 1) * 128], rhs=xnTs[:, k, :], start=(k == 0), stop=(k == 7))
                    return ins
                P.op("pe", mms, reads=[t_wsl[sl_], t_xnTs], writes=[t_ps_x])
                copy_op(ev_eng(), sdst[:, smb0 + mb, :], ps_x[:, 0:16], [t_ps_x], [t_sdst])

        def proj_tm(sl_, tok_ap_fn, reads, evac_fn):
            i = nxt("mm")

            def mm(e, i=i):
                for k in range(8):
                    ins = e.matmul(ps_mm[i][:, 0:256], lhsT=tok_ap_fn(k), rhs=wsl[sl_][:, k, :], start=(k == 0), stop=(k == 7))
                return ins
            P.op("pe", mm, reads=[t_wsl[sl_]] + list(reads), writes=[t_ps_mm[i]])
            evac_fn(ps_mm[i], t_ps_mm[i])

        def proj_tm_s(sl_, M, c0, evac_fn):
            def mm(e):
                for k in range(8):
                    ins = e.matmul(ps_x[0:M, 0:256], lhsT=xnTs[:, k, c0:c0 + M], rhs=wsl[sl_][:, k, :], start=(k == 0), stop=(k == 7))
                return ins
            P.op("pe", mm, reads=[t_wsl[sl_], t_xnTs], writes=[t_ps_x])
            evac_fn()

        def grp_tiles(g):
            d = DIL[g]
            nsb = NT // d
            return [(r, sb_) for r in range(d) for sb_ in range(nsb)]

        def tile_tok_slice(g, r, sb_):
            d = DIL[g]
            base = d * 128 * sb_ + r
            return slice(base, base + d * 127 + 1, d)

        def nat_tiles_of(g, r, sb_):
            d = DIL[g]
            return list(range(d * sb_, d * sb_ + d))

        win_out = (o_w0p, o_w1p, o_w2p)
        win_s_out = (o_w0s, o_w1s, o_w2s)
        load_w(0)
        load_w(1)
        load_w(2)
        def win_rows(g, r):
            dd = DIL[g]
            if dd == 1:
                return win_out[g]
            return win_out[g].rearrange("(j r) c -> r j c", r=dd)[r]

        def do_group(g):
            d = DIL[g]
            nsb = NT // d
            tiles = grp_tiles(g)
            n = 3 * g
            load_w(n + 3)
            proj_fm(n % NW, QT, t_QT, QTs, t_QTs, 2 * g)
            n = 3 * g + 1
            load_w(n + 3)
            proj_fm(n % NW, KT, t_KT, KTs, t_KTs, 2 * g)
            for r in range(d):
                sb_ = nsb - 1
                tsl = tile_tok_slice(g, r, sb_)

                def evk(pst, tps, r=r):
                    ws_ = nxt("wst")
                    P.op("act", lambda e: e.copy(out=wst[ws_][:, 0:256], in_=pst[:, 0:256]), reads=[tps], writes=[t_wst[ws_]])
                    rows = win_rows(g, r)
                    P.op("sp", lambda e: e.dma_start(out=rows[:, 0:256], in_=wst[ws_][:, 0:256]), reads=[t_wst[ws_]], dma=True)
                proj_tm(n % NW, (lambda k, tsl=tsl: xnT[:, k, tsl]), [t_xnT[j] for j in nat_tiles_of(g, r, sb_)], evk)

            def evks():
                P.op("act", lambda e: e.copy(out=wins[0:16, g, 0:256], in_=ps_x[0:16, 0:256]), reads=[t_ps_x], writes=[t_wins])
            proj_tm_s(n % NW, 16, 0, evks)
            n = 3 * g + 2
            load_w(n + 3)
            for gi_, (r, sb_) in enumerate(tiles):
                tsl = tile_tok_slice(g, r, sb_)
                is_win = (sb_ == nsb - 1)

                def evv(pst, tps, gi_=gi_, is_win=is_win, r=r):
                    P.op("dve", lambda e: e.tensor_copy(out=Vg[:, gi_, :, 0:64], in_=pst[:, 0:256].rearrange("p (h d) -> p h d", h=4)),
                         reads=[tps], writes=[t_Vg[gi_]])
                    if is_win:
                        ws_ = nxt("wst")
                        P.op("act", lambda e: e.copy(out=wst[ws_][:, 256:512], in_=pst[:, 0:256]), reads=[tps], writes=[t_wst[ws_]])
                        rows = win_rows(g, r)
                        P.op("sp", lambda e: e.dma_start(out=rows[:, 256:512], in_=wst[ws_][:, 256:512]), reads=[t_wst[ws_]], dma=True)
                proj_tm(n % NW, (lambda k, tsl=tsl: xnT[:, k, tsl]), [t_xnT[j] for j in nat_tiles_of(g, r, sb_)], evv)

            def evvs():
                P.op("act", lambda e: e.copy(out=wins[0:16, g, 256:512], in_=ps_x[0:16, 0:256]), reads=[t_ps_x], writes=[t_wins])
            proj_tm_s(n % NW, 16, 0, evvs)
            P.op("sp", lambda e: e.dma_start(out=win_s_out[g], in_=wins[0:16, g, :]), reads=[t_wins], dma=True)
            for bb in range(4):
                def evvn(bb=bb):
                    P.op("dve", lambda e: e.tensor_copy(out=Vns[0:4, bb, g, :, 0:64], in_=ps_x[0:4, 0:256].rearrange("p (h d) -> p h d", h=4)),
                         reads=[t_ps_x], writes=[t_Vns])
                proj_tm_s(n % NW, 4, 4 * bb, evvn)
            allqk = [x for row in t_QT for x in row] + [x for row in t_KT for x in row]
            for gi_, (r, sb_) in enumerate(tiles):
                qsl = tile_tok_slice(g, r, sb_)
                blocks = ([(0, (r, sb_ - 1))] if sb_ > 0 else []) + [(1, (r, sb_))]
                io = nxt("o")
                for h in range(4):
                    pr, hf = h // 2, h % 2
                    psl = slice(64 * hf, 64 * hf + 64)
                    isx = nxt("s")

                    def qk(e, isx=isx, pr=pr, psl=psl, qsl=qsl, blocks=blocks):
                        for (blk, (kr, ksb)) in blocks:
                            ksl = tile_tok_slice(g, kr, ksb)
                            ins = e.matmul(ps_s[isx][:, blk * 128:(blk + 1) * 128], lhsT=KT[psl, pr, ksl], rhs=QT[psl, pr, qsl], start=True, stop=True)
                        return ins
                    P.op("pe", qk, reads=allqk, writes=[t_ps_s[isx]])
                    b0 = blocks[0][0]
                    csl = slice(b0 * 128, 256)
                    isb = nxt("sb")
                    P.op("dve", (lambda e, isx=isx, isb=isb, csl=csl, h=h, b0=b0: e.scalar_tensor_tensor(
                        out=sbf[isb][:, csl], in0=ps_s[isx][:, csl], scalar=SCALE,
                        in1=biasT[:, g * 4 + h, b0:2, :].rearrange("p b q -> p (b q)"), op0=ALU.mult, op1=ALU.add)),
                        reads=[t_ps_s[isx], t_biasT], writes=[t_sbf[isb]])
                    ipt = nxt("pt")
                    P.op("act", (lambda e, isb=isb, ipt=ipt, csl=csl: e.activation(out=PT[ipt][:, csl], in_=sbf[isb][:, csl], func=AF.Exp)),
                         reads=[t_sbf[isb]], writes=[t_PT[ipt]])

                    def pv(e, ipt=ipt, io=io, h=h, blocks=blocks):
                        nb = len(blocks)
                        for bi, (blk, (kr, ksb)) in enumerate(blocks):
                            kgi = kr * nsb + ksb
                            ins = e.matmul(ps_o[io][:, h * 65:(h + 1) * 65], lhsT=PT[ipt][:, blk * 128:(blk + 1) * 128], rhs=Vg[:, kgi, h, 0:65],
                                           start=(bi == 0), stop=(bi == nb - 1))
                        return ins
                    P.op("pe", pv, reads=[t_PT[ipt]] + [t_Vg[kr * nsb + ksb] for (_, (kr, ksb)) in blocks], writes=[t_ps_o[io]])
                copy_op(ev_eng(), Og[:, 16 * g + gi_, :], ps_o[io][:, 0:260], [t_ps_o[io]], [t_Og[16 * g + gi_]])

        for g in range(3):
            do_group(g)

        n = 9
        load_w(n + 3)
        proj_fm(n % NW, QmT, t_QmT, QTs, t_QTs, 6)
        def do_gate(gc):
            n = 10 + gc
            load_w(n + 3)
            for i in range(NT):
                def evg(pst, tps, i=i, gc=gc):
                    P.op("act", lambda e: e.activation(out=gate[:, i, gc * 256:(gc + 1) * 256], in_=pst[:, 0:256], func=AF.Silu),
                         reads=[tps], writes=[t_gate[i]])
                proj_tm(n % NW, (lambda k, i=i: xnT[:, k, i * 128:(i + 1) * 128]), [t_xnT[i]], evg)
            for bb in range(4):
                def evgs(bb=bb, gc=gc):
                    P.op("act", lambda e: e.activation(out=gate_s[0:4, bb, gc * 256:(gc + 1) * 256], in_=ps_x[0:4, 0:256], func=AF.Silu),
                         reads=[t_ps_x], writes=[t_gate_s])
                proj_tm_s(n % NW, 4, 4 * bb, evgs)
        for gc in range(2):
            do_gate(gc)

        barrier()
        l0_keep = apos[0]
        apos[0] = L0
        wout = carve_bf(4 * 1024).rearrange("p (k n) -> p k n", k=4)
        t_wout = Tk()
        P.op("pool", lambda e: e.dma_start(out=wout, in_=w_out_a.rearrange("(k p) n -> p k n", p=128)), writes=[t_wout], dma=True)
        hbuf = [carve_f(512) for _ in range(2)]
        t_hbuf = [Tk(), Tk()]
        hb = [carve_bf(512) for _ in range(2)]
        t_hb = [Tk(), Tk()]
        hT = [carve_bf(512).rearrange("p (k t) -> p k t", k=4) for _ in range(2)]
        t_hT = [Tk(), Tk()]
        rcp = [carve_f(8) for _ in range(2)]
        t_rcp = [Tk(), Tk()]
        ytmp = [carve_f(1024) for _ in range(2)]
        t_ytmp = [Tk(), Tk()]
        cst = carve_f(2048).rearrange("p (t n) -> p t n", t=4)
        ckb = carve_bf(4 * 256).rearrange("p (t n) -> p t n", t=4)
        cKT = carve_bf(8 * 128).rearrange("p (i n) -> p i n", i=8)
        cV = carve_bf(4 * 4 * 80).rearrange("p (t h d) -> p t h d", t=4, h=4)
        PTz = carve_bf(16)
        PTn = carve_bf(16)
        sbn = carve_f(16)
        t_cst, t_ckb, t_cKT, t_cV, t_PTz, t_PTn, t_sbn = Tk(), Tk(), Tk(), Tk(), Tk(), Tk(), Tk()
        assert apos[0] <= L0 + 8192 + 64 + 4096, apos[0] - L0
        apos[0] = l0_keep
        xres = xld
        t_xres = t_xld
        t_x1 = [Tk() for _ in range(NT + 1)]
        rr.update({"hb": 0})
        CTX = dict(PT=PT, t_PT=t_PT, ytmp=ytmp, t_ytmp=t_ytmp, xres=xres, t_xres=t_xres,
                   cst=cst, ckb=ckb, cKT=cKT, cV=cV, t_cst=t_cst, t_ckb=t_ckb, t_cKT=t_cKT, t_cV=t_cV)
        P.op("pool", lambda e: e.memset(cV[:, :, :, 64:65], 1.0), writes=[t_cV])
        P.op("pool", lambda e: e.memset(PTz[:, :], 0.0), writes=[t_PTz])

        def cross_attn(k_ap_fn, t_k, v_ap_fn, t_v, qT_ap_fn, q_reads, nq, ps_out, t_ps_out):
            PT_, t_PT_ = CTX["PT"], CTX["t_PT"]
            for h in range(4):
                pr, hf = h // 2, h % 2
                psl = slice(64 * hf, 64 * hf + 64)
                isx = nxt("s")

                def qk(e, isx=isx, pr=pr, psl=psl):
                    for blk in range(2):
                        ins = e.matmul(ps_s[isx][:, blk * 128:blk * 128 + nq], lhsT=k_ap_fn(psl, pr, blk), rhs=qT_ap_fn(psl, pr),
                                       start=True, stop=True)
                    return ins
                P.op("pe", qk, reads=[t_k] + list(q_reads), writes=[t_ps_s[isx]])
                ipt = nxt("pt")
                P.op("act", (lambda e, isx=isx, ipt=ipt: e.activation(
                    out=PT_[ipt][:, :].rearrange("p (b q) -> p b q", b=2)[:, :, 0:nq],
                    in_=ps_s[isx][:, 0:256].rearrange("p (b q) -> p b q", b=2)[:, :, 0:nq], func=AF.Exp, scale=SCALE)),
                    reads=[t_ps_s[isx]], writes=[t_PT_[ipt]])

                def pv(e, ipt=ipt, h=h):
                    for blk in range(2):
                        ins = e.matmul(ps_out[0:nq, h * 65:(h + 1) * 65], lhsT=PT_[ipt][:, blk * 128:blk * 128 + nq], rhs=v_ap_fn(blk, h),
                                       start=(blk == 0), stop=(blk == 1))
                    return ins
                P.op("pe", pv, reads=[t_PT_[ipt], t_v], writes=[t_ps_out])

        def normalize_o(ps_in, t_ps_in, nq, dst_ap, t_dst, t_rc, rc):
            v = ps_in[0:nq, 0:260].rearrange("p (h d) -> p h d", h=4)
            P.op("dve", lambda e: e.reciprocal(out=rc[0:nq, 0:4], in_=v[:, :, 64]), reads=[t_ps_in], writes=[t_rc])
            P.op("dve", lambda e: e.tensor_tensor(out=dst_ap.rearrange("p (h d) -> p h d", h=4), in0=v[:, :, 0:64],
                                                  in1=rc[0:nq, 0:4].unsqueeze(2).to_broadcast([nq, 4, 64]), op=ALU.mult),
                 reads=[t_ps_in, t_rc], writes=[t_dst])

        def post_a(nq, hbuf_ap, t_hb_in, gate_ap, t_gate_in, nk, hb_t, t_hb_t, hT_t, t_hT_t, col0):
            P.op("dve", lambda e: e.tensor_tensor(out=hb_t[0:nq, 0:nk * 128], in0=hbuf_ap, in1=gate_ap, op=ALU.mult),
                 reads=[t_hb_in, t_gate_in], writes=[t_hb_t])

            def tr(e):
                for k in range(nk):
                    ins = e.transpose(out=ps_t[:, k * 128:k * 128 + nq], in_=hb_t[0:nq, k * 128:(k + 1) * 128], identity=ident[0:nq, 0:nq])
                return ins
            P.op("pe", tr, reads=[t_hb_t, t_ident], writes=[t_ps_t])
            copy_op(ev_eng(), hT_t[:, 0:nk, col0:col0 + nq], ps_t[:].rearrange("p (k m) -> p k m", k=8)[:, 0:nk, 0:nq], [t_ps_t], [t_hT_t])

        def post_b(gi_post, nq, nk, hT_t, t_hT_t, wout_t, t_wout_t, ys_, x_src_fn, x_dst_fn):
            ytmp_, t_ytmp_, xres_, t_xres_ = CTX["ytmp"], CTX["t_ytmp"], CTX["xres"], CTX["t_xres"]
            for nb in range(2):
                def mm(e, nb=nb):
                    for k in range(nk):
                        ins = e.matmul(ps_mm[nb][0:nq, :], lhsT=hT_t[:, k, 0:nq], rhs=wout_t[:, k, nb * 512:(nb + 1) * 512], start=(k == 0), stop=(k == nk - 1))
                    return ins
                P.op("pe", mm, reads=[t_hT_t, t_wout_t], writes=[t_ps_mm[nb]])
                copy_op("act" if nb == 0 else "dve", ytmp_[ys_][0:nq, nb * 512:(nb + 1) * 512], ps_mm[nb][0:nq, :], [t_ps_mm[nb]], [t_ytmp_[ys_]])
            x_src_fn(ys_)
            rmsnorm(ytmp_[ys_][0:nq, :], nq, gi_post, ytmp_[ys_][0:nq, :], t_ytmp_[ys_], t_ytmp_[ys_])
            P.op("dve", lambda e: e.tensor_tensor(out=xres_[ys_][0:nq, :], in0=xres_[ys_][0:nq, :], in1=ytmp_[ys_][0:nq, :], op=ALU.add),
                 reads=[t_ytmp_[ys_], t_xres_[ys_]], writes=[t_xres_[ys_]])
            x_dst_fn(ys_)

        def do_tile(Tq):
            hs_ = nxt("hb")
            io = nxt("o")
            srcs = [(0, 0, Tq, ident[:, :])]
            srcs += [(1, r, Tq // 4, perm_lhsT(1, r, Tq)) for r in range(4)]
            srcs += [(2, r, 0, perm_lhsT(2, r, Tq)) for r in range(16)]

            def comb(e):
                ns = len(srcs)
                for si, (g, r, sb_, lt) in enumerate(srcs):
                    gi_ = r * (NT // DIL[g]) + sb_
                    ins = e.matmul(ps_o[io][:, 0:260], lhsT=lt, rhs=Og[:, 16 * g + gi_, :], start=(si == 0), stop=(si == ns - 1))
                return ins
            P.op("pe", comb, reads=[t_ident, t_M] + [t_Og[16 * g + r * (NT // DIL[g]) + sb_] for (g, r, sb_, _) in srcs], writes=[t_ps_o[io]])
            normalize_o(ps_o[io], t_ps_o[io], 128, hbuf[hs_][:, 0:256], t_hbuf[hs_], t_rcp[hs_], rcp[hs_])
            io2 = nxt("o")
            cross_attn((lambda psl, pr, blk: KmT[psl, 0, pr, blk * 128:(blk + 1) * 128]), t_KmT[0],
                       (lambda blk, h: Vm[:, 0, blk, h, 0:65]), t_Vm[0],
                       (lambda psl, pr: QmT[psl, pr, Tq * 128:(Tq + 1) * 128]), [x for row in t_QmT for x in row],
                       128, ps_o[io2], t_ps_o[io2])
            normalize_o(ps_o[io2], t_ps_o[io2], 128, hbuf[hs_][:, 256:512], t_hbuf[hs_], t_rcp[hs_], rcp[hs_])
            post_a(128, hbuf[hs_][:, :], t_hbuf[hs_], gate[:, Tq, :], t_gate[Tq], 4, hb[hs_], t_hb[hs_], hT[hs_], t_hT[hs_], 0)

            def xsrc(ys_):
                P.op("sp", lambda e: e.dma_start(out=xres[ys_][:, :], in_=x_p[Tq * 128:(Tq + 1) * 128, :]), writes=[t_xres[ys_]], dma=True)

            def xdst(ys_):
                P.op("sp", lambda e: e.dma_start(out=x1_d[Tq * 128:(Tq + 1) * 128, :], in_=xres[ys_][:, :]), reads=[t_xres[ys_]], writes=[t_x1[Tq]],
                     dma=True, sem_tk=t_xres[ys_])
                if "l0out" in DBG:
                    P.op("sp", lambda e: e.dma_start(out=y_p[Tq * 128:(Tq + 1) * 128, :], in_=xres[ys_][:, :]), reads=[t_xres[ys_]], dma=True)
            post_b(G_POST[0], 128, 4, hT[hs_], t_hT[hs_], wout, t_wout, hs_, xsrc, xdst)

        for Tq in range(NT):
            do_tile(Tq)

        def load_piece(src_ap, nt_):
            cst, ckb, cKT, cV = CTX["cst"], CTX["ckb"], CTX["cKT"], CTX["cV"]
            t_cst, t_ckb, t_cKT, t_cV = CTX["t_cst"], CTX["t_ckb"], CTX["t_cKT"], CTX["t_cV"]
            P.op("sp", lambda e: e.dma_start(out=cst[:, 0:nt_, :], in_=src_ap), writes=[t_cst], dma=True)
            P.op("dve", lambda e: e.tensor_copy(out=ckb[:, 0:nt_, :], in_=cst[:, 0:nt_, 0:256]), reads=[t_cst], writes=[t_ckb])
            P.op("act", lambda e: e.copy(out=cV[:, 0:nt_, :, 0:64], in_=cst[:, 0:nt_, 256:512].rearrange("p t (h d) -> p t h d", h=4)),
                 reads=[t_cst], writes=[t_cV])

            def tr(e):
                for t_ in range(nt_):
                    for pr in range(2):
                        idx = t_ * 2 + pr
                        ins = e.transpose(out=ps_t[:, idx * 128:(idx + 1) * 128], in_=ckb[:, t_, pr * 128:(pr + 1) * 128], identity=ident[:, :])
                return ins
            P.op("pe", tr, reads=[t_ckb, t_ident], writes=[t_ps_t])
            copy_op(ev_eng(), cKT[:, 0:2 * nt_, :], ps_t[:].rearrange("p (k m) -> p k m", k=8)[:, 0:2 * nt_, :], [t_ps_t], [t_cKT])

        def sample_batch(bb):
            io = nxt("o")
            pso, tpso = ps_o[io], t_ps_o[io]
            first = [True]

            def pv_mm(e, out_ap, lhsT, rhs):
                ins = e.matmul(out_ap, lhsT=lhsT, rhs=rhs, start=first[0], stop=False, skip_group_check=True)
                first[0] = False
                return ins
            qs = slice(4 * bb, 4 * bb + 4)
            for g in range(3):
                d = DIL[g]
                if g == 0:
                    load_piece(c_w0[bb].rearrange("(j o) c -> j o c", o=1), 1)
                elif g == 1:
                    load_piece(c_w1[bb].rearrange("(j r) c -> j r c", r=4), 4)
                else:
                    load_piece(c_w2[bb].rearrange("(j r) c -> j r c", r=16)[:, 0:4, :], 4)
                for h in range(4):
                    pr, hf = h // 2, h % 2
                    psl = slice(64 * hf, 64 * hf + 64)
                    isx = nxt("s")
                    if g == 0:
                        def qk(e, isx=isx, pr=pr, psl=psl):
                            e.matmul(ps_s[isx][:, 0:4], lhsT=cKT[psl, pr, :], rhs=QTs[psl, pr, qs], start=True, stop=True)
                            return e.matmul(ps_s[isx][0:4, 8:12], lhsT=KTs[psl, pr, qs], rhs=QTs[psl, pr, qs], start=True, stop=True)
                        P.op("pe", qk, reads=[t_cKT, t_QTs, t_KTs], writes=[t_ps_s[isx]])
                        isb = nxt("sb")
                        P.op("dve", (lambda e, isx=isx, isb=isb, h=h: e.scalar_tensor_tensor(
                            out=sbf[isb][:, 0:4], in0=ps_s[isx][:, 0:4], scalar=SCALE, in1=biasT[:, h, 0, 0:4], op0=ALU.mult, op1=ALU.add)),
                            reads=[t_ps_s[isx], t_biasT], writes=[t_sbf[isb]])
                        P.op("dve", (lambda e, isx=isx, h=h: e.scalar_tensor_tensor(
                            out=sbn[0:4, 0:4], in0=ps_s[isx][0:4, 8:12], scalar=SCALE, in1=biasT[0:4, h, 1, 0:4], op0=ALU.mult, op1=ALU.add)),
                            reads=[t_ps_s[isx], t_biasT], writes=[t_sbn])
                        ipt = nxt("pt")
                        P.op("act", (lambda e, isb=isb, ipt=ipt: e.activation(out=PT[ipt][:, 0:4], in_=sbf[isb][:, 0:4], func=AF.Exp)),
                             reads=[t_sbf[isb]], writes=[t_PT[ipt]])
                        P.op("act", lambda e: e.activation(out=PTn[0:4, 0:4], in_=sbn[0:4, 0:4], func=AF.Exp), reads=[t_sbn], writes=[t_PTn])

                        def pv(e, ipt=ipt, h=h):
                            pv_mm(e, pso[0:4, h * 65:(h + 1) * 65], PT[ipt][:, 0:4], cV[:, 0, h, 0:65])
                            return pv_mm(e, pso[0:4, h * 65:(h + 1) * 65], PTn[0:4, 0:4], Vns[0:4, bb, 0, h, 0:65])
                        P.op("pe", pv, reads=[t_PT[ipt], t_PTn, t_cV, t_Vns], writes=[tpso])
                    else:
                        def qk(e, isx=isx, pr=pr, psl=psl, g=g):
                            for t_ in range(4):
                                e.matmul(ps_s[isx][:, t_:t_ + 1], lhsT=cKT[psl, 2 * t_ + pr, :], rhs=QTs[psl, 2 * g + pr, 4 * bb + t_:4 * bb + t_ + 1],
                                         start=True, stop=True)
                            return e.matmul(ps_s[isx][0:4, 8:12], lhsT=KTs[psl, 2 * g + pr, qs], rhs=QTs[psl, 2 * g + pr, qs], start=True, stop=True)
                        P.op("pe", qk, reads=[t_cKT, t_QTs, t_KTs], writes=[t_ps_s[isx]])
                        P.op("act", (lambda e, isx=isx, h=h, g=g: e.activation(out=PTz[:, 0:16:5], in_=ps_s[isx][:, 0:4], func=AF.Exp,
                                                                              bias=biasT[:, g * 4 + h, 0, 0:1], scale=SCALE)),
                             reads=[t_ps_s[isx], t_biasT], writes=[t_PTz])
                        P.op("dve", (lambda e, isx=isx, h=h, g=g: e.scalar_tensor_tensor(
                            out=sbn[0:4, 0:4], in0=ps_s[isx][0:4, 8:12], scalar=SCALE, in1=biasT[0:4, g * 4 + h, 1, 0:4], op0=ALU.mult, op1=ALU.add)),
                            reads=[t_ps_s[isx], t_biasT], writes=[t_sbn])
                        P.op("act", lambda e: e.activation(out=sbn[0:4, 4:8], in_=sbn[0:4, 0:4], func=AF.Exp), reads=[t_sbn], writes=[t_sbn])
                        P.op("dve", lambda e: e.tensor_tensor(out=PTn[0:4, 0:4], in0=sbn[0:4, 4:8], in1=identf[0:4, 0:4], op=ALU.mult),
                             reads=[t_sbn, t_ident], writes=[t_PTn])

                        def pv(e, h=h, g=g):
                            for t_ in range(4):
                                pv_mm(e, pso[0:4, h * 65:(h + 1) * 65], PTz[:, 4 * t_:4 * t_ + 4], cV[:, t_, h, 0:65])
                            return pv_mm(e, pso[0:4, h * 65:(h + 1) * 65], PTn[0:4, 0:4], Vns[0:4, bb, g, h, 0:65])
                        P.op("pe", pv, reads=[t_PTz, t_PTn, t_cV, t_Vns], writes=[tpso])
            hs_ = 0
            normalize_o(pso, tpso, 4, hbuf[hs_][0:4, 0:256], t_hbuf[hs_], t_rcp[hs_], rcp[hs_])
            load_piece(c_mem[0, bb].rearrange("(b j) c -> j b c", b=2), 2)
            io2 = nxt("o")
            cross_attn((lambda psl, pr, blk: cKT[psl, 2 * blk + pr, :]), t_cKT, (lambda blk, h: cV[:, blk, h, 0:65]), t_cV,
                       (lambda psl, pr: QTs[psl, 6 + pr, qs]), [t_QTs], 4, ps_o[io2], t_ps_o[io2])
            normalize_o(ps_o[io2], t_ps_o[io2], 4, hbuf[hs_][0:4, 256:512], t_hbuf[hs_], t_rcp[hs_], rcp[hs_])
            post_a(4, hbuf[hs_][0:4, :], t_hbuf[hs_], gate_s[0:4, bb, :], t_gate_s, 4, hb[hs_], t_hb[hs_], hT[1], t_hT[1], 4 * bb)

        for bb in range(0 if "nosample" in DBG else 4):
            sample_batch(bb)

        def xsrc_s(ys_):
            P.op("sp", lambda e: e.dma_start(out=xres[ys_][0:16, :], in_=x_s), writes=[t_xres[ys_]], dma=True)

        def xdst_s(ys_):
            P.op("sp", lambda e: e.dma_start(out=x1s_d, in_=xres[ys_][0:16, :]), reads=[t_xres[ys_]], writes=[t_x1[NT]], dma=True, sem_tk=t_xres[ys_])
            if "l0out" in DBG:
                P.op("sp", lambda e: e.dma_start(out=y_s, in_=xres[ys_][0:16, :]), reads=[t_xres[ys_]], dma=True)
        if "nosample" not in DBG:
            post_b(G_POST[0], 16, 4, hT[1], t_hT[1], wout, t_wout, 1, xsrc_s, xdst_s)
        print('n_ops', len(P.ops))
        if 'dump' in DBG:
            for _i, _o in enumerate(P.ops):
                print('OP', _i, _o.eng, _o.line, 'dma' if _o.is_dma else '')

        if STAGE >= 2:
            barrier()
            apos[0] = 0
            atop[0] = ARENA_WORDS
            load_gains([(0, norm_pre, 1), (1, norm_post, 1)])
            Wb = carve_bf(8 * B_IN).rearrange("p (k n) -> p k n", k=8)
            t_Wb = Tk()
            for c0 in range(0, B_IN, 464):
                P.op("pool", (lambda e, c0=c0: e.dma_start(out=Wb[:, :, c0:c0 + 464], in_=w_in_b[:, c0:c0 + 464].rearrange("(k p) n -> p k n", p=128))),
                     writes=[t_Wb], dma=True)
            Wo = carve_bf(8 * 1024).rearrange("p (k n) -> p k n", k=8)
            t_Wo = Tk()
            P.op("pool", lambda e: e.dma_start(out=Wo, in_=w_out_b.rearrange("(k p) n -> p k n", p=128)), writes=[t_Wo], dma=True)
            wup = carve_bf(768)
            aup = carve_bf(768)
            t_lora = Tk()
            P.op("pool", lambda e: e.dma_start(out=wup[0:64, :], in_=r_wup), writes=[t_lora], dma=True)
            P.op("pool", lambda e: e.dma_start(out=aup[64:128, :], in_=r_aup), writes=[t_lora], dma=True)
            par = carve_f(64)
            t_par = Tk()
            P_MU, P_W0, P_A0, P_KK, P_KA, P_OMKA, P_RK = 0, 19, 25, 31, 37, 43, 49
            for (src, c0, nm) in ((r_mu, P_MU, 19), (r_w0, P_W0, 6), (r_a0, P_A0, 6), (r_kk, P_KK, 6), (r_ka, P_KA, 6), (r_rk, P_RK, 6)):
                P.op("sp", (lambda e, src=src, c0=c0, nm=nm: e.dma_start(out=par[:, c0:c0 + nm], in_=src.rearrange("o (m p) -> p (o m)", p=128),
                                                                         allow_slow_non_contiguous=True)), writes=[t_par], dma=True)
            P.op("dve", lambda e: e.tensor_scalar(out=par[:, P_OMKA:P_OMKA + 6], in0=par[:, P_KA:P_KA + 6], scalar1=-1.0, scalar2=1.0, op0=ALU.mult, op1=ALU.add),
                 reads=[t_par], writes=[t_par])
            lnw_b = carve_f(768)
            lnb_b = carve_f(768)
            t_ln = Tk()
            for (dst, src) in ((lnw_b, r_lnw), (lnb_b, r_lnb)):
                apb = bass.AP(tensor=src.tensor, offset=0, ap=[[0, 128], [1, 768]])
                P.op("sp", (lambda e, dst=dst, apb=apb: e.dma_start(out=dst, in_=apb)), writes=[t_ln], dma=True)
            SA, SB, SC, SD, SE, SF, SG = [carve_f(768).rearrange("p (m j) -> p m j", m=6) for _ in range(7)]
            t_S = {k_: Tk() for k_ in "ABCDEFG"}
            t_mt1 = t_S["A"]
            mtmp1 = SA[:, :, :].rearrange("p m j -> p (m j)")
            ML = carve_bf(512).rearrange("p (h i) -> p h i", h=4)
            MU = carve_bf(512).rearrange("p (h i) -> p h i", h=4)
            MUi = carve_bf(512).rearrange("p (h i) -> p h i", h=4)
            Irep = carve_bf(512).rearrange("p (h i) -> p h i", h=4)
            rmask = carve_bf(768).rearrange("p (m j) -> p m j", m=6)
            bd = carve_f(128)
            hsel = carve_bf(2)
            t_cst1 = Tk()
            for (Mt, cm, step, cmp_) in ((ML, 1, -1, ALU.is_gt), (MU, -1, 1, ALU.is_gt), (MUi, -1, 1, ALU.is_ge), (Irep, 1, -1, ALU.is_equal)):
                P.op("pool", lambda e: e.memset(mtmp1[:, 0:512], 1.0), writes=[t_mt1])

                def mk(e, cm=cm, step=step, cmp_=cmp_):
                    v = mtmp1[:, 0:512].rearrange("p (h i) -> p h i", h=4)
                    return e.affine_select(out=v, in_=v, pattern=[[0, 4], [step, 128]], compare_op=cmp_, fill=0.0, base=0, channel_multiplier=cm)
                P.op("pool", mk, reads=[t_mt1], writes=[t_mt1])
                P.op("dve", (lambda e, Mt=Mt: e.tensor_copy(out=Mt, in_=mtmp1[:, 0:512].rearrange("p (h i) -> p h i", h=4))), reads=[t_mt1], writes=[t_cst1])

            def mk2(e):
                e.memset(rmask[:, :, :], 1.0)
                e.memset(rmask[:, :, 0:1], 0.0)
                e.memset(bd[:, :], 0.0)
                e.memset(bd[0:64, 0:64], 1.0)
                e.memset(bd[64:128, 64:128], 1.0)
                e.memset(hsel[:, :], 0.0)
                e.memset(hsel[0:64, 0:1], 1.0)
                return e.memset(hsel[64:128, 1:2], 1.0)
            P.op("pool", mk2, writes=[t_cst1])

            xld1 = carve_f(1024)
            xnb1 = carve_bf(1024)
            xnTc = carve_bf(8 * 128).rearrange("p (k t) -> p k t", k=8)
            xnTs1 = carve_bf(8 * 16).rearrange("p (k t) -> p k t", k=8)
            t_xld1, t_xnb1, t_xnTc, t_xnTs1 = Tk(), Tk(), Tk(), Tk()
            cols = carve_f(19 * 129).rearrange("p (m j) -> p m j", m=19)
            t_cols = Tk()
            lastc = carve_f(20)
            t_xs = t_cols
            QmTc = carve_bf(2 * 128).rearrange("p (m j) -> p m j", m=2)
            t_QmTc = Tk()
            gt = carve_bf(1024)
            t_gt = Tk()
            lw = carve_bf(128)
            t_lw = Tk()
            outs_w0 = apos[0]
            t_o = {k_: Tk() for k_ in ("AT", "BT", "KT", "KH", "BH", "RT", "RKT", "XV")}
            BT, KT1, KH, BH, RKT, XV = [carve_bf(768).rearrange("p (m j) -> p m j", m=6) for _ in range(6)]
            AT2 = carve_bf(12 * 128).rearrange("p (h j) -> p h j", h=12)
            RT2 = carve_bf(12 * 128).rearrange("p (h j) -> p h j", h=12)
            P.op("pool", lambda e: e.memset(AT2[:, :, :], 0.0), writes=[t_o["AT"]])
            P.op("pool", lambda e: e.memset(RT2[:, :, :], 0.0), writes=[t_o["RT"]])
            Vt = carve_bf(768)
            Kh = carve_bf(768)
            Bh = carve_bf(768)
            t_Vt, t_Kh, t_Bh = Tk(), Tk(), Tk()
            Lh = [carve_bf(512).rearrange("p (h i) -> p h i", h=4) for _ in range(3)]
            Xh = [carve_bf(512).rearrange("p (h i) -> p h i", h=4) for _ in range(3)]
            Qh = [carve_bf(512).rearrange("p (h i) -> p h i", h=4) for _ in range(3)]
            t_Lh, t_Xh, t_Qh = [Tk() for _ in range(3)], [Tk() for _ in range(3)], [Tk() for _ in range(3)]
            Qf = carve_bf(12 * 128).rearrange("p (h i) -> p h i", h=12)
            Aak = carve_bf(12 * 128).rearrange("p (h i) -> p h i", h=12)
            Ark = carve_bf(12 * 128).rearrange("p (h i) -> p h i", h=12)
            Arb = carve_bf(12 * 128).rearrange("p (h i) -> p h i", h=12)
            t_Qf, t_Aak, t_Ark, t_Arb = Tk(), Tk(), Tk(), Tk()
            zb_w0 = apos[0]
            Zb = carve_bf(768)
            Ub = carve_bf(768)
            t_Zb, t_Ub = Tk(), Tk()
            Yf = SD[:, :, :].rearrange("p m j -> p (m j)")
            t_Yf = t_S["D"]
            St = carve_f(384).rearrange("p (m v) -> p m v", m=6)
            Sbf = carve_bf(384).rearrange("p (m v) -> p m v", m=6)
            t_St = Tk()
            wc = carve_f(8)
            t_wc = Tk()
            gsm = carve_f(96)
            t_gsm = Tk()
            hbuf1 = carve_f(1024)
            hb1 = carve_bf(1024)
            hT1 = carve_bf(8 * 128).rearrange("p (k t) -> p k t", k=8)
            t_hbuf1, t_hb1, t_hT1 = Tk(), Tk(), Tk()
            rcp1 = carve_f(8)
            t_rcp1 = Tk()
            xres1 = carve_f(1024)
            t_xres1 = Tk()
            PT1 = [carve_bf(256) for _ in range(2)]
            t_PT1 = [Tk(), Tk()]
            svst = arena[:, zb_w0:zb_w0 + 768].rearrange("p (h k) -> p h k", h=12)
            t_svst = Tk()
            print("L1 arena words", apos[0])
            C0 = 0.6065306597126334
            psq = [ps_s[0], ps_s[1], ps_o[0], ps_o[1]]
            t_psq = [t_ps_s[0], t_ps_s[1], t_ps_o[0], t_ps_o[1]]
            rr.update({"q": 0})

            def nq4():
                rr["q"] += 1
                return rr["q"] % 4

            def tt(eng, out, in0, in1, op, reads, writes):
                P.op(eng, lambda e: e.tensor_tensor(out=out, in0=in0, in1=in1, op=op), reads=reads, writes=writes)

            def bc6(col0, C):
                return par[:, col0:col0 + 6].unsqueeze(2).to_broadcast([128, 6, C])

            def rwkv_chunk(C, xn_ap, t_xn, mode, idx):
                first = (idx == 0) if mode == "p" else True
                last = (idx == NT - 1) if mode == "p" else True
                groups = [(16, 3), (0, 4), (4, 4), (8, 4), (12, 4), (19, 2)]
                for (m0, nm) in groups:
                    i = nxt("mm")

                    def mm(e, m0=m0, nm=nm, i=i):
                        for mi in range(nm):
                            m = m0 + mi
                            for k in range(8):
                                ins = e.matmul(ps_mm[i][:, mi * 128:mi * 128 + C], lhsT=Wb[:, k, m * 128:(m + 1) * 128], rhs=xn_ap(k),
                                               start=(k == 0), stop=(k == 7))
                        return ins
                    P.op("pe", mm, reads=[t_Wb, t_xn], writes=[t_ps_mm[i]])
                    src = ps_mm[i][:, 0:nm * 128].rearrange("p (m j) -> p m j", m=nm)[:, :, 0:C]
                    if m0 < 19:
                        copy_op("act", cols[:, m0:m0 + nm, 1:1 + C], src, [t_ps_mm[i]], [t_cols])
                    else:
                        copy_op("act", QmTc[:, :, 0:C], src, [t_ps_mm[i]], [t_QmTc])
                for gc in range(2):
                    i = nxt("mm")

                    def mmg(e, gc=gc, i=i):
                        for k in range(8):
                            ins = e.matmul(ps_mm[i][0:C, :], lhsT=xn_ap(k), rhs=Wb[:, k, 2688 + gc * 512:2688 + (gc + 1) * 512], start=(k == 0), stop=(k == 7))
                        return ins
                    P.op("pe", mmg, reads=[t_Wb, t_xn], writes=[t_ps_mm[i]])
                    P.op("act", (lambda e, gc=gc, i=i: e.activation(out=gt[0:C, gc * 512:(gc + 1) * 512], in_=ps_mm[i][0:C, :], func=AF.Silu)),
                         reads=[t_ps_mm[i]], writes=[t_gt])
                t_lastc = Tk()
                P.op("act", lambda e: e.copy(out=lastc[:, 0:19], in_=cols[:, :, C]), reads=[t_cols], writes=[t_lastc])
                if last:
                    dst = (o_shp if mode == "p" else o_shs[idx:idx + 1, :]).rearrange("o (m p) -> p (o m)", p=128)
                    P.op("sp", lambda e: e.dma_start(out=dst, in_=lastc[:, 0:19], allow_slow_non_contiguous=True), reads=[t_lastc], dma=True)
                for (m0, nm) in ((0, 6), (6, 6), (12, 6), (18, 1)):
                    cur = cols[:, m0:m0 + nm, 1:1 + C]
                    prv = cols[:, m0:m0 + nm, 0:C]
                    tmp = SG[:, 0:nm, 0:C]
                    tt("dve", tmp, prv, cur, ALU.subtract, [t_cols], [t_S["G"]])
                    tt("dve", tmp, tmp, par[:, P_MU + m0:P_MU + m0 + nm].unsqueeze(2).to_broadcast([128, nm, C]), ALU.mult, [t_S["G"], t_par], [t_S["G"]])
                    tt("dve", cur, tmp, cur, ALU.add, [t_S["G"], t_cols], [t_cols])
                P.op("act", lambda e: e.copy(out=cols[:, :, 0], in_=lastc[:, 0:19]), reads=[t_lastc, t_cols], writes=[t_cols])

                class _XS:
                    def __getitem__(self, key):
                        p_, m_, j_ = key
                        assert j_ == slice(0, C)
                        return cols[p_, m_, 1:1 + C]
                xs = _XS()
                xr, xk, xv_ = xs[:, 0:6, 0:C], xs[:, 6:12, 0:C], xs[:, 12:18, 0:C]
                P.op("act", lambda e: e.activation(out=lw[0:64, 0:C], in_=xs[0:64, 18, 0:C], func=AF.Tanh), reads=[t_xs], writes=[t_lw])
                P.op("dve", lambda e: e.tensor_copy(out=lw[64:128, 0:C], in_=xs[64:128, 18, 0:C]), reads=[t_xs], writes=[t_lw])
                sigw, asig = SA[:, :, 0:C], SB[:, :, 0:C]
                for (which, wt, rows, pcol, dstS, tS) in (("w", wup, slice(0, 64), P_W0, SA, "A"), ("a", aup, slice(64, 128), P_A0, SB, "B")):
                    for (p0, np_) in ((0, 4), (4, 2)):
                        q = nq4()

                        def mml(e, wt=wt, rows=rows, p0=p0, np_=np_, q=q):
                            for pi in range(np_):
                                p = p0 + pi
                                ins = e.matmul(psq[q][:, pi * 128:pi * 128 + C], lhsT=wt[rows, p * 128:(p + 1) * 128], rhs=lw[rows, 0:C], start=True, stop=True)
                            return ins
                        P.op("pe", mml, reads=[t_lora, t_lw], writes=[t_psq[q]])
                        for pi in range(np_):
                            p = p0 + pi
                            P.op("act", (lambda e, pi=pi, p=p, q=q, pcol=pcol, dstS=dstS: e.activation(
                                out=dstS[:, p, 0:C], in_=psq[q][:, pi * 128:pi * 128 + C], func=AF.Sigmoid, bias=par[:, pcol + p:pcol + p + 1], scale=1.0)),
                                reads=[t_psq[q], t_par], writes=[t_S[tS]])
                cs = SC[:, :, 0:C]
                if C == 128:
                    P.op("dve", lambda e: e.tensor_tensor_scan(out=SC[:, :, :].rearrange("p m j -> p (m j)"), data0=rmask[:, :, :].rearrange("p m j -> p (m j)"),
                                                               data1=SA[:, :, :].rearrange("p m j -> p (m j)"), initial=0.0, op0=ALU.mult, op1=ALU.add),
                         reads=[t_S["A"], t_cst1], writes=[t_S["C"]])
                else:
                    for p in range(6):
                        P.op("dve", (lambda e, p=p: e.tensor_tensor_scan(out=SC[:, p, 0:C], data0=rmask[:, p, 0:C], data1=SA[:, p, 0:C], initial=0.0,
                                                                         op0=ALU.mult, op1=ALU.add)), reads=[t_S["A"], t_cst1], writes=[t_S["C"]])
                csC = SC[:, :, C - 1:C]
                eP, eH, eA, eN = SE[:, :, 0:C], SD[:, :, 0:C], SA[:, :, 0:C], SC[:, :, 0:C]
                P.op("act", lambda e: e.activation(out=eP, in_=cs, func=AF.Exp, scale=-C0), reads=[t_S["C"]], writes=[t_S["E"]])
                tt("dve", eH, csC.to_broadcast([128, 6, C]), cs, ALU.subtract, [t_S["C"]], [t_S["D"]])
                P.op("act", lambda e: e.activation(out=eH, in_=eH, func=AF.Exp, scale=-C0), reads=[t_S["D"]], writes=[t_S["D"]])
                P.op("act", lambda e: e.activation(out=wc[:, 0:6], in_=SC[:, :, C - 1], func=AF.Exp, scale=-C0), reads=[t_S["C"]], writes=[t_wc])
                tt("dve", eA, cs, sigw, ALU.subtract, [t_S["C"], t_S["A"]], [t_S["A"]])
                P.op("act", lambda e: e.activation(out=eA, in_=eA, func=AF.Exp, scale=-C0), reads=[t_S["A"]], writes=[t_S["A"]])
                P.op("act", lambda e: e.activation(out=eN, in_=cs, func=AF.Exp, scale=C0), reads=[t_S["C"], t_S["D"], t_S["E"], t_S["A"], t_wc], writes=[t_S["C"]])
                kk, g_ = SF[:, :, 0:C], SG[:, :, 0:C]
                tt("dve", kk, xk, bc6(P_KK, C), ALU.mult, [t_xs, t_par], [t_S["F"]])
                tt("dve", g_, kk, kk, ALU.mult, [t_S["F"]], [t_S["G"]])
                qa, qb_ = nq4(), nq4()

                def mmn(e):
                    e.matmul(psq[qa][:, 0:4 * 128].rearrange("p (m j) -> p m j", m=4)[:, :, 0:C], lhsT=bd[:, :], rhs=SG[:, 0:4, 0:C], start=True, stop=True)
                    return e.matmul(psq[qb_][:, 0:2 * 128].rearrange("p (m j) -> p m j", m=2)[:, :, 0:C], lhsT=bd[:, :], rhs=SG[:, 4:6, 0:C], start=True, stop=True)
                P.op("pe", mmn, reads=[t_S["G"], t_cst1], writes=[t_psq[qa], t_psq[qb_]])
                P.op("act", lambda e: e.activation(out=SG[:, 0:4, 0:C], in_=psq[qa][:, 0:512].rearrange("p (m j) -> p m j", m=4)[:, :, 0:C], func=AF.Sqrt),
                     reads=[t_psq[qa]], writes=[t_S["G"]])
                P.op("act", lambda e: e.activation(out=SG[:, 4:6, 0:C], in_=psq[qb_][:, 0:256].rearrange("p (m j) -> p m j", m=2)[:, :, 0:C], func=AF.Sqrt),
                     reads=[t_psq[qb_]], writes=[t_S["G"]])
                P.op("dve", lambda e: e.tensor_scalar_max(out=g_, in0=g_, scalar1=1e-12), reads=[t_S["G"]], writes=[t_S["G"]])
                P.op("dve", lambda e: e.reciprocal(out=g_, in_=g_), reads=[t_S["G"]], writes=[t_S["G"]])
                tt("dve", kk, kk, g_, ALU.mult, [t_S["F"], t_S["G"]], [t_S["F"]])
                for hf_ in range(2):
                    rs_ = slice(64 * hf_, 64 * hf_ + 64)
                    P.op("dve", (lambda e, hf_=hf_, rs_=rs_: e.scalar_tensor_tensor(out=AT2[rs_, hf_:12:2, 0:C], in0=SF[rs_, :, 0:C], scalar=-1.0,
                                                                                   in1=SA[rs_, :, 0:C], op0=ALU.mult, op1=ALU.mult)),
                         reads=[t_S["F"], t_S["A"]], writes=[t_o["AT"]])
                tt("dve", g_, kk, asig, ALU.mult, [t_S["F"], t_S["B"]], [t_S["G"]])
                tt("dve", BT[:, :, 0:C], g_, eN, ALU.mult, [t_S["G"], t_S["C"]], [t_o["BT"]])
                tt("dve", BH[:, :, 0:C], g_, eH, ALU.mult, [t_S["G"], t_S["D"]], [t_o["BH"]])
                tt("dve", asig, asig, bc6(P_KA, C), ALU.mult, [t_S["B"], t_par, t_S["G"]], [t_S["B"]])
                tt("dve", asig, asig, bc6(P_OMKA, C), ALU.add, [t_S["B"], t_par], [t_S["B"]])
                tt("dve", asig, asig, xk, ALU.mult, [t_S["B"], t_xs], [t_S["B"]])
                tt("dve", KT1[:, :, 0:C], asig, eN, ALU.mult, [t_S["B"], t_S["C"]], [t_o["KT"]])
                tt("dve", KH[:, :, 0:C], asig, eH, ALU.mult, [t_S["B"], t_S["D"]], [t_o["KH"]])
                tt("dve", kk, xr, bc6(P_RK, C), ALU.mult, [t_xs, t_par, t_o["AT"], t_S["G"]], [t_S["F"]])
                tt("dve", RKT[:, :, 0:C], kk, asig, ALU.mult, [t_S["F"], t_S["B"]], [t_o["RKT"]])
                for hf_ in range(2):
                    rs_ = slice(64 * hf_, 64 * hf_ + 64)
                    tt("dve", RT2[rs_, hf_:12:2, 0:C], cols[rs_, 0:6, 1:1 + C], SE[rs_, :, 0:C], ALU.mult, [t_xs, t_S["E"]], [t_o["RT"]])
                P.op("act", lambda e: e.copy(out=XV[:, :, 0:C], in_=xv_), reads=[t_xs], writes=[t_o["XV"]])
                for (srcT, tsrc, dstT, tdst) in ((XV, "XV", Vt, t_Vt), (KH, "KH", Kh, t_Kh), (BH, "BH", Bh, t_Bh)):
                    def tr(e, srcT=srcT):
                        for p in range(6):
                            ins = e.transpose(out=ps_t[0:C, p * 128:(p + 1) * 128], in_=srcT[:, p, 0:C], identity=ident[:, :])
                        return ins
                    P.op("pe", tr, reads=[t_o[tsrc], t_ident], writes=[t_ps_t])
                    copy_op(ev_eng(), dstT[0:C, :], ps_t[0:C, 0:768], [t_ps_t], [tdst])
                nlev = max(1, int(math.ceil(math.log2(C))))

                def pair_mm(q, l_fn, r_fn, hg, reads):
                    def f(e):
                        for hi in range(4):
                            h = 4 * hg + hi
                            ins = e.matmul(psq[q][0:C, hi * 128:hi * 128 + C], lhsT=l_fn(h), rhs=r_fn(h), start=True, stop=True)
                        return ins
                    P.op("pe", f, reads=reads, writes=[t_psq[q]])

                A2 = lambda h: AT2[:, h, 0:C]
                R2 = lambda h: RT2[:, h, 0:C]
                Bp = lambda h: BT[:, h // 2, 0:C]
                Kp = lambda h: KT1[:, h // 2, 0:C]

                def pv4(q):
                    return psq[q][0:C, :].rearrange("p (h i) -> p h i", h=4)[:, :, 0:C]

                def sq_mm(q, lT, rT, reads, acc_ident_rhs=None):
                    def f(e):
                        for hi in range(4):
                            if acc_ident_rhs is not None:
                                e.matmul(psq[q][0:C, hi * 128:hi * 128 + C], lhsT=ident[0:C, 0:C], rhs=acc_ident_rhs[0:C, hi, 0:C], start=True, stop=False)
                            ins = e.matmul(psq[q][0:C, hi * 128:hi * 128 + C], lhsT=lT[0:C, hi, 0:C], rhs=rT[0:C, hi, 0:C],
                                           start=(acc_ident_rhs is None), stop=True)
                        return ins
                    P.op("pe", f, reads=reads + [t_ident], writes=[t_psq[q]])

                for hg in range(3):
                    q = nq4()
                    pair_mm(q, A2, Bp, hg, [t_o["AT"], t_o["BT"]])
                    tt("dve", Lh[hg][0:C, :, 0:C], pv4(q), ML[0:C, :, 0:C], ALU.mult, [t_psq[q], t_cst1], [t_Lh[hg]])
                    q = nq4()
                    pair_mm(q, Bp, A2, hg, [t_o["AT"], t_o["BT"]])
                    tt("dve", Xh[hg][0:C, :, 0:C], pv4(q), MU[0:C, :, 0:C], ALU.mult, [t_psq[q], t_cst1], [t_Xh[hg]])
                    tt("pool", Qh[hg][0:C, :, 0:C], Xh[hg][0:C, :, 0:C], Irep[0:C, :, 0:C], ALU.add, [t_Xh[hg], t_cst1], [t_Qh[hg]])
                for j in range(nlev - 1):
                    need_x = (j + 1 < nlev - 1)
                    for hg in range(3):
                        q = nq4()
                        sq_mm(q, Xh[hg], Lh[hg], [t_Xh[hg], t_Lh[hg]])
                        qx = None
                        if need_x:
                            qx = nq4()
                            sq_mm(qx, Lh[hg], Xh[hg], [t_Xh[hg], t_Lh[hg]])
                        copy_op("act", Lh[hg][0:C, :, 0:C], pv4(q), [t_psq[q]], [t_Lh[hg]])
                        if need_x:
                            copy_op("dve", Xh[hg][0:C, :, 0:C], pv4(qx), [t_psq[qx]], [t_Xh[hg]])
                    for hg in range(3):
                        q = nq4()
                        sq_mm(q, Lh[hg], Qh[hg], [t_Lh[hg], t_Qh[hg]], acc_ident_rhs=Qh[hg])
                        if j == nlev - 2:
                            copy_op("act", Qf[0:C, 4 * hg:4 * hg + 4, 0:C], pv4(q), [t_psq[q]], [t_Qf])
                        else:
                            copy_op("act", Qh[hg][0:C, :, 0:C], pv4(q), [t_psq[q]], [t_Qh[hg]])
                for hg in range(3):
                    if nlev == 1:
                        copy_op("act", Qf[0:C, 4 * hg:4 * hg + 4, 0:C], Qh[hg][0:C, :, 0:C], [t_Qh[hg]], [t_Qf])
                    q = nq4()
                    pair_mm(q, Kp, A2, hg, [t_o["KT"], t_o["AT"]])
                    tt("dve", Aak[0:C, 4 * hg:4 * hg + 4, 0:C], pv4(q), MU[0:C, :, 0:C], ALU.mult, [t_psq[q], t_cst1], [t_Aak])
                    q = nq4()
                    pair_mm(q, Kp, R2, hg, [t_o["KT"], t_o["RT"]])
                    tt("dve", Ark[0:C, 4 * hg:4 * hg + 4, 0:C], pv4(q), MUi[0:C, :, 0:C], ALU.mult, [t_psq[q], t_cst1], [t_Ark])
                    q = nq4()
                    pair_mm(q, Bp, R2, hg, [t_o["BT"], t_o["RT"]])
                    tt("dve", Arb[0:C, 4 * hg:4 * hg + 4, 0:C], pv4(q), MUi[0:C, :, 0:C], ALU.mult, [t_psq[q], t_cst1], [t_Arb])
                if first:
                    if mode == "p":
                        P.op("pool", lambda e: e.memset(St[:, :, :], 0.0), writes=[t_St])
                        P.op("pool", lambda e: e.memset(Sbf[:, :, :], 0.0), writes=[t_St])
                    else:
                        P.op("sp", lambda e: e.dma_start(out=svst[0:64, :, :], in_=s_wkv[idx].rearrange("h v k -> v h k")), writes=[t_svst, t_Zb, t_Ub], dma=True, sem_tk=t_svst)

                        def trs(e):
                            for p in range(6):
                                ins = e.transpose(out=ps_x[:, p * 64:(p + 1) * 64], in_=svst[0:64, 2 * p:2 * p + 2, :].rearrange("v h k -> v (h k)"),
                                                  identity=identf[0:64, 0:64])
                            return ins
                        P.op("pe", trs, reads=[t_svst, t_ident], writes=[t_ps_x])
                        P.op("act", lambda e: e.copy(out=St[:, :, :], in_=ps_x[:, 0:384].rearrange("p (m v) -> p m v", m=6)), reads=[t_ps_x], writes=[t_St])
                        P.op("dve", lambda e: e.tensor_copy(out=Sbf[:, :, :], in_=ps_x[:, 0:384].rearrange("p (m v) -> p m v", m=6)), reads=[t_ps_x], writes=[t_St])
                def head_cols(h):
                    return slice(h * 64, (h + 1) * 64)

                def seq_mm(name, fn_terms, reads, dst_banks):
                    def f(e):
                        for h in range(12):
                            bank, hc = (dst_banks[0], h) if h < 8 else (dst_banks[1], h - 8)
                            terms = fn_terms(h)
                            for ti, (lT, r_) in enumerate(terms):
                                ins = e.matmul(psq[bank][0:C, hc * 64:(hc + 1) * 64], lhsT=lT, rhs=r_, start=(ti == 0), stop=(ti == len(terms) - 1))
                        return ins
                    P.op("pe", f, reads=reads, writes=[t_psq[dst_banks[0]], t_psq[dst_banks[1]]])

                def hsl(h):
                    p, hf = h // 2, h % 2
                    return slice(64 * hf, 64 * hf + 64), p

                def evac768(dst, banks, tdst, as_f32=False):
                    copy_op("act", dst[0:C, 0:512], psq[banks[0]][0:C, 0:512], [t_psq[banks[0]]], [tdst])
                    copy_op("dve", dst[0:C, 512:768], psq[banks[1]][0:C, 0:256], [t_psq[banks[1]]], [tdst])

                seq_mm("Z", lambda h: [(AT2[:, h, 0:C], Sbf[:, h // 2, :]), (Aak[0:C, h, 0:C], Vt[0:C, head_cols(h)])],
                       [t_o["AT"], t_St, t_Aak, t_Vt], (0, 1))
                evac768(Zb, (0, 1), t_Zb)
                seq_mm("U", lambda h: [(Qf[0:C, h, 0:C], Zb[0:C, head_cols(h)])], [t_Qf, t_Zb], (2, 3))
                evac768(Ub, (2, 3), t_Ub)
                seq_mm("Y", lambda h: [(RT2[:, h, 0:C], Sbf[:, h // 2, :]), (Ark[0:C, h, 0:C], Vt[0:C, head_cols(h)]),
                                       (Arb[0:C, h, 0:C], Ub[0:C, head_cols(h)])],
                       [t_o["RT"], t_St, t_Ark, t_Vt, t_Arb, t_Ub], (0, 1))
                evac768(Yf, (0, 1), t_Yf)

                def snew(e):
                    for h in range(12):
                        psl, p = hsl(h)
                        e.matmul(ps_x[psl, p * 64:(p + 1) * 64], lhsT=Kh[0:C, head_cols(h)], rhs=Vt[0:C, head_cols(h)], start=True, stop=False)
                        ins = e.matmul(ps_x[psl, p * 64:(p + 1) * 64], lhsT=Bh[0:C, head_cols(h)], rhs=Ub[0:C, head_cols(h)], start=False, stop=True)
                    return ins
                P.op("pe", snew, reads=[t_Kh, t_Vt, t_Bh, t_Ub], writes=[t_ps_x])
                tt("dve", St[:, :, :], St[:, :, :], wc[:, 0:6].unsqueeze(2).to_broadcast([128, 6, 64]), ALU.mult, [t_St, t_wc], [t_St])
                tt("dve", St[:, :, :], St[:, :, :], ps_x[:, 0:384].rearrange("p (m v) -> p m v", m=6), ALU.add, [t_St, t_ps_x], [t_St])
                P.op("dve", lambda e: e.tensor_copy(out=Sbf[:, :, :], in_=St[:, :, :]), reads=[t_St], writes=[t_St])
                if last:
                    def trs2(e):
                        for p in range(6):
                            ins = e.transpose(out=ps_x[0:64, p * 128:(p + 1) * 128] if False else ps_mm[0][0:64, p * 64:(p + 1) * 64], in_=St[:, p, :], identity=identf[:, :])
                        return ins
                    def trs3(e):
                        for p in range(6):
                            bank = ps_mm[0] if p < 4 else ps_mm[1]
                            pc = p if p < 4 else p - 4
                            ins = e.transpose(out=bank[0:64, pc * 128:(pc + 1) * 128], in_=St[:, p, :], identity=identf[:, :])
                        return ins
                    P.op("pe", trs3, reads=[t_St, t_ident], writes=[t_ps_mm[0], t_ps_mm[1]])
                    P.op("act", lambda e: e.copy(out=svst[0:64, 0:8, :].rearrange("v h k -> v (h k)"), in_=ps_mm[0][0:64, 0:512]), reads=[t_ps_mm[0]], writes=[t_svst, t_Zb, t_Ub])
                    P.op("act", lambda e: e.copy(out=svst[0:64, 8:12, :].rearrange("v h k -> v (h k)"), in_=ps_mm[1][0:64, 0:256]), reads=[t_ps_mm[1]], writes=[t_svst, t_Zb, t_Ub])
                    dsto = (o_wkvp if mode == "p" else o_wkvs[idx]).rearrange("h v k -> v h k")
                    P.op("sp", lambda e: e.dma_start(out=dsto, in_=svst[0:64, :, :]), reads=[t_svst, t_Zb, t_Ub], dma=True, sem_tk=t_svst)
                Y3 = Yf[0:C, :].rearrange("p (h d) -> p h d", h=12)
                sqv = SF[0:C, :, :].rearrange("p m j -> p (m j)").rearrange("p (h d) -> p h d", h=12)
                P.op("dve", lambda e: e.reduce_sum(out=gsm[0:C, 0:12], in_=Y3, axis=AX.X), reads=[t_Yf], writes=[t_gsm])
                P.op("act", lambda e: e.activation(out=sqv, in_=Y3, func=AF.Square), reads=[t_Yf, t_o["RKT"]], writes=[t_S["F"]])
                P.op("dve", lambda e: e.reduce_sum(out=gsm[0:C, 12:24], in_=sqv, axis=AX.X), reads=[t_S["F"]], writes=[t_gsm])
                P.op("dve", lambda e: e.tensor_scalar(out=gsm[0:C, 24:36], in0=gsm[0:C, 0:12], scalar1=1.0 / 64, scalar2=None, op0=ALU.mult),
                     reads=[t_gsm], writes=[t_gsm])
                tt("dve", gsm[0:C, 36:48], gsm[0:C, 24:36], gsm[0:C, 24:36], ALU.mult, [t_gsm], [t_gsm])
                P.op("dve", lambda e: e.scalar_tensor_tensor(out=gsm[0:C, 48:60], in0=gsm[0:C, 12:24], scalar=1.0 / 64, in1=gsm[0:C, 36:48],
                                                             op0=ALU.mult, op1=ALU.subtract), reads=[t_gsm], writes=[t_gsm])
                P.op("dve", lambda e: e.tensor_scalar(out=gsm[0:C, 48:60], in0=gsm[0:C, 48:60], scalar1=64e-5, scalar2=None, op0=ALU.add),
                     reads=[t_gsm], writes=[t_gsm])
                P.op("act", lambda e: e.activation(out=gsm[0:C, 60:72], in_=gsm[0:C, 48:60], func=AF.Sqrt), reads=[t_gsm], writes=[t_gsm])
                P.op("dve", lambda e: e.reciprocal(out=gsm[0:C, 72:84], in_=gsm[0:C, 60:72]), reads=[t_gsm], writes=[t_gsm])
                hb3 = hbuf1[0:C, 0:768].rearrange("p (h d) -> p h d", h=12)
                tt("dve", hb3, Y3, gsm[0:C, 24:36].unsqueeze(2).to_broadcast([C, 12, 64]), ALU.subtract, [t_Yf, t_gsm], [t_hbuf1])
                tt("dve", hb3, hb3, gsm[0:C, 72:84].unsqueeze(2).to_broadcast([C, 12, 64]), ALU.mult, [t_hbuf1, t_gsm], [t_hbuf1])
                tt("dve", hbuf1[0:C, 0:768], hbuf1[0:C, 0:768], lnw_b[0:C, :], ALU.mult, [t_hbuf1, t_ln], [t_hbuf1])
                tt("dve", hbuf1[0:C, 0:768], hbuf1[0:C, 0:768], lnb_b[0:C, :], ALU.add, [t_hbuf1, t_ln], [t_hbuf1])

                def bon(e):
                    for p in range(6):
                        ins = e.matmul(ps_x[0:C, 400 + 2 * p:402 + 2 * p], lhsT=RKT[:, p, 0:C], rhs=hsel[:, 0:2], start=True, stop=True)
                    return ins
                P.op("pe", bon, reads=[t_o["RKT"], t_cst1], writes=[t_ps_x])
                P.op("act", lambda e: e.copy(out=gsm[0:C, 84:96], in_=ps_x[0:C, 400:412]), reads=[t_ps_x], writes=[t_gsm])
                tt("dve", sqv, Vt[0:C, :].rearrange("p (h d) -> p h d", h=12), gsm[0:C, 84:96].unsqueeze(2).to_broadcast([C, 12, 64]), ALU.mult,
                   [t_Vt, t_gsm, t_S["F"]], [t_S["F"]])
                tt("dve", hb3, hb3, sqv, ALU.add, [t_hbuf1, t_S["F"]], [t_hbuf1])
                io2 = nxt("o")
                if mode == "p":
                    cross_attn((lambda psl, pr, blk: KmT[psl, 1, pr, blk * 128:(blk + 1) * 128]), t_KmT[1],
                               (lambda blk, h: Vm[:, 1, blk, h, 0:65]), t_Vm[1],
                               (lambda psl, pr: QmTc[psl, pr, 0:C]), [t_QmTc], C, ps_o[io2], t_ps_o[io2])
                else:
                    barrier()
                    P.op("pool", lambda e: e.memset(cV1[:, :, :, 64:65], 1.0), writes=[t_cV1])
                    load_piece(c_mem[1, idx].rearrange("(b j) c -> j b c", b=2), 2)
                    cross_attn((lambda psl, pr, blk: cKT1[psl, 2 * blk + pr, :]), t_cKT1, (lambda blk, h: cV1[:, blk, h, 0:65]), t_cV1,
                               (lambda psl, pr: QmTc[psl, pr, 0:C]), [t_QmTc], C, ps_o[io2], t_ps_o[io2])
                normalize_o(ps_o[io2], t_ps_o[io2], C, hbuf1[0:C, 768:1024], t_hbuf1, t_rcp1, rcp1)
                col0 = 0 if mode == "p" else 4 * idx
                post_a(C, hbuf1[0:C, :], t_hbuf1, gt[0:C, :], t_gt, 8, hb1, t_hb1, hT1, t_hT1, col0)
                if mode == "p":
                    def xsrc(ys_):
                        P.op("sp", lambda e: e.dma_start(out=xres1[:, :], in_=x1_d[idx * 128:(idx + 1) * 128, :]), reads=[t_x1[idx]], writes=[t_xres1], dma=True)

                    def xdst(ys_):
                        P.op("sp", lambda e: e.dma_start(out=y_p[idx * 128:(idx + 1) * 128, :], in_=xres1[:, :]), reads=[t_xres1], dma=True)
                    post_b(G_POST[1], 128, 8, hT1, t_hT1, Wo, t_Wo, 0, xsrc, xdst)
                else:
                    barrier()

            keep1 = apos[0]
            apos[0] = outs_w0
            cst1 = carve_f(1024).rearrange("p (t n) -> p t n", t=2)
            ckb1 = carve_bf(2 * 256).rearrange("p (t n) -> p t n", t=2)
            cKT1 = carve_bf(4 * 128).rearrange("p (i n) -> p i n", i=4)
            cV1 = carve_bf(2 * 4 * 80).rearrange("p (t h d) -> p t h d", t=2, h=4)
            t_cst1b, t_ckb1, t_cKT1, t_cV1 = Tk(), Tk(), Tk(), Tk()
            apos[0] = keep1
            CTX.update(PT=PT1, t_PT=t_PT1, ytmp=[hbuf1] * 2, t_ytmp=[t_hbuf1] * 2, xres=[xres1] * 2, t_xres=[t_xres1] * 2,
                       cst=cst1, ckb=ckb1, cKT=cKT1, cV=cV1, t_cst=t_cst1b, t_ckb=t_ckb1, t_cKT=t_cKT1, t_cV=t_cV1)
            print("L1 arena words (final)", apos[0])

            def prompt_chunk(c):
                P.op("sp", lambda e: e.dma_start(out=xld1[:, :], in_=x1_d[c * 128:(c + 1) * 128, :]), reads=[t_x1[c]], writes=[t_xld1], dma=True)
                rmsnorm(xld1[:, :], 128, G_PRE[1], xnb1[:, :], t_xld1, t_xnb1)

                def tr(e):
                    for k in range(8):
                        ins = e.transpose(out=ps_t[:, k * 128:(k + 1) * 128], in_=xnb1[:, k * 128:(k + 1) * 128], identity=ident[:, :])
                    return ins
                P.op("pe", tr, reads=[t_xnb1, t_ident], writes=[t_ps_t])
                copy_op(ev_eng(), xnTc[:, :, :], ps_t[:].rearrange("p (k m) -> p k m", k=8), [t_ps_t], [t_xnTc])
                if c == 0:
                    P.op("pool", lambda e: e.memset(cols[:, :, 0:1], 0.0), writes=[t_cols])
                rwkv_chunk(128, (lambda k: xnTc[:, k, :]), t_xnTc, "p", c)
            for c in range(NT if "l1few" not in DBG else 2):
                prompt_chunk(c)

            def sample_l1():
                P.op("sp", lambda e: e.dma_start(out=xld1[0:16, :], in_=x1s_d), reads=[t_x1[NT]], writes=[t_xld1], dma=True)
                rmsnorm(xld1[0:16, :], 16, G_PRE[1], xnb1[0:16, :], t_xld1, t_xnb1)

                def tr(e):
                    for k in range(8):
                        ins = e.transpose(out=ps_t[:, k * 128:k * 128 + 16], in_=xnb1[0:16, k * 128:(k + 1) * 128], identity=ident[0:16, 0:16])
                    return ins
                P.op("pe", tr, reads=[t_xnb1, t_ident], writes=[t_ps_t])
                copy_op(ev_eng(), xnTs1[:, :, :], ps_t[:].rearrange("p (k m) -> p k m", k=8)[:, :, 0:16], [t_ps_t], [t_xnTs1])
                for bb in range(4):
                    P.op("sp", (lambda e, bb=bb: e.dma_start(out=cols[:, :, 0], in_=s_shift[bb:bb + 1, :].rearrange("o (m p) -> p (o m)", p=128),
                                                             allow_slow_non_contiguous=True)), writes=[t_cols], dma=True)
                    rwkv_chunk(4, (lambda k, bb=bb: xnTs1[:, k, 4 * bb:4 * bb + 4]), t_xnTs1, "s", bb)

                def xsrc_s(ys_):
                    P.op("sp", lambda e: e.dma_start(out=xres1[0:16, :], in_=x1s_d), reads=[t_x1[NT]], writes=[t_xres1], dma=True)

                def xdst_s(ys_):
                    P.op("sp", lambda e: e.dma_start(out=y_s, in_=xres1[0:16, :]), reads=[t_xres1], dma=True)
                post_b(G_POST[1], 16, 8, hT1, t_hT1, Wo, t_Wo, 0, xsrc_s, xdst_s)
            if "nosample1" not in DBG:
                sample_l1()

        print('total_ops', len(P.ops))
        if 'dump2' in DBG:
            for _i, _o in enumerate(P.ops):
                print('OP', _i, _o.eng, _o.line, 'dma' if _o.is_dma else '')
        P.finalize_and_emit(st)
    return nc


def layer0(env):
    pass


_CACHE = {}


def kernel(x_prompt, x_sample, mem_prompt, cache_mem_kv, cache_win0, cache_win1, cache_win2, state_wkv,
           state_shift, norm_pre, norm_post, norm_mem, w_mem_kv, rel_bias, w_in_a, w_out_a, w_in_b, w_out_b,
           rwkv_mu, rwkv_w0, rwkv_w_up, rwkv_a0, rwkv_a_up, rwkv_k_k, rwkv_k_a, rwkv_r_k, rwkv_ln_w, rwkv_ln_b):
    f = lambda a: np.ascontiguousarray(np.asarray(a, dtype=np.float32))
    if "nc" not in _CACHE:
        _CACHE["nc"] = build_program()
    nc = _CACHE["nc"]
    oh = _onehot_const()
    in_maps = []
    for c in range(NCORES):
        sl = slice(4 * c, 4 * c + 4)
        in_maps.append({
            "x_p": f(x_prompt[c]),
            "x_s": f(x_sample[sl]).reshape(16, D),
            "mem_p": f(mem_prompt[c]),
            "c_mem": f(cache_mem_kv[:, sl]).reshape(2, 4, 256, 512),
            "c_w0": f(cache_win0[0, sl]).reshape(4, 128, 512),
            "c_w1": f(cache_win1[0, sl]).reshape(4, 512, 512),
            "c_w2": f(cache_win2[0, sl]).reshape(4, 2048, 512),
            "s_wkv": f(state_wkv[0, sl]),
            "s_shift": f(state_shift[0, sl]),
            "norm_pre": f(norm_pre), "norm_post": f(norm_post), "norm_mem": f(norm_mem),
            "w_mem": f(w_mem_kv), "rel_bias": f(rel_bias),
            "w_in_a": f(w_in_a[0]), "w_out_a": f(w_out_a[0]), "w_in_b": f(w_in_b[0]), "w_out_b": f(w_out_b[0]),
            "c_onehot": oh,
            "r_mu": f(rwkv_mu).reshape(1, C_SHIFT), "r_w0": f(rwkv_w0).reshape(1, 768), "r_wup": f(rwkv_w_up).reshape(64, 768),
            "r_a0": f(rwkv_a0).reshape(1, 768), "r_aup": f(rwkv_a_up).reshape(64, 768), "r_kk": f(rwkv_k_k).reshape(1, 768),
            "r_ka": f(rwkv_k_a).reshape(1, 768), "r_rk": f(rwkv_r_k).reshape(1, 768), "r_lnw": f(rwkv_ln_w).reshape(1, 768),
            "r_lnb": f(rwkv_ln_b).reshape(1, 768),
        })
    res = run_bass_kernel_spmd(nc, in_maps, core_ids=list(range(NCORES)))
    R = res.results
    cat = lambda k: np.stack([np.asarray(R[c][k]) for c in range(NCORES)])
    y_prompt = cat("y_p")
    y_sample = cat("y_s").reshape(32, 4, D)
    new_mem = cat("o_mem").transpose(1, 0, 2, 3).reshape(2, 8, 256, 2, 4, 64)
    w0p = cat("o_w0p").reshape(1, 8, 128, 2, 4, 64)
    w1p = cat("o_w1p").reshape(1, 8, 512, 2, 4, 64)
    w2p = cat("o_w2p").reshape(1, 8, 2048, 2, 4, 64)
    w0s = cat("o_w0s").reshape(1, 32, 4, 2, 4, 64)
    w1s = cat("o_w1s").reshape(1, 32, 4, 2, 4, 64)
    w2s = cat("o_w2s").reshape(1, 32, 4, 2, 4, 64)
    wkvp = cat("o_wkvp").reshape(1, 8, 12, 64, 64)
    wkvs = cat("o_wkvs").reshape(1, 32, 12, 64, 64)
    shp = cat("o_shp").reshape(1, 8, C_SHIFT)
    shs = cat("o_shs").reshape(1, 32, C_SHIFT)
    outs = (y_prompt, y_sample, new_mem, w0p, w1p, w2p, w0s, w1s, w2s, wkvp, wkvs, shp, shs)
    return tuple(np.ascontiguousarray(o.astype(np.float32)) for o in outs)
```

```python
import math
from contextlib import ExitStack
import numpy as np
import concourse.bass as bass
import concourse.mybir as mybir
from concourse.bass_utils import run_bass_kernel_spmd

F32 = mybir.dt.float32
BF16 = mybir.dt.bfloat16
AF = mybir.ActivationFunctionType
ALU = mybir.AluOpType
AX = mybir.AxisListType

ENGS = ("pe", "act", "dve", "pool", "sp")
NCORES = 8
T = 2048
D = 1024
NT = 16
NEG = -30000.0
SCALE = 0.125
DIL = (1, 4, 16)
RMS_EPS = 1e-6
C_SHIFT = 2432
A_IN = 3072
B_IN = 3712


class Tk:
    __slots__ = ("name", "lw", "rd", "dsem", "dcount", "excl")

    def __init__(self, name="", excl=False):
        self.name = name
        self.excl = excl
        self.lw = None
        self.rd = []
        self.dsem = None
        self.dcount = 0


class Op:
    __slots__ = ("eng", "fn", "deps", "is_dma", "pos", "signal", "cnt", "sem_tk", "waits", "line")


class Prog:
    def __init__(self, nc):
        self.nc = nc
        self.ops = []
        self.eng_ops = {e: [] for e in ENGS}
        self.last_dma = {}

    def op(self, eng, fn, reads=(), writes=(), dma=False, sem_tk=None, extra_deps=()):
        if len(self.ops) >= getattr(self, "maxops", 10 ** 9):
            return None
        o = Op()
        import sys as _sys
        o.line = _sys._getframe(1).f_lineno
        o.eng = eng
        o.fn = fn
        o.is_dma = dma
        o.signal = dma
        o.cnt = 0
        o.waits = []
        deps = []
        for r in reads:
            if r.lw is not None:
                deps.append((r.lw, "raw"))
            if r.excl:
                for rr in r.rd:
                    deps.append((rr, "war"))
        for w in writes:
            if w.lw is not None:
                deps.append((w.lw, "waw"))
            for rr in w.rd:
                deps.append((rr, "war"))
        for xd in extra_deps:
            deps.append((xd, "raw"))
        fdeps = []
        seen = set()
        for d, kind in deps:
            if d is o:
                continue
            if (not d.is_dma) and d.eng == eng and (not dma):
                if eng == "pe" or kind != "raw":
                    continue
            if id(d) in seen:
                continue
            seen.add(id(d))
            fdeps.append(d)
        o.deps = fdeps
        for r in (reads if fn is not None else ()):
            if r.excl:
                r.rd = [o]
            else:
                r.rd.append(o)
        for w in (writes if fn is not None else ()):
            w.lw = o
            w.rd = []
        if dma:
            if sem_tk is None:
                sem_tk = (list(writes) + list(reads))[0]
            o.sem_tk = sem_tk
            sem_tk.dcount += 1
            o.cnt = sem_tk.dcount * 16
            self.last_dma[id(sem_tk)] = o
        else:
            o.sem_tk = None
        o.pos = len(self.eng_ops[eng])
        self.eng_ops[eng].append(o)
        self.ops.append(o)
        return o

    def finalize_and_emit(self, stack):
        nc = self.nc
        waited_pos = {e: {p: -1 for p in ENGS} for e in ENGS}
        waited_dma = {e: {} for e in ENGS}
        for o in self.ops:
            e = o.eng
            for d in o.deps:
                if d.is_dma:
                    key = id(d.sem_tk)
                    if waited_dma[e].get(key, 0) >= d.cnt:
                        continue
                    waited_dma[e][key] = d.cnt
                    o.waits.append(d)
                else:
                    if waited_pos[e][d.eng] >= d.pos:
                        continue
                    waited_pos[e][d.eng] = d.pos
                    d.signal = True
                    o.waits.append(d)
        for e in ENGS:
            c = 0
            for o in self.eng_ops[e]:
                if not o.is_dma and o.signal:
                    c += 1
                    o.cnt = c
        esem = {e: stack.enter_context(nc.semaphore("es_" + e)) for e in ENGS}
        nd = 0
        for o in self.ops:
            if o.is_dma and o.sem_tk.dsem is None:
                o.sem_tk.dsem = stack.enter_context(nc.semaphore("ds%d" % nd))
                nd += 1
        self.n_dma_sems = nd
        block = stack.enter_context(nc.Block())

        def emit(engobj, elist):
            for o in elist:
                for d in o.waits:
                    if d.is_dma:
                        engobj.wait_ge(d.sem_tk.dsem, d.cnt)
                    else:
                        engobj.wait_ge(esem[d.eng], d.cnt)
                if o.fn is None:
                    continue
                ins = o.fn(engobj)
                if o.is_dma:
                    ins.then_inc(o.sem_tk.dsem, 16)
                elif o.signal:
                    ins.then_inc(esem[o.eng], 1)

        @block.tensor
        def _(pe):
            emit(pe, self.eng_ops["pe"])

        @block.scalar
        def _(act):
            emit(act, self.eng_ops["act"])

        @block.vector
        def _(dve):
            emit(dve, self.eng_ops["dve"])

        @block.gpsimd
        def _(pool):
            emit(pool, self.eng_ops["pool"])

        @block.sync
        def _(sp):
            emit(sp, self.eng_ops["sp"])
            done = set()
            for o in self.ops:
                if o.is_dma and id(o.sem_tk) not in done:
                    done.add(id(o.sem_tk))
                    sp.wait_ge(o.sem_tk.dsem, 16 * o.sem_tk.dcount)


def _t5_bucket_np(dist):
    dist = np.asarray(dist, dtype=np.int32)
    d = np.maximum(dist, 1).astype(np.float32)
    large = 16 + (np.log(d / np.float32(16.0)) / np.float32(math.log(2048 / 16)) * np.float32(16.0)).astype(np.int32)
    large = np.minimum(large, 31)
    return np.where(dist < 16, dist, large)


def _onehot_const():
    oh = np.zeros((33, 3, 384), np.float32)
    for g in range(3):
        for u in range(383):
            dist = u - 127
            if 0 <= dist <= 128:
                b = int(_t5_bucket_np(DIL[g] * dist))
                oh[b, g, u] = 1.0
            else:
                oh[32, g, u] = NEG
        oh[32, g, 383] = NEG
    return oh.reshape(33, 3 * 384)


STAGE = 2
DBG = set()


def build_program():
    nc = bass.Bass("TRN2", target_bir_lowering=False)
    try:
        nc.allow_low_precision("bf16 matmul operands with fp32 accumulation (matches problem tolerance)")
    except Exception:
        pass
    try:
        nc.allow_non_contiguous_dma("strided window / toeplitz accesses")
    except Exception:
        pass
    P = Prog(nc)
    for _d in DBG:
        if _d.startswith('maxops='):
            P.maxops = int(_d.split('=')[1])

    def din(name, shape):
        return nc.dram_tensor(name, list(shape), F32, kind="ExternalInput").ap()

    def dout(name, shape):
        return nc.dram_tensor(name, list(shape), F32, kind="ExternalOutput").ap()

    x_p = din("x_p", [T, D])
    x_s = din("x_s", [16, D])
    mem_p = din("mem_p", [256, D])
    c_mem = din("c_mem", [2, 4, 256, 512])
    c_w0 = din("c_w0", [4, 128, 512])
    c_w1 = din("c_w1", [4, 512, 512])
    c_w2 = din("c_w2", [4, 2048, 512])
    s_wkv = din("s_wkv", [4, 12, 64, 64])
    s_shift = din("s_shift", [4, C_SHIFT])
    norm_pre = din("norm_pre", [2, D])
    norm_post = din("norm_post", [2, D])
    norm_mem = din("norm_mem", [2, D])
    w_mem = din("w_mem", [2, D, 512])
    rel_bias = din("rel_bias", [32, 12])
    w_in_a = din("w_in_a", [D, A_IN])
    w_out_a = din("w_out_a", [512, D])
    w_in_b = din("w_in_b", [D, B_IN])
    w_out_b = din("w_out_b", [D, D])
    onehot = din("c_onehot", [33, 3 * 384])
    r_mu = din("r_mu", [1, C_SHIFT])
    r_w0 = din("r_w0", [1, 768])
    r_wup = din("r_wup", [64, 768])
    r_a0 = din("r_a0", [1, 768])
    r_aup = din("r_aup", [64, 768])
    r_kk = din("r_kk", [1, 768])
    r_ka = din("r_ka", [1, 768])
    r_rk = din("r_rk", [1, 768])
    r_lnw = din("r_lnw", [1, 768])
    r_lnb = din("r_lnb", [1, 768])
    y_p = dout("y_p", [T, D])
    y_s = dout("y_s", [16, D])
    o_mem = dout("o_mem", [2, 256, 512])
    o_w0p = dout("o_w0p", [128, 512])
    o_w1p = dout("o_w1p", [512, 512])
    o_w2p = dout("o_w2p", [2048, 512])
    o_w0s = dout("o_w0s", [16, 512])
    o_w1s = dout("o_w1s", [16, 512])
    o_w2s = dout("o_w2s", [16, 512])
    o_wkvp = dout("o_wkvp", [12, 64, 64])
    o_wkvs = dout("o_wkvs", [4, 12, 64, 64])
    o_shp = dout("o_shp", [1, C_SHIFT])
    o_shs = dout("o_shs", [4, C_SHIFT])
    x1_d = nc.dram_tensor("x1_d", [T, D], F32, kind="Internal").ap()
    x1s_d = nc.dram_tensor("x1s_d", [16, D], F32, kind="Internal").ap()
    e_d = nc.dram_tensor("e_d", [12, 3 * 384], F32, kind="Internal").ap()
    out_tokens = []

    with ExitStack() as st:
        def sb(name, shape, dt):
            return st.enter_context(nc.sbuf_tensor(name, list(shape), dt))

        def psum(name, shape, dt):
            return st.enter_context(nc.psum_tensor(name, list(shape), dt))

        ps_mm = [psum("ps_mm%d" % i, [128, 512], F32) for i in range(2)]
        ps_s = [psum("ps_s%d" % i, [128, 512], F32) for i in range(2)]
        ps_o = [psum("ps_o%d" % i, [128, 512], F32) for i in range(2)]
        ps_t = psum("ps_t", [128, 1024], BF16)
        ps_x = psum("ps_x", [128, 512], F32)
        t_ps_mm = [Tk("ps_mm0", True), Tk("ps_mm1", True)]
        t_ps_s = [Tk("ps_s0", True), Tk("ps_s1", True)]
        t_ps_o = [Tk("ps_o0", True), Tk("ps_o1", True)]
        t_ps_t = Tk("ps_t", True)
        t_ps_x = Tk("ps_x", True)
        rr = {"mm": 0, "s": 0, "o": 0, "ev": 0}

        def nxt(k):
            rr[k] += 1
            return rr[k] & 1

        def ev_eng():
            rr["ev"] += 1
            return "act" if (rr["ev"] & 1) else "dve"

        def copy_op(eng, out, in_, reads, writes):
            if eng == "act":
                P.op("act", lambda e: e.copy(out=out, in_=in_), reads=reads, writes=writes)
            else:
                P.op(eng, lambda e: e.tensor_copy(out=out, in_=in_), reads=reads, writes=writes)

        identf = sb("identf", [128, 128], F32)
        ident = sb("ident", [128, 128], BF16)
        t_ident = Tk()

        P.op("pool", lambda e: e.memset(identf[:], 0.0), writes=[t_ident])
        P.op("pool", lambda e: e.affine_select(out=identf[:], in_=identf[:], pattern=[[-1, 128]], compare_op=ALU.not_equal,
                                               fill=1.0, base=0, channel_multiplier=1), reads=[t_ident], writes=[t_ident])
        P.op("dve", lambda e: e.tensor_copy(out=ident[:], in_=identf[:]), reads=[t_ident], writes=[t_ident])

        ARENA_WORDS = 49000
        arena = sb("arena", [128, ARENA_WORDS], F32)
        gains4 = sb("gains", [128, 2, 1024], F32)
        gmem = arena[:, 0:2048].rearrange("p (a d) -> p a d", a=2)
        barsb = sb("barsb", [128, 16], F32)
        t_bar = {e: Tk() for e in ("pe", "act", "dve", "pool")}
        t_barinit = Tk()
        P.op("pool", lambda e: e.memset(barsb[:, :], 0.0), writes=[t_barinit])

        def barrier():
            dmas = list(P.last_dma.values())
            P.op("pe", lambda e: e.transpose(out=ps_t[0:16, 0:16], in_=ident[0:16, 0:16], identity=ident[0:16, 0:16]),
                 reads=[t_ident], writes=[t_ps_t, t_bar["pe"]], extra_deps=dmas)
            P.op("act", lambda e: e.copy(out=barsb[:, 0:1], in_=barsb[:, 8:9]), reads=[t_barinit], writes=[t_bar["act"]], extra_deps=dmas)
            P.op("dve", lambda e: e.tensor_copy(out=barsb[:, 1:2], in_=barsb[:, 9:10]), reads=[t_barinit], writes=[t_bar["dve"]], extra_deps=dmas)
            P.op("pool", lambda e: e.memset(barsb[:, 2:3], 0.0), writes=[t_bar["pool"]], extra_deps=dmas)
            allb = list(t_bar.values())
            P.op("pe", lambda e: e.transpose(out=ps_t[0:16, 0:16], in_=ident[0:16, 0:16], identity=ident[0:16, 0:16]),
                 reads=[t_ident] + allb, writes=[t_ps_t])
            P.op("act", lambda e: e.copy(out=barsb[:, 3:4], in_=barsb[:, 8:9]), reads=allb)
            P.op("dve", lambda e: e.tensor_copy(out=barsb[:, 4:5], in_=barsb[:, 9:10]), reads=allb)
            P.op("pool", lambda e: e.memset(barsb[:, 5:6], 0.0), reads=allb)
            P.op("sp", None, reads=allb)

        class _G:
            def __getitem__(self, key):
                p, gi, c = key
                return gains4[p, gi, c] if gi < 2 else gmem[p, gi - 4, c]
        gains = _G()
        t_gains = Tk()
        def load_gains(items):
            for (i, src, l) in items:
                ap_b = bass.AP(tensor=src.tensor, offset=l * D, ap=[[0, 128], [1, D]])
                P.op("sp", (lambda e, i=i, ap_b=ap_b: e.dma_start(out=gains[:, i, :], in_=ap_b)), writes=[t_gains], dma=True)
        load_gains([(0, norm_pre, 0), (1, norm_post, 0), (4, norm_mem, 0), (5, norm_mem, 1)])
        G_PRE, G_POST, G_MEM = (0, 0), (1, 1), (4, 5)

        ss = sb("ss", [128, 8], F32)
        junk = sb("junk", [128, 1024], BF16)
        t_junk = Tk()

        def rmsnorm(src_ap, np_, gi, dst_ap, t_src, t_dst, extra_reads=()):
            t_ss = Tk()
            P.op("act", lambda e: e.activation(out=junk[:np_, :], in_=src_ap, func=AF.Square),
                 reads=[t_src], writes=[t_junk])
            P.op("dve", lambda e: e.reduce_sum(out=ss[:np_, 0:1], in_=junk[:np_, :], axis=AX.X),
                 reads=[t_junk], writes=[t_ss])
            P.op("dve", lambda e: e.tensor_scalar(out=ss[:np_, 1:2], in0=ss[:np_, 0:1], scalar1=1.0 / D, scalar2=RMS_EPS,
                                                  op0=ALU.mult, op1=ALU.add), reads=[t_ss], writes=[t_ss])
            P.op("act", lambda e: e.activation(out=ss[:np_, 2:3], in_=ss[:np_, 1:2], func=AF.Sqrt), reads=[t_ss], writes=[t_ss])
            P.op("dve", lambda e: e.reciprocal(out=ss[:np_, 3:4], in_=ss[:np_, 2:3]), reads=[t_ss], writes=[t_ss])
            P.op("dve", lambda e: e.scalar_tensor_tensor(out=dst_ap, in0=src_ap, scalar=ss[:np_, 3:4], in1=gains[:np_, gi, :],
                                                         op0=ALU.mult, op1=ALU.mult),
                 reads=[t_src, t_ss, t_gains] + list(extra_reads), writes=[t_dst])

        apos = [0]
        atop = [ARENA_WORDS]

        def carve_top(nwords):
            atop[0] -= nwords
            assert atop[0] >= apos[0]
            return arena[:, atop[0]:atop[0] + nwords]

        def carve(nbytes):
            w0 = apos[0]
            nw = (nbytes + 3) // 4
            apos[0] += nw
            assert apos[0] <= atop[0], ("arena overflow", apos[0], atop[0])
            return arena[:, w0:w0 + nw]

        def carve_bf(n):
            return carve(2 * n).bitcast(BF16)

        def carve_f(n):
            return carve(4 * n)

        KmT = sb("KmT", [128, 2, 2, 256], BF16)
        Vm = sb("Vm", [128, 2, 2, 4, 80], BF16)
        t_KmT = [Tk(), Tk()]
        t_Vm = [Tk(), Tk()]
        mark = apos[0]
        carve_f(2048)
        memx = carve_f(2 * 1024).rearrange("p (b d) -> p b d", b=2)
        memn = carve_bf(2 * 1024).rearrange("p (b d) -> p b d", b=2)
        memnT = carve_bf(8 * 256).rearrange("p (k m) -> p k m", k=8)
        wm = carve_bf(8 * 512).rearrange("p (k n) -> p k n", k=8)
        kvst = carve_f(2 * 512).rearrange("p (b n) -> p b n", b=2)
        t_memx, t_memn, t_memnT, t_wm, t_kvst = Tk(), Tk(), Tk(), Tk(), [Tk(), Tk()]
        P.op("sp", lambda e: e.dma_start(out=memx, in_=mem_p.rearrange("(b p) d -> p b d", p=128)), writes=[t_memx], dma=True)
        if "novm" not in DBG:
            P.op("pool", lambda e: e.memset(Vm[:, :, :, :, 64:65], 1.0), writes=t_Vm)
        for l in range(0 if "nomem" in DBG else 2):
            P.op("pool", (lambda e, l=l: e.dma_start(out=wm, in_=w_mem[l].rearrange("(k p) n -> p k n", p=128))),
                 writes=[t_wm], dma=True)
            for b in range(2):
                rmsnorm(memx[:, b, :], 128, G_MEM[l], memn[:, b, :], t_memx, t_memn)

                def tr(e, b=b):
                    for k in range(8):
                        ins = e.transpose(out=ps_t[:, k * 128:(k + 1) * 128], in_=memn[:, b, k * 128:(k + 1) * 128], identity=ident[:])
                    return ins
                P.op("pe", tr, reads=[t_memn, t_ident], writes=[t_ps_t])
                copy_op(ev_eng(), memnT[:, :, b * 128:(b + 1) * 128], ps_t[:].rearrange("p (k m) -> p k m", k=8), [t_ps_t], [t_memnT])
            for b in range(2):
                i = nxt("mm")

                def mm(e, b=b, i=i):
                    for k in range(8):
                        ins = e.matmul(ps_mm[i][:, :], lhsT=memnT[:, k, b * 128:(b + 1) * 128], rhs=wm[:, k, :], start=(k == 0), stop=(k == 7))
                    return ins
                P.op("pe", mm, reads=[t_memnT, t_wm], writes=[t_ps_mm[i]])
                P.op("act", (lambda e, b=b, i=i: e.copy(out=kvst[:, b, :], in_=ps_mm[i][:, :])), reads=[t_ps_mm[i]], writes=[t_kvst[b]])
                P.op("dve", (lambda e, b=b, i=i, l=l: e.tensor_copy(out=Vm[:, l, b, :, 0:64],
                                                                    in_=ps_mm[i][:, 256:512].rearrange("p (h d) -> p h d", h=4))),
                     reads=[t_ps_mm[i]], writes=[t_Vm[l]])
                P.op("sp", (lambda e, b=b, l=l: e.dma_start(out=o_mem[l, b * 128:(b + 1) * 128, :], in_=kvst[:, b, :])),
                     reads=[t_kvst[b]], dma=True)
                out_tokens.append(t_kvst[b])
            for pr in range(2):
                i = nxt("mm")

                def mmk(e, pr=pr, i=i):
                    for k in range(8):
                        ins = e.matmul(ps_mm[i][:, 0:256], lhsT=wm[:, k, pr * 128:(pr + 1) * 128], rhs=memnT[:, k, :], start=(k == 0), stop=(k == 7))
                    return ins
                P.op("pe", mmk, reads=[t_memnT, t_wm], writes=[t_ps_mm[i]])
                copy_op(ev_eng(), KmT[:, l, pr, :], ps_mm[i][:, 0:256], [t_ps_mm[i]], [t_KmT[l]])


        biasT = carve_top(12 * 2 * 128).rearrange("p (a b q) -> p a b q", a=12, b=2)
        t_biasT = Tk()
        M1e = carve_top(264).bitcast(BF16)
        M1o = carve_top(264).bitcast(BF16)
        M2e = carve_top(1032).bitcast(BF16)
        M2o = carve_top(1032).bitcast(BF16)
        t_M = Tk()
        mark = apos[0]
        rb33 = carve_f(12)
        oh33 = carve_f(1152)
        esb = carve_f(1152)
        mtmp = carve_f(2064)
        t_rb, t_oh, t_esb, t_mtmp, t_ed = Tk(), Tk(), Tk(), Tk(), Tk()
        P.op("pool", lambda e: e.memset(rb33[32:33, :], 1.0), writes=[t_rb])
        P.op("sp", lambda e: e.dma_start(out=rb33[0:32, :], in_=rel_bias), writes=[t_rb], dma=True)
        P.op("sp", lambda e: e.dma_start(out=oh33[0:33, :], in_=onehot), writes=[t_oh], dma=True)
        for g in range(3):
            P.op("pe", (lambda e, g=g: e.matmul(ps_x[0:12, 0:384], lhsT=rb33[0:33, :], rhs=oh33[0:33, g * 384:(g + 1) * 384], start=True, stop=True)),
                 reads=[t_rb, t_oh], writes=[t_ps_x])
            P.op("act", (lambda e, g=g: e.copy(out=esb[0:12, g * 384:(g + 1) * 384], in_=ps_x[0:12, 0:384])), reads=[t_ps_x], writes=[t_esb])
        P.op("sp", lambda e: e.dma_start(out=e_d, in_=esb[0:12, :]), reads=[t_esb], writes=[t_ed], dma=True, sem_tk=t_esb)
        bstage = carve_f(24 * 128).rearrange("p (a q) -> p a q", a=24)
        Jrev = carve_f(128)
        t_bst, t_J = Tk(), Tk()

        P.op("pool", lambda e: e.memset(Jrev[:, :], 0.0), writes=[t_J])
        P.op("pool", lambda e: e.affine_select(out=Jrev[:, :], in_=Jrev[:, :], pattern=[[1, 128]], compare_op=ALU.not_equal, fill=1.0, base=-127,
                                               channel_multiplier=1), reads=[t_J], writes=[t_J])
        for g in range(3):
            for h in range(4):
                for blk in range(2):
                    off = (4 * g + h) * 1152 + g * 384 + (128 if blk == 0 else 0)
                    src = bass.AP(tensor=e_d.tensor, offset=off, ap=[[1, 128], [1, 128]])
                    P.op("sp", (lambda e, g=g, h=h, blk=blk, src=src: e.dma_start(out=bstage[:, (g * 4 + h) * 2 + blk, :], in_=src)),
                         reads=[t_ed], writes=[t_bst], dma=True)
        for a4 in range(6):
            i = nxt("mm")
            P.op("pe", (lambda e, a4=a4, i=i: e.matmul(ps_mm[i][:, :], lhsT=Jrev[:, :], rhs=bstage[:, 4 * a4:4 * a4 + 4, :].rearrange("p a q -> p (a q)"),
                                                         start=True, stop=True)), reads=[t_J, t_bst], writes=[t_ps_mm[i]])
            copy_op(ev_eng(), biasT.rearrange("p a b q -> p (a b q)")[:, 512 * a4:512 * (a4 + 1)], ps_mm[i][:, :], [t_ps_mm[i]], [t_biasT])
        for (Mt, ncol, base, cm) in ((M1e, 528, 16, 4), (M1o, 528, 15, 4), (M2e, 2064, 16, 16), (M2o, 2064, 15, 16)):
            P.op("pool", (lambda e, ncol=ncol: e.memset(mtmp[:, 0:ncol], 0.0)), writes=[t_mtmp])
            P.op("pool", (lambda e, ncol=ncol, base=base, cm=cm: e.affine_select(out=mtmp[:, 0:ncol], in_=mtmp[:, 0:ncol], pattern=[[-1, ncol]],
                                                                                compare_op=ALU.not_equal, fill=1.0, base=base, channel_multiplier=cm)),
                 reads=[t_mtmp], writes=[t_mtmp])
            P.op("dve", (lambda e, Mt=Mt, ncol=ncol: e.tensor_copy(out=Mt[:, :], in_=mtmp[:, 0:ncol])), reads=[t_mtmp], writes=[t_M])

        def perm_lhsT(g, r, Tq):
            if g == 1:
                Tp = Tq % 4
                if r % 2 == 0:
                    s0 = 16 + 128 * Tp - r
                    return M1e[:, s0:s0 + 128]
                s0 = 15 + 128 * Tp - r
                return M1o[:, s0:s0 + 128]
            if r % 2 == 0:
                s0 = 16 + 128 * Tq - r
                return M2e[:, s0:s0 + 128]
            s0 = 15 + 128 * Tq - r
            return M2o[:, s0:s0 + 128]

        barrier()
        apos[0] = 0
        L0 = apos[0]
        xnT = carve_bf(8 * 2048).rearrange("p (k t) -> p k t", k=8)
        xnTs = carve_bf(8 * 16).rearrange("p (k t) -> p k t", k=8)
        NW = 4
        wsl = [carve_bf(8 * 256).rearrange("p (k n) -> p k n", k=8) for _ in range(NW)]
        t_wsl = [Tk() for _ in range(NW)]
        xld = [carve_f(1024) for _ in range(2)]
        t_xld = [Tk(), Tk()]
        xnb = [carve_bf(1024) for _ in range(2)]
        t_xnb = [Tk(), Tk()]
        t_xnT = [Tk() for _ in range(NT)]
        t_xnTs = Tk()
        QT = carve_bf(2 * 2048).rearrange("p (m t) -> p m t", m=2)
        KT = carve_bf(2 * 2048).rearrange("p (m t) -> p m t", m=2)
        QmT = carve_bf(2 * 2048).rearrange("p (m t) -> p m t", m=2)
        t_QT = [[Tk() for _ in range(4)] for _ in range(2)]
        t_KT = [[Tk() for _ in range(4)] for _ in range(2)]
        t_QmT = [[Tk() for _ in range(4)] for _ in range(2)]
        QTs = carve_bf(8 * 16).rearrange("p (m t) -> p m t", m=8)
        KTs = carve_bf(6 * 16).rearrange("p (m t) -> p m t", m=6)
        t_QTs, t_KTs = Tk(), Tk()
        gate = carve_bf(16 * 512).rearrange("p (i n) -> p i n", i=16)
        t_gate = [Tk() for _ in range(NT)]
        gate_s = carve_bf(4 * 512).rearrange("p (b n) -> p b n", b=4)
        t_gate_s = Tk()
        Vg = carve_bf(16 * 4 * 80).rearrange("p (i h d) -> p i h d", i=16, h=4)
        t_Vg = [Tk() for _ in range(NT)]
        Vns = carve_bf(4 * 3 * 4 * 80).rearrange("p (b g h d) -> p b g h d", b=4, g=3, h=4)
        t_Vns = Tk()
        Og = carve_bf(48 * 260).rearrange("p (i n) -> p i n", i=48)
        t_Og = [Tk() for _ in range(48)]
        wst = [carve_f(512) for _ in range(2)]
        t_wst = [Tk(), Tk()]
        wins = carve_f(3 * 512).rearrange("p (g n) -> p g n", g=3)
        t_wins = Tk()
        sbf = [carve_f(256) for _ in range(2)]
        t_sbf = [Tk(), Tk()]
        PT = [carve_bf(256) for _ in range(2)]
        t_PT = [Tk(), Tk()]
        rr.update({"w": 0, "sb": 0, "pt": 0, "wst": 0})

        P.op("pool", lambda e: e.memset(Vg[:, :, :, 64:65], 1.0), writes=t_Vg)
        P.op("pool", lambda e: e.memset(Vns[0:4, :, :, :, 64:65], 1.0), writes=[t_Vns])

        def phaseA(src_dram, gi, lname):
            for i in range(NT + 1):
                s_ = i & 1
                np_ = 128 if i < NT else 16
                if i < NT:
                    P.op("sp", (lambda e, i=i, s_=s_: e.dma_start(out=xld[s_][:, :], in_=src_dram[0][i * 128:(i + 1) * 128, :])),
                         writes=[t_xld[s_]], dma=True)
                else:
                    P.op("sp", (lambda e, s_=s_: e.dma_start(out=xld[s_][0:16, :], in_=src_dram[1])), writes=[t_xld[s_]], dma=True)
                rmsnorm(xld[s_][0:np_, :], np_, gi, xnb[s_][0:np_, :], t_xld[s_], t_xnb[s_])

                def tr(e, s_=s_, np_=np_):
                    for k in range(8):
                        ins = e.transpose(out=ps_t[:, k * 128:k * 128 + np_], in_=xnb[s_][0:np_, k * 128:(k + 1) * 128], identity=ident[0:np_, 0:np_])
                    return ins
                P.op("pe", tr, reads=[t_xnb[s_], t_ident], writes=[t_ps_t])
                if i < NT:
                    copy_op(ev_eng(), xnT[:, :, i * 128:(i + 1) * 128], ps_t[:].rearrange("p (k m) -> p k m", k=8), [t_ps_t], [t_xnT[i]])
                else:
                    copy_op(ev_eng(), xnTs[:, :, :], ps_t[:].rearrange("p (k m) -> p k m", k=8)[:, :, 0:16], [t_ps_t], [t_xnTs])

        phaseA((x_p, x_s), G_PRE[0], "l0")

        chunk_cols = []
        for g in range(3):
            chunk_cols += [("q", g, 256 * g), ("k", g, 768 + 256 * g), ("v", g, 1536 + 256 * g)]
        chunk_cols += [("qm", 0, 2304), ("gate", 0, 2560), ("gate", 1, 2816)]
        wstate = {"loaded": 0}

        def load_w(n):
            if n >= len(chunk_cols) or n < wstate["loaded"]:
                return
            assert n == wstate["loaded"]
            wstate["loaded"] += 1
            c0 = chunk_cols[n][2]
            sl_ = n % NW
            P.op("pool", (lambda e, c0=c0, sl_=sl_: e.dma_start(out=wsl[sl_], in_=w_in_a[:, c0:c0 + 256].rearrange("(k p) n -> p k n", p=128))),
                 writes=[t_wsl[sl_]], dma=True)

        def proj_fm(sl_, dst, t_dst, sdst, t_sdst, smb0):
            for mb in range(2):
                for tg in range(4):
                    i = nxt("mm")

                    def mm(e, mb=mb, tg=tg, i=i):
                        for k in range(8):
                            ins = e.matmul(ps_mm[i][:, :], lhsT=wsl[sl_][:, k, mb * 128:(mb + 1) * 128], rhs=xnT[:, k, tg * 512:(tg + 1) * 512],
                                           start=(k == 0), stop=(k == 7))
                        return ins
                    P.op("pe", mm, reads=[t_wsl[sl_]] + t_xnT[4 * tg:4 * tg + 4], writes=[t_ps_mm[i]])
                    copy_op(ev_eng(), dst[:, mb, tg * 512:(tg + 1) * 512], ps_mm[i][:, :], [t_ps_mm[i]], [t_dst[mb][tg]])
                def mms(e, mb=mb):
                    for k in range(8):
                        ins = e.matmul(ps_x[:, 0:16], lhsT=wsl[sl_][:, k, mb * 128:(mb + 1) * 128], rhs=xnTs[:, k, :], start=(k == 0), stop=(k == 7))
                    return ins
                P.op("pe", mms, reads=[t_wsl[sl_], t_xnTs], writes=[t_ps_x])
                copy_op(ev_eng(), sdst[:, smb0 + mb, :], ps_x[:, 0:16], [t_ps_x], [t_sdst])

        def proj_tm(sl_, tok_ap_fn, reads, evac_fn):
            i = nxt("mm")

            def mm(e, i=i):
                for k in range(8):
                    ins = e.matmul(ps_mm[i][:, 0:256], lhsT=tok_ap_fn(k), rhs=wsl[sl_][:, k, :], start=(k == 0), stop=(k == 7))
                return ins
            P.op("pe", mm, reads=[t_wsl[sl_]] + list(reads), writes=[t_ps_mm[i]])
            evac_fn(ps_mm[i], t_ps_mm[i])

        def proj_tm_s(sl_, M, c0, evac_fn):
            def mm(e):
                for k in range(8):
                    ins = e.matmul(ps_x[0:M, 0:256], lhsT=xnTs[:, k, c0:c0 + M], rhs=wsl[sl_][:, k, :], start=(k == 0), stop=(k == 7))
                return ins
            P.op("pe", mm, reads=[t_wsl[sl_], t_xnTs], writes=[t_ps_x])
            evac_fn()

        def grp_tiles(g):
            d = DIL[g]
            nsb = NT // d
            return [(r, sb_) for r in range(d) for sb_ in range(nsb)]

        def tile_tok_slice(g, r, sb_):
            d = DIL[g]
            base = d * 128 * sb_ + r
            return slice(base, base + d * 127 + 1, d)

        def nat_tiles_of(g, r, sb_):
            d = DIL[g]
            return list(range(d * sb_, d * sb_ + d))

        win_out = (o_w0p, o_w1p, o_w2p)
        win_s_out = (o_w0s, o_w1s, o_w2s)
        load_w(0)
        load_w(1)
        load_w(2)
        def win_rows(g, r):
            dd = DIL[g]
            if dd == 1:
                return win_out[g]
            return win_out[g].rearrange("(j r) c -> r j c", r=dd)[r]

        def do_group(g):
            d = DIL[g]
            nsb = NT // d
            tiles = grp_tiles(g)
            n = 3 * g
            load_w(n + 3)
            proj_fm(n % NW, QT, t_QT, QTs, t_QTs, 2 * g)
            n = 3 * g + 1
            load_w(n + 3)
            proj_fm(n % NW, KT, t_KT, KTs, t_KTs, 2 * g)
            for r in range(d):
                sb_ = nsb - 1
                tsl = tile_tok_slice(g, r, sb_)

                def evk(pst, tps, r=r):
                    ws_ = nxt("wst")
                    P.op("act", lambda e: e.copy(out=wst[ws_][:, 0:256], in_=pst[:, 0:256]), reads=[tps], writes=[t_wst[ws_]])
                    rows = win_rows(g, r)
                    P.op("sp", lambda e: e.dma_start(out=rows[:, 0:256], in_=wst[ws_][:, 0:256]), reads=[t_wst[ws_]], dma=True)
                proj_tm(n % NW, (lambda k, tsl=tsl: xnT[:, k, tsl]), [t_xnT[j] for j in nat_tiles_of(g, r, sb_)], evk)

            def evks():
                P.op("act", lambda e: e.copy(out=wins[0:16, g, 0:256], in_=ps_x[0:16, 0:256]), reads=[t_ps_x], writes=[t_wins])
            proj_tm_s(n % NW, 16, 0, evks)
            n = 3 * g + 2
            load_w(n + 3)
            for gi_, (r, sb_) in enumerate(tiles):
                tsl = tile_tok_slice(g, r, sb_)
                is_win = (sb_ == nsb - 1)

                def evv(pst, tps, gi_=gi_, is_win=is_win, r=r):
                    P.op("dve", lambda e: e.tensor_copy(out=Vg[:, gi_, :, 0:64], in_=pst[:, 0:256].rearrange("p (h d) -> p h d", h=4)),
                         reads=[tps], writes=[t_Vg[gi_]])
                    if is_win:
                        ws_ = nxt("wst")
                        P.op("act", lambda e: e.copy(out=wst[ws_][:, 256:512], in_=pst[:, 0:256]), reads=[tps], writes=[t_wst[ws_]])
                        rows = win_rows(g, r)
                        P.op("sp", lambda e: e.dma_start(out=rows[:, 256:512], in_=wst[ws_][:, 256:512]), reads=[t_wst[ws_]], dma=True)
                proj_tm(n % NW, (lambda k, tsl=tsl: xnT[:, k, tsl]), [t_xnT[j] for j in nat_tiles_of(g, r, sb_)], evv)

            def evvs():
                P.op("act", lambda e: e.copy(out=wins[0:16, g, 256:512], in_=ps_x[0:16, 0:256]), reads=[t_ps_x], writes=[t_wins])
            proj_tm_s(n % NW, 16, 0, evvs)
            P.op("sp", lambda e: e.dma_start(out=win_s_out[g], in_=wins[0:16, g, :]), reads=[t_wins], dma=True)
            for bb in range(4):
                def evvn(bb=bb):
                    P.op("dve", lambda e: e.tensor_copy(out=Vns[0:4, bb, g, :, 0:64], in_=ps_x[0:4, 0:256].rearrange("p (h d) -> p h d", h=4)),
                         reads=[t_ps_x], writes=[t_Vns])
                proj_tm_s(n % NW, 4, 4 * bb, evvn)
            allqk = [x for row in t_QT for x in row] + [x for row in t_KT for x in row]
            for gi_, (r, sb_) in enumerate(tiles):
                qsl = tile_tok_slice(g, r, sb_)
                blocks = ([(0, (r, sb_ - 1))] if sb_ > 0 else []) + [(1, (r, sb_))]
                io = nxt("o")
                for h in range(4):
                    pr, hf = h // 2, h % 2
                    psl = slice(64 * hf, 64 * hf + 64)
                    isx = nxt("s")

                    def qk(e, isx=isx, pr=pr, psl=psl, qsl=qsl, blocks=blocks):
                        for (blk, (kr, ksb)) in blocks:
                            ksl = tile_tok_slice(g, kr, ksb)
                            ins = e.matmul(ps_s[isx][:, blk * 128:(blk + 1) * 128], lhsT=KT[psl, pr, ksl], rhs=QT[psl, pr, qsl], start=True, stop=True)
                        return ins
                    P.op("pe", qk, reads=allqk, writes=[t_ps_s[isx]])
                    b0 = blocks[0][0]
                    csl = slice(b0 * 128, 256)
                    isb = nxt("sb")
                    P.op("dve", (lambda e, isx=isx, isb=isb, csl=csl, h=h, b0=b0: e.scalar_tensor_tensor(
                        out=sbf[isb][:, csl], in0=ps_s[isx][:, csl], scalar=SCALE,
                        in1=biasT[:, g * 4 + h, b0:2, :].rearrange("p b q -> p (b q)"), op0=ALU.mult, op1=ALU.add)),
                        reads=[t_ps_s[isx], t_biasT], writes=[t_sbf[isb]])
                    ipt = nxt("pt")
                    P.op("act", (lambda e, isb=isb, ipt=ipt, csl=csl: e.activation(out=PT[ipt][:, csl], in_=sbf[isb][:, csl], func=AF.Exp)),
                         reads=[t_sbf[isb]], writes=[t_PT[ipt]])

                    def pv(e, ipt=ipt, io=io, h=h, blocks=blocks):
                        nb = len(blocks)
                        for bi, (blk, (kr, ksb)) in enumerate(blocks):
                            kgi = kr * nsb + ksb
                            ins = e.matmul(ps_o[io][:, h * 65:(h + 1) * 65], lhsT=PT[ipt][:, blk * 128:(blk + 1) * 128], rhs=Vg[:, kgi, h, 0:65],
                                           start=(bi == 0), stop=(bi == nb - 1))
                        return ins
                    P.op("pe", pv, reads=[t_PT[ipt]] + [t_Vg[kr * nsb + ksb] for (_, (kr, ksb)) in blocks], writes=[t_ps_o[io]])
                copy_op(ev_eng(), Og[:, 16 * g + gi_, :], ps_o[io][:, 0:260], [t_ps_o[io]], [t_Og[16 * g + gi_]])

        for g in range(3):
            do_group(g)

        n = 9
        load_w(n + 3)
        proj_fm(n % NW, QmT, t_QmT, QTs, t_QTs, 6)
        def do_gate(gc):
            n = 10 + gc
            load_w(n + 3)
            for i in range(NT):
                def evg(pst, tps, i=i, gc=gc):
                    P.op("act", lambda e: e.activation(out=gate[:, i, gc * 256:(gc + 1) * 256], in_=pst[:, 0:256], func=AF.Silu),
                         reads=[tps], writes=[t_gate[i]])
                proj_tm(n % NW, (lambda k, i=i: xnT[:, k, i * 128:(i + 1) * 128]), [t_xnT[i]], evg)
            for bb in range(4):
                def evgs(bb=bb, gc=gc):
                    P.op("act", lambda e: e.activation(out=gate_s[0:4, bb, gc * 256:(gc + 1) * 256], in_=ps_x[0:4, 0:256], func=AF.Silu),
                         reads=[t_ps_x], writes=[t_gate_s])
                proj_tm_s(n % NW, 4, 4 * bb, evgs)
        for gc in range(2):
            do_gate(gc)

        barrier()
        l0_keep = apos[0]
        apos[0] = L0
        wout = carve_bf(4 * 1024).rearrange("p (k n) -> p k n", k=4)
        t_wout = Tk()
        P.op("pool", lambda e: e.dma_start(out=wout, in_=w_out_a.rearrange("(k p) n -> p k n", p=128)), writes=[t_wout], dma=True)
        hbuf = [carve_f(512) for _ in range(2)]
        t_hbuf = [Tk(), Tk()]
        hb = [carve_bf(512) for _ in range(2)]
        t_hb = [Tk(), Tk()]
        hT = [carve_bf(512).rearrange("p (k t) -> p k t", k=4) for _ in range(2)]
        t_hT = [Tk(), Tk()]
        rcp = [carve_f(8) for _ in range(2)]
        t_rcp = [Tk(), Tk()]
        ytmp = [carve_f(1024) for _ in range(2)]
        t_ytmp = [Tk(), Tk()]
        cst = carve_f(2048).rearrange("p (t n) -> p t n", t=4)
        ckb = carve_bf(4 * 256).rearrange("p (t n) -> p t n", t=4)
        cKT = carve_bf(8 * 128).rearrange("p (i n) -> p i n", i=8)
        cV = carve_bf(4 * 4 * 80).rearrange("p (t h d) -> p t h d", t=4, h=4)
        PTz = carve_bf(16)
        PTn = carve_bf(16)
        sbn = carve_f(16)
        t_cst, t_ckb, t_cKT, t_cV, t_PTz, t_PTn, t_sbn = Tk(), Tk(), Tk(), Tk(), Tk(), Tk(), Tk()
        assert apos[0] <= L0 + 8192 + 64 + 4096, apos[0] - L0
        apos[0] = l0_keep
        xres = xld
        t_xres = t_xld
        t_x1 = [Tk() for _ in range(NT + 1)]
        rr.update({"hb": 0})
        CTX = dict(PT=PT, t_PT=t_PT, ytmp=ytmp, t_ytmp=t_ytmp, xres=xres, t_xres=t_xres,
                   cst=cst, ckb=ckb, cKT=cKT, cV=cV, t_cst=t_cst, t_ckb=t_ckb, t_cKT=t_cKT, t_cV=t_cV)
        P.op("pool", lambda e: e.memset(cV[:, :, :, 64:65], 1.0), writes=[t_cV])
        P.op("pool", lambda e: e.memset(PTz[:, :], 0.0), writes=[t_PTz])

        def cross_attn(k_ap_fn, t_k, v_ap_fn, t_v, qT_ap_fn, q_reads, nq, ps_out, t_ps_out):
            PT_, t_PT_ = CTX["PT"], CTX["t_PT"]
            for h in range(4):
                pr, hf = h // 2, h % 2
                psl = slice(64 * hf, 64 * hf + 64)
                isx = nxt("s")

                def qk(e, isx=isx, pr=pr, psl=psl):
                    for blk in range(2):
                        ins = e.matmul(ps_s[isx][:, blk * 128:blk * 128 + nq], lhsT=k_ap_fn(psl, pr, blk), rhs=qT_ap_fn(psl, pr),
                                       start=True, stop=True)
                    return ins
                P.op("pe", qk, reads=[t_k] + list(q_reads), writes=[t_ps_s[isx]])
                ipt = nxt("pt")
                P.op("act", (lambda e, isx=isx, ipt=ipt: e.activation(
                    out=PT_[ipt][:, :].rearrange("p (b q) -> p b q", b=2)[:, :, 0:nq],
                    in_=ps_s[isx][:, 0:256].rearrange("p (b q) -> p b q", b=2)[:, :, 0:nq], func=AF.Exp, scale=SCALE)),
                    reads=[t_ps_s[isx]], writes=[t_PT_[ipt]])

                def pv(e, ipt=ipt, h=h):
                    for blk in range(2):
                        ins = e.matmul(ps_out[0:nq, h * 65:(h + 1) * 65], lhsT=PT_[ipt][:, blk * 128:blk * 128 + nq], rhs=v_ap_fn(blk, h),
                                       start=(blk == 0), stop=(blk == 1))
                    return ins
                P.op("pe", pv, reads=[t_PT_[ipt], t_v], writes=[t_ps_out])

        def normalize_o(ps_in, t_ps_in, nq, dst_ap, t_dst, t_rc, rc):
            v = ps_in[0:nq, 0:260].rearrange("p (h d) -> p h d", h=4)
            P.op("dve", lambda e: e.reciprocal(out=rc[0:nq, 0:4], in_=v[:, :, 64]), reads=[t_ps_in], writes=[t_rc])
            P.op("dve", lambda e: e.tensor_tensor(out=dst_ap.rearrange("p (h d) -> p h d", h=4), in0=v[:, :, 0:64],
                                                  in1=rc[0:nq, 0:4].unsqueeze(2).to_broadcast([nq, 4, 64]), op=ALU.mult),
                 reads=[t_ps_in, t_rc], writes=[t_dst])

        def post_a(nq, hbuf_ap, t_hb_in, gate_ap, t_gate_in, nk, hb_t, t_hb_t, hT_t, t_hT_t, col0):
            P.op("dve", lambda e: e.tensor_tensor(out=hb_t[0:nq, 0:nk * 128], in0=hbuf_ap, in1=gate_ap, op=ALU.mult),
                 reads=[t_hb_in, t_gate_in], writes=[t_hb_t])

            def tr(e):
                for k in range(nk):
                    ins = e.transpose(out=ps_t[:, k * 128:k * 128 + nq], in_=hb_t[0:nq, k * 128:(k + 1) * 128], identity=ident[0:nq, 0:nq])
                return ins
            P.op("pe", tr, reads=[t_hb_t, t_ident], writes=[t_ps_t])
            copy_op(ev_eng(), hT_t[:, 0:nk, col0:col0 + nq], ps_t[:].rearrange("p (k m) -> p k m", k=8)[:, 0:nk, 0:nq], [t_ps_t], [t_hT_t])

        def post_b(gi_post, nq, nk, hT_t, t_hT_t, wout_t, t_wout_t, ys_, x_src_fn, x_dst_fn):
            ytmp_, t_ytmp_, xres_, t_xres_ = CTX["ytmp"], CTX["t_ytmp"], CTX["xres"], CTX["t_xres"]
            for nb in range(2):
                def mm(e, nb=nb):
                    for k in range(nk):
                        ins = e.matmul(ps_mm[nb][0:nq, :], lhsT=hT_t[:, k, 0:nq], rhs=wout_t[:, k, nb * 512:(nb + 1) * 512], start=(k == 0), stop=(k == nk - 1))
                    return ins
                P.op("pe", mm, reads=[t_hT_t, t_wout_t], writes=[t_ps_mm[nb]])
                copy_op("act" if nb == 0 else "dve", ytmp_[ys_][0:nq, nb * 512:(nb + 1) * 512], ps_mm[nb][0:nq, :], [t_ps_mm[nb]], [t_ytmp_[ys_]])
            x_src_fn(ys_)
            rmsnorm(ytmp_[ys_][0:nq, :], nq, gi_post, ytmp_[ys_][0:nq, :], t_ytmp_[ys_], t_ytmp_[ys_])
            P.op("dve", lambda e: e.tensor_tensor(out=xres_[ys_][0:nq, :], in0=xres_[ys_][0:nq, :], in1=ytmp_[ys_][0:nq, :], op=ALU.add),
                 reads=[t_ytmp_[ys_], t_xres_[ys_]], writes=[t_xres_[ys_]])
            x_dst_fn(ys_)

        def do_tile(Tq):
            hs_ = nxt("hb")
            io = nxt("o")
            srcs = [(0, 0, Tq, ident[:, :])]
            srcs += [(1, r, Tq // 4, perm_lhsT(1, r, Tq)) for r in range(4)]
            srcs += [(2, r, 0, perm_lhsT(2, r, Tq)) for r in range(16)]

            def comb(e):
                ns = len(srcs)
                for si, (g, r, sb_, lt) in enumerate(srcs):
                    gi_ = r * (NT // DIL[g]) + sb_
                    ins = e.matmul(ps_o[io][:, 0:260], lhsT=lt, rhs=Og[:, 16 * g + gi_, :], start=(si == 0), stop=(si == ns - 1))
                return ins
            P.op("pe", comb, reads=[t_ident, t_M] + [t_Og[16 * g + r * (NT // DIL[g]) + sb_] for (g, r, sb_, _) in srcs], writes=[t_ps_o[io]])
            normalize_o(ps_o[io], t_ps_o[io], 128, hbuf[hs_][:, 0:256], t_hbuf[hs_], t_rcp[hs_], rcp[hs_])
            io2 = nxt("o")
            cross_attn((lambda psl, pr, blk: KmT[psl, 0, pr, blk * 128:(blk + 1) * 128]), t_KmT[0],
                       (lambda blk, h: Vm[:, 0, blk, h, 0:65]), t_Vm[0],
                       (lambda psl, pr: QmT[psl, pr, Tq * 128:(Tq + 1) * 128]), [x for row in t_QmT for x in row],
                       128, ps_o[io2], t_ps_o[io2])
            normalize_o(ps_o[io2], t_ps_o[io2], 128, hbuf[hs_][:, 256:512], t_hbuf[hs_], t_rcp[hs_], rcp[hs_])
            post_a(128, hbuf[hs_][:, :], t_hbuf[hs_], gate[:, Tq, :], t_gate[Tq], 4, hb[hs_], t_hb[hs_], hT[hs_], t_hT[hs_], 0)

            def xsrc(ys_):
                P.op("sp", lambda e: e.dma_start(out=xres[ys_][:, :], in_=x_p[Tq * 128:(Tq + 1) * 128, :]), writes=[t_xres[ys_]], dma=True)

            def xdst(ys_):
                P.op("sp", lambda e: e.dma_start(out=x1_d[Tq * 128:(Tq + 1) * 128, :], in_=xres[ys_][:, :]), reads=[t_xres[ys_]], writes=[t_x1[Tq]],
                     dma=True, sem_tk=t_xres[ys_])
                if "l0out" in DBG:
                    P.op("sp", lambda e: e.dma_start(out=y_p[Tq * 128:(Tq + 1) * 128, :], in_=xres[ys_][:, :]), reads=[t_xres[ys_]], dma=True)
            post_b(G_POST[0], 128, 4, hT[hs_], t_hT[hs_], wout, t_wout, hs_, xsrc, xdst)

        for Tq in range(NT):
            do_tile(Tq)

        def load_piece(src_ap, nt_):
            cst, ckb, cKT, cV = CTX["cst"], CTX["ckb"], CTX["cKT"], CTX["cV"]
            t_cst, t_ckb, t_cKT, t_cV = CTX["t_cst"], CTX["t_ckb"], CTX["t_cKT"], CTX["t_cV"]
            P.op("sp", lambda e: e.dma_start(out=cst[:, 0:nt_, :], in_=src_ap), writes=[t_cst], dma=True)
            P.op("dve", lambda e: e.tensor_copy(out=ckb[:, 0:nt_, :], in_=cst[:, 0:nt_, 0:256]), reads=[t_cst], writes=[t_ckb])
            P.op("act", lambda e: e.copy(out=cV[:, 0:nt_, :, 0:64], in_=cst[:, 0:nt_, 256:512].rearrange("p t (h d) -> p t h d", h=4)),
                 reads=[t_cst], writes=[t_cV])

            def tr(e):
                for t_ in range(nt_):
                    for pr in range(2):
                        idx = t_ * 2 + pr
                        ins = e.transpose(out=ps_t[:, idx * 128:(idx + 1) * 128], in_=ckb[:, t_, pr * 128:(pr + 1) * 128], identity=ident[:, :])
                return ins
            P.op("pe", tr, reads=[t_ckb, t_ident], writes=[t_ps_t])
            copy_op(ev_eng(), cKT[:, 0:2 * nt_, :], ps_t[:].rearrange("p (k m) -> p k m", k=8)[:, 0:2 * nt_, :], [t_ps_t], [t_cKT])

        def sample_batch(bb):
            io = nxt("o")
            pso, tpso = ps_o[io], t_ps_o[io]
            first = [True]

            def pv_mm(e, out_ap, lhsT, rhs):
                ins = e.matmul(out_ap, lhsT=lhsT, rhs=rhs, start=first[0], stop=False, skip_group_check=True)
                first[0] = False
                return ins
            qs = slice(4 * bb, 4 * bb + 4)
            for g in range(3):
                d = DIL[g]
                if g == 0:
                    load_piece(c_w0[bb].rearrange("(j o) c -> j o c", o=1), 1)
                elif g == 1:
                    load_piece(c_w1[bb].rearrange("(j r) c -> j r c", r=4), 4)
                else:
                    load_piece(c_w2[bb].rearrange("(j r) c -> j r c", r=16)[:, 0:4, :], 4)
                for h in range(4):
                    pr, hf = h // 2, h % 2
                    psl = slice(64 * hf, 64 * hf + 64)
                    isx = nxt("s")
                    if g == 0:
                        def qk(e, isx=isx, pr=pr, psl=psl):
                            e.matmul(ps_s[isx][:, 0:4], lhsT=cKT[psl, pr, :], rhs=QTs[psl, pr, qs], start=True, stop=True)
                            return e.matmul(ps_s[isx][0:4, 8:12], lhsT=KTs[psl, pr, qs], rhs=QTs[psl, pr, qs], start=True, stop=True)
                        P.op("pe", qk, reads=[t_cKT, t_QTs, t_KTs], writes=[t_ps_s[isx]])
                        isb = nxt("sb")
                        P.op("dve", (lambda e, isx=isx, isb=isb, h=h: e.scalar_tensor_tensor(
                            out=sbf[isb][:, 0:4], in0=ps_s[isx][:, 0:4], scalar=SCALE, in1=biasT[:, h, 0, 0:4], op0=ALU.mult, op1=ALU.add)),
                            reads=[t_ps_s[isx], t_biasT], writes=[t_sbf[isb]])
                        P.op("dve", (lambda e, isx=isx, h=h: e.scalar_tensor_tensor(
                            out=sbn[0:4, 0:4], in0=ps_s[isx][0:4, 8:12], scalar=SCALE, in1=biasT[0:4, h, 1, 0:4], op0=ALU.mult, op1=ALU.add)),
                            reads=[t_ps_s[isx], t_biasT], writes=[t_sbn])
                        ipt = nxt("pt")
                        P.op("act", (lambda e, isb=isb, ipt=ipt: e.activation(out=PT[ipt][:, 0:4], in_=sbf[isb][:, 0:4], func=AF.Exp)),
                             reads=[t_sbf[isb]], writes=[t_PT[ipt]])
                        P.op("act", lambda e: e.activation(out=PTn[0:4, 0:4], in_=sbn[0:4, 0:4], func=AF.Exp), reads=[t_sbn], writes=[t_PTn])

                        def pv(e, ipt=ipt, h=h):
                            pv_mm(e, pso[0:4, h * 65:(h + 1) * 65], PT[ipt][:, 0:4], cV[:, 0, h, 0:65])
                            return pv_mm(e, pso[0:4, h * 65:(h + 1) * 65], PTn[0:4, 0:4], Vns[0:4, bb, 0, h, 0:65])
                        P.op("pe", pv, reads=[t_PT[ipt], t_PTn, t_cV, t_Vns], writes=[tpso])
                    else:
                        def qk(e, isx=isx, pr=pr, psl=psl, g=g):
                            for t_ in range(4):
                                e.matmul(ps_s[isx][:, t_:t_ + 1], lhsT=cKT[psl, 2 * t_ + pr, :], rhs=QTs[psl, 2 * g + pr, 4 * bb + t_:4 * bb + t_ + 1],
                                         start=True, stop=True)
                            return e.matmul(ps_s[isx][0:4, 8:12], lhsT=KTs[psl, 2 * g + pr, qs], rhs=QTs[psl, 2 * g + pr, qs], start=True, stop=True)
                        P.op("pe", qk, reads=[t_cKT, t_QTs, t_KTs], writes=[t_ps_s[isx]])
                        P.op("act", (lambda e, isx=isx, h=h, g=g: e.activation(out=PTz[:, 0:16:5], in_=ps_s[isx][:, 0:4], func=AF.Exp,
                                                                              bias=biasT[:, g * 4 + h, 0, 0:1], scale=SCALE)),
                             reads=[t_ps_s[isx], t_biasT], writes=[t_PTz])
                        P.op("dve", (lambda e, isx=isx, h=h, g=g: e.scalar_tensor_tensor(
                            out=sbn[0:4, 0:4], in0=ps_s[isx][0:4, 8:12], scalar=SCALE, in1=biasT[0:4, g * 4 + h, 1, 0:4], op0=ALU.mult, op1=ALU.add)),
                            reads=[t_ps_s[isx], t_biasT], writes=[t_sbn])
                        P.op("act", lambda e: e.activation(out=sbn[0:4, 4:8], in_=sbn[0:4, 0:4], func=AF.Exp), reads=[t_sbn], writes=[t_sbn])
                        P.op("dve", lambda e: e.tensor_tensor(out=PTn[0:4, 0:4], in0=sbn[0:4, 4:8], in1=identf[0:4, 0:4], op=ALU.mult),
                             reads=[t_sbn, t_ident], writes=[t_PTn])

                        def pv(e, h=h, g=g):
                            for t_ in range(4):
                                pv_mm(e, pso[0:4, h * 65:(h + 1) * 65], PTz[:, 4 * t_:4 * t_ + 4], cV[:, t_, h, 0:65])
                            return pv_mm(e, pso[0:4, h * 65:(h + 1) * 65], PTn[0:4, 0:4], Vns[0:4, bb, g, h, 0:65])
                        P.op("pe", pv, reads=[t_PTz, t_PTn, t_cV, t_Vns], writes=[tpso])
            hs_ = 0
            normalize_o(pso, tpso, 4, hbuf[hs_][0:4, 0:256], t_hbuf[hs_], t_rcp[hs_], rcp[hs_])
            load_piece(c_mem[0, bb].rearrange("(b j) c -> j b c", b=2), 2)
            io2 = nxt("o")
            cross_attn((lambda psl, pr, blk: cKT[psl, 2 * blk + pr, :]), t_cKT, (lambda blk, h: cV[:, blk, h, 0:65]), t_cV,
                       (lambda psl, pr: QTs[psl, 6 + pr, qs]), [t_QTs], 4, ps_o[io2], t_ps_o[io2])
            normalize_o(ps_o[io2], t_ps_o[io2], 4, hbuf[hs_][0:4, 256:512], t_hbuf[hs_], t_rcp[hs_], rcp[hs_])
            post_a(4, hbuf[hs_][0:4, :], t_hbuf[hs_], gate_s[0:4, bb, :], t_gate_s, 4, hb[hs_], t_hb[hs_], hT[1], t_hT[1], 4 * bb)

        for bb in range(0 if "nosample" in DBG else 4):
            sample_batch(bb)

        def xsrc_s(ys_):
            P.op("sp", lambda e: e.dma_start(out=xres[ys_][0:16, :], in_=x_s), writes=[t_xres[ys_]], dma=True)

        def xdst_s(ys_):
            P.op("sp", lambda e: e.dma_start(out=x1s_d, in_=xres[ys_][0:16, :]), reads=[t_xres[ys_]], writes=[t_x1[NT]], dma=True, sem_tk=t_xres[ys_])
            if "l0out" in DBG:
                P.op("sp", lambda e: e.dma_start(out=y_s, in_=xres[ys_][0:16, :]), reads=[t_xres[ys_]], dma=True)
        if "nosample" not in DBG:
            post_b(G_POST[0], 16, 4, hT[1], t_hT[1], wout, t_wout, 1, xsrc_s, xdst_s)
        print('n_ops', len(P.ops))
        if 'dump' in DBG:
            for _i, _o in enumerate(P.ops):
                print('OP', _i, _o.eng, _o.line, 'dma' if _o.is_dma else '')

        if STAGE >= 2:
            barrier()
            apos[0] = 0
            atop[0] = ARENA_WORDS
            load_gains([(0, norm_pre, 1), (1, norm_post, 1)])
            Wb = carve_bf(8 * B_IN).rearrange("p (k n) -> p k n", k=8)
            t_Wb = Tk()
            for c0 in range(0, B_IN, 464):
                P.op("pool", (lambda e, c0=c0: e.dma_start(out=Wb[:, :, c0:c0 + 464], in_=w_in_b[:, c0:c0 + 464].rearrange("(k p) n -> p k n", p=128))),
                     writes=[t_Wb], dma=True)
            Wo = carve_bf(8 * 1024).rearrange("p (k n) -> p k n", k=8)
            t_Wo = Tk()
            P.op("pool", lambda e: e.dma_start(out=Wo, in_=w_out_b.rearrange("(k p) n -> p k n", p=128)), writes=[t_Wo], dma=True)
            wup = carve_bf(768)
            aup = carve_bf(768)
            t_lora = Tk()
            P.op("pool", lambda e: e.dma_start(out=wup[0:64, :], in_=r_wup), writes=[t_lora], dma=True)
            P.op("pool", lambda e: e.dma_start(out=aup[64:128, :], in_=r_aup), writes=[t_lora], dma=True)
            par = carve_f(64)
            t_par = Tk()
            P_MU, P_W0, P_A0, P_KK, P_KA, P_OMKA, P_RK = 0, 19, 25, 31, 37, 43, 49
            for (src, c0, nm) in ((r_mu, P_MU, 19), (r_w0, P_W0, 6), (r_a0, P_A0, 6), (r_kk, P_KK, 6), (r_ka, P_KA, 6), (r_rk, P_RK, 6)):
                P.op("sp", (lambda e, src=src, c0=c0, nm=nm: e.dma_start(out=par[:, c0:c0 + nm], in_=src.rearrange("o (m p) -> p (o m)", p=128),
                                                                         allow_slow_non_contiguous=True)), writes=[t_par], dma=True)
            P.op("dve", lambda e: e.tensor_scalar(out=par[:, P_OMKA:P_OMKA + 6], in0=par[:, P_KA:P_KA + 6], scalar1=-1.0, scalar2=1.0, op0=ALU.mult, op1=ALU.add),
                 reads=[t_par], writes=[t_par])
            lnw_b = carve_f(768)
            lnb_b = carve_f(768)
            t_ln = Tk()
            for (dst, src) in ((lnw_b, r_lnw), (lnb_b, r_lnb)):
                apb = bass.AP(tensor=src.tensor, offset=0, ap=[[0, 128], [1, 768]])
                P.op("sp", (lambda e, dst=dst, apb=apb: e.dma_start(out=dst, in_=apb)), writes=[t_ln], dma=True)
            SA, SB, SC, SD, SE, SF, SG = [carve_f(768).rearrange("p (m j) -> p m j", m=6) for _ in range(7)]
            t_S = {k_: Tk() for k_ in "ABCDEFG"}
            t_mt1 = t_S["A"]
            mtmp1 = SA[:, :, :].rearrange("p m j -> p (m j)")
            ML = carve_bf(512).rearrange("p (h i) -> p h i", h=4)
            MU = carve_bf(512).rearrange("p (h i) -> p h i", h=4)
            MUi = carve_bf(512).rearrange("p (h i) -> p h i", h=4)
            Irep = carve_bf(512).rearrange("p (h i) -> p h i", h=4)
            rmask = carve_bf(768).rearrange("p (m j) -> p m j", m=6)
            bd = carve_f(128)
            hsel = carve_bf(2)
            t_cst1 = Tk()
            for (Mt, cm, step, cmp_) in ((ML, 1, -1, ALU.is_gt), (MU, -1, 1, ALU.is_gt), (MUi, -1, 1, ALU.is_ge), (Irep, 1, -1, ALU.is_equal)):
                P.op("pool", lambda e: e.memset(mtmp1[:, 0:512], 1.0), writes=[t_mt1])

                def mk(e, cm=cm, step=step, cmp_=cmp_):
                    v = mtmp1[:, 0:512].rearrange("p (h i) -> p h i", h=4)
                    return e.affine_select(out=v, in_=v, pattern=[[0, 4], [step, 128]], compare_op=cmp_, fill=0.0, base=0, channel_multiplier=cm)
                P.op("pool", mk, reads=[t_mt1], writes=[t_mt1])
                P.op("dve", (lambda e, Mt=Mt: e.tensor_copy(out=Mt, in_=mtmp1[:, 0:512].rearrange("p (h i) -> p h i", h=4))), reads=[t_mt1], writes=[t_cst1])

            def mk2(e):
                e.memset(rmask[:, :, :], 1.0)
                e.memset(rmask[:, :, 0:1], 0.0)
                e.memset(bd[:, :], 0.0)
                e.memset(bd[0:64, 0:64], 1.0)
                e.memset(bd[64:128, 64:128], 1.0)
                e.memset(hsel[:, :], 0.0)
                e.memset(hsel[0:64, 0:1], 1.0)
                return e.memset(hsel[64:128, 1:2], 1.0)
            P.op("pool", mk2, writes=[t_cst1])

            xld1 = carve_f(1024)
            xnb1 = carve_bf(1024)
            xnTc = carve_bf(8 * 128).rearrange("p (k t) -> p k t", k=8)
            xnTs1 = carve_bf(8 * 16).rearrange("p (k t) -> p k t", k=8)
            t_xld1, t_xnb1, t_xnTc, t_xnTs1 = Tk(), Tk(), Tk(), Tk()
            cols = carve_f(19 * 129).rearrange("p (m j) -> p m j", m=19)
            t_cols = Tk()
            lastc = carve_f(20)
            t_xs = t_cols
            QmTc = carve_bf(2 * 128).rearrange("p (m j) -> p m j", m=2)
            t_QmTc = Tk()
            gt = carve_bf(1024)
            t_gt = Tk()
            lw = carve_bf(128)
            t_lw = Tk()
            outs_w0 = apos[0]
            t_o = {k_: Tk() for k_ in ("AT", "BT", "KT", "KH", "BH", "RT", "RKT", "XV")}
            BT, KT1, KH, BH, RKT, XV = [carve_bf(768).rearrange("p (m j) -> p m j", m=6) for _ in range(6)]
            AT2 = carve_bf(12 * 128).rearrange("p (h j) -> p h j", h=12)
            RT2 = carve_bf(12 * 128).rearrange("p (h j) -> p h j", h=12)
            P.op("pool", lambda e: e.memset(AT2[:, :, :], 0.0), writes=[t_o["AT"]])
            P.op("pool", lambda e: e.memset(RT2[:, :, :], 0.0), writes=[t_o["RT"]])
            Vt = carve_bf(768)
            Kh = carve_bf(768)
            Bh = carve_bf(768)
            t_Vt, t_Kh, t_Bh = Tk(), Tk(), Tk()
            Lh = [carve_bf(512).rearrange("p (h i) -> p h i", h=4) for _ in range(3)]
            Xh = [carve_bf(512).rearrange("p (h i) -> p h i", h=4) for _ in range(3)]
            Qh = [carve_bf(512).rearrange("p (h i) -> p h i", h=4) for _ in range(3)]
            t_Lh, t_Xh, t_Qh = [Tk() for _ in range(3)], [Tk() for _ in range(3)], [Tk() for _ in range(3)]
            Qf = carve_bf(12 * 128).rearrange("p (h i) -> p h i", h=12)
            Aak = carve_bf(12 * 128).rearrange("p (h i) -> p h i", h=12)
            Ark = carve_bf(12 * 128).rearrange("p (h i) -> p h i", h=12)
            Arb = carve_bf(12 * 128).rearrange("p (h i) -> p h i", h=12)
            t_Qf, t_Aak, t_Ark, t_Arb = Tk(), Tk(), Tk(), Tk()
            zb_w0 = apos[0]
            Zb = carve_bf(768)
            Ub = carve_bf(768)
            t_Zb, t_Ub = Tk(), Tk()
            Yf = SD[:, :, :].rearrange("p m j -> p (m j)")
            t_Yf = t_S["D"]
            St = carve_f(384).rearrange("p (m v) -> p m v", m=6)
            Sbf = carve_bf(384).rearrange("p (m v) -> p m v", m=6)
            t_St = Tk()
            wc = carve_f(8)
            t_wc = Tk()
            gsm = carve_f(96)
            t_gsm = Tk()
            hbuf1 = carve_f(1024)
            hb1 = carve_bf(1024)
            hT1 = carve_bf(8 * 128).rearrange("p (k t) -> p k t", k=8)
            t_hbuf1, t_hb1, t_hT1 = Tk(), Tk(), Tk()
            rcp1 = carve_f(8)
            t_rcp1 = Tk()
            xres1 = carve_f(1024)
            t_xres1 = Tk()
            PT1 = [carve_bf(256) for _ in range(2)]
            t_PT1 = [Tk(), Tk()]
            svst = arena[:, zb_w0:zb_w0 + 768].rearrange("p (h k) -> p h k", h=12)
            t_svst = Tk()
            print("L1 arena words", apos[0])
            C0 = 0.6065306597126334
            psq = [ps_s[0], ps_s[1], ps_o[0], ps_o[1]]
            t_psq = [t_ps_s[0], t_ps_s[1], t_ps_o[0], t_ps_o[1]]
            rr.update({"q": 0})

            def nq4():
                rr["q"] += 1
                return rr["q"] % 4

            def tt(eng, out, in0, in1, op, reads, writes):
                P.op(eng, lambda e: e.tensor_tensor(out=out, in0=in0, in1=in1, op=op), reads=reads, writes=writes)

            def bc6(col0, C):
                return par[:, col0:col0 + 6].unsqueeze(2).to_broadcast([128, 6, C])

            def rwkv_chunk(C, xn_ap, t_xn, mode, idx):
                first = (idx == 0) if mode == "p" else True
                last = (idx == NT - 1) if mode == "p" else True
                groups = [(16, 3), (0, 4), (4, 4), (8, 4), (12, 4), (19, 2)]
                for (m0, nm) in groups:
                    i = nxt("mm")

                    def mm(e, m0=m0, nm=nm, i=i):
                        for mi in range(nm):
                            m = m0 + mi
                            for k in range(8):
                                ins = e.matmul(ps_mm[i][:, mi * 128:mi * 128 + C], lhsT=Wb[:, k, m * 128:(m + 1) * 128], rhs=xn_ap(k),
                                               start=(k == 0), stop=(k == 7))
                        return ins
                    P.op("pe", mm, reads=[t_Wb, t_xn], writes=[t_ps_mm[i]])
                    src = ps_mm[i][:, 0:nm * 128].rearrange("p (m j) -> p m j", m=nm)[:, :, 0:C]
                    if m0 < 19:
                        copy_op("act", cols[:, m0:m0 + nm, 1:1 + C], src, [t_ps_mm[i]], [t_cols])
                    else:
                        copy_op("act", QmTc[:, :, 0:C], src, [t_ps_mm[i]], [t_QmTc])
                for gc in range(2):
                    i = nxt("mm")

                    def mmg(e, gc=gc, i=i):
                        for k in range(8):
                            ins = e.matmul(ps_mm[i][0:C, :], lhsT=xn_ap(k), rhs=Wb[:, k, 2688 + gc * 512:2688 + (gc + 1) * 512], start=(k == 0), stop=(k == 7))
                        return ins
                    P.op("pe", mmg, reads=[t_Wb, t_xn], writes=[t_ps_mm[i]])
                    P.op("act", (lambda e, gc=gc, i=i: e.activation(out=gt[0:C, gc * 512:(gc + 1) * 512], in_=ps_mm[i][0:C, :], func=AF.Silu)),
                         reads=[t_ps_mm[i]], writes=[t_gt])
                t_lastc = Tk()
                P.op("act", lambda e: e.copy(out=lastc[:, 0:19], in_=cols[:, :, C]), reads=[t_cols], writes=[t_lastc])
                if last:
                    dst = (o_shp if mode == "p" else o_shs[idx:idx + 1, :]).rearrange("o (m p) -> p (o m)", p=128)
                    P.op("sp", lambda e: e.dma_start(out=dst, in_=lastc[:, 0:19], allow_slow_non_contiguous=True), reads=[t_lastc], dma=True)
                for (m0, nm) in ((0, 6), (6, 6), (12, 6), (18, 1)):
                    cur = cols[:, m0:m0 + nm, 1:1 + C]
                    prv = cols[:, m0:m0 + nm, 0:C]
                    tmp = SG[:, 0:nm, 0:C]
                    tt("dve", tmp, prv, cur, ALU.subtract, [t_cols], [t_S["G"]])
                    tt("dve", tmp, tmp, par[:, P_MU + m0:P_MU + m0 + nm].unsqueeze(2).to_broadcast([128, nm, C]), ALU.mult, [t_S["G"], t_par], [t_S["G"]])
                    tt("dve", cur, tmp, cur, ALU.add, [t_S["G"], t_cols], [t_cols])
                P.op("act", lambda e: e.copy(out=cols[:, :, 0], in_=lastc[:, 0:19]), reads=[t_lastc, t_cols], writes=[t_cols])

                class _XS:
                    def __getitem__(self, key):
                        p_, m_, j_ = key
                        assert j_ == slice(0, C)
                        return cols[p_, m_, 1:1 + C]
                xs = _XS()
                xr, xk, xv_ = xs[:, 0:6, 0:C], xs[:, 6:12, 0:C], xs[:, 12:18, 0:C]
                P.op("act", lambda e: e.activation(out=lw[0:64, 0:C], in_=xs[0:64, 18, 0:C], func=AF.Tanh), reads=[t_xs], writes=[t_lw])
                P.op("dve", lambda e: e.tensor_copy(out=lw[64:128, 0:C], in_=xs[64:128, 18, 0:C]), reads=[t_xs], writes=[t_lw])
                sigw, asig = SA[:, :, 0:C], SB[:, :, 0:C]
                for (which, wt, rows, pcol, dstS, tS) in (("w", wup, slice(0, 64), P_W0, SA, "A"), ("a", aup, slice(64, 128), P_A0, SB, "B")):
                    for (p0, np_) in ((0, 4), (4, 2)):
                        q = nq4()

                        def mml(e, wt=wt, rows=rows, p0=p0, np_=np_, q=q):
                            for pi in range(np_):
                                p = p0 + pi
                                ins = e.matmul(psq[q][:, pi * 128:pi * 128 + C], lhsT=wt[rows, p * 128:(p + 1) * 128], rhs=lw[rows, 0:C], start=True, stop=True)
                            return ins
                        P.op("pe", mml, reads=[t_lora, t_lw], writes=[t_psq[q]])
                        for pi in range(np_):
                            p = p0 + pi
                            P.op("act", (lambda e, pi=pi, p=p, q=q, pcol=pcol, dstS=dstS: e.activation(
                                out=dstS[:, p, 0:C], in_=psq[q][:, pi * 128:pi * 128 + C], func=AF.Sigmoid, bias=par[:, pcol + p:pcol + p + 1], scale=1.0)),
                                reads=[t_psq[q], t_par], writes=[t_S[tS]])
                cs = SC[:, :, 0:C]
                if C == 128:
                    P.op("dve", lambda e: e.tensor_tensor_scan(out=SC[:, :, :].rearrange("p m j -> p (m j)"), data0=rmask[:, :, :].rearrange("p m j -> p (m j)"),
                                                               data1=SA[:, :, :].rearrange("p m j -> p (m j)"), initial=0.0, op0=ALU.mult, op1=ALU.add),
                         reads=[t_S["A"], t_cst1], writes=[t_S["C"]])
                else:
                    for p in range(6):
                        P.op("dve", (lambda e, p=p: e.tensor_tensor_scan(out=SC[:, p, 0:C], data0=rmask[:, p, 0:C], data1=SA[:, p, 0:C], initial=0.0,
                                                                         op0=ALU.mult, op1=ALU.add)), reads=[t_S["A"], t_cst1], writes=[t_S["C"]])
                csC = SC[:, :, C - 1:C]
                eP, eH, eA, eN = SE[:, :, 0:C], SD[:, :, 0:C], SA[:, :, 0:C], SC[:, :, 0:C]
                P.op("act", lambda e: e.activation(out=eP, in_=cs, func=AF.Exp, scale=-C0), reads=[t_S["C"]], writes=[t_S["E"]])
                tt("dve", eH, csC.to_broadcast([128, 6, C]), cs, ALU.subtract, [t_S["C"]], [t_S["D"]])
                P.op("act", lambda e: e.activation(out=eH, in_=eH, func=AF.Exp, scale=-C0), reads=[t_S["D"]], writes=[t_S["D"]])
                P.op("act", lambda e: e.activation(out=wc[:, 0:6], in_=SC[:, :, C - 1], func=AF.Exp, scale=-C0), reads=[t_S["C"]], writes=[t_wc])
                tt("dve", eA, cs, sigw, ALU.subtract, [t_S["C"], t_S["A"]], [t_S["A"]])
                P.op("act", lambda e: e.activation(out=eA, in_=eA, func=AF.Exp, scale=-C0), reads=[t_S["A"]], writes=[t_S["A"]])
                P.op("act", lambda e: e.activation(out=eN, in_=cs, func=AF.Exp, scale=C0), reads=[t_S["C"], t_S["D"], t_S["E"], t_S["A"], t_wc], writes=[t_S["C"]])
                kk, g_ = SF[:, :, 0:C], SG[:, :, 0:C]
                tt("dve", kk, xk, bc6(P_KK, C), ALU.mult, [t_xs, t_par], [t_S["F"]])
                tt("dve", g_, kk, kk, ALU.mult, [t_S["F"]], [t_S["G"]])
                qa, qb_ = nq4(), nq4()

                def mmn(e):
                    e.matmul(psq[qa][:, 0:4 * 128].rearrange("p (m j) -> p m j", m=4)[:, :, 0:C], lhsT=bd[:, :], rhs=SG[:, 0:4, 0:C], start=True, stop=True)
                    return e.matmul(psq[qb_][:, 0:2 * 128].rearrange("p (m j) -> p m j", m=2)[:, :, 0:C], lhsT=bd[:, :], rhs=SG[:, 4:6, 0:C], start=True, stop=True)
                P.op("pe", mmn, reads=[t_S["G"], t_cst1], writes=[t_psq[qa], t_psq[qb_]])
                P.op("act", lambda e: e.activation(out=SG[:, 0:4, 0:C], in_=psq[qa][:, 0:512].rearrange("p (m j) -> p m j", m=4)[:, :, 0:C], func=AF.Sqrt),
                     reads=[t_psq[qa]], writes=[t_S["G"]])
                P.op("act", lambda e: e.activation(out=SG[:, 4:6, 0:C], in_=psq[qb_][:, 0:256].rearrange("p (m j) -> p m j", m=2)[:, :, 0:C], func=AF.Sqrt),
                     reads=[t_psq[qb_]], writes=[t_S["G"]])
                P.op("dve", lambda e: e.tensor_scalar_max(out=g_, in0=g_, scalar1=1e-12), reads=[t_S["G"]], writes=[t_S["G"]])
                P.op("dve", lambda e: e.reciprocal(out=g_, in_=g_), reads=[t_S["G"]], writes=[t_S["G"]])
                tt("dve", kk, kk, g_, ALU.mult, [t_S["F"], t_S["G"]], [t_S["F"]])
                for hf_ in range(2):
                    rs_ = slice(64 * hf_, 64 * hf_ + 64)
                    P.op("dve", (lambda e, hf_=hf_, rs_=rs_: e.scalar_tensor_tensor(out=AT2[rs_, hf_:12:2, 0:C], in0=SF[rs_, :, 0:C], scalar=-1.0,
                                                                                   in1=SA[rs_, :, 0:C], op0=ALU.mult, op1=ALU.mult)),
                         reads=[t_S["F"], t_S["A"]], writes=[t_o["AT"]])
                tt("dve", g_, kk, asig, ALU.mult, [t_S["F"], t_S["B"]], [t_S["G"]])
                tt("dve", BT[:, :, 0:C], g_, eN, ALU.mult, [t_S["G"], t_S["C"]], [t_o["BT"]])
                tt("dve", BH[:, :, 0:C], g_, eH, ALU.mult, [t_S["G"], t_S["D"]], [t_o["BH"]])
                km = hbuf1[:, 0:768].rearrange("p (m j) -> p m j", m=6)[:, :, 0:C]
                tt("pool", km, asig, bc6(P_KA, C), ALU.mult, [t_S["B"], t_par], [t_hbuf1])
                tt("pool", km, km, bc6(P_OMKA, C), ALU.add, [t_hbuf1, t_par], [t_hbuf1])
                tt("pool", km, km, xk, ALU.mult, [t_hbuf1, t_xs], [t_hbuf1])
                tt("pool", KT1[:, :, 0:C], km, eN, ALU.mult, [t_hbuf1, t_S["C"]], [t_o["KT"]])
                tt("pool", KH[:, :, 0:C], km, eH, ALU.mult, [t_hbuf1, t_S["D"]], [t_o["KH"]])
                tt("pool", km, km, bc6(P_RK, C), ALU.mult, [t_hbuf1, t_par], [t_hbuf1])
                tt("pool", RKT[:, :, 0:C], km, xr, ALU.mult, [t_hbuf1, t_xs], [t_o["RKT"]])
                for hf_ in range(2):
                    rs_ = slice(64 * hf_, 64 * hf_ + 64)
                    tt("pool", RT2[rs_, hf_:12:2, 0:C], cols[rs_, 0:6, 1:1 + C], SE[rs_, :, 0:C], ALU.mult, [t_xs, t_S["E"]], [t_o["RT"]])
                P.op("act", lambda e: e.copy(out=XV[:, :, 0:C], in_=xv_), reads=[t_xs], writes=[t_o["XV"]])
                for (srcT, tsrc, dstT, tdst) in ((XV, "XV", Vt, t_Vt), (KH, "KH", Kh, t_Kh), (BH, "BH", Bh, t_Bh)):
                    def tr(e, srcT=srcT):
                        for p in range(6):
                            ins = e.transpose(out=ps_t[0:C, p * 128:(p + 1) * 128], in_=srcT[:, p, 0:C], identity=ident[:, :])
                        return ins
                    P.op("pe", tr, reads=[t_o[tsrc], t_ident], writes=[t_ps_t])
                    copy_op(ev_eng(), dstT[0:C, :], ps_t[0:C, 0:768], [t_ps_t], [tdst])
                nlev = max(1, int(math.ceil(math.log2(C))))

                def pair_mm(q, l_fn, r_fn, hg, reads):
                    def f(e):
                        for hi in range(4):
                            h = 4 * hg + hi
                            ins = e.matmul(psq[q][0:C, hi * 128:hi * 128 + C], lhsT=l_fn(h), rhs=r_fn(h), start=True, stop=True)
                        return ins
                    P.op("pe", f, reads=reads, writes=[t_psq[q]])

                A2 = lambda h: AT2[:, h, 0:C]
                R2 = lambda h: RT2[:, h, 0:C]
                Bp = lambda h: BT[:, h // 2, 0:C]
                Kp = lambda h: KT1[:, h // 2, 0:C]

                def pv4(q):
                    return psq[q][0:C, :].rearrange("p (h i) -> p h i", h=4)[:, :, 0:C]

                def sq_mm(q, lT, rT, reads, acc_ident_rhs=None):
                    def f(e):
                        for hi in range(4):
                            if acc_ident_rhs is not None:
                                e.matmul(psq[q][0:C, hi * 128:hi * 128 + C], lhsT=ident[0:C, 0:C], rhs=acc_ident_rhs[0:C, hi, 0:C], start=True, stop=False)
                            ins = e.matmul(psq[q][0:C, hi * 128:hi * 128 + C], lhsT=lT[0:C, hi, 0:C], rhs=rT[0:C, hi, 0:C],
                                           start=(acc_ident_rhs is None), stop=True)
                        return ins
                    P.op("pe", f, reads=reads + [t_ident], writes=[t_psq[q]])

                for hg in range(3):
                    q = nq4()
                    pair_mm(q, A2, Bp, hg, [t_o["AT"], t_o["BT"]])
                    tt("dve", Lh[hg][0:C, :, 0:C], pv4(q), ML[0:C, :, 0:C], ALU.mult, [t_psq[q], t_cst1], [t_Lh[hg]])
                    q = nq4()
                    pair_mm(q, Bp, A2, hg, [t_o["AT"], t_o["BT"]])
                    tt("dve", Xh[hg][0:C, :, 0:C], pv4(q), MU[0:C, :, 0:C], ALU.mult, [t_psq[q], t_cst1], [t_Xh[hg]])
                    tt("pool", Qh[hg][0:C, :, 0:C], Xh[hg][0:C, :, 0:C], Irep[0:C, :, 0:C], ALU.add, [t_Xh[hg], t_cst1], [t_Qh[hg]])
                for j in range(nlev - 1):
                    need_x = (j + 1 < nlev - 1)
                    for hg in range(3):
                        q = nq4()
                        sq_mm(q, Xh[hg], Lh[hg], [t_Xh[hg], t_Lh[hg]])
                        qx = None
                        if need_x:
                            qx = nq4()
                            sq_mm(qx, Lh[hg], Xh[hg], [t_Xh[hg], t_Lh[hg]])
                        copy_op("act", Lh[hg][0:C, :, 0:C], pv4(q), [t_psq[q]], [t_Lh[hg]])
                        if need_x:
                            copy_op("dve", Xh[hg][0:C, :, 0:C], pv4(qx), [t_psq[qx]], [t_Xh[hg]])
                    for hg in range(3):
                        q = nq4()
                        sq_mm(q, Lh[hg], Qh[hg], [t_Lh[hg], t_Qh[hg]], acc_ident_rhs=Qh[hg])
                        if j == nlev - 2:
                            copy_op("act", Qf[0:C, 4 * hg:4 * hg + 4, 0:C], pv4(q), [t_psq[q]], [t_Qf])
                        else:
                            copy_op("act", Qh[hg][0:C, :, 0:C], pv4(q), [t_psq[q]], [t_Qh[hg]])
                for hg in range(3):
                    if nlev == 1:
                        copy_op("act", Qf[0:C, 4 * hg:4 * hg + 4, 0:C], Qh[hg][0:C, :, 0:C], [t_Qh[hg]], [t_Qf])
                    q = nq4()
                    pair_mm(q, Kp, A2, hg, [t_o["KT"], t_o["AT"]])
                    tt("dve", Aak[0:C, 4 * hg:4 * hg + 4, 0:C], pv4(q), MU[0:C, :, 0:C], ALU.mult, [t_psq[q], t_cst1], [t_Aak])
                    q = nq4()
                    pair_mm(q, Kp, R2, hg, [t_o["KT"], t_o["RT"]])
                    tt("dve", Ark[0:C, 4 * hg:4 * hg + 4, 0:C], pv4(q), MUi[0:C, :, 0:C], ALU.mult, [t_psq[q], t_cst1], [t_Ark])
                    q = nq4()
                    pair_mm(q, Bp, R2, hg, [t_o["BT"], t_o["RT"]])
                    tt("dve", Arb[0:C, 4 * hg:4 * hg + 4, 0:C], pv4(q), MUi[0:C, :, 0:C], ALU.mult, [t_psq[q], t_cst1], [t_Arb])
                if first:
                    if mode == "p":
                        P.op("pool", lambda e: e.memset(St[:, :, :], 0.0), writes=[t_St])
                        P.op("pool", lambda e: e.memset(Sbf[:, :, :], 0.0), writes=[t_St])
                    else:
                        P.op("sp", lambda e: e.dma_start(out=svst[0:64, :, :], in_=s_wkv[idx].rearrange("h v k -> v h k")), writes=[t_svst, t_Zb, t_Ub], dma=True, sem_tk=t_svst)

                        def trs(e):
                            for p in range(6):
                                ins = e.transpose(out=ps_x[:, p * 64:(p + 1) * 64], in_=svst[0:64, 2 * p:2 * p + 2, :].rearrange("v h k -> v (h k)"),
                                                  identity=identf[0:64, 0:64])
                            return ins
                        P.op("pe", trs, reads=[t_svst, t_ident], writes=[t_ps_x])
                        P.op("act", lambda e: e.copy(out=St[:, :, :], in_=ps_x[:, 0:384].rearrange("p (m v) -> p m v", m=6)), reads=[t_ps_x], writes=[t_St])
                        P.op("dve", lambda e: e.tensor_copy(out=Sbf[:, :, :], in_=ps_x[:, 0:384].rearrange("p (m v) -> p m v", m=6)), reads=[t_ps_x], writes=[t_St])
                def head_cols(h):
                    return slice(h * 64, (h + 1) * 64)

                def seq_mm(name, fn_terms, reads, dst_banks):
                    def f(e):
                        for h in range(12):
                            bank, hc = (dst_banks[0], h) if h < 8 else (dst_banks[1], h - 8)
                            terms = fn_terms(h)
                            for ti, (lT, r_) in enumerate(terms):
                                ins = e.matmul(psq[bank][0:C, hc * 64:(hc + 1) * 64], lhsT=lT, rhs=r_, start=(ti == 0), stop=(ti == len(terms) - 1))
                        return ins
                    P.op("pe", f, reads=reads, writes=[t_psq[dst_banks[0]], t_psq[dst_banks[1]]])

                def hsl(h):
                    p, hf = h // 2, h % 2
                    return slice(64 * hf, 64 * hf + 64), p

                def evac768(dst, banks, tdst, as_f32=False):
                    copy_op("act", dst[0:C, 0:512], psq[banks[0]][0:C, 0:512], [t_psq[banks[0]]], [tdst])
                    copy_op("dve", dst[0:C, 512:768], psq[banks[1]][0:C, 0:256], [t_psq[banks[1]]], [tdst])

                seq_mm("Z", lambda h: [(AT2[:, h, 0:C], Sbf[:, h // 2, :]), (Aak[0:C, h, 0:C], Vt[0:C, head_cols(h)])],
                       [t_o["AT"], t_St, t_Aak, t_Vt], (0, 1))
                evac768(Zb, (0, 1), t_Zb)
                seq_mm("U", lambda h: [(Qf[0:C, h, 0:C], Zb[0:C, head_cols(h)])], [t_Qf, t_Zb], (2, 3))
                evac768(Ub, (2, 3), t_Ub)
                seq_mm("Y", lambda h: [(RT2[:, h, 0:C], Sbf[:, h // 2, :]), (Ark[0:C, h, 0:C], Vt[0:C, head_cols(h)]),
                                       (Arb[0:C, h, 0:C], Ub[0:C, head_cols(h)])],
                       [t_o["RT"], t_St, t_Ark, t_Vt, t_Arb, t_Ub], (0, 1))
                evac768(Yf, (0, 1), t_Yf)

                def snew(e):
                    for h in range(12):
                        psl, p = hsl(h)
                        e.matmul(ps_x[psl, p * 64:(p + 1) * 64], lhsT=Kh[0:C, head_cols(h)], rhs=Vt[0:C, head_cols(h)], start=True, stop=False)
                        ins = e.matmul(ps_x[psl, p * 64:(p + 1) * 64], lhsT=Bh[0:C, head_cols(h)], rhs=Ub[0:C, head_cols(h)], start=False, stop=True)
                    return ins
                P.op("pe", snew, reads=[t_Kh, t_Vt, t_Bh, t_Ub], writes=[t_ps_x])
                tt("dve", St[:, :, :], St[:, :, :], wc[:, 0:6].unsqueeze(2).to_broadcast([128, 6, 64]), ALU.mult, [t_St, t_wc], [t_St])
                tt("dve", St[:, :, :], St[:, :, :], ps_x[:, 0:384].rearrange("p (m v) -> p m v", m=6), ALU.add, [t_St, t_ps_x], [t_St])
                P.op("dve", lambda e: e.tensor_copy(out=Sbf[:, :, :], in_=St[:, :, :]), reads=[t_St], writes=[t_St])
                if last:
                    def trs2(e):
                        for p in range(6):
                            ins = e.transpose(out=ps_x[0:64, p * 128:(p + 1) * 128] if False else ps_mm[0][0:64, p * 64:(p + 1) * 64], in_=St[:, p, :], identity=identf[:, :])
                        return ins
                    def trs3(e):
                        for p in range(6):
                            bank = ps_mm[0] if p < 4 else ps_mm[1]
                            pc = p if p < 4 else p - 4
                            ins = e.transpose(out=bank[0:64, pc * 128:(pc + 1) * 128], in_=St[:, p, :], identity=identf[:, :])
                        return ins
                    P.op("pe", trs3, reads=[t_St, t_ident], writes=[t_ps_mm[0], t_ps_mm[1]])
                    P.op("act", lambda e: e.copy(out=svst[0:64, 0:8, :].rearrange("v h k -> v (h k)"), in_=ps_mm[0][0:64, 0:512]), reads=[t_ps_mm[0]], writes=[t_svst, t_Zb, t_Ub])
                    P.op("act", lambda e: e.copy(out=svst[0:64, 8:12, :].rearrange("v h k -> v (h k)"), in_=ps_mm[1][0:64, 0:256]), reads=[t_ps_mm[1]], writes=[t_svst, t_Zb, t_Ub])
                    dsto = (o_wkvp if mode == "p" else o_wkvs[idx]).rearrange("h v k -> v h k")
                    P.op("sp", lambda e: e.dma_start(out=dsto, in_=svst[0:64, :, :]), reads=[t_svst, t_Zb, t_Ub], dma=True, sem_tk=t_svst)
                Y3 = Yf[0:C, :].rearrange("p (h d) -> p h d", h=12)
                sqv = SF[0:C, :, :].rearrange("p m j -> p (m j)").rearrange("p (h d) -> p h d", h=12)
                P.op("dve", lambda e: e.reduce_sum(out=gsm[0:C, 0:12], in_=Y3, axis=AX.X), reads=[t_Yf], writes=[t_gsm])
                P.op("act", lambda e: e.activation(out=sqv, in_=Y3, func=AF.Square), reads=[t_Yf, t_o["RKT"]], writes=[t_S["F"]])
                P.op("dve", lambda e: e.reduce_sum(out=gsm[0:C, 12:24], in_=sqv, axis=AX.X), reads=[t_S["F"]], writes=[t_gsm])
                P.op("dve", lambda e: e.tensor_scalar(out=gsm[0:C, 24:36], in0=gsm[0:C, 0:12], scalar1=1.0 / 64, scalar2=None, op0=ALU.mult),
                     reads=[t_gsm], writes=[t_gsm])
                tt("dve", gsm[0:C, 36:48], gsm[0:C, 24:36], gsm[0:C, 24:36], ALU.mult, [t_gsm], [t_gsm])
                P.op("dve", lambda e: e.scalar_tensor_tensor(out=gsm[0:C, 48:60], in0=gsm[0:C, 12:24], scalar=1.0 / 64, in1=gsm[0:C, 36:48],
                                                             op0=ALU.mult, op1=ALU.subtract), reads=[t_gsm], writes=[t_gsm])
                P.op("dve", lambda e: e.tensor_scalar(out=gsm[0:C, 48:60], in0=gsm[0:C, 48:60], scalar1=64e-5, scalar2=None, op0=ALU.add),
                     reads=[t_gsm], writes=[t_gsm])
                P.op("act", lambda e: e.activation(out=gsm[0:C, 60:72], in_=gsm[0:C, 48:60], func=AF.Sqrt), reads=[t_gsm], writes=[t_gsm])
                P.op("dve", lambda e: e.reciprocal(out=gsm[0:C, 72:84], in_=gsm[0:C, 60:72]), reads=[t_gsm], writes=[t_gsm])
                hb3 = hbuf1[0:C, 0:768].rearrange("p (h d) -> p h d", h=12)
                tt("dve", hb3, Y3, gsm[0:C, 24:36].unsqueeze(2).to_broadcast([C, 12, 64]), ALU.subtract, [t_Yf, t_gsm], [t_hbuf1])
                tt("dve", hb3, hb3, gsm[0:C, 72:84].unsqueeze(2).to_broadcast([C, 12, 64]), ALU.mult, [t_hbuf1, t_gsm], [t_hbuf1])
                tt("dve", hbuf1[0:C, 0:768], hbuf1[0:C, 0:768], lnw_b[0:C, :], ALU.mult, [t_hbuf1, t_ln], [t_hbuf1])
                tt("dve", hbuf1[0:C, 0:768], hbuf1[0:C, 0:768], lnb_b[0:C, :], ALU.add, [t_hbuf1, t_ln], [t_hbuf1])

                def bon(e):
                    for p in range(6):
                        ins = e.matmul(ps_x[0:C, 400 + 2 * p:402 + 2 * p], lhsT=RKT[:, p, 0:C], rhs=hsel[:, 0:2], start=True, stop=True)
                    return ins
                P.op("pe", bon, reads=[t_o["RKT"], t_cst1], writes=[t_ps_x])
                P.op("act", lambda e: e.copy(out=gsm[0:C, 84:96], in_=ps_x[0:C, 400:412]), reads=[t_ps_x], writes=[t_gsm])
                tt("dve", sqv, Vt[0:C, :].rearrange("p (h d) -> p h d", h=12), gsm[0:C, 84:96].unsqueeze(2).to_broadcast([C, 12, 64]), ALU.mult,
                   [t_Vt, t_gsm, t_S["F"]], [t_S["F"]])
                tt("dve", hb3, hb3, sqv, ALU.add, [t_hbuf1, t_S["F"]], [t_hbuf1])
                io2 = nxt("o")
                if mode == "p":
                    cross_attn((lambda psl, pr, blk: KmT[psl, 1, pr, blk * 128:(blk + 1) * 128]), t_KmT[1],
                               (lambda blk, h: Vm[:, 1, blk, h, 0:65]), t_Vm[1],
                               (lambda psl, pr: QmTc[psl, pr, 0:C]), [t_QmTc], C, ps_o[io2], t_ps_o[io2])
                else:
                    barrier()
                    P.op("pool", lambda e: e.memset(cV1[:, :, :, 64:65], 1.0), writes=[t_cV1])
                    load_piece(c_mem[1, idx].rearrange("(b j) c -> j b c", b=2), 2)
                    cross_attn((lambda psl, pr, blk: cKT1[psl, 2 * blk + pr, :]), t_cKT1, (lambda blk, h: cV1[:, blk, h, 0:65]), t_cV1,
                               (lambda psl, pr: QmTc[psl, pr, 0:C]), [t_QmTc], C, ps_o[io2], t_ps_o[io2])
                normalize_o(ps_o[io2], t_ps_o[io2], C, hbuf1[0:C, 768:1024], t_hbuf1, t_rcp1, rcp1)
                col0 = 0 if mode == "p" else 4 * idx
                post_a(C, hbuf1[0:C, :], t_hbuf1, gt[0:C, :], t_gt, 8, hb1, t_hb1, hT1, t_hT1, col0)
                if mode == "p":
                    def xsrc(ys_):
                        P.op("sp", lambda e: e.dma_start(out=xres1[:, :], in_=x1_d[idx * 128:(idx + 1) * 128, :]), reads=[t_x1[idx]], writes=[t_xres1], dma=True)

                    def xdst(ys_):
                        P.op("sp", lambda e: e.dma_start(out=y_p[idx * 128:(idx + 1) * 128, :], in_=xres1[:, :]), reads=[t_xres1], dma=True)
                    post_b(G_POST[1], 128, 8, hT1, t_hT1, Wo, t_Wo, 0, xsrc, xdst)
                else:
                    barrier()

            keep1 = apos[0]
            apos[0] = outs_w0
            cst1 = carve_f(1024).rearrange("p (t n) -> p t n", t=2)
            ckb1 = carve_bf(2 * 256).rearrange("p (t n) -> p t n", t=2)
            cKT1 = carve_bf(4 * 128).rearrange("p (i n) -> p i n", i=4)
            cV1 = carve_bf(2 * 4 * 80).rearrange("p (t h d) -> p t h d", t=2, h=4)
            t_cst1b, t_ckb1, t_cKT1, t_cV1 = Tk(), Tk(), Tk(), Tk()
            apos[0] = keep1
            CTX.update(PT=PT1, t_PT=t_PT1, ytmp=[hbuf1] * 2, t_ytmp=[t_hbuf1] * 2, xres=[xres1] * 2, t_xres=[t_xres1] * 2,
                       cst=cst1, ckb=ckb1, cKT=cKT1, cV=cV1, t_cst=t_cst1b, t_ckb=t_ckb1, t_cKT=t_cKT1, t_cV=t_cV1)
            print("L1 arena words (final)", apos[0])

            def prompt_chunk(c):
                P.op("sp", lambda e: e.dma_start(out=xld1[:, :], in_=x1_d[c * 128:(c + 1) * 128, :]), reads=[t_x1[c]], writes=[t_xld1], dma=True)
                rmsnorm(xld1[:, :], 128, G_PRE[1], xnb1[:, :], t_xld1, t_xnb1)

                def tr(e):
                    for k in range(8):
                        ins = e.transpose(out=ps_t[:, k * 128:(k + 1) * 128], in_=xnb1[:, k * 128:(k + 1) * 128], identity=ident[:, :])
                    return ins
                P.op("pe", tr, reads=[t_xnb1, t_ident], writes=[t_ps_t])
                copy_op(ev_eng(), xnTc[:, :, :], ps_t[:].rearrange("p (k m) -> p k m", k=8), [t_ps_t], [t_xnTc])
                if c == 0:
                    P.op("pool", lambda e: e.memset(cols[:, :, 0:1], 0.0), writes=[t_cols])
                rwkv_chunk(128, (lambda k: xnTc[:, k, :]), t_xnTc, "p", c)
            for c in range(NT if "l1few" not in DBG else 2):
                prompt_chunk(c)

            def sample_l1():
                P.op("sp", lambda e: e.dma_start(out=xld1[0:16, :], in_=x1s_d), reads=[t_x1[NT]], writes=[t_xld1], dma=True)
                rmsnorm(xld1[0:16, :], 16, G_PRE[1], xnb1[0:16, :], t_xld1, t_xnb1)

                def tr(e):
                    for k in range(8):
                        ins = e.transpose(out=ps_t[:, k * 128:k * 128 + 16], in_=xnb1[0:16, k * 128:(k + 1) * 128], identity=ident[0:16, 0:16])
                    return ins
                P.op("pe", tr, reads=[t_xnb1, t_ident], writes=[t_ps_t])
                copy_op(ev_eng(), xnTs1[:, :, :], ps_t[:].rearrange("p (k m) -> p k m", k=8)[:, :, 0:16], [t_ps_t], [t_xnTs1])
                for bb in range(4):
                    P.op("sp", (lambda e, bb=bb: e.dma_start(out=cols[:, :, 0], in_=s_shift[bb:bb + 1, :].rearrange("o (m p) -> p (o m)", p=128),
                                                             allow_slow_non_contiguous=True)), writes=[t_cols], dma=True)
                    rwkv_chunk(4, (lambda k, bb=bb: xnTs1[:, k, 4 * bb:4 * bb + 4]), t_xnTs1, "s", bb)

                def xsrc_s(ys_):
                    P.op("sp", lambda e: e.dma_start(out=xres1[0:16, :], in_=x1s_d), reads=[t_x1[NT]], writes=[t_xres1], dma=True)

                def xdst_s(ys_):
                    P.op("sp", lambda e: e.dma_start(out=y_s, in_=xres1[0:16, :]), reads=[t_xres1], dma=True)
                post_b(G_POST[1], 16, 8, hT1, t_hT1, Wo, t_Wo, 0, xsrc_s, xdst_s)
            if "nosample1" not in DBG:
                sample_l1()

        print('total_ops', len(P.ops))
        if 'dump2' in DBG:
            for _i, _o in enumerate(P.ops):
                print('OP', _i, _o.eng, _o.line, 'dma' if _o.is_dma else '')
        P.finalize_and_emit(st)
    return nc


def layer0(env):
    pass


_CACHE = {}


def kernel(x_prompt, x_sample, mem_prompt, cache_mem_kv, cache_win0, cache_win1, cache_win2, state_wkv,
           state_shift, norm_pre, norm_post, norm_mem, w_mem_kv, rel_bias, w_in_a, w_out_a, w_in_b, w_out_b,
           rwkv_mu, rwkv_w0, rwkv_w_up, rwkv_a0, rwkv_a_up, rwkv_k_k, rwkv_k_a, rwkv_r_k, rwkv_ln_w, rwkv_ln_b):
    f = lambda a: np.ascontiguousarray(np.asarray(a, dtype=np.float32))
    if "nc" not in _CACHE:
        _CACHE["nc"] = build_program()
    nc = _CACHE["nc"]
    oh = _onehot_const()
    in_maps = []
    for c in range(NCORES):
        sl = slice(4 * c, 4 * c + 4)
        in_maps.append({
            "x_p": f(x_prompt[c]),
            "x_s": f(x_sample[sl]).reshape(16, D),
            "mem_p": f(mem_prompt[c]),
            "c_mem": f(cache_mem_kv[:, sl]).reshape(2, 4, 256, 512),
            "c_w0": f(cache_win0[0, sl]).reshape(4, 128, 512),
            "c_w1": f(cache_win1[0, sl]).reshape(4, 512, 512),
            "c_w2": f(cache_win2[0, sl]).reshape(4, 2048, 512),
            "s_wkv": f(state_wkv[0, sl]),
            "s_shift": f(state_shift[0, sl]),
            "norm_pre": f(norm_pre), "norm_post": f(norm_post), "norm_mem": f(norm_mem),
            "w_mem": f(w_mem_kv), "rel_bias": f(rel_bias),
            "w_in_a": f(w_in_a[0]), "w_out_a": f(w_out_a[0]), "w_in_b": f(w_in_b[0]), "w_out_b": f(w_out_b[0]),
            "c_onehot": oh,
            "r_mu": f(rwkv_mu).reshape(1, C_SHIFT), "r_w0": f(rwkv_w0).reshape(1, 768), "r_wup": f(rwkv_w_up).reshape(64, 768),
            "r_a0": f(rwkv_a0).reshape(1, 768), "r_aup": f(rwkv_a_up).reshape(64, 768), "r_kk": f(rwkv_k_k).reshape(1, 768),
            "r_ka": f(rwkv_k_a).reshape(1, 768), "r_rk": f(rwkv_r_k).reshape(1, 768), "r_lnw": f(rwkv_ln_w).reshape(1, 768),
            "r_lnb": f(rwkv_ln_b).reshape(1, 768),
        })
    res = run_bass_kernel_spmd(nc, in_maps, core_ids=list(range(NCORES)))
    R = res.results
    cat = lambda k: np.stack([np.asarray(R[c][k]) for c in range(NCORES)])
    y_prompt = cat("y_p")
    y_sample = cat("y_s").reshape(32, 4, D)
    new_mem = cat("o_mem").transpose(1, 0, 2, 3).reshape(2, 8, 256, 2, 4, 64)
    w0p = cat("o_w0p").reshape(1, 8, 128, 2, 4, 64)
    w1p = cat("o_w1p").reshape(1, 8, 512, 2, 4, 64)
    w2p = cat("o_w2p").reshape(1, 8, 2048, 2, 4, 64)
    w0s = cat("o_w0s").reshape(1, 32, 4, 2, 4, 64)
    w1s = cat("o_w1s").reshape(1, 32, 4, 2, 4, 64)
    w2s = cat("o_w2s").reshape(1, 32, 4, 2, 4, 64)
    wkvp = cat("o_wkvp").reshape(1, 8, 12, 64, 64)
    wkvs = cat("o_wkvs").reshape(1, 32, 12, 64, 64)
    shp = cat("o_shp").reshape(1, 8, C_SHIFT)
    shs = cat("o_shs").reshape(1, 32, C_SHIFT)
    outs = (y_prompt, y_sample, new_mem, w0p, w1p, w2p, w0s, w1s, w2s, wkvp, wkvs, shp, shs)
    return tuple(np.ascontiguousarray(o.astype(np.float32)) for o in outs)
```

```python
import math
from contextlib import ExitStack
import numpy as np
import concourse.bass as bass
import concourse.mybir as mybir
from concourse.bass_utils import run_bass_kernel_spmd

F32 = mybir.dt.float32
BF16 = mybir.dt.bfloat16
AF = mybir.ActivationFunctionType
ALU = mybir.AluOpType
AX = mybir.AxisListType

ENGS = ("pe", "act", "dve", "pool", "sp")
NCORES = 8
T = 2048
D = 1024
NT = 16
NEG = -30000.0
SCALE = 0.125
DIL = (1, 4, 16)
RMS_EPS = 1e-6
C_SHIFT = 2432
A_IN = 3072
B_IN = 3712


class Tk:
    __slots__ = ("name", "lw", "rd", "dsem", "dcount", "excl")

    def __init__(self, name="", excl=False):
        self.name = name
        self.excl = excl
        self.lw = None
        self.rd = []
        self.dsem = None
        self.dcount = 0


class Op:
    __slots__ = ("eng", "fn", "deps", "is_dma", "pos", "signal", "cnt", "sem_tk", "waits", "line")


class Prog:
    def __init__(self, nc):
        self.nc = nc
        self.ops = []
        self.eng_ops = {e: [] for e in ENGS}
        self.last_dma = {}

    def op(self, eng, fn, reads=(), writes=(), dma=False, sem_tk=None, extra_deps=()):
        if len(self.ops) >= getattr(self, "maxops", 10 ** 9):
            return None
        o = Op()
        import sys as _sys
        o.line = _sys._getframe(1).f_lineno
        o.eng = eng
        o.fn = fn
        o.is_dma = dma
        o.signal = dma
        o.cnt = 0
        o.waits = []
        deps = []
        for r in reads:
            if r.lw is not None:
                deps.append((r.lw, "raw"))
            if r.excl:
                for rr in r.rd:
                    deps.append((rr, "war"))
        for w in writes:
            if w.lw is not None:
                deps.append((w.lw, "waw"))
            for rr in w.rd:
                deps.append((rr, "war"))
        for xd in extra_deps:
            deps.append((xd, "raw"))
        fdeps = []
        seen = set()
        for d, kind in deps:
            if d is o:
                continue
            if (not d.is_dma) and d.eng == eng and (not dma):
                if eng == "pe" or kind != "raw":
                    continue
            if id(d) in seen:
                continue
            seen.add(id(d))
            fdeps.append(d)
        o.deps = fdeps
        for r in (reads if fn is not None else ()):
            if r.excl:
                r.rd = [o]
            else:
                r.rd.append(o)
        for w in (writes if fn is not None else ()):
            w.lw = o
            w.rd = []
        if dma:
            if sem_tk is None:
                sem_tk = (list(writes) + list(reads))[0]
            o.sem_tk = sem_tk
            sem_tk.dcount += 1
            o.cnt = sem_tk.dcount * 16
            self.last_dma[id(sem_tk)] = o
        else:
            o.sem_tk = None
        o.pos = len(self.eng_ops[eng])
        self.eng_ops[eng].append(o)
        self.ops.append(o)
        return o

    def finalize_and_emit(self, stack):
        nc = self.nc
        waited_pos = {e: {p: -1 for p in ENGS} for e in ENGS}
        waited_dma = {e: {} for e in ENGS}
        for o in self.ops:
            e = o.eng
            for d in o.deps:
                if d.is_dma:
                    key = id(d.sem_tk)
                    if waited_dma[e].get(key, 0) >= d.cnt:
                        continue
                    waited_dma[e][key] = d.cnt
                    o.waits.append(d)
                else:
                    if waited_pos[e][d.eng] >= d.pos:
                        continue
                    waited_pos[e][d.eng] = d.pos
                    d.signal = True
                    o.waits.append(d)
        for e in ENGS:
            c = 0
            for o in self.eng_ops[e]:
                if not o.is_dma and o.signal:
                    c += 1
                    o.cnt = c
        esem = {e: stack.enter_context(nc.semaphore("es_" + e)) for e in ENGS}
        nd = 0
        for o in self.ops:
            if o.is_dma and o.sem_tk.dsem is None:
                o.sem_tk.dsem = stack.enter_context(nc.semaphore("ds%d" % nd))
                nd += 1
        self.n_dma_sems = nd
        block = stack.enter_context(nc.Block())

        def emit(engobj, elist):
            for o in elist:
                for d in o.waits:
                    if d.is_dma:
                        engobj.wait_ge(d.sem_tk.dsem, d.cnt)
                    else:
                        engobj.wait_ge(esem[d.eng], d.cnt)
                if o.fn is None:
                    continue
                ins = o.fn(engobj)
                if o.is_dma:
                    ins.then_inc(o.sem_tk.dsem, 16)
                elif o.signal:
                    ins.then_inc(esem[o.eng], 1)

        @block.tensor
        def _(pe):
            emit(pe, self.eng_ops["pe"])

        @block.scalar
        def _(act):
            emit(act, self.eng_ops["act"])

        @block.vector
        def _(dve):
            emit(dve, self.eng_ops["dve"])

        @block.gpsimd
        def _(pool):
            emit(pool, self.eng_ops["pool"])

        @block.sync
        def _(sp):
            emit(sp, self.eng_ops["sp"])
            done = set()
            for o in self.ops:
                if o.is_dma and id(o.sem_tk) not in done:
                    done.add(id(o.sem_tk))
                    sp.wait_ge(o.sem_tk.dsem, 16 * o.sem_tk.dcount)


def _t5_bucket_np(dist):
    dist = np.asarray(dist, dtype=np.int32)
    d = np.maximum(dist, 1).astype(np.float32)
    large = 16 + (np.log(d / np.float32(16.0)) / np.float32(math.log(2048 / 16)) * np.float32(16.0)).astype(np.int32)
    large = np.minimum(large, 31)
    return np.where(dist < 16, dist, large)


def _onehot_const():
    oh = np.zeros((33, 3, 384), np.float32)
    for g in range(3):
        for u in range(383):
            dist = u - 127
            if 0 <= dist <= 128:
                b = int(_t5_bucket_np(DIL[g] * dist))
                oh[b, g, u] = 1.0
            else:
                oh[32, g, u] = NEG
        oh[32, g, 383] = NEG
    return oh.reshape(33, 3 * 384)


STAGE = 2
DBG = set()


def build_program():
    nc = bass.Bass("TRN2", target_bir_lowering=False)
    try:
        nc.allow_low_precision("bf16 matmul operands with fp32 accumulation (matches problem tolerance)")
    except Exception:
        pass
    try:
        nc.allow_non_contiguous_dma("strided window / toeplitz accesses")
    except Exception:
        pass
    P = Prog(nc)
    for _d in DBG:
        if _d.startswith('maxops='):
            P.maxops = int(_d.split('=')[1])

    def din(name, shape):
        return nc.dram_tensor(name, list(shape), F32, kind="ExternalInput").ap()

    def dout(name, shape):
        return nc.dram_tensor(name, list(shape), F32, kind="ExternalOutput").ap()

    x_p = din("x_p", [T, D])
    x_s = din("x_s", [16, D])
    mem_p = din("mem_p", [256, D])
    c_mem = din("c_mem", [2, 4, 256, 512])
    c_w0 = din("c_w0", [4, 128, 512])
    c_w1 = din("c_w1", [4, 512, 512])
    c_w2 = din("c_w2", [4, 2048, 512])
    s_wkv = din("s_wkv", [4, 12, 64, 64])
    s_shift = din("s_shift", [4, C_SHIFT])
    norm_pre = din("norm_pre", [2, D])
    norm_post = din("norm_post", [2, D])
    norm_mem = din("norm_mem", [2, D])
    w_mem = din("w_mem", [2, D, 512])
    rel_bias = din("rel_bias", [32, 12])
    w_in_a = din("w_in_a", [D, A_IN])
    w_out_a = din("w_out_a", [512, D])
    w_in_b = din("w_in_b", [D, B_IN])
    w_out_b = din("w_out_b", [D, D])
    onehot = din("c_onehot", [33, 3 * 384])
    r_mu = din("r_mu", [1, C_SHIFT])
    r_w0 = din("r_w0", [1, 768])
    r_wup = din("r_wup", [64, 768])
    r_a0 = din("r_a0", [1, 768])
    r_aup = din("r_aup", [64, 768])
    r_kk = din("r_kk", [1, 768])
    r_ka = din("r_ka", [1, 768])
    r_rk = din("r_rk", [1, 768])
    r_lnw = din("r_lnw", [1, 768])
    r_lnb = din("r_lnb", [1, 768])
    y_p = dout("y_p", [T, D])
    y_s = dout("y_s", [16, D])
    o_mem = dout("o_mem", [2, 256, 512])
    o_w0p = dout("o_w0p", [128, 512])
    o_w1p = dout("o_w1p", [512, 512])
    o_w2p = dout("o_w2p", [2048, 512])
    o_w0s = dout("o_w0s", [16, 512])
    o_w1s = dout("o_w1s", [16, 512])
    o_w2s = dout("o_w2s", [16, 512])
    o_wkvp = dout("o_wkvp", [12, 64, 64])
    o_wkvs = dout("o_wkvs", [4, 12, 64, 64])
    o_shp = dout("o_shp", [1, C_SHIFT])
    o_shs = dout("o_shs", [4, C_SHIFT])
    x1_d = nc.dram_tensor("x1_d", [T, D], F32, kind="Internal").ap()
    x1s_d = nc.dram_tensor("x1s_d", [16, D], F32, kind="Internal").ap()
    e_d = nc.dram_tensor("e_d", [12, 3 * 384], F32, kind="Internal").ap()
    out_tokens = []

    with ExitStack() as st:
        def sb(name, shape, dt):
            return st.enter_context(nc.sbuf_tensor(name, list(shape), dt))

        def psum(name, shape, dt):
            return st.enter_context(nc.psum_tensor(name, list(shape), dt))

        ps_mm = [psum("ps_mm%d" % i, [128, 512], F32) for i in range(2)]
        ps_s = [psum("ps_s%d" % i, [128, 512], F32) for i in range(2)]
        ps_o = [psum("ps_o%d" % i, [128, 512], F32) for i in range(2)]
        ps_t = psum("ps_t", [128, 1024], BF16)
        ps_x = psum("ps_x", [128, 512], F32)
        t_ps_mm = [Tk("ps_mm0", True), Tk("ps_mm1", True)]
        t_ps_s = [Tk("ps_s0", True), Tk("ps_s1", True)]
        t_ps_o = [Tk("ps_o0", True), Tk("ps_o1", True)]
        t_ps_t = Tk("ps_t", True)
        t_ps_x = Tk("ps_x", True)
        rr = {"mm": 0, "s": 0, "o": 0, "ev": 0}

        def nxt(k):
            rr[k] += 1
            return rr[k] & 1

        def ev_eng():
            rr["ev"] += 1
            return "act" if (rr["ev"] & 1) else "dve"

        def copy_op(eng, out, in_, reads, writes):
            if eng == "act":
                P.op("act", lambda e: e.copy(out=out, in_=in_), reads=reads, writes=writes)
            else:
                P.op(eng, lambda e: e.tensor_copy(out=out, in_=in_), reads=reads, writes=writes)

        identf = sb("identf", [128, 128], F32)
        ident = sb("ident", [128, 128], BF16)
        t_ident = Tk()

        P.op("pool", lambda e: e.memset(identf[:], 0.0), writes=[t_ident])
        P.op("pool", lambda e: e.affine_select(out=identf[:], in_=identf[:], pattern=[[-1, 128]], compare_op=ALU.not_equal,
                                               fill=1.0, base=0, channel_multiplier=1), reads=[t_ident], writes=[t_ident])
        P.op("dve", lambda e: e.tensor_copy(out=ident[:], in_=identf[:]), reads=[t_ident], writes=[t_ident])

        ARENA_WORDS = 49000
        arena = sb("arena", [128, ARENA_WORDS], F32)
        gains4 = sb("gains", [128, 2, 1024], F32)
        gmem = arena[:, 0:2048].rearrange("p (a d) -> p a d", a=2)
        barsb = sb("barsb", [128, 16], F32)
        t_bar = {e: Tk() for e in ("pe", "act", "dve", "pool")}
        t_barinit = Tk()
        P.op("pool", lambda e: e.memset(barsb[:, :], 0.0), writes=[t_barinit])

        def barrier():
            dmas = list(P.last_dma.values())
            P.op("pe", lambda e: e.transpose(out=ps_t[0:16, 0:16], in_=ident[0:16, 0:16], identity=ident[0:16, 0:16]),
                 reads=[t_ident], writes=[t_ps_t, t_bar["pe"]], extra_deps=dmas)
            P.op("act", lambda e: e.copy(out=barsb[:, 0:1], in_=barsb[:, 8:9]), reads=[t_barinit], writes=[t_bar["act"]], extra_deps=dmas)
            P.op("dve", lambda e: e.tensor_copy(out=barsb[:, 1:2], in_=barsb[:, 9:10]), reads=[t_barinit], writes=[t_bar["dve"]], extra_deps=dmas)
            P.op("pool", lambda e: e.memset(barsb[:, 2:3], 0.0), writes=[t_bar["pool"]], extra_deps=dmas)
            allb = list(t_bar.values())
            P.op("pe", lambda e: e.transpose(out=ps_t[0:16, 0:16], in_=ident[0:16, 0:16], identity=ident[0:16, 0:16]),
                 reads=[t_ident] + allb, writes=[t_ps_t])
            P.op("act", lambda e: e.copy(out=barsb[:, 3:4], in_=barsb[:, 8:9]), reads=allb)
            P.op("dve", lambda e: e.tensor_copy(out=barsb[:, 4:5], in_=barsb[:, 9:10]), reads=allb)
            P.op("pool", lambda e: e.memset(barsb[:, 5:6], 0.0), reads=allb)
            P.op("sp", None, reads=allb)

        class _G:
            def __getitem__(self, key):
                p, gi, c = key
                return gains4[p, gi, c] if gi < 2 else gmem[p, gi - 4, c]
        gains = _G()
        t_gains = Tk()
        def load_gains(items):
            for (i, src, l) in items:
                ap_b = bass.AP(tensor=src.tensor, offset=l * D, ap=[[0, 128], [1, D]])
                P.op("sp", (lambda e, i=i, ap_b=ap_b: e.dma_start(out=gains[:, i, :], in_=ap_b)), writes=[t_gains], dma=True)
        load_gains([(0, norm_pre, 0), (1, norm_post, 0), (4, norm_mem, 0), (5, norm_mem, 1)])
        G_PRE, G_POST, G_MEM = (0, 0), (1, 1), (4, 5)

        ss = sb("ss", [128, 8], F32)
        junk = sb("junk", [128, 1024], BF16)
        t_junk = Tk()

        def rmsnorm(src_ap, np_, gi, dst_ap, t_src, t_dst, extra_reads=()):
            t_ss = Tk()
            P.op("act", lambda e: e.activation(out=junk[:np_, :], in_=src_ap, func=AF.Square),
                 reads=[t_src], writes=[t_junk])
            P.op("dve", lambda e: e.reduce_sum(out=ss[:np_, 0:1], in_=junk[:np_, :], axis=AX.X),
                 reads=[t_junk], writes=[t_ss])
            P.op("dve", lambda e: e.tensor_scalar(out=ss[:np_, 1:2], in0=ss[:np_, 0:1], scalar1=1.0 / D, scalar2=RMS_EPS,
                                                  op0=ALU.mult, op1=ALU.add), reads=[t_ss], writes=[t_ss])
            P.op("act", lambda e: e.activation(out=ss[:np_, 2:3], in_=ss[:np_, 1:2], func=AF.Sqrt), reads=[t_ss], writes=[t_ss])
            P.op("dve", lambda e: e.reciprocal(out=ss[:np_, 3:4], in_=ss[:np_, 2:3]), reads=[t_ss], writes=[t_ss])
            P.op("dve", lambda e: e.scalar_tensor_tensor(out=dst_ap, in0=src_ap, scalar=ss[:np_, 3:4], in1=gains[:np_, gi, :],
                                                         op0=ALU.mult, op1=ALU.mult),
                 reads=[t_src, t_ss, t_gains] + list(extra_reads), writes=[t_dst])

        apos = [0]
        atop = [ARENA_WORDS]

        def carve_top(nwords):
            atop[0] -= nwords
            assert atop[0] >= apos[0]
            return arena[:, atop[0]:atop[0] + nwords]

        def carve(nbytes):
            w0 = apos[0]
            nw = (nbytes + 3) // 4
            apos[0] += nw
            assert apos[0] <= atop[0], ("arena overflow", apos[0], atop[0])
            return arena[:, w0:w0 + nw]

        def carve_bf(n):
            return carve(2 * n).bitcast(BF16)

        def carve_f(n):
            return carve(4 * n)

        KmT = sb("KmT", [128, 2, 2, 256], BF16)
        Vm = sb("Vm", [128, 2, 2, 4, 80], BF16)
        t_KmT = [Tk(), Tk()]
        t_Vm = [Tk(), Tk()]
        mark = apos[0]
        carve_f(2048)
        memx = carve_f(2 * 1024).rearrange("p (b d) -> p b d", b=2)
        memn = carve_bf(2 * 1024).rearrange("p (b d) -> p b d", b=2)
        memnT = carve_bf(8 * 256).rearrange("p (k m) -> p k m", k=8)
        wm = carve_bf(8 * 512).rearrange("p (k n) -> p k n", k=8)
        kvst = carve_f(2 * 512).rearrange("p (b n) -> p b n", b=2)
        t_memx, t_memn, t_memnT, t_wm, t_kvst = Tk(), Tk(), Tk(), Tk(), [Tk(), Tk()]
        P.op("sp", lambda e: e.dma_start(out=memx, in_=mem_p.rearrange("(b p) d -> p b d", p=128)), writes=[t_memx], dma=True)
        if "novm" not in DBG:
            P.op("pool", lambda e: e.memset(Vm[:, :, :, :, 64:65], 1.0), writes=t_Vm)
        for l in range(0 if "nomem" in DBG else 2):
            P.op("pool", (lambda e, l=l: e.dma_start(out=wm, in_=w_mem[l].rearrange("(k p) n -> p k n", p=128))),
                 writes=[t_wm], dma=True)
            for b in range(2):
                rmsnorm(memx[:, b, :], 128, G_MEM[l], memn[:, b, :], t_memx, t_memn)

                def tr(e, b=b):
                    for k in range(8):
                        ins = e.transpose(out=ps_t[:, k * 128:(k + 1) * 128], in_=memn[:, b, k * 128:(k + 1) * 128], identity=ident[:])
                    return ins
                P.op("pe", tr, reads=[t_memn, t_ident], writes=[t_ps_t])
                copy_op(ev_eng(), memnT[:, :, b * 128:(b + 1) * 128], ps_t[:].rearrange("p (k m) -> p k m", k=8), [t_ps_t], [t_memnT])
            for b in range(2):
                i = nxt("mm")

                def mm(e, b=b, i=i):
                    for k in range(8):
                        ins = e.matmul(ps_mm[i][:, :], lhsT=memnT[:, k, b * 128:(b + 1) * 128], rhs=wm[:, k, :], start=(k == 0), stop=(k == 7))
                    return ins
                P.op("pe", mm, reads=[t_memnT, t_wm], writes=[t_ps_mm[i]])
                P.op("act", (lambda e, b=b, i=i: e.copy(out=kvst[:, b, :], in_=ps_mm[i][:, :])), reads=[t_ps_mm[i]], writes=[t_kvst[b]])
                P.op("dve", (lambda e, b=b, i=i, l=l: e.tensor_copy(out=Vm[:, l, b, :, 0:64],
                                                                    in_=ps_mm[i][:, 256:512].rearrange("p (h d) -> p h d", h=4))),
                     reads=[t_ps_mm[i]], writes=[t_Vm[l]])
                P.op("sp", (lambda e, b=b, l=l: e.dma_start(out=o_mem[l, b * 128:(b + 1) * 128, :], in_=kvst[:, b, :])),
                     reads=[t_kvst[b]], dma=True)
                out_tokens.append(t_kvst[b])
            for pr in range(2):
                i = nxt("mm")

                def mmk(e, pr=pr, i=i):
                    for k in range(8):
                        ins = e.matmul(ps_mm[i][:, 0:256], lhsT=wm[:, k, pr * 128:(pr + 1) * 128], rhs=memnT[:, k, :], start=(k == 0), stop=(k == 7))
                    return ins
                P.op("pe", mmk, reads=[t_memnT, t_wm], writes=[t_ps_mm[i]])
                copy_op(ev_eng(), KmT[:, l, pr, :], ps_mm[i][:, 0:256], [t_ps_mm[i]], [t_KmT[l]])


        biasT = carve_top(12 * 2 * 128).rearrange("p (a b q) -> p a b q", a=12, b=2)
        t_biasT = Tk()
        M1e = carve_top(264).bitcast(BF16)
        M1o = carve_top(264).bitcast(BF16)
        M2e = carve_top(1032).bitcast(BF16)
        M2o = carve_top(1032).bitcast(BF16)
        t_M = Tk()
        mark = apos[0]
        rb33 = carve_f(12)
        oh33 = carve_f(1152)
        esb = carve_f(1152)
        mtmp = carve_f(2064)
        t_rb, t_oh, t_esb, t_mtmp, t_ed = Tk(), Tk(), Tk(), Tk(), Tk()
        P.op("pool", lambda e: e.memset(rb33[32:33, :], 1.0), writes=[t_rb])
        P.op("sp", lambda e: e.dma_start(out=rb33[0:32, :], in_=rel_bias), writes=[t_rb], dma=True)
        P.op("sp", lambda e: e.dma_start(out=oh33[0:33, :], in_=onehot), writes=[t_oh], dma=True)
        for g in range(3):
            P.op("pe", (lambda e, g=g: e.matmul(ps_x[0:12, 0:384], lhsT=rb33[0:33, :], rhs=oh33[0:33, g * 384:(g + 1) * 384], start=True, stop=True)),
                 reads=[t_rb, t_oh], writes=[t_ps_x])
            P.op("act", (lambda e, g=g: e.copy(out=esb[0:12, g * 384:(g + 1) * 384], in_=ps_x[0:12, 0:384])), reads=[t_ps_x], writes=[t_esb])
        P.op("sp", lambda e: e.dma_start(out=e_d, in_=esb[0:12, :]), reads=[t_esb], writes=[t_ed], dma=True, sem_tk=t_esb)
        bstage = carve_f(24 * 128).rearrange("p (a q) -> p a q", a=24)
        Jrev = carve_f(128)
        t_bst, t_J = Tk(), Tk()

        P.op("pool", lambda e: e.memset(Jrev[:, :], 0.0), writes=[t_J])
        P.op("pool", lambda e: e.affine_select(out=Jrev[:, :], in_=Jrev[:, :], pattern=[[1, 128]], compare_op=ALU.not_equal, fill=1.0, base=-127,
                                               channel_multiplier=1), reads=[t_J], writes=[t_J])
        for g in range(3):
            for h in range(4):
                for blk in range(2):
                    off = (4 * g + h) * 1152 + g * 384 + (128 if blk == 0 else 0)
                    src = bass.AP(tensor=e_d.tensor, offset=off, ap=[[1, 128], [1, 128]])
                    P.op("sp", (lambda e, g=g, h=h, blk=blk, src=src: e.dma_start(out=bstage[:, (g * 4 + h) * 2 + blk, :], in_=src)),
                         reads=[t_ed], writes=[t_bst], dma=True)
        for a4 in range(6):
            i = nxt("mm")
            P.op("pe", (lambda e, a4=a4, i=i: e.matmul(ps_mm[i][:, :], lhsT=Jrev[:, :], rhs=bstage[:, 4 * a4:4 * a4 + 4, :].rearrange("p a q -> p (a q)"),
                                                         start=True, stop=True)), reads=[t_J, t_bst], writes=[t_ps_mm[i]])
            copy_op(ev_eng(), biasT.rearrange("p a b q -> p (a b q)")[:, 512 * a4:512 * (a4 + 1)], ps_mm[i][:, :], [t_ps_mm[i]], [t_biasT])
        for (Mt, ncol, base, cm) in ((M1e, 528, 16, 4), (M1o, 528, 15, 4), (M2e, 2064, 16, 16), (M2o, 2064, 15, 16)):
            P.op("pool", (lambda e, ncol=ncol: e.memset(mtmp[:, 0:ncol], 0.0)), writes=[t_mtmp])
            P.op("pool", (lambda e, ncol=ncol, base=base, cm=cm: e.affine_select(out=mtmp[:, 0:ncol], in_=mtmp[:, 0:ncol], pattern=[[-1, ncol]],
                                                                                compare_op=ALU.not_equal, fill=1.0, base=base, channel_multiplier=cm)),
                 reads=[t_mtmp], writes=[t_mtmp])
            P.op("dve", (lambda e, Mt=Mt, ncol=ncol: e.tensor_copy(out=Mt[:, :], in_=mtmp[:, 0:ncol])), reads=[t_mtmp], writes=[t_M])

        def perm_lhsT(g, r, Tq):
            if g == 1:
                Tp = Tq % 4
                if r % 2 == 0:
                    s0 = 16 + 128 * Tp - r
                    return M1e[:, s0:s0 + 128]
                s0 = 15 + 128 * Tp - r
                return M1o[:, s0:s0 + 128]
            if r % 2 == 0:
                s0 = 16 + 128 * Tq - r
                return M2e[:, s0:s0 + 128]
            s0 = 15 + 128 * Tq - r
            return M2o[:, s0:s0 + 128]

        barrier()
        apos[0] = 0
        L0 = apos[0]
        xnT = carve_bf(8 * 2048).rearrange("p (k t) -> p k t", k=8)
        xnTs = carve_bf(8 * 16).rearrange("p (k t) -> p k t", k=8)
        NW = 4
        wsl = [carve_bf(8 * 256).rearrange("p (k n) -> p k n", k=8) for _ in range(NW)]
        t_wsl = [Tk() for _ in range(NW)]
        xld = [carve_f(1024) for _ in range(2)]
        t_xld = [Tk(), Tk()]
        xnb = [carve_bf(1024) for _ in range(2)]
        t_xnb = [Tk(), Tk()]
        t_xnT = [Tk() for _ in range(NT)]
        t_xnTs = Tk()
        QT = carve_bf(2 * 2048).rearrange("p (m t) -> p m t", m=2)
        KT = carve_bf(2 * 2048).rearrange("p (m t) -> p m t", m=2)
        QmT = carve_bf(2 * 2048).rearrange("p (m t) -> p m t", m=2)
        t_QT = [[Tk() for _ in range(4)] for _ in range(2)]
        t_KT = [[Tk() for _ in range(4)] for _ in range(2)]
        t_QmT = [[Tk() for _ in range(4)] for _ in range(2)]
        QTs = carve_bf(8 * 16).rearrange("p (m t) -> p m t", m=8)
        KTs = carve_bf(6 * 16).rearrange("p (m t) -> p m t", m=6)
        t_QTs, t_KTs = Tk(), Tk()
        gate = carve_bf(16 * 512).rearrange("p (i n) -> p i n", i=16)
        t_gate = [Tk() for _ in range(NT)]
        gate_s = carve_bf(4 * 512).rearrange("p (b n) -> p b n", b=4)
        t_gate_s = Tk()
        Vg = carve_bf(16 * 4 * 80).rearrange("p (i h d) -> p i h d", i=16, h=4)
        t_Vg = [Tk() for _ in range(NT)]
        Vns = carve_bf(4 * 3 * 4 * 80).rearrange("p (b g h d) -> p b g h d", b=4, g=3, h=4)
        t_Vns = Tk()
        Og = carve_bf(48 * 260).rearrange("p (i n) -> p i n", i=48)
        t_Og = [Tk() for _ in range(48)]
        wst = [carve_f(512) for _ in range(2)]
        t_wst = [Tk(), Tk()]
        wins = carve_f(3 * 512).rearrange("p (g n) -> p g n", g=3)
        t_wins = Tk()
        sbf = [carve_f(256) for _ in range(2)]
        t_sbf = [Tk(), Tk()]
        PT = [carve_bf(256) for _ in range(2)]
        t_PT = [Tk(), Tk()]
        rr.update({"w": 0, "sb": 0, "pt": 0, "wst": 0})

        P.op("pool", lambda e: e.memset(Vg[:, :, :, 64:65], 1.0), writes=t_Vg)
        P.op("pool", lambda e: e.memset(Vns[0:4, :, :, :, 64:65], 1.0), writes=[t_Vns])

        def phaseA(src_dram, gi, lname):
            for i in range(NT + 1):
                s_ = i & 1
                np_ = 128 if i < NT else 16
                if i < NT:
                    P.op("sp", (lambda e, i=i, s_=s_: e.dma_start(out=xld[s_][:, :], in_=src_dram[0][i * 128:(i + 1) * 128, :])),
                         writes=[t_xld[s_]], dma=True)
                else:
                    P.op("sp", (lambda e, s_=s_: e.dma_start(out=xld[s_][0:16, :], in_=src_dram[1])), writes=[t_xld[s_]], dma=True)
                rmsnorm(xld[s_][0:np_, :], np_, gi, xnb[s_][0:np_, :], t_xld[s_], t_xnb[s_])

                def tr(e, s_=s_, np_=np_):
                    for k in range(8):
                        ins = e.transpose(out=ps_t[:, k * 128:k * 128 + np_], in_=xnb[s_][0:np_, k * 128:(k + 1) * 128], identity=ident[0:np_, 0:np_])
                    return ins
                P.op("pe", tr, reads=[t_xnb[s_], t_ident], writes=[t_ps_t])
                if i < NT:
                    copy_op(ev_eng(), xnT[:, :, i * 128:(i + 1) * 128], ps_t[:].rearrange("p (k m) -> p k m", k=8), [t_ps_t], [t_xnT[i]])
                else:
                    copy_op(ev_eng(), xnTs[:, :, :], ps_t[:].rearrange("p (k m) -> p k m", k=8)[:, :, 0:16], [t_ps_t], [t_xnTs])

        phaseA((x_p, x_s), G_PRE[0], "l0")

        chunk_cols = []
        for g in range(3):
            chunk_cols += [("q", g, 256 * g), ("k", g, 768 + 256 * g), ("v", g, 1536 + 256 * g)]
        chunk_cols += [("qm", 0, 2304), ("gate", 0, 2560), ("gate", 1, 2816)]
        wstate = {"loaded": 0}

        def load_w(n):
            if n >= len(chunk_cols) or n < wstate["loaded"]:
                return
            assert n == wstate["loaded"]
            wstate["loaded"] += 1
            c0 = chunk_cols[n][2]
            sl_ = n % NW
            P.op("pool", (lambda e, c0=c0, sl_=sl_: e.dma_start(out=wsl[sl_], in_=w_in_a[:, c0:c0 + 256].rearrange("(k p) n -> p k n", p=128))),
                 writes=[t_wsl[sl_]], dma=True)

        def proj_fm(sl_, dst, t_dst, sdst, t_sdst, smb0):
            for mb in range(2):
                for tg in range(4):
                    i = nxt("mm")

                    def mm(e, mb=mb, tg=tg, i=i):
                        for k in range(8):
                            ins = e.matmul(ps_mm[i][:, :], lhsT=wsl[sl_][:, k, mb * 128:(mb + 1) * 128], rhs=xnT[:, k, tg * 512:(tg + 1) * 512],
                                           start=(k == 0), stop=(k == 7))
                        return ins
                    P.op("pe", mm, reads=[t_wsl[sl_]] + t_xnT[4 * tg:4 * tg + 4], writes=[t_ps_mm[i]])
                    copy_op(ev_eng(), dst[:, mb, tg * 512:(tg + 1) * 512], ps_mm[i][:, :], [t_ps_mm[i]], [t_dst[mb][tg]])
                def mms(e, mb=mb):
                    for k in range(8):
                        ins = e.matmul(ps_x[:, 0:16], lhsT=wsl[sl_][:, k, mb * 128:(mb + 1) * 128], rhs=xnTs[:, k, :], start=(k == 0), stop=(k == 7))
                    return ins
                P.op("pe", mms, reads=[t_wsl[sl_], t_xnTs], writes=[t_ps_x])
                copy_op(ev_eng(), sdst[:, smb0 + mb, :], ps_x[:, 0:16], [t_ps_x], [t_sdst])

        def proj_tm(sl_, tok_ap_fn, reads, evac_fn):
            i = nxt("mm")

            def mm(e, i=i):
                for k in range(8):
                    ins = e.matmul(ps_mm[i][:, 0:256], lhsT=tok_ap_fn(k), rhs=wsl[sl_][:, k, :], start=(k == 0), stop=(k == 7))
                return ins
            P.op("pe", mm, reads=[t_wsl[sl_]] + list(reads), writes=[t_ps_mm[i]])
            evac_fn(ps_mm[i], t_ps_mm[i])

        def proj_tm_s(sl_, M, c0, evac_fn):
            def mm(e):
                for k in range(8):
                    ins = e.matmul(ps_x[0:M, 0:256], lhsT=xnTs[:, k, c0:c0 + M], rhs=wsl[sl_][:, k, :], start=(k == 0), stop=(k == 7))
                return ins
            P.op("pe", mm, reads=[t_wsl[sl_], t_xnTs], writes=[t_ps_x])
            evac_fn()

        def grp_tiles(g):
            d = DIL[g]
            nsb = NT // d
            return [(r, sb_) for r in range(d) for sb_ in range(nsb)]

        def tile_tok_slice(g, r, sb_):
            d = DIL[g]
            base = d * 128 * sb_ + r
            return slice(base, base + d * 127 + 1, d)

        def nat_tiles_of(g, r, sb_):
            d = DIL[g]
            return list(range(d * sb_, d * sb_ + d))

        win_out = (o_w0p, o_w1p, o_w2p)
        win_s_out = (o_w0s, o_w1s, o_w2s)
        load_w(0)
        load_w(1)
        load_w(2)
        def win_rows(g, r):
            dd = DIL[g]
            if dd == 1:
                return win_out[g]
            return win_out[g].rearrange("(j r) c -> r j c", r=dd)[r]

        def do_group(g):
            d = DIL[g]
            nsb = NT // d
            tiles = grp_tiles(g)
            n = 3 * g
            load_w(n + 3)
            proj_fm(n % NW, QT, t_QT, QTs, t_QTs, 2 * g)
            n = 3 * g + 1
            load_w(n + 3)
            proj_fm(n % NW, KT, t_KT, KTs, t_KTs, 2 * g)
            for r in range(d):
                sb_ = nsb - 1
                tsl = tile_tok_slice(g, r, sb_)

                def evk(pst, tps, r=r):
                    ws_ = nxt("wst")
                    P.op("act", lambda e: e.copy(out=wst[ws_][:, 0:256], in_=pst[:, 0:256]), reads=[tps], writes=[t_wst[ws_]])
                    rows = win_rows(g, r)
                    P.op("sp", lambda e: e.dma_start(out=rows[:, 0:256], in_=wst[ws_][:, 0:256]), reads=[t_wst[ws_]], dma=True)
                proj_tm(n % NW, (lambda k, tsl=tsl: xnT[:, k, tsl]), [t_xnT[j] for j in nat_tiles_of(g, r, sb_)], evk)

            def evks():
                P.op("act", lambda e: e.copy(out=wins[0:16, g, 0:256], in_=ps_x[0:16, 0:256]), reads=[t_ps_x], writes=[t_wins])
            proj_tm_s(n % NW, 16, 0, evks)
            n = 3 * g + 2
            load_w(n + 3)
            for gi_, (r, sb_) in enumerate(tiles):
                tsl = tile_tok_slice(g, r, sb_)
                is_win = (sb_ == nsb - 1)

                def evv(pst, tps, gi_=gi_, is_win=is_win, r=r):
                    P.op("dve", lambda e: e.tensor_copy(out=Vg[:, gi_, :, 0:64], in_=pst[:, 0:256].rearrange("p (h d) -> p h d", h=4)),
                         reads=[tps], writes=[t_Vg[gi_]])
                    if is_win:
                        ws_ = nxt("wst")
                        P.op("act", lambda e: e.copy(out=wst[ws_][:, 256:512], in_=pst[:, 0:256]), reads=[tps], writes=[t_wst[ws_]])
                        rows = win_rows(g, r)
                        P.op("sp", lambda e: e.dma_start(out=rows[:, 256:512], in_=wst[ws_][:, 256:512]), reads=[t_wst[ws_]], dma=True)
                proj_tm(n % NW, (lambda k, tsl=tsl: xnT[:, k, tsl]), [t_xnT[j] for j in nat_tiles_of(g, r, sb_)], evv)

            def evvs():
                P.op("act", lambda e: e.copy(out=wins[0:16, g, 256:512], in_=ps_x[0:16, 0:256]), reads=[t_ps_x], writes=[t_wins])
            proj_tm_s(n % NW, 16, 0, evvs)
            P.op("sp", lambda e: e.dma_start(out=win_s_out[g], in_=wins[0:16, g, :]), reads=[t_wins], dma=True)
            for bb in range(4):
                def evvn(bb=bb):
                    P.op("dve", lambda e: e.tensor_copy(out=Vns[0:4, bb, g, :, 0:64], in_=ps_x[0:4, 0:256].rearrange("p (h d) -> p h d", h=4)),
                         reads=[t_ps_x], writes=[t_Vns])
                proj_tm_s(n % NW, 4, 4 * bb, evvn)
            allqk = [x for row in t_QT for x in row] + [x for row in t_KT for x in row]
            pending = []
            for gi_, (r, sb_) in enumerate(tiles):
                qsl = tile_tok_slice(g, r, sb_)
                blocks = ([(0, (r, sb_ - 1))] if sb_ > 0 else []) + [(1, (r, sb_))]
                io = nxt("o")
                for h in range(4):
                    pr, hf = h // 2, h % 2
                    psl = slice(64 * hf, 64 * hf + 64)
                    isx = nxt("s")

                    def qk(e, isx=isx, pr=pr, psl=psl, qsl=qsl, blocks=blocks):
                        for (blk, (kr, ksb)) in blocks:
                            ksl = tile_tok_slice(g, kr, ksb)
                            ins = e.matmul(ps_s[isx][:, blk * 128:(blk + 1) * 128], lhsT=KT[psl, pr, ksl], rhs=QT[psl, pr, qsl], start=True, stop=True)
                        return ins
                    P.op("pe", qk, reads=allqk, writes=[t_ps_s[isx]])
                    b0 = blocks[0][0]
                    csl = slice(b0 * 128, 256)
                    isb = nxt("sb")
                    P.op("dve", (lambda e, isx=isx, isb=isb, csl=csl, h=h, b0=b0: e.scalar_tensor_tensor(
                        out=sbf[isb][:, csl], in0=ps_s[isx][:, csl], scalar=SCALE,
                        in1=biasT[:, g * 4 + h, b0:2, :].rearrange("p b q -> p (b q)"), op0=ALU.mult, op1=ALU.add)),
                        reads=[t_ps_s[isx], t_biasT], writes=[t_sbf[isb]])
                    ipt = nxt("pt")
                    P.op("act", (lambda e, isb=isb, ipt=ipt, csl=csl: e.activation(out=PT[ipt][:, csl], in_=sbf[isb][:, csl], func=AF.Exp)),
                         reads=[t_sbf[isb]], writes=[t_PT[ipt]])
                    if pending:
                        pending.pop(0)()

                    def later(ipt=ipt, io=io, h=h, blocks=blocks, gi_=gi_):
                        def pv(e):
                            nb = len(blocks)
                            for bi, (blk, (kr, ksb)) in enumerate(blocks):
                                kgi = kr * nsb + ksb
                                ins = e.matmul(ps_o[io][:, h * 65:(h + 1) * 65], lhsT=PT[ipt][:, blk * 128:(blk + 1) * 128], rhs=Vg[:, kgi, h, 0:65],
                                               start=(bi == 0), stop=(bi == nb - 1))
                            return ins
                        P.op("pe", pv, reads=[t_PT[ipt]] + [t_Vg[kr * nsb + ksb] for (_, (kr, ksb)) in blocks], writes=[t_ps_o[io]])
                        if h == 3:
                            copy_op(ev_eng(), Og[:, 16 * g + gi_, :], ps_o[io][:, 0:260], [t_ps_o[io]], [t_Og[16 * g + gi_]])
                    pending.append(later)
            while pending:
                pending.pop(0)()

        for g in range(3):
            do_group(g)

        n = 9
        load_w(n + 3)
        proj_fm(n % NW, QmT, t_QmT, QTs, t_QTs, 6)
        def do_gate(gc):
            n = 10 + gc
            load_w(n + 3)
            for i in range(NT):
                def evg(pst, tps, i=i, gc=gc):
                    P.op("act", lambda e: e.activation(out=gate[:, i, gc * 256:(gc + 1) * 256], in_=pst[:, 0:256], func=AF.Silu),
                         reads=[tps], writes=[t_gate[i]])
                proj_tm(n % NW, (lambda k, i=i: xnT[:, k, i * 128:(i + 1) * 128]), [t_xnT[i]], evg)
            for bb in range(4):
                def evgs(bb=bb, gc=gc):
                    P.op("act", lambda e: e.activation(out=gate_s[0:4, bb, gc * 256:(gc + 1) * 256], in_=ps_x[0:4, 0:256], func=AF.Silu),
                         reads=[t_ps_x], writes=[t_gate_s])
                proj_tm_s(n % NW, 4, 4 * bb, evgs)
        for gc in range(2):
            do_gate(gc)

        barrier()
        l0_keep = apos[0]
        apos[0] = L0
        wout = carve_bf(4 * 1024).rearrange("p (k n) -> p k n", k=4)
        t_wout = Tk()
        P.op("pool", lambda e: e.dma_start(out=wout, in_=w_out_a.rearrange("(k p) n -> p k n", p=128)), writes=[t_wout], dma=True)
        hbuf = [carve_f(512) for _ in range(2)]
        t_hbuf = [Tk(), Tk()]
        hb = [carve_bf(512) for _ in range(2)]
        t_hb = [Tk(), Tk()]
        hT = [carve_bf(512).rearrange("p (k t) -> p k t", k=4) for _ in range(2)]
        t_hT = [Tk(), Tk()]
        rcp = [carve_f(8) for _ in range(2)]
        t_rcp = [Tk(), Tk()]
        ytmp = [carve_f(1024) for _ in range(2)]
        t_ytmp = [Tk(), Tk()]
        cst = carve_f(2048).rearrange("p (t n) -> p t n", t=4)
        ckb = carve_bf(4 * 256).rearrange("p (t n) -> p t n", t=4)
        cKT = carve_bf(8 * 128).rearrange("p (i n) -> p i n", i=8)
        cV = carve_bf(4 * 4 * 80).rearrange("p (t h d) -> p t h d", t=4, h=4)
        PTz = carve_bf(16)
        PTn = carve_bf(16)
        sbn = carve_f(16)
        t_cst, t_ckb, t_cKT, t_cV, t_PTz, t_PTn, t_sbn = Tk(), Tk(), Tk(), Tk(), Tk(), Tk(), Tk()
        assert apos[0] <= L0 + 8192 + 64 + 4096, apos[0] - L0
        apos[0] = l0_keep
        xres = xld
        t_xres = t_xld
        t_x1 = [Tk() for _ in range(NT + 1)]
        rr.update({"hb": 0})
        CTX = dict(PT=PT, t_PT=t_PT, ytmp=ytmp, t_ytmp=t_ytmp, xres=xres, t_xres=t_xres,
                   cst=cst, ckb=ckb, cKT=cKT, cV=cV, t_cst=t_cst, t_ckb=t_ckb, t_cKT=t_cKT, t_cV=t_cV)
        P.op("pool", lambda e: e.memset(cV[:, :, :, 64:65], 1.0), writes=[t_cV])
        P.op("pool", lambda e: e.memset(PTz[:, :], 0.0), writes=[t_PTz])

        def cross_attn(k_ap_fn, t_k, v_ap_fn, t_v, qT_ap_fn, q_reads, nq, ps_out, t_ps_out):
            PT_, t_PT_ = CTX["PT"], CTX["t_PT"]
            for h in range(4):
                pr, hf = h // 2, h % 2
                psl = slice(64 * hf, 64 * hf + 64)
                isx = nxt("s")

                def qk(e, isx=isx, pr=pr, psl=psl):
                    for blk in range(2):
                        ins = e.matmul(ps_s[isx][:, blk * 128:blk * 128 + nq], lhsT=k_ap_fn(psl, pr, blk), rhs=qT_ap_fn(psl, pr),
                                       start=True, stop=True)
                    return ins
                P.op("pe", qk, reads=[t_k] + list(q_reads), writes=[t_ps_s[isx]])
                ipt = nxt("pt")
                P.op("act", (lambda e, isx=isx, ipt=ipt: e.activation(
                    out=PT_[ipt][:, :].rearrange("p (b q) -> p b q", b=2)[:, :, 0:nq],
                    in_=ps_s[isx][:, 0:256].rearrange("p (b q) -> p b q", b=2)[:, :, 0:nq], func=AF.Exp, scale=SCALE)),
                    reads=[t_ps_s[isx]], writes=[t_PT_[ipt]])

                def pv(e, ipt=ipt, h=h):
                    for blk in range(2):
                        ins = e.matmul(ps_out[0:nq, h * 65:(h + 1) * 65], lhsT=PT_[ipt][:, blk * 128:blk * 128 + nq], rhs=v_ap_fn(blk, h),
                                       start=(blk == 0), stop=(blk == 1))
                    return ins
                P.op("pe", pv, reads=[t_PT_[ipt], t_v], writes=[t_ps_out])

        def normalize_o(ps_in, t_ps_in, nq, dst_ap, t_dst, t_rc, rc):
            v = ps_in[0:nq, 0:260].rearrange("p (h d) -> p h d", h=4)
            P.op("dve", lambda e: e.reciprocal(out=rc[0:nq, 0:4], in_=v[:, :, 64]), reads=[t_ps_in], writes=[t_rc])
            P.op("dve", lambda e: e.tensor_tensor(out=dst_ap.rearrange("p (h d) -> p h d", h=4), in0=v[:, :, 0:64],
                                                  in1=rc[0:nq, 0:4].unsqueeze(2).to_broadcast([nq, 4, 64]), op=ALU.mult),
                 reads=[t_ps_in, t_rc], writes=[t_dst])

        def post_a(nq, hbuf_ap, t_hb_in, gate_ap, t_gate_in, nk, hb_t, t_hb_t, hT_t, t_hT_t, col0):
            P.op("dve", lambda e: e.tensor_tensor(out=hb_t[0:nq, 0:nk * 128], in0=hbuf_ap, in1=gate_ap, op=ALU.mult),
                 reads=[t_hb_in, t_gate_in], writes=[t_hb_t])

            def tr(e):
                for k in range(nk):
                    ins = e.transpose(out=ps_t[:, k * 128:k * 128 + nq], in_=hb_t[0:nq, k * 128:(k + 1) * 128], identity=ident[0:nq, 0:nq])
                return ins
            P.op("pe", tr, reads=[t_hb_t, t_ident], writes=[t_ps_t])
            copy_op(ev_eng(), hT_t[:, 0:nk, col0:col0 + nq], ps_t[:].rearrange("p (k m) -> p k m", k=8)[:, 0:nk, 0:nq], [t_ps_t], [t_hT_t])

        def post_b(gi_post, nq, nk, hT_t, t_hT_t, wout_t, t_wout_t, ys_, x_src_fn, x_dst_fn):
            ytmp_, t_ytmp_, xres_, t_xres_ = CTX["ytmp"], CTX["t_ytmp"], CTX["xres"], CTX["t_xres"]
            for nb in range(2):
                def mm(e, nb=nb):
                    for k in range(nk):
                        ins = e.matmul(ps_mm[nb][0:nq, :], lhsT=hT_t[:, k, 0:nq], rhs=wout_t[:, k, nb * 512:(nb + 1) * 512], start=(k == 0), stop=(k == nk - 1))
                    return ins
                P.op("pe", mm, reads=[t_hT_t, t_wout_t], writes=[t_ps_mm[nb]])
                copy_op("act" if nb == 0 else "dve", ytmp_[ys_][0:nq, nb * 512:(nb + 1) * 512], ps_mm[nb][0:nq, :], [t_ps_mm[nb]], [t_ytmp_[ys_]])
            x_src_fn(ys_)
            rmsnorm(ytmp_[ys_][0:nq, :], nq, gi_post, ytmp_[ys_][0:nq, :], t_ytmp_[ys_], t_ytmp_[ys_])
            P.op("dve", lambda e: e.tensor_tensor(out=xres_[ys_][0:nq, :], in0=xres_[ys_][0:nq, :], in1=ytmp_[ys_][0:nq, :], op=ALU.add),
                 reads=[t_ytmp_[ys_], t_xres_[ys_]], writes=[t_xres_[ys_]])
            x_dst_fn(ys_)

        tile_hs = {}

        def tile_X(Tq):
            hs_ = nxt("hb")
            tile_hs[Tq] = hs_
            io = nxt("o")
            srcs = [(0, 0, Tq, ident[:, :])]
            srcs += [(1, r, Tq // 4, perm_lhsT(1, r, Tq)) for r in range(4)]
            srcs += [(2, r, 0, perm_lhsT(2, r, Tq)) for r in range(16)]

            def comb(e):
                ns = len(srcs)
                for si, (g, r, sb_, lt) in enumerate(srcs):
                    gi_ = r * (NT // DIL[g]) + sb_
                    ins = e.matmul(ps_o[io][:, 0:260], lhsT=lt, rhs=Og[:, 16 * g + gi_, :], start=(si == 0), stop=(si == ns - 1))
                return ins
            P.op("pe", comb, reads=[t_ident, t_M] + [t_Og[16 * g + r * (NT // DIL[g]) + sb_] for (g, r, sb_, _) in srcs], writes=[t_ps_o[io]])
            normalize_o(ps_o[io], t_ps_o[io], 128, hbuf[hs_][:, 0:256], t_hbuf[hs_], t_rcp[hs_], rcp[hs_])
            io2 = nxt("o")
            cross_attn((lambda psl, pr, blk: KmT[psl, 0, pr, blk * 128:(blk + 1) * 128]), t_KmT[0],
                       (lambda blk, h: Vm[:, 0, blk, h, 0:65]), t_Vm[0],
                       (lambda psl, pr: QmT[psl, pr, Tq * 128:(Tq + 1) * 128]), [x for row in t_QmT for x in row],
                       128, ps_o[io2], t_ps_o[io2])
            normalize_o(ps_o[io2], t_ps_o[io2], 128, hbuf[hs_][:, 256:512], t_hbuf[hs_], t_rcp[hs_], rcp[hs_])

        def tile_Y(Tq):
            hs_ = tile_hs[Tq]
            post_a(128, hbuf[hs_][:, :], t_hbuf[hs_], gate[:, Tq, :], t_gate[Tq], 4, hb[hs_], t_hb[hs_], hT[hs_], t_hT[hs_], 0)

            def xsrc(ys_):
                P.op("sp", lambda e: e.dma_start(out=xres[ys_][:, :], in_=x_p[Tq * 128:(Tq + 1) * 128, :]), writes=[t_xres[ys_]], dma=True)

            def xdst(ys_):
                P.op("sp", lambda e: e.dma_start(out=x1_d[Tq * 128:(Tq + 1) * 128, :], in_=xres[ys_][:, :]), reads=[t_xres[ys_]], writes=[t_x1[Tq]],
                     dma=True, sem_tk=t_xres[ys_])
                if "l0out" in DBG:
                    P.op("sp", lambda e: e.dma_start(out=y_p[Tq * 128:(Tq + 1) * 128, :], in_=xres[ys_][:, :]), reads=[t_xres[ys_]], dma=True)
            post_b(G_POST[0], 128, 4, hT[hs_], t_hT[hs_], wout, t_wout, hs_, xsrc, xdst)

        tile_X(0)
        for Tq in range(1, NT):
            tile_X(Tq)
            tile_Y(Tq - 1)
        tile_Y(NT - 1)

        def load_piece(src_ap, nt_):
            cst, ckb, cKT, cV = CTX["cst"], CTX["ckb"], CTX["cKT"], CTX["cV"]
            t_cst, t_ckb, t_cKT, t_cV = CTX["t_cst"], CTX["t_ckb"], CTX["t_cKT"], CTX["t_cV"]
            P.op("sp", lambda e: e.dma_start(out=cst[:, 0:nt_, :], in_=src_ap), writes=[t_cst], dma=True)
            P.op("dve", lambda e: e.tensor_copy(out=ckb[:, 0:nt_, :], in_=cst[:, 0:nt_, 0:256]), reads=[t_cst], writes=[t_ckb])
            P.op("act", lambda e: e.copy(out=cV[:, 0:nt_, :, 0:64], in_=cst[:, 0:nt_, 256:512].rearrange("p t (h d) -> p t h d", h=4)),
                 reads=[t_cst], writes=[t_cV])

            def tr(e):
                for t_ in range(nt_):
                    for pr in range(2):
                        idx = t_ * 2 + pr
                        ins = e.transpose(out=ps_t[:, idx * 128:(idx + 1) * 128], in_=ckb[:, t_, pr * 128:(pr + 1) * 128], identity=ident[:, :])
                return ins
            P.op("pe", tr, reads=[t_ckb, t_ident], writes=[t_ps_t])
            copy_op(ev_eng(), cKT[:, 0:2 * nt_, :], ps_t[:].rearrange("p (k m) -> p k m", k=8)[:, 0:2 * nt_, :], [t_ps_t], [t_cKT])

        def sample_batch(bb):
            io = nxt("o")
            pso, tpso = ps_o[io], t_ps_o[io]
            first = [True]

            def pv_mm(e, out_ap, lhsT, rhs):
                ins = e.matmul(out_ap, lhsT=lhsT, rhs=rhs, start=first[0], stop=False, skip_group_check=True)
                first[0] = False
                return ins
            qs = slice(4 * bb, 4 * bb + 4)
            for g in range(3):
                d = DIL[g]
                if g == 0:
                    load_piece(c_w0[bb].rearrange("(j o) c -> j o c", o=1), 1)
                elif g == 1:
                    load_piece(c_w1[bb].rearrange("(j r) c -> j r c", r=4), 4)
                else:
                    load_piece(c_w2[bb].rearrange("(j r) c -> j r c", r=16)[:, 0:4, :], 4)
                for h in range(4):
                    pr, hf = h // 2, h % 2
                    psl = slice(64 * hf, 64 * hf + 64)
                    isx = nxt("s")
                    if g == 0:
                        def qk(e, isx=isx, pr=pr, psl=psl):
                            e.matmul(ps_s[isx][:, 0:4], lhsT=cKT[psl, pr, :], rhs=QTs[psl, pr, qs], start=True, stop=True)
                            return e.matmul(ps_s[isx][0:4, 8:12], lhsT=KTs[psl, pr, qs], rhs=QTs[psl, pr, qs], start=True, stop=True)
                        P.op("pe", qk, reads=[t_cKT, t_QTs, t_KTs], writes=[t_ps_s[isx]])
                        isb = nxt("sb")
                        P.op("dve", (lambda e, isx=isx, isb=isb, h=h: e.scalar_tensor_tensor(
                            out=sbf[isb][:, 0:4], in0=ps_s[isx][:, 0:4], scalar=SCALE, in1=biasT[:, h, 0, 0:4], op0=ALU.mult, op1=ALU.add)),
                            reads=[t_ps_s[isx], t_biasT], writes=[t_sbf[isb]])
                        P.op("dve", (lambda e, isx=isx, h=h: e.scalar_tensor_tensor(
                            out=sbn[0:4, 0:4], in0=ps_s[isx][0:4, 8:12], scalar=SCALE, in1=biasT[0:4, h, 1, 0:4], op0=ALU.mult, op1=ALU.add)),
                            reads=[t_ps_s[isx], t_biasT], writes=[t_sbn])
                        ipt = nxt("pt")
                        P.op("act", (lambda e, isb=isb, ipt=ipt: e.activation(out=PT[ipt][:, 0:4], in_=sbf[isb][:, 0:4], func=AF.Exp)),
                             reads=[t_sbf[isb]], writes=[t_PT[ipt]])
                        P.op("act", lambda e: e.activation(out=PTn[0:4, 0:4], in_=sbn[0:4, 0:4], func=AF.Exp), reads=[t_sbn], writes=[t_PTn])

                        def pv(e, ipt=ipt, h=h):
                            pv_mm(e, pso[0:4, h * 65:(h + 1) * 65], PT[ipt][:, 0:4], cV[:, 0, h, 0:65])
                            return pv_mm(e, pso[0:4, h * 65:(h + 1) * 65], PTn[0:4, 0:4], Vns[0:4, bb, 0, h, 0:65])
                        P.op("pe", pv, reads=[t_PT[ipt], t_PTn, t_cV, t_Vns], writes=[tpso])
                    else:
                        def qk(e, isx=isx, pr=pr, psl=psl, g=g):
                            for t_ in range(4):
                                e.matmul(ps_s[isx][:, t_:t_ + 1], lhsT=cKT[psl, 2 * t_ + pr, :], rhs=QTs[psl, 2 * g + pr, 4 * bb + t_:4 * bb + t_ + 1],
                                         start=True, stop=True)
                            return e.matmul(ps_s[isx][0:4, 8:12], lhsT=KTs[psl, 2 * g + pr, qs], rhs=QTs[psl, 2 * g + pr, qs], start=True, stop=True)
                        P.op("pe", qk, reads=[t_cKT, t_QTs, t_KTs], writes=[t_ps_s[isx]])
                        P.op("act", (lambda e, isx=isx, h=h, g=g: e.activation(out=PTz[:, 0:16:5], in_=ps_s[isx][:, 0:4], func=AF.Exp,
                                                                              bias=biasT[:, g * 4 + h, 0, 0:1], scale=SCALE)),
                             reads=[t_ps_s[isx], t_biasT], writes=[t_PTz])
                        P.op("dve", (lambda e, isx=isx, h=h, g=g: e.scalar_tensor_tensor(
                            out=sbn[0:4, 0:4], in0=ps_s[isx][0:4, 8:12], scalar=SCALE, in1=biasT[0:4, g * 4 + h, 1, 0:4], op0=ALU.mult, op1=ALU.add)),
                            reads=[t_ps_s[isx], t_biasT], writes=[t_sbn])
                        P.op("act", lambda e: e.activation(out=sbn[0:4, 4:8], in_=sbn[0:4, 0:4], func=AF.Exp), reads=[t_sbn], writes=[t_sbn])
                        P.op("dve", lambda e: e.tensor_tensor(out=PTn[0:4, 0:4], in0=sbn[0:4, 4:8], in1=identf[0:4, 0:4], op=ALU.mult),
                             reads=[t_sbn, t_ident], writes=[t_PTn])

                        def pv(e, h=h, g=g):
                            for t_ in range(4):
                                pv_mm(e, pso[0:4, h * 65:(h + 1) * 65], PTz[:, 4 * t_:4 * t_ + 4], cV[:, t_, h, 0:65])
                            return pv_mm(e, pso[0:4, h * 65:(h + 1) * 65], PTn[0:4, 0:4], Vns[0:4, bb, g, h, 0:65])
                        P.op("pe", pv, reads=[t_PTz, t_PTn, t_cV, t_Vns], writes=[tpso])
            hs_ = 0
            normalize_o(pso, tpso, 4, hbuf[hs_][0:4, 0:256], t_hbuf[hs_], t_rcp[hs_], rcp[hs_])
            load_piece(c_mem[0, bb].rearrange("(b j) c -> j b c", b=2), 2)
            io2 = nxt("o")
            cross_attn((lambda psl, pr, blk: cKT[psl, 2 * blk + pr, :]), t_cKT, (lambda blk, h: cV[:, blk, h, 0:65]), t_cV,
                       (lambda psl, pr: QTs[psl, 6 + pr, qs]), [t_QTs], 4, ps_o[io2], t_ps_o[io2])
            normalize_o(ps_o[io2], t_ps_o[io2], 4, hbuf[hs_][0:4, 256:512], t_hbuf[hs_], t_rcp[hs_], rcp[hs_])
            post_a(4, hbuf[hs_][0:4, :], t_hbuf[hs_], gate_s[0:4, bb, :], t_gate_s, 4, hb[hs_], t_hb[hs_], hT[1], t_hT[1], 4 * bb)

        for bb in range(0 if "nosample" in DBG else 4):
            sample_batch(bb)

        def xsrc_s(ys_):
            P.op("sp", lambda e: e.dma_start(out=xres[ys_][0:16, :], in_=x_s), writes=[t_xres[ys_]], dma=True)

        def xdst_s(ys_):
            P.op("sp", lambda e: e.dma_start(out=x1s_d, in_=xres[ys_][0:16, :]), reads=[t_xres[ys_]], writes=[t_x1[NT]], dma=True, sem_tk=t_xres[ys_])
            if "l0out" in DBG:
                P.op("sp", lambda e: e.dma_start(out=y_s, in_=xres[ys_][0:16, :]), reads=[t_xres[ys_]], dma=True)
        if "nosample" not in DBG:
            post_b(G_POST[0], 16, 4, hT[1], t_hT[1], wout, t_wout, 1, xsrc_s, xdst_s)
        print('n_ops', len(P.ops))
        if 'dump' in DBG:
            for _i, _o in enumerate(P.ops):
                print('OP', _i, _o.eng, _o.line, 'dma' if _o.is_dma else '')

        if STAGE >= 2:
            barrier()
            apos[0] = 0
            atop[0] = ARENA_WORDS
            load_gains([(0, norm_pre, 1), (1, norm_post, 1)])
            Wb = carve_bf(8 * B_IN).rearrange("p (k n) -> p k n", k=8)
            t_Wb = Tk()
            for c0 in range(0, B_IN, 464):
                P.op("pool", (lambda e, c0=c0: e.dma_start(out=Wb[:, :, c0:c0 + 464], in_=w_in_b[:, c0:c0 + 464].rearrange("(k p) n -> p k n", p=128))),
                     writes=[t_Wb], dma=True)
            Wo = carve_bf(8 * 1024).rearrange("p (k n) -> p k n", k=8)
            t_Wo = Tk()
            P.op("pool", lambda e: e.dma_start(out=Wo, in_=w_out_b.rearrange("(k p) n -> p k n", p=128)), writes=[t_Wo], dma=True)
            wup = carve_bf(768)
            aup = carve_bf(768)
            t_lora = Tk()
            P.op("pool", lambda e: e.dma_start(out=wup[0:64, :], in_=r_wup), writes=[t_lora], dma=True)
            P.op("pool", lambda e: e.dma_start(out=aup[64:128, :], in_=r_aup), writes=[t_lora], dma=True)
            par = carve_f(64)
            t_par = Tk()
            P_MU, P_W0, P_A0, P_KK, P_KA, P_OMKA, P_RK = 0, 19, 25, 31, 37, 43, 49
            for (src, c0, nm) in ((r_mu, P_MU, 19), (r_w0, P_W0, 6), (r_a0, P_A0, 6), (r_kk, P_KK, 6), (r_ka, P_KA, 6), (r_rk, P_RK, 6)):
                P.op("sp", (lambda e, src=src, c0=c0, nm=nm: e.dma_start(out=par[:, c0:c0 + nm], in_=src.rearrange("o (m p) -> p (o m)", p=128),
                                                                         allow_slow_non_contiguous=True)), writes=[t_par], dma=True)
            P.op("dve", lambda e: e.tensor_scalar(out=par[:, P_OMKA:P_OMKA + 6], in0=par[:, P_KA:P_KA + 6], scalar1=-1.0, scalar2=1.0, op0=ALU.mult, op1=ALU.add),
                 reads=[t_par], writes=[t_par])
            lnw_b = carve_f(768)
            lnb_b = carve_f(768)
            t_ln = Tk()
            for (dst, src) in ((lnw_b, r_lnw), (lnb_b, r_lnb)):
                apb = bass.AP(tensor=src.tensor, offset=0, ap=[[0, 128], [1, 768]])
                P.op("sp", (lambda e, dst=dst, apb=apb: e.dma_start(out=dst, in_=apb)), writes=[t_ln], dma=True)
            SA, SB, SC, SD, SE, SF, SG = [carve_f(768).rearrange("p (m j) -> p m j", m=6) for _ in range(7)]
            t_S = {k_: Tk() for k_ in "ABCDEFG"}
            t_mt1 = t_S["A"]
            mtmp1 = SA[:, :, :].rearrange("p m j -> p (m j)")
            ML = carve_bf(512).rearrange("p (h i) -> p h i", h=4)
            MU = carve_bf(512).rearrange("p (h i) -> p h i", h=4)
            MUi = carve_bf(512).rearrange("p (h i) -> p h i", h=4)
            Irep = carve_bf(512).rearrange("p (h i) -> p h i", h=4)
            rmask = carve_bf(768).rearrange("p (m j) -> p m j", m=6)
            bd = carve_f(128)
            hsel = carve_bf(2)
            t_cst1 = Tk()
            for (Mt, cm, step, cmp_) in ((ML, 1, -1, ALU.is_gt), (MU, -1, 1, ALU.is_gt), (MUi, -1, 1, ALU.is_ge), (Irep, 1, -1, ALU.is_equal)):
                P.op("pool", lambda e: e.memset(mtmp1[:, 0:512], 1.0), writes=[t_mt1])

                def mk(e, cm=cm, step=step, cmp_=cmp_):
                    v = mtmp1[:, 0:512].rearrange("p (h i) -> p h i", h=4)
                    return e.affine_select(out=v, in_=v, pattern=[[0, 4], [step, 128]], compare_op=cmp_, fill=0.0, base=0, channel_multiplier=cm)
                P.op("pool", mk, reads=[t_mt1], writes=[t_mt1])
                P.op("dve", (lambda e, Mt=Mt: e.tensor_copy(out=Mt, in_=mtmp1[:, 0:512].rearrange("p (h i) -> p h i", h=4))), reads=[t_mt1], writes=[t_cst1])

            def mk2(e):
                e.memset(rmask[:, :, :], 1.0)
                e.memset(rmask[:, :, 0:1], 0.0)
                e.memset(bd[:, :], 0.0)
                e.memset(bd[0:64, 0:64], 1.0)
                e.memset(bd[64:128, 64:128], 1.0)
                e.memset(hsel[:, :], 0.0)
                e.memset(hsel[0:64, 0:1], 1.0)
                return e.memset(hsel[64:128, 1:2], 1.0)
            P.op("pool", mk2, writes=[t_cst1])

            xld1 = carve_f(1024)
            xnb1 = carve_bf(1024)
            xnTc = carve_bf(8 * 128).rearrange("p (k t) -> p k t", k=8)
            xnTs1 = carve_bf(8 * 16).rearrange("p (k t) -> p k t", k=8)
            t_xld1, t_xnb1, t_xnTc, t_xnTs1 = Tk(), Tk(), Tk(), Tk()
            cols = carve_f(19 * 129).rearrange("p (m j) -> p m j", m=19)
            t_cols = Tk()
            lastc = carve_f(20)
            t_xs = t_cols
            QmTc = carve_bf(2 * 128).rearrange("p (m j) -> p m j", m=2)
            t_QmTc = Tk()
            gt = carve_bf(1024)
            t_gt = Tk()
            lw = carve_bf(128)
            t_lw = Tk()
            outs_w0 = apos[0]
            t_o = {k_: Tk() for k_ in ("AT", "BT", "KT", "KH", "BH", "RT", "RKT", "XV")}
            BT, KT1, KH, BH, RKT, XV = [carve_bf(768).rearrange("p (m j) -> p m j", m=6) for _ in range(6)]
            AT2 = carve_bf(12 * 128).rearrange("p (h j) -> p h j", h=12)
            RT2 = carve_bf(12 * 128).rearrange("p (h j) -> p h j", h=12)
            P.op("pool", lambda e: e.memset(AT2[:, :, :], 0.0), writes=[t_o["AT"]])
            P.op("pool", lambda e: e.memset(RT2[:, :, :], 0.0), writes=[t_o["RT"]])
            Vt = carve_bf(768)
            Kh = carve_bf(768)
            Bh = carve_bf(768)
            t_Vt, t_Kh, t_Bh = Tk(), Tk(), Tk()
            Lh = [carve_bf(512).rearrange("p (h i) -> p h i", h=4) for _ in range(3)]
            Xh = [carve_bf(512).rearrange("p (h i) -> p h i", h=4) for _ in range(3)]
            Qh = [carve_bf(512).rearrange("p (h i) -> p h i", h=4) for _ in range(3)]
            t_Lh, t_Xh, t_Qh = [Tk() for _ in range(3)], [Tk() for _ in range(3)], [Tk() for _ in range(3)]
            Qf = carve_bf(12 * 128).rearrange("p (h i) -> p h i", h=12)
            Aak = carve_bf(12 * 128).rearrange("p (h i) -> p h i", h=12)
            Ark = carve_bf(12 * 128).rearrange("p (h i) -> p h i", h=12)
            Arb = carve_bf(12 * 128).rearrange("p (h i) -> p h i", h=12)
            t_Qf, t_Aak, t_Ark, t_Arb = Tk(), Tk(), Tk(), Tk()
            zb_w0 = apos[0]
            Zb = carve_bf(768)
            Ub = carve_bf(768)
            t_Zb, t_Ub = Tk(), Tk()
            Yf = SD[:, :, :].rearrange("p m j -> p (m j)")
            t_Yf = t_S["D"]
            St = carve_f(384).rearrange("p (m v) -> p m v", m=6)
            Sbf = carve_bf(384).rearrange("p (m v) -> p m v", m=6)
            t_St = Tk()
            wc = carve_f(8)
            t_wc = Tk()
            gsm = carve_f(96)
            t_gsm = Tk()
            hbuf1 = carve_f(1024)
            hb1 = carve_bf(1024)
            hT1 = carve_bf(8 * 128).rearrange("p (k t) -> p k t", k=8)
            t_hbuf1, t_hb1, t_hT1 = Tk(), Tk(), Tk()
            rcp1 = carve_f(8)
            t_rcp1 = Tk()
            xres1 = carve_f(1024)
            t_xres1 = Tk()
            PT1 = [carve_bf(256) for _ in range(2)]
            t_PT1 = [Tk(), Tk()]
            svst = arena[:, zb_w0:zb_w0 + 768].rearrange("p (h k) -> p h k", h=12)
            t_svst = Tk()
            print("L1 arena words", apos[0])
            C0 = 0.6065306597126334
            psq = [ps_s[0], ps_s[1], ps_o[0], ps_o[1]]
            t_psq = [t_ps_s[0], t_ps_s[1], t_ps_o[0], t_ps_o[1]]
            rr.update({"q": 0})

            def nq4():
                rr["q"] += 1
                return rr["q"] % 4

            def tt(eng, out, in0, in1, op, reads, writes):
                P.op(eng, lambda e: e.tensor_tensor(out=out, in0=in0, in1=in1, op=op), reads=reads, writes=writes)

            def bc6(col0, C):
                return par[:, col0:col0 + 6].unsqueeze(2).to_broadcast([128, 6, C])

            def rwkv_chunk(C, xn_ap, t_xn, mode, idx):
                first = (idx == 0) if mode == "p" else True
                last = (idx == NT - 1) if mode == "p" else True
                groups = [(16, 3), (0, 4), (4, 4), (8, 4), (12, 4), (19, 2)]
                for (m0, nm) in groups:
                    i = nxt("mm")

                    def mm(e, m0=m0, nm=nm, i=i):
                        for mi in range(nm):
                            m = m0 + mi
                            for k in range(8):
                                ins = e.matmul(ps_mm[i][:, mi * 128:mi * 128 + C], lhsT=Wb[:, k, m * 128:(m + 1) * 128], rhs=xn_ap(k),
                                               start=(k == 0), stop=(k == 7))
                        return ins
                    P.op("pe", mm, reads=[t_Wb, t_xn], writes=[t_ps_mm[i]])
                    src = ps_mm[i][:, 0:nm * 128].rearrange("p (m j) -> p m j", m=nm)[:, :, 0:C]
                    if m0 < 19:
                        copy_op("act", cols[:, m0:m0 + nm, 1:1 + C], src, [t_ps_mm[i]], [t_cols])
                    else:
                        copy_op("act", QmTc[:, :, 0:C], src, [t_ps_mm[i]], [t_QmTc])
                for gc in range(2):
                    i = nxt("mm")

                    def mmg(e, gc=gc, i=i):
                        for k in range(8):
                            ins = e.matmul(ps_mm[i][0:C, :], lhsT=xn_ap(k), rhs=Wb[:, k, 2688 + gc * 512:2688 + (gc + 1) * 512], start=(k == 0), stop=(k == 7))
                        return ins
                    P.op("pe", mmg, reads=[t_Wb, t_xn], writes=[t_ps_mm[i]])
                    P.op("act", (lambda e, gc=gc, i=i: e.activation(out=gt[0:C, gc * 512:(gc + 1) * 512], in_=ps_mm[i][0:C, :], func=AF.Silu)),
                         reads=[t_ps_mm[i]], writes=[t_gt])
                t_lastc = Tk()
                P.op("act", lambda e: e.copy(out=lastc[:, 0:19], in_=cols[:, :, C]), reads=[t_cols], writes=[t_lastc])
                if last:
                    dst = (o_shp if mode == "p" else o_shs[idx:idx + 1, :]).rearrange("o (m p) -> p (o m)", p=128)
                    P.op("sp", lambda e: e.dma_start(out=dst, in_=lastc[:, 0:19], allow_slow_non_contiguous=True), reads=[t_lastc], dma=True)
                for (m0, nm) in ((0, 6), (6, 6), (12, 6), (18, 1)):
                    cur = cols[:, m0:m0 + nm, 1:1 + C]
                    prv = cols[:, m0:m0 + nm, 0:C]
                    tmp = SG[:, 0:nm, 0:C]
                    tt("dve", tmp, prv, cur, ALU.subtract, [t_cols], [t_S["G"]])
                    tt("dve", tmp, tmp, par[:, P_MU + m0:P_MU + m0 + nm].unsqueeze(2).to_broadcast([128, nm, C]), ALU.mult, [t_S["G"], t_par], [t_S["G"]])
                    tt("dve", cur, tmp, cur, ALU.add, [t_S["G"], t_cols], [t_cols])
                P.op("act", lambda e: e.copy(out=cols[:, :, 0], in_=lastc[:, 0:19]), reads=[t_lastc, t_cols], writes=[t_cols])

                class _XS:
                    def __getitem__(self, key):
                        p_, m_, j_ = key
                        assert j_ == slice(0, C)
                        return cols[p_, m_, 1:1 + C]
                xs = _XS()
                xr, xk, xv_ = xs[:, 0:6, 0:C], xs[:, 6:12, 0:C], xs[:, 12:18, 0:C]
                P.op("act", lambda e: e.activation(out=lw[0:64, 0:C], in_=xs[0:64, 18, 0:C], func=AF.Tanh), reads=[t_xs], writes=[t_lw])
                P.op("dve", lambda e: e.tensor_copy(out=lw[64:128, 0:C], in_=xs[64:128, 18, 0:C]), reads=[t_xs], writes=[t_lw])
                sigw, asig = SA[:, :, 0:C], SB[:, :, 0:C]
                for (which, wt, rows, pcol, dstS, tS) in (("w", wup, slice(0, 64), P_W0, SA, "A"), ("a", aup, slice(64, 128), P_A0, SB, "B")):
                    for (p0, np_) in ((0, 4), (4, 2)):
                        q = nq4()

                        def mml(e, wt=wt, rows=rows, p0=p0, np_=np_, q=q):
                            for pi in range(np_):
                                p = p0 + pi
                                ins = e.matmul(psq[q][:, pi * 128:pi * 128 + C], lhsT=wt[rows, p * 128:(p + 1) * 128], rhs=lw[rows, 0:C], start=True, stop=True)
                            return ins
                        P.op("pe", mml, reads=[t_lora, t_lw], writes=[t_psq[q]])
                        for pi in range(np_):
                            p = p0 + pi
                            P.op("act", (lambda e, pi=pi, p=p, q=q, pcol=pcol, dstS=dstS: e.activation(
                                out=dstS[:, p, 0:C], in_=psq[q][:, pi * 128:pi * 128 + C], func=AF.Sigmoid, bias=par[:, pcol + p:pcol + p + 1], scale=1.0)),
                                reads=[t_psq[q], t_par], writes=[t_S[tS]])
                cs = SC[:, :, 0:C]
                if C == 128:
                    P.op("dve", lambda e: e.tensor_tensor_scan(out=SC[:, :, :].rearrange("p m j -> p (m j)"), data0=rmask[:, :, :].rearrange("p m j -> p (m j)"),
                                                               data1=SA[:, :, :].rearrange("p m j -> p (m j)"), initial=0.0, op0=ALU.mult, op1=ALU.add),
                         reads=[t_S["A"], t_cst1], writes=[t_S["C"]])
                else:
                    for p in range(6):
                        P.op("dve", (lambda e, p=p: e.tensor_tensor_scan(out=SC[:, p, 0:C], data0=rmask[:, p, 0:C], data1=SA[:, p, 0:C], initial=0.0,
                                                                         op0=ALU.mult, op1=ALU.add)), reads=[t_S["A"], t_cst1], writes=[t_S["C"]])
                csC = SC[:, :, C - 1:C]
                eP, eH, eA, eN = SE[:, :, 0:C], SD[:, :, 0:C], SA[:, :, 0:C], SC[:, :, 0:C]
                P.op("act", lambda e: e.activation(out=eP, in_=cs, func=AF.Exp, scale=-C0), reads=[t_S["C"]], writes=[t_S["E"]])
                tt("dve", eH, csC.to_broadcast([128, 6, C]), cs, ALU.subtract, [t_S["C"]], [t_S["D"]])
                P.op("act", lambda e: e.activation(out=eH, in_=eH, func=AF.Exp, scale=-C0), reads=[t_S["D"]], writes=[t_S["D"]])
                P.op("act", lambda e: e.activation(out=wc[:, 0:6], in_=SC[:, :, C - 1], func=AF.Exp, scale=-C0), reads=[t_S["C"]], writes=[t_wc])
                tt("dve", eA, cs, sigw, ALU.subtract, [t_S["C"], t_S["A"]], [t_S["A"]])
                P.op("act", lambda e: e.activation(out=eA, in_=eA, func=AF.Exp, scale=-C0), reads=[t_S["A"]], writes=[t_S["A"]])
                P.op("act", lambda e: e.activation(out=eN, in_=cs, func=AF.Exp, scale=C0), reads=[t_S["C"], t_S["D"], t_S["E"], t_S["A"], t_wc], writes=[t_S["C"]])
                kk, g_ = SF[:, :, 0:C], SG[:, :, 0:C]
                tt("dve", kk, xk, bc6(P_KK, C), ALU.mult, [t_xs, t_par], [t_S["F"]])
                tt("dve", g_, kk, kk, ALU.mult, [t_S["F"]], [t_S["G"]])
                qa, qb_ = nq4(), nq4()

                def mmn(e):
                    e.matmul(psq[qa][:, 0:4 * 128].rearrange("p (m j) -> p m j", m=4)[:, :, 0:C], lhsT=bd[:, :], rhs=SG[:, 0:4, 0:C], start=True, stop=True)
                    return e.matmul(psq[qb_][:, 0:2 * 128].rearrange("p (m j) -> p m j", m=2)[:, :, 0:C], lhsT=bd[:, :], rhs=SG[:, 4:6, 0:C], start=True, stop=True)
                P.op("pe", mmn, reads=[t_S["G"], t_cst1], writes=[t_psq[qa], t_psq[qb_]])
                P.op("act", lambda e: e.activation(out=SG[:, 0:4, 0:C], in_=psq[qa][:, 0:512].rearrange("p (m j) -> p m j", m=4)[:, :, 0:C], func=AF.Sqrt),
                     reads=[t_psq[qa]], writes=[t_S["G"]])
                P.op("act", lambda e: e.activation(out=SG[:, 4:6, 0:C], in_=psq[qb_][:, 0:256].rearrange("p (m j) -> p m j", m=2)[:, :, 0:C], func=AF.Sqrt),
                     reads=[t_psq[qb_]], writes=[t_S["G"]])
                P.op("dve", lambda e: e.tensor_scalar_max(out=g_, in0=g_, scalar1=1e-12), reads=[t_S["G"]], writes=[t_S["G"]])
                P.op("dve", lambda e: e.reciprocal(out=g_, in_=g_), reads=[t_S["G"]], writes=[t_S["G"]])
                tt("dve", kk, kk, g_, ALU.mult, [t_S["F"], t_S["G"]], [t_S["F"]])
                for hf_ in range(2):
                    rs_ = slice(64 * hf_, 64 * hf_ + 64)
                    P.op("dve", (lambda e, hf_=hf_, rs_=rs_: e.scalar_tensor_tensor(out=AT2[rs_, hf_:12:2, 0:C], in0=SF[rs_, :, 0:C], scalar=-1.0,
                                                                                   in1=SA[rs_, :, 0:C], op0=ALU.mult, op1=ALU.mult)),
                         reads=[t_S["F"], t_S["A"]], writes=[t_o["AT"]])
                tt("dve", g_, kk, asig, ALU.mult, [t_S["F"], t_S["B"]], [t_S["G"]])
                tt("dve", BT[:, :, 0:C], g_, eN, ALU.mult, [t_S["G"], t_S["C"]], [t_o["BT"]])
                tt("dve", BH[:, :, 0:C], g_, eH, ALU.mult, [t_S["G"], t_S["D"]], [t_o["BH"]])
                km = hbuf1[:, 0:768].rearrange("p (m j) -> p m j", m=6)[:, :, 0:C]
                tt("pool", km, asig, bc6(P_KA, C), ALU.mult, [t_S["B"], t_par], [t_hbuf1])
                tt("pool", km, km, bc6(P_OMKA, C), ALU.add, [t_hbuf1, t_par], [t_hbuf1])
                tt("pool", km, km, xk, ALU.mult, [t_hbuf1, t_xs], [t_hbuf1])
                tt("pool", KT1[:, :, 0:C], km, eN, ALU.mult, [t_hbuf1, t_S["C"]], [t_o["KT"]])
                tt("pool", KH[:, :, 0:C], km, eH, ALU.mult, [t_hbuf1, t_S["D"]], [t_o["KH"]])
                tt("pool", km, km, bc6(P_RK, C), ALU.mult, [t_hbuf1, t_par], [t_hbuf1])
                tt("pool", RKT[:, :, 0:C], km, xr, ALU.mult, [t_hbuf1, t_xs], [t_o["RKT"]])
                for hf_ in range(2):
                    rs_ = slice(64 * hf_, 64 * hf_ + 64)
                    tt("pool", RT2[rs_, hf_:12:2, 0:C], cols[rs_, 0:6, 1:1 + C], SE[rs_, :, 0:C], ALU.mult, [t_xs, t_S["E"]], [t_o["RT"]])
                P.op("act", lambda e: e.copy(out=XV[:, :, 0:C], in_=xv_), reads=[t_xs], writes=[t_o["XV"]])
                for (srcT, tsrc, dstT, tdst) in ((XV, "XV", Vt, t_Vt), (KH, "KH", Kh, t_Kh), (BH, "BH", Bh, t_Bh)):
                    def tr(e, srcT=srcT):
                        for p in range(6):
                            ins = e.transpose(out=ps_t[0:C, p * 128:(p + 1) * 128], in_=srcT[:, p, 0:C], identity=ident[:, :])
                        return ins
                    P.op("pe", tr, reads=[t_o[tsrc], t_ident], writes=[t_ps_t])
                    copy_op(ev_eng(), dstT[0:C, :], ps_t[0:C, 0:768], [t_ps_t], [tdst])
                nlev = max(1, int(math.ceil(math.log2(C))))

                def pair_mm(q, l_fn, r_fn, hg, reads):
                    def f(e):
                        for hi in range(4):
                            h = 4 * hg + hi
                            ins = e.matmul(psq[q][0:C, hi * 128:hi * 128 + C], lhsT=l_fn(h), rhs=r_fn(h), start=True, stop=True)
                        return ins
                    P.op("pe", f, reads=reads, writes=[t_psq[q]])

                A2 = lambda h: AT2[:, h, 0:C]
                R2 = lambda h: RT2[:, h, 0:C]
                Bp = lambda h: BT[:, h // 2, 0:C]
                Kp = lambda h: KT1[:, h // 2, 0:C]

                def pv4(q):
                    return psq[q][0:C, :].rearrange("p (h i) -> p h i", h=4)[:, :, 0:C]

                def sq_mm(q, lT, rT, reads, acc_ident_rhs=None):
                    def f(e):
                        for hi in range(4):
                            if acc_ident_rhs is not None:
                                e.matmul(psq[q][0:C, hi * 128:hi * 128 + C], lhsT=ident[0:C, 0:C], rhs=acc_ident_rhs[0:C, hi, 0:C], start=True, stop=False)
                            ins = e.matmul(psq[q][0:C, hi * 128:hi * 128 + C], lhsT=lT[0:C, hi, 0:C], rhs=rT[0:C, hi, 0:C],
                                           start=(acc_ident_rhs is None), stop=True)
                        return ins
                    P.op("pe", f, reads=reads + [t_ident], writes=[t_psq[q]])

                for hg in range(3):
                    q = nq4()
                    pair_mm(q, A2, Bp, hg, [t_o["AT"], t_o["BT"]])
                    tt("dve", Lh[hg][0:C, :, 0:C], pv4(q), ML[0:C, :, 0:C], ALU.mult, [t_psq[q], t_cst1], [t_Lh[hg]])
                    q = nq4()
                    pair_mm(q, Bp, A2, hg, [t_o["AT"], t_o["BT"]])
                    tt("dve", Xh[hg][0:C, :, 0:C], pv4(q), MU[0:C, :, 0:C], ALU.mult, [t_psq[q], t_cst1], [t_Xh[hg]])
                    tt("pool", Qh[hg][0:C, :, 0:C], Xh[hg][0:C, :, 0:C], Irep[0:C, :, 0:C], ALU.add, [t_Xh[hg], t_cst1], [t_Qh[hg]])
                for j in range(nlev - 1):
                    need_x = (j + 1 < nlev - 1)
                    for hg in range(3):
                        q = nq4()
                        sq_mm(q, Xh[hg], Lh[hg], [t_Xh[hg], t_Lh[hg]])
                        qx = None
                        if need_x:
                            qx = nq4()
                            sq_mm(qx, Lh[hg], Xh[hg], [t_Xh[hg], t_Lh[hg]])
                        copy_op("act", Lh[hg][0:C, :, 0:C], pv4(q), [t_psq[q]], [t_Lh[hg]])
                        if need_x:
                            copy_op("dve", Xh[hg][0:C, :, 0:C], pv4(qx), [t_psq[qx]], [t_Xh[hg]])
                    for hg in range(3):
                        q = nq4()
                        sq_mm(q, Lh[hg], Qh[hg], [t_Lh[hg], t_Qh[hg]], acc_ident_rhs=Qh[hg])
                        if j == nlev - 2:
                            copy_op("act", Qf[0:C, 4 * hg:4 * hg + 4, 0:C], pv4(q), [t_psq[q]], [t_Qf])
                        else:
                            copy_op("act", Qh[hg][0:C, :, 0:C], pv4(q), [t_psq[q]], [t_Qh[hg]])
                for hg in range(3):
                    if nlev == 1:
                        copy_op("act", Qf[0:C, 4 * hg:4 * hg + 4, 0:C], Qh[hg][0:C, :, 0:C], [t_Qh[hg]], [t_Qf])
                    q = nq4()
                    pair_mm(q, Kp, A2, hg, [t_o["KT"], t_o["AT"]])
                    tt("dve", Aak[0:C, 4 * hg:4 * hg + 4, 0:C], pv4(q), MU[0:C, :, 0:C], ALU.mult, [t_psq[q], t_cst1], [t_Aak])
                    q = nq4()
                    pair_mm(q, Kp, R2, hg, [t_o["KT"], t_o["RT"]])
                    tt("dve", Ark[0:C, 4 * hg:4 * hg + 4, 0:C], pv4(q), MUi[0:C, :, 0:C], ALU.mult, [t_psq[q], t_cst1], [t_Ark])
                    q = nq4()
                    pair_mm(q, Bp, R2, hg, [t_o["BT"], t_o["RT"]])
                    tt("dve", Arb[0:C, 4 * hg:4 * hg + 4, 0:C], pv4(q), MUi[0:C, :, 0:C], ALU.mult, [t_psq[q], t_cst1], [t_Arb])
                if first:
                    if mode == "p":
                        P.op("pool", lambda e: e.memset(St[:, :, :], 0.0), writes=[t_St])
                        P.op("pool", lambda e: e.memset(Sbf[:, :, :], 0.0), writes=[t_St])
                    else:
                        P.op("sp", lambda e: e.dma_start(out=svst[0:64, :, :], in_=s_wkv[idx].rearrange("h v k -> v h k")), writes=[t_svst, t_Zb, t_Ub], dma=True, sem_tk=t_svst)

                        def trs(e):
                            for p in range(6):
                                ins = e.transpose(out=ps_x[:, p * 64:(p + 1) * 64], in_=svst[0:64, 2 * p:2 * p + 2, :].rearrange("v h k -> v (h k)"),
                                                  identity=identf[0:64, 0:64])
                            return ins
                        P.op("pe", trs, reads=[t_svst, t_ident], writes=[t_ps_x])
                        P.op("act", lambda e: e.copy(out=St[:, :, :], in_=ps_x[:, 0:384].rearrange("p (m v) -> p m v", m=6)), reads=[t_ps_x], writes=[t_St])
                        P.op("dve", lambda e: e.tensor_copy(out=Sbf[:, :, :], in_=ps_x[:, 0:384].rearrange("p (m v) -> p m v", m=6)), reads=[t_ps_x], writes=[t_St])
                def head_cols(h):
                    return slice(h * 64, (h + 1) * 64)

                def seq_mm(name, fn_terms, reads, dst_banks):
                    def f(e):
                        for h in range(12):
                            bank, hc = (dst_banks[0], h) if h < 8 else (dst_banks[1], h - 8)
                            terms = fn_terms(h)
                            for ti, (lT, r_) in enumerate(terms):
                                ins = e.matmul(psq[bank][0:C, hc * 64:(hc + 1) * 64], lhsT=lT, rhs=r_, start=(ti == 0), stop=(ti == len(terms) - 1))
                        return ins
                    P.op("pe", f, reads=reads, writes=[t_psq[dst_banks[0]], t_psq[dst_banks[1]]])

                def hsl(h):
                    p, hf = h // 2, h % 2
                    return slice(64 * hf, 64 * hf + 64), p

                def evac768(dst, banks, tdst, as_f32=False):
                    copy_op("act", dst[0:C, 0:512], psq[banks[0]][0:C, 0:512], [t_psq[banks[0]]], [tdst])
                    copy_op("dve", dst[0:C, 512:768], psq[banks[1]][0:C, 0:256], [t_psq[banks[1]]], [tdst])

                seq_mm("Z", lambda h: [(AT2[:, h, 0:C], Sbf[:, h // 2, :]), (Aak[0:C, h, 0:C], Vt[0:C, head_cols(h)])],
                       [t_o["AT"], t_St, t_Aak, t_Vt], (0, 1))
                evac768(Zb, (0, 1), t_Zb)
                seq_mm("U", lambda h: [(Qf[0:C, h, 0:C], Zb[0:C, head_cols(h)])], [t_Qf, t_Zb], (2, 3))
                evac768(Ub, (2, 3), t_Ub)
                seq_mm("Y", lambda h: [(RT2[:, h, 0:C], Sbf[:, h // 2, :]), (Ark[0:C, h, 0:C], Vt[0:C, head_cols(h)]),
                                       (Arb[0:C, h, 0:C], Ub[0:C, head_cols(h)])],
                       [t_o["RT"], t_St, t_Ark, t_Vt, t_Arb, t_Ub], (0, 1))
                evac768(Yf, (0, 1), t_Yf)

                def snew(e):
                    for h in range(12):
                        psl, p = hsl(h)
                        e.matmul(ps_x[psl, p * 64:(p + 1) * 64], lhsT=Kh[0:C, head_cols(h)], rhs=Vt[0:C, head_cols(h)], start=True, stop=False)
                        ins = e.matmul(ps_x[psl, p * 64:(p + 1) * 64], lhsT=Bh[0:C, head_cols(h)], rhs=Ub[0:C, head_cols(h)], start=False, stop=True)
                    return ins
                P.op("pe", snew, reads=[t_Kh, t_Vt, t_Bh, t_Ub], writes=[t_ps_x])
                tt("dve", St[:, :, :], St[:, :, :], wc[:, 0:6].unsqueeze(2).to_broadcast([128, 6, 64]), ALU.mult, [t_St, t_wc], [t_St])
                tt("dve", St[:, :, :], St[:, :, :], ps_x[:, 0:384].rearrange("p (m v) -> p m v", m=6), ALU.add, [t_St, t_ps_x], [t_St])
                P.op("dve", lambda e: e.tensor_copy(out=Sbf[:, :, :], in_=St[:, :, :]), reads=[t_St], writes=[t_St])
                if last:
                    def trs2(e):
                        for p in range(6):
                            ins = e.transpose(out=ps_x[0:64, p * 128:(p + 1) * 128] if False else ps_mm[0][0:64, p * 64:(p + 1) * 64], in_=St[:, p, :], identity=identf[:, :])
                        return ins
                    def trs3(e):
                        for p in range(6):
                            bank = ps_mm[0] if p < 4 else ps_mm[1]
                            pc = p if p < 4 else p - 4
                            ins = e.transpose(out=bank[0:64, pc * 128:(pc + 1) * 128], in_=St[:, p, :], identity=identf[:, :])
                        return ins
                    P.op("pe", trs3, reads=[t_St, t_ident], writes=[t_ps_mm[0], t_ps_mm[1]])
                    P.op("act", lambda e: e.copy(out=svst[0:64, 0:8, :].rearrange("v h k -> v (h k)"), in_=ps_mm[0][0:64, 0:512]), reads=[t_ps_mm[0]], writes=[t_svst, t_Zb, t_Ub])
                    P.op("act", lambda e: e.copy(out=svst[0:64, 8:12, :].rearrange("v h k -> v (h k)"), in_=ps_mm[1][0:64, 0:256]), reads=[t_ps_mm[1]], writes=[t_svst, t_Zb, t_Ub])
                    dsto = (o_wkvp if mode == "p" else o_wkvs[idx]).rearrange("h v k -> v h k")
                    P.op("sp", lambda e: e.dma_start(out=dsto, in_=svst[0:64, :, :]), reads=[t_svst, t_Zb, t_Ub], dma=True, sem_tk=t_svst)
                Y3 = Yf[0:C, :].rearrange("p (h d) -> p h d", h=12)
                sqv = SF[0:C, :, :].rearrange("p m j -> p (m j)").rearrange("p (h d) -> p h d", h=12)
                P.op("dve", lambda e: e.reduce_sum(out=gsm[0:C, 0:12], in_=Y3, axis=AX.X), reads=[t_Yf], writes=[t_gsm])
                P.op("act", lambda e: e.activation(out=sqv, in_=Y3, func=AF.Square), reads=[t_Yf, t_o["RKT"]], writes=[t_S["F"]])
                P.op("dve", lambda e: e.reduce_sum(out=gsm[0:C, 12:24], in_=sqv, axis=AX.X), reads=[t_S["F"]], writes=[t_gsm])
                P.op("dve", lambda e: e.tensor_scalar(out=gsm[0:C, 24:36], in0=gsm[0:C, 0:12], scalar1=1.0 / 64, scalar2=None, op0=ALU.mult),
                     reads=[t_gsm], writes=[t_gsm])
                tt("dve", gsm[0:C, 36:48], gsm[0:C, 24:36], gsm[0:C, 24:36], ALU.mult, [t_gsm], [t_gsm])
                P.op("dve", lambda e: e.scalar_tensor_tensor(out=gsm[0:C, 48:60], in0=gsm[0:C, 12:24], scalar=1.0 / 64, in1=gsm[0:C, 36:48],
                                                             op0=ALU.mult, op1=ALU.subtract), reads=[t_gsm], writes=[t_gsm])
                P.op("dve", lambda e: e.tensor_scalar(out=gsm[0:C, 48:60], in0=gsm[0:C, 48:60], scalar1=64e-5, scalar2=None, op0=ALU.add),
                     reads=[t_gsm], writes=[t_gsm])
                P.op("act", lambda e: e.activation(out=gsm[0:C, 60:72], in_=gsm[0:C, 48:60], func=AF.Sqrt), reads=[t_gsm], writes=[t_gsm])
                P.op("dve", lambda e: e.reciprocal(out=gsm[0:C, 72:84], in_=gsm[0:C, 60:72]), reads=[t_gsm], writes=[t_gsm])
                hb3 = hbuf1[0:C, 0:768].rearrange("p (h d) -> p h d", h=12)
                tt("dve", hb3, Y3, gsm[0:C, 24:36].unsqueeze(2).to_broadcast([C, 12, 64]), ALU.subtract, [t_Yf, t_gsm], [t_hbuf1])
                tt("dve", hb3, hb3, gsm[0:C, 72:84].unsqueeze(2).to_broadcast([C, 12, 64]), ALU.mult, [t_hbuf1, t_gsm], [t_hbuf1])
                tt("dve", hbuf1[0:C, 0:768], hbuf1[0:C, 0:768], lnw_b[0:C, :], ALU.mult, [t_hbuf1, t_ln], [t_hbuf1])
                tt("dve", hbuf1[0:C, 0:768], hbuf1[0:C, 0:768], lnb_b[0:C, :], ALU.add, [t_hbuf1, t_ln], [t_hbuf1])

                def bon(e):
                    for p in range(6):
                        ins = e.matmul(ps_x[0:C, 400 + 2 * p:402 + 2 * p], lhsT=RKT[:, p, 0:C], rhs=hsel[:, 0:2], start=True, stop=True)
                    return ins
                P.op("pe", bon, reads=[t_o["RKT"], t_cst1], writes=[t_ps_x])
                P.op("act", lambda e: e.copy(out=gsm[0:C, 84:96], in_=ps_x[0:C, 400:412]), reads=[t_ps_x], writes=[t_gsm])
                tt("dve", sqv, Vt[0:C, :].rearrange("p (h d) -> p h d", h=12), gsm[0:C, 84:96].unsqueeze(2).to_broadcast([C, 12, 64]), ALU.mult,
                   [t_Vt, t_gsm, t_S["F"]], [t_S["F"]])
                tt("dve", hb3, hb3, sqv, ALU.add, [t_hbuf1, t_S["F"]], [t_hbuf1])
                io2 = nxt("o")
                if mode == "p":
                    cross_attn((lambda psl, pr, blk: KmT[psl, 1, pr, blk * 128:(blk + 1) * 128]), t_KmT[1],
                               (lambda blk, h: Vm[:, 1, blk, h, 0:65]), t_Vm[1],
                               (lambda psl, pr: QmTc[psl, pr, 0:C]), [t_QmTc], C, ps_o[io2], t_ps_o[io2])
                else:
                    barrier()
                    P.op("pool", lambda e: e.memset(cV1[:, :, :, 64:65], 1.0), writes=[t_cV1])
                    load_piece(c_mem[1, idx].rearrange("(b j) c -> j b c", b=2), 2)
                    cross_attn((lambda psl, pr, blk: cKT1[psl, 2 * blk + pr, :]), t_cKT1, (lambda blk, h: cV1[:, blk, h, 0:65]), t_cV1,
                               (lambda psl, pr: QmTc[psl, pr, 0:C]), [t_QmTc], C, ps_o[io2], t_ps_o[io2])
                normalize_o(ps_o[io2], t_ps_o[io2], C, hbuf1[0:C, 768:1024], t_hbuf1, t_rcp1, rcp1)
                col0 = 0 if mode == "p" else 4 * idx
                post_a(C, hbuf1[0:C, :], t_hbuf1, gt[0:C, :], t_gt, 8, hb1, t_hb1, hT1, t_hT1, col0)
                if mode == "p":
                    def xsrc(ys_):
                        P.op("sp", lambda e: e.dma_start(out=xres1[:, :], in_=x1_d[idx * 128:(idx + 1) * 128, :]), reads=[t_x1[idx]], writes=[t_xres1], dma=True)

                    def xdst(ys_):
                        P.op("sp", lambda e: e.dma_start(out=y_p[idx * 128:(idx + 1) * 128, :], in_=xres1[:, :]), reads=[t_xres1], dma=True)
                    post_b(G_POST[1], 128, 8, hT1, t_hT1, Wo, t_Wo, 0, xsrc, xdst)
                else:
                    barrier()

            keep1 = apos[0]
            apos[0] = outs_w0
            cst1 = carve_f(1024).rearrange("p (t n) -> p t n", t=2)
            ckb1 = carve_bf(2 * 256).rearrange("p (t n) -> p t n", t=2)
            cKT1 = carve_bf(4 * 128).rearrange("p (i n) -> p i n", i=4)
            cV1 = carve_bf(2 * 4 * 80).rearrange("p (t h d) -> p t h d", t=2, h=4)
            t_cst1b, t_ckb1, t_cKT1, t_cV1 = Tk(), Tk(), Tk(), Tk()
            apos[0] = keep1
            CTX.update(PT=PT1, t_PT=t_PT1, ytmp=[hbuf1] * 2, t_ytmp=[t_hbuf1] * 2, xres=[xres1] * 2, t_xres=[t_xres1] * 2,
                       cst=cst1, ckb=ckb1, cKT=cKT1, cV=cV1, t_cst=t_cst1b, t_ckb=t_ckb1, t_cKT=t_cKT1, t_cV=t_cV1)
            print("L1 arena words (final)", apos[0])

            def prompt_chunk(c):
                P.op("sp", lambda e: e.dma_start(out=xld1[:, :], in_=x1_d[c * 128:(c + 1) * 128, :]), reads=[t_x1[c]], writes=[t_xld1], dma=True)
                rmsnorm(xld1[:, :], 128, G_PRE[1], xnb1[:, :], t_xld1, t_xnb1)

                def tr(e):
                    for k in range(8):
                        ins = e.transpose(out=ps_t[:, k * 128:(k + 1) * 128], in_=xnb1[:, k * 128:(k + 1) * 128], identity=ident[:, :])
                    return ins
                P.op("pe", tr, reads=[t_xnb1, t_ident], writes=[t_ps_t])
                copy_op(ev_eng(), xnTc[:, :, :], ps_t[:].rearrange("p (k m) -> p k m", k=8), [t_ps_t], [t_xnTc])
                if c == 0:
                    P.op("pool", lambda e: e.memset(cols[:, :, 0:1], 0.0), writes=[t_cols])
                rwkv_chunk(128, (lambda k: xnTc[:, k, :]), t_xnTc, "p", c)
            for c in range(NT if "l1few" not in DBG else 2):
                prompt_chunk(c)

            def sample_l1():
                P.op("sp", lambda e: e.dma_start(out=xld1[0:16, :], in_=x1s_d), reads=[t_x1[NT]], writes=[t_xld1], dma=True)
                rmsnorm(xld1[0:16, :], 16, G_PRE[1], xnb1[0:16, :], t_xld1, t_xnb1)

                def tr(e):
                    for k in range(8):
                        ins = e.transpose(out=ps_t[:, k * 128:k * 128 + 16], in_=xnb1[0:16, k * 128:(k + 1) * 128], identity=ident[0:16, 0:16])
                    return ins
                P.op("pe", tr, reads=[t_xnb1, t_ident], writes=[t_ps_t])
                copy_op(ev_eng(), xnTs1[:, :, :], ps_t[:].rearrange("p (k m) -> p k m", k=8)[:, :, 0:16], [t_ps_t], [t_xnTs1])
                for bb in range(4):
                    P.op("sp", (lambda e, bb=bb: e.dma_start(out=cols[:, :, 0], in_=s_shift[bb:bb + 1, :].rearrange("o (m p) -> p (o m)", p=128),
                                                             allow_slow_non_contiguous=True)), writes=[t_cols], dma=True)
                    rwkv_chunk(4, (lambda k, bb=bb: xnTs1[:, k, 4 * bb:4 * bb + 4]), t_xnTs1, "s", bb)

                def xsrc_s(ys_):
                    P.op("sp", lambda e: e.dma_start(out=xres1[0:16, :], in_=x1s_d), reads=[t_x1[NT]], writes=[t_xres1], dma=True)

                def xdst_s(ys_):
                    P.op("sp", lambda e: e.dma_start(out=y_s, in_=xres1[0:16, :]), reads=[t_xres1], dma=True)
                post_b(G_POST[1], 16, 8, hT1, t_hT1, Wo, t_Wo, 0, xsrc_s, xdst_s)
            if "nosample1" not in DBG:
                sample_l1()

        print('total_ops', len(P.ops))
        if 'dump2' in DBG:
            for _i, _o in enumerate(P.ops):
                print('OP', _i, _o.eng, _o.line, 'dma' if _o.is_dma else '')
        P.finalize_and_emit(st)
    return nc


def layer0(env):
    pass


_CACHE = {}


def kernel(x_prompt, x_sample, mem_prompt, cache_mem_kv, cache_win0, cache_win1, cache_win2, state_wkv,
           state_shift, norm_pre, norm_post, norm_mem, w_mem_kv, rel_bias, w_in_a, w_out_a, w_in_b, w_out_b,
           rwkv_mu, rwkv_w0, rwkv_w_up, rwkv_a0, rwkv_a_up, rwkv_k_k, rwkv_k_a, rwkv_r_k, rwkv_ln_w, rwkv_ln_b):
    f = lambda a: np.ascontiguousarray(np.asarray(a, dtype=np.float32))
    if "nc" not in _CACHE:
        _CACHE["nc"] = build_program()
    nc = _CACHE["nc"]
    oh = _onehot_const()
    in_maps = []
    for c in range(NCORES):
        sl = slice(4 * c, 4 * c + 4)
        in_maps.append({
            "x_p": f(x_prompt[c]),
            "x_s": f(x_sample[sl]).reshape(16, D),
            "mem_p": f(mem_prompt[c]),
            "c_mem": f(cache_mem_kv[:, sl]).reshape(2, 4, 256, 512),
            "c_w0": f(cache_win0[0, sl]).reshape(4, 128, 512),
            "c_w1": f(cache_win1[0, sl]).reshape(4, 512, 512),
            "c_w2": f(cache_win2[0, sl]).reshape(4, 2048, 512),
            "s_wkv": f(state_wkv[0, sl]),
            "s_shift": f(state_shift[0, sl]),
            "norm_pre": f(norm_pre), "norm_post": f(norm_post), "norm_mem": f(norm_mem),
            "w_mem": f(w_mem_kv), "rel_bias": f(rel_bias),
            "w_in_a": f(w_in_a[0]), "w_out_a": f(w_out_a[0]), "w_in_b": f(w_in_b[0]), "w_out_b": f(w_out_b[0]),
            "c_onehot": oh,
            "r_mu": f(rwkv_mu).reshape(1, C_SHIFT), "r_w0": f(rwkv_w0).reshape(1, 768), "r_wup": f(rwkv_w_up).reshape(64, 768),
            "r_a0": f(rwkv_a0).reshape(1, 768), "r_aup": f(rwkv_a_up).reshape(64, 768), "r_kk": f(rwkv_k_k).reshape(1, 768),
            "r_ka": f(rwkv_k_a).reshape(1, 768), "r_rk": f(rwkv_r_k).reshape(1, 768), "r_lnw": f(rwkv_ln_w).reshape(1, 768),
            "r_lnb": f(rwkv_ln_b).reshape(1, 768),
        })
    res = run_bass_kernel_spmd(nc, in_maps, core_ids=list(range(NCORES)))
    R = res.results
    cat = lambda k: np.stack([np.asarray(R[c][k]) for c in range(NCORES)])
    y_prompt = cat("y_p")
    y_sample = cat("y_s").reshape(32, 4, D)
    new_mem = cat("o_mem").transpose(1, 0, 2, 3).reshape(2, 8, 256, 2, 4, 64)
    w0p = cat("o_w0p").reshape(1, 8, 128, 2, 4, 64)
    w1p = cat("o_w1p").reshape(1, 8, 512, 2, 4, 64)
    w2p = cat("o_w2p").reshape(1, 8, 2048, 2, 4, 64)
    w0s = cat("o_w0s").reshape(1, 32, 4, 2, 4, 64)
    w1s = cat("o_w1s").reshape(1, 32, 4, 2, 4, 64)
    w2s = cat("o_w2s").reshape(1, 32, 4, 2, 4, 64)
    wkvp = cat("o_wkvp").reshape(1, 8, 12, 64, 64)
    wkvs = cat("o_wkvs").reshape(1, 32, 12, 64, 64)
    shp = cat("o_shp").reshape(1, 8, C_SHIFT)
    shs = cat("o_shs").reshape(1, 32, C_SHIFT)
    outs = (y_prompt, y_sample, new_mem, w0p, w1p, w2p, w0s, w1s, w2s, wkvp, wkvs, shp, shs)
    return tuple(np.ascontiguousarray(o.astype(np.float32)) for o in outs)
```

```python
import math
from contextlib import ExitStack
import numpy as np
import concourse.bass as bass
import concourse.mybir as mybir
from concourse.bass_utils import run_bass_kernel_spmd

F32 = mybir.dt.float32
BF16 = mybir.dt.bfloat16
AF = mybir.ActivationFunctionType
ALU = mybir.AluOpType
AX = mybir.AxisListType

ENGS = ("pe", "act", "dve", "pool", "sp")
NCORES = 8
T = 2048
D = 1024
NT = 16
NEG = -30000.0
SCALE = 0.125
DIL = (1, 4, 16)
RMS_EPS = 1e-6
C_SHIFT = 2432
A_IN = 3072
B_IN = 3712


class Tk:
    __slots__ = ("name", "lw", "rd", "dsem", "dcount", "excl")

    def __init__(self, name="", excl=False):
        self.name = name
        self.excl = excl
        self.lw = None
        self.rd = []
        self.dsem = None
        self.dcount = 0


class Op:
    __slots__ = ("eng", "fn", "deps", "is_dma", "pos", "signal", "cnt", "sem_tk", "waits", "line")


class Prog:
    def __init__(self, nc):
        self.nc = nc
        self.ops = []
        self.eng_ops = {e: [] for e in ENGS}
        self.last_dma = {}

    def op(self, eng, fn, reads=(), writes=(), dma=False, sem_tk=None, extra_deps=()):
        if len(self.ops) >= getattr(self, "maxops", 10 ** 9):
            return None
        o = Op()
        import sys as _sys
        o.line = _sys._getframe(1).f_lineno
        o.eng = eng
        o.fn = fn
        o.is_dma = dma
        o.signal = dma
        o.cnt = 0
        o.waits = []
        deps = []
        for r in reads:
            if r.lw is not None:
                deps.append((r.lw, "raw"))
            if r.excl:
                for rr in r.rd:
                    deps.append((rr, "war"))
        for w in writes:
            if w.lw is not None:
                deps.append((w.lw, "waw"))
            for rr in w.rd:
                deps.append((rr, "war"))
        for xd in extra_deps:
            deps.append((xd, "raw"))
        fdeps = []
        seen = set()
        for d, kind in deps:
            if d is o:
                continue
            if (not d.is_dma) and d.eng == eng and (not dma):
                if eng == "pe" or kind != "raw":
                    continue
            if id(d) in seen:
                continue
            seen.add(id(d))
            fdeps.append(d)
        o.deps = fdeps
        for r in (reads if fn is not None else ()):
            if r.excl:
                r.rd = [o]
            else:
                r.rd.append(o)
        for w in (writes if fn is not None else ()):
            w.lw = o
            w.rd = []
        if dma:
            if sem_tk is None:
                sem_tk = (list(writes) + list(reads))[0]
            o.sem_tk = sem_tk
            sem_tk.dcount += 1
            o.cnt = sem_tk.dcount * 16
            self.last_dma[id(sem_tk)] = o
        else:
            o.sem_tk = None
        o.pos = len(self.eng_ops[eng])
        self.eng_ops[eng].append(o)
        self.ops.append(o)
        return o

    def finalize_and_emit(self, stack):
        nc = self.nc
        waited_pos = {e: {p: -1 for p in ENGS} for e in ENGS}
        waited_dma = {e: {} for e in ENGS}
        for o in self.ops:
            e = o.eng
            for d in o.deps:
                if d.is_dma:
                    key = id(d.sem_tk)
                    if waited_dma[e].get(key, 0) >= d.cnt:
                        continue
                    waited_dma[e][key] = d.cnt
                    o.waits.append(d)
                else:
                    if waited_pos[e][d.eng] >= d.pos:
                        continue
                    waited_pos[e][d.eng] = d.pos
                    d.signal = True
                    o.waits.append(d)
        for e in ENGS:
            c = 0
            for o in self.eng_ops[e]:
                if not o.is_dma and o.signal:
                    c += 1
                    o.cnt = c
        esem = {e: stack.enter_context(nc.semaphore("es_" + e)) for e in ENGS}
        nd = 0
        for o in self.ops:
            if o.is_dma and o.sem_tk.dsem is None:
                o.sem_tk.dsem = stack.enter_context(nc.semaphore("ds%d" % nd))
                nd += 1
        self.n_dma_sems = nd
        block = stack.enter_context(nc.Block())

        def emit(engobj, elist):
            for o in elist:
                for d in o.waits:
                    if d.is_dma:
                        engobj.wait_ge(d.sem_tk.dsem, d.cnt)
                    else:
                        engobj.wait_ge(esem[d.eng], d.cnt)
                if o.fn is None:
                    continue
                ins = o.fn(engobj)
                if o.is_dma:
                    ins.then_inc(o.sem_tk.dsem, 16)
                elif o.signal:
                    ins.then_inc(esem[o.eng], 1)

        @block.tensor
        def _(pe):
            emit(pe, self.eng_ops["pe"])

        @block.scalar
        def _(act):
            emit(act, self.eng_ops["act"])

        @block.vector
        def _(dve):
            emit(dve, self.eng_ops["dve"])

        @block.gpsimd
        def _(pool):
            emit(pool, self.eng_ops["pool"])

        @block.sync
        def _(sp):
            emit(sp, self.eng_ops["sp"])
            done = set()
            for o in self.ops:
                if o.is_dma and id(o.sem_tk) not in done:
                    done.add(id(o.sem_tk))
                    sp.wait_ge(o.sem_tk.dsem, 16 * o.sem_tk.dcount)


def _t5_bucket_np(dist):
    dist = np.asarray(dist, dtype=np.int32)
    d = np.maximum(dist, 1).astype(np.float32)
    large = 16 + (np.log(d / np.float32(16.0)) / np.float32(math.log(2048 / 16)) * np.float32(16.0)).astype(np.int32)
    large = np.minimum(large, 31)
    return np.where(dist < 16, dist, large)


def _onehot_const():
    oh = np.zeros((33, 3, 384), np.float32)
    for g in range(3):
        for u in range(383):
            dist = u - 127
            if 0 <= dist <= 128:
                b = int(_t5_bucket_np(DIL[g] * dist))
                oh[b, g, u] = 1.0
            else:
                oh[32, g, u] = NEG
        oh[32, g, 383] = NEG
    return oh.reshape(33, 3 * 384)


STAGE = 2
DBG = set()


def build_program():
    nc = bass.Bass("TRN2", target_bir_lowering=False)
    try:
        nc.allow_low_precision("bf16 matmul operands with fp32 accumulation (matches problem tolerance)")
    except Exception:
        pass
    try:
        nc.allow_non_contiguous_dma("strided window / toeplitz accesses")
    except Exception:
        pass
    P = Prog(nc)
    for _d in DBG:
        if _d.startswith('maxops='):
            P.maxops = int(_d.split('=')[1])

    def din(name, shape):
        return nc.dram_tensor(name, list(shape), F32, kind="ExternalInput").ap()

    def dout(name, shape):
        return nc.dram_tensor(name, list(shape), F32, kind="ExternalOutput").ap()

    x_p = din("x_p", [T, D])
    x_s = din("x_s", [16, D])
    mem_p = din("mem_p", [256, D])
    c_mem = din("c_mem", [2, 4, 256, 512])
    c_w0 = din("c_w0", [4, 128, 512])
    c_w1 = din("c_w1", [4, 512, 512])
    c_w2 = din("c_w2", [4, 2048, 512])
    s_wkv = din("s_wkv", [4, 12, 64, 64])
    s_shift = din("s_shift", [4, C_SHIFT])
    norm_pre = din("norm_pre", [2, D])
    norm_post = din("norm_post", [2, D])
    norm_mem = din("norm_mem", [2, D])
    w_mem = din("w_mem", [2, D, 512])
    rel_bias = din("rel_bias", [32, 12])
    w_in_a = din("w_in_a", [D, A_IN])
    w_out_a = din("w_out_a", [512, D])
    w_in_b = din("w_in_b", [D, B_IN])
    w_out_b = din("w_out_b", [D, D])
    onehot = din("c_onehot", [33, 3 * 384])
    r_mu = din("r_mu", [1, C_SHIFT])
    r_w0 = din("r_w0", [1, 768])
    r_wup = din("r_wup", [64, 768])
    r_a0 = din("r_a0", [1, 768])
    r_aup = din("r_aup", [64, 768])
    r_kk = din("r_kk", [1, 768])
    r_ka = din("r_ka", [1, 768])
    r_rk = din("r_rk", [1, 768])
    r_lnw = din("r_lnw", [1, 768])
    r_lnb = din("r_lnb", [1, 768])
    y_p = dout("y_p", [T, D])
    y_s = dout("y_s", [16, D])
    o_mem = dout("o_mem", [2, 256, 512])
    o_w0p = dout("o_w0p", [128, 512])
    o_w1p = dout("o_w1p", [512, 512])
    o_w2p = dout("o_w2p", [2048, 512])
    o_w0s = dout("o_w0s", [16, 512])
    o_w1s = dout("o_w1s", [16, 512])
    o_w2s = dout("o_w2s", [16, 512])
    o_wkvp = dout("o_wkvp", [12, 64, 64])
    o_wkvs = dout("o_wkvs", [4, 12, 64, 64])
    o_shp = dout("o_shp", [1, C_SHIFT])
    o_shs = dout("o_shs", [4, C_SHIFT])
    x1_d = nc.dram_tensor("x1_d", [T, D], F32, kind="Internal").ap()
    x1s_d = nc.dram_tensor("x1s_d", [16, D], F32, kind="Internal").ap()
    e_d = nc.dram_tensor("e_d", [12, 3 * 384], F32, kind="Internal").ap()
    out_tokens = []

    with ExitStack() as st:
        def sb(name, shape, dt):
            return st.enter_context(nc.sbuf_tensor(name, list(shape), dt))

        def psum(name, shape, dt):
            return st.enter_context(nc.psum_tensor(name, list(shape), dt))

        ps_mm = [psum("ps_mm%d" % i, [128, 512], F32) for i in range(2)]
        ps_s = [psum("ps_s%d" % i, [128, 512], F32) for i in range(2)]
        ps_o = [psum("ps_o%d" % i, [128, 512], F32) for i in range(2)]
        ps_t = psum("ps_t", [128, 1024], BF16)
        ps_x = psum("ps_x", [128, 512], F32)
        t_ps_mm = [Tk("ps_mm0", True), Tk("ps_mm1", True)]
        t_ps_s = [Tk("ps_s0", True), Tk("ps_s1", True)]
        t_ps_o = [Tk("ps_o0", True), Tk("ps_o1", True)]
        t_ps_t = Tk("ps_t", True)
        t_ps_x = Tk("ps_x", True)
        rr = {"mm": 0, "s": 0, "o": 0, "ev": 0}

        def nxt(k):
            rr[k] += 1
            return rr[k] & 1

        def ev_eng():
            rr["ev"] += 1
            return "act" if (rr["ev"] & 1) else "dve"

        def copy_op(eng, out, in_, reads, writes):
            if eng == "act":
                P.op("act", lambda e: e.copy(out=out, in_=in_), reads=reads, writes=writes)
            else:
                P.op(eng, lambda e: e.tensor_copy(out=out, in_=in_), reads=reads, writes=writes)

        identf = sb("identf", [128, 128], F32)
        ident = sb("ident", [128, 128], BF16)
        t_ident = Tk()

        P.op("pool", lambda e: e.memset(identf[:], 0.0), writes=[t_ident])
        P.op("pool", lambda e: e.affine_select(out=identf[:], in_=identf[:], pattern=[[-1, 128]], compare_op=ALU.not_equal,
                                               fill=1.0, base=0, channel_multiplier=1), reads=[t_ident], writes=[t_ident])
        P.op("dve", lambda e: e.tensor_copy(out=ident[:], in_=identf[:]), reads=[t_ident], writes=[t_ident])

        ARENA_WORDS = 49000
        arena = sb("arena", [128, ARENA_WORDS], F32)
        gains4 = sb("gains", [128, 2, 1024], F32)
        gmem = arena[:, 0:2048].rearrange("p (a d) -> p a d", a=2)
        barsb = sb("barsb", [128, 16], F32)
        t_bar = {e: Tk() for e in ("pe", "act", "dve", "pool")}
        t_barinit = Tk()
        P.op("pool", lambda e: e.memset(barsb[:, :], 0.0), writes=[t_barinit])

        def barrier():
            dmas = list(P.last_dma.values())
            P.op("pe", lambda e: e.transpose(out=ps_t[0:16, 0:16], in_=ident[0:16, 0:16], identity=ident[0:16, 0:16]),
                 reads=[t_ident], writes=[t_ps_t, t_bar["pe"]], extra_deps=dmas)
            P.op("act", lambda e: e.copy(out=barsb[:, 0:1], in_=barsb[:, 8:9]), reads=[t_barinit], writes=[t_bar["act"]], extra_deps=dmas)
            P.op("dve", lambda e: e.tensor_copy(out=barsb[:, 1:2], in_=barsb[:, 9:10]), reads=[t_barinit], writes=[t_bar["dve"]], extra_deps=dmas)
            P.op("pool", lambda e: e.memset(barsb[:, 2:3], 0.0), writes=[t_bar["pool"]], extra_deps=dmas)
            allb = list(t_bar.values())
            P.op("pe", lambda e: e.transpose(out=ps_t[0:16, 0:16], in_=ident[0:16, 0:16], identity=ident[0:16, 0:16]),
                 reads=[t_ident] + allb, writes=[t_ps_t])
            P.op("act", lambda e: e.copy(out=barsb[:, 3:4], in_=barsb[:, 8:9]), reads=allb)
            P.op("dve", lambda e: e.tensor_copy(out=barsb[:, 4:5], in_=barsb[:, 9:10]), reads=allb)
            P.op("pool", lambda e: e.memset(barsb[:, 5:6], 0.0), reads=allb)
            P.op("sp", None, reads=allb)

        class _G:
            def __getitem__(self, key):
                p, gi, c = key
                return gains4[p, gi, c] if gi < 2 else gmem[p, gi - 4, c]
        gains = _G()
        t_gains = Tk()
        def load_gains(items):
            for (i, src, l) in items:
                ap_b = bass.AP(tensor=src.tensor, offset=l * D, ap=[[0, 128], [1, D]])
                P.op("sp", (lambda e, i=i, ap_b=ap_b: e.dma_start(out=gains[:, i, :], in_=ap_b)), writes=[t_gains], dma=True)
        load_gains([(0, norm_pre, 0), (1, norm_post, 0), (4, norm_mem, 0), (5, norm_mem, 1)])
        G_PRE, G_POST, G_MEM = (0, 0), (1, 1), (4, 5)

        ss = sb("ss", [128, 8], F32)
        junk = sb("junk", [128, 1024], BF16)
        t_junk = Tk()

        def rmsnorm(src_ap, np_, gi, dst_ap, t_src, t_dst, extra_reads=()):
            t_ss = Tk()
            P.op("act", lambda e: e.activation(out=junk[:np_, :], in_=src_ap, func=AF.Square),
                 reads=[t_src], writes=[t_junk])
            P.op("dve", lambda e: e.reduce_sum(out=ss[:np_, 0:1], in_=junk[:np_, :], axis=AX.X),
                 reads=[t_junk], writes=[t_ss])
            P.op("dve", lambda e: e.tensor_scalar(out=ss[:np_, 1:2], in0=ss[:np_, 0:1], scalar1=1.0 / D, scalar2=RMS_EPS,
                                                  op0=ALU.mult, op1=ALU.add), reads=[t_ss], writes=[t_ss])
            P.op("act", lambda e: e.activation(out=ss[:np_, 2:3], in_=ss[:np_, 1:2], func=AF.Sqrt), reads=[t_ss], writes=[t_ss])
            P.op("dve", lambda e: e.reciprocal(out=ss[:np_, 3:4], in_=ss[:np_, 2:3]), reads=[t_ss], writes=[t_ss])
            P.op("dve", lambda e: e.scalar_tensor_tensor(out=dst_ap, in0=src_ap, scalar=ss[:np_, 3:4], in1=gains[:np_, gi, :],
                                                         op0=ALU.mult, op1=ALU.mult),
                 reads=[t_src, t_ss, t_gains] + list(extra_reads), writes=[t_dst])

        apos = [0]
        atop = [ARENA_WORDS]

        def carve_top(nwords):
            atop[0] -= nwords
            assert atop[0] >= apos[0]
            return arena[:, atop[0]:atop[0] + nwords]

        def carve(nbytes):
            w0 = apos[0]
            nw = (nbytes + 3) // 4
            apos[0] += nw
            assert apos[0] <= atop[0], ("arena overflow", apos[0], atop[0])
            return arena[:, w0:w0 + nw]

        def carve_bf(n):
            return carve(2 * n).bitcast(BF16)

        def carve_f(n):
            return carve(4 * n)

        KmT = sb("KmT", [128, 2, 2, 256], BF16)
        Vm = sb("Vm", [128, 2, 2, 4, 80], BF16)
        t_KmT = [Tk(), Tk()]
        t_Vm = [Tk(), Tk()]
        mark = apos[0]
        carve_f(2048)
        memx = carve_f(2 * 1024).rearrange("p (b d) -> p b d", b=2)
        memn = carve_bf(2 * 1024).rearrange("p (b d) -> p b d", b=2)
        memnT = carve_bf(8 * 256).rearrange("p (k m) -> p k m", k=8)
        wm = carve_bf(8 * 512).rearrange("p (k n) -> p k n", k=8)
        kvst = carve_f(2 * 512).rearrange("p (b n) -> p b n", b=2)
        t_memx, t_memn, t_memnT, t_wm, t_kvst = Tk(), Tk(), Tk(), Tk(), [Tk(), Tk()]
        P.op("sp", lambda e: e.dma_start(out=memx, in_=mem_p.rearrange("(b p) d -> p b d", p=128)), writes=[t_memx], dma=True)
        if "novm" not in DBG:
            P.op("pool", lambda e: e.memset(Vm[:, :, :, :, 64:65], 1.0), writes=t_Vm)
        for l in range(0 if "nomem" in DBG else 2):
            P.op("pool", (lambda e, l=l: e.dma_start(out=wm, in_=w_mem[l].rearrange("(k p) n -> p k n", p=128))),
                 writes=[t_wm], dma=True)
            for b in range(2):
                rmsnorm(memx[:, b, :], 128, G_MEM[l], memn[:, b, :], t_memx, t_memn)

                def tr(e, b=b):
                    for k in range(8):
                        ins = e.transpose(out=ps_t[:, k * 128:(k + 1) * 128], in_=memn[:, b, k * 128:(k + 1) * 128], identity=ident[:])
                    return ins
                P.op("pe", tr, reads=[t_memn, t_ident], writes=[t_ps_t])
                copy_op(ev_eng(), memnT[:, :, b * 128:(b + 1) * 128], ps_t[:].rearrange("p (k m) -> p k m", k=8), [t_ps_t], [t_memnT])
            for b in range(2):
                i = nxt("mm")

                def mm(e, b=b, i=i):
                    for k in range(8):
                        ins = e.matmul(ps_mm[i][:, :], lhsT=memnT[:, k, b * 128:(b + 1) * 128], rhs=wm[:, k, :], start=(k == 0), stop=(k == 7))
                    return ins
                P.op("pe", mm, reads=[t_memnT, t_wm], writes=[t_ps_mm[i]])
                P.op("act", (lambda e, b=b, i=i: e.copy(out=kvst[:, b, :], in_=ps_mm[i][:, :])), reads=[t_ps_mm[i]], writes=[t_kvst[b]])
                P.op("dve", (lambda e, b=b, i=i, l=l: e.tensor_copy(out=Vm[:, l, b, :, 0:64],
                                                                    in_=ps_mm[i][:, 256:512].rearrange("p (h d) -> p h d", h=4))),
                     reads=[t_ps_mm[i]], writes=[t_Vm[l]])
                P.op("sp", (lambda e, b=b, l=l: e.dma_start(out=o_mem[l, b * 128:(b + 1) * 128, :], in_=kvst[:, b, :])),
                     reads=[t_kvst[b]], dma=True)
                out_tokens.append(t_kvst[b])
            for pr in range(2):
                i = nxt("mm")

                def mmk(e, pr=pr, i=i):
                    for k in range(8):
                        ins = e.matmul(ps_mm[i][:, 0:256], lhsT=wm[:, k, pr * 128:(pr + 1) * 128], rhs=memnT[:, k, :], start=(k == 0), stop=(k == 7))
                    return ins
                P.op("pe", mmk, reads=[t_memnT, t_wm], writes=[t_ps_mm[i]])
                copy_op(ev_eng(), KmT[:, l, pr, :], ps_mm[i][:, 0:256], [t_ps_mm[i]], [t_KmT[l]])


        biasT = carve_top(12 * 2 * 128).rearrange("p (a b q) -> p a b q", a=12, b=2)
        t_biasT = Tk()
        M1e = carve_top(264).bitcast(BF16)
        M1o = carve_top(264).bitcast(BF16)
        M2e = carve_top(1032).bitcast(BF16)
        M2o = carve_top(1032).bitcast(BF16)
        t_M = Tk()
        mark = apos[0]
        rb33 = carve_f(12)
        oh33 = carve_f(1152)
        esb = carve_f(1152)
        mtmp = carve_f(2064)
        t_rb, t_oh, t_esb, t_mtmp, t_ed = Tk(), Tk(), Tk(), Tk(), Tk()
        P.op("pool", lambda e: e.memset(rb33[32:33, :], 1.0), writes=[t_rb])
        P.op("sp", lambda e: e.dma_start(out=rb33[0:32, :], in_=rel_bias), writes=[t_rb], dma=True)
        P.op("sp", lambda e: e.dma_start(out=oh33[0:33, :], in_=onehot), writes=[t_oh], dma=True)
        for g in range(3):
            P.op("pe", (lambda e, g=g: e.matmul(ps_x[0:12, 0:384], lhsT=rb33[0:33, :], rhs=oh33[0:33, g * 384:(g + 1) * 384], start=True, stop=True)),
                 reads=[t_rb, t_oh], writes=[t_ps_x])
            P.op("act", (lambda e, g=g: e.copy(out=esb[0:12, g * 384:(g + 1) * 384], in_=ps_x[0:12, 0:384])), reads=[t_ps_x], writes=[t_esb])
        P.op("sp", lambda e: e.dma_start(out=e_d, in_=esb[0:12, :]), reads=[t_esb], writes=[t_ed], dma=True, sem_tk=t_esb)
        bstage = carve_f(24 * 128).rearrange("p (a q) -> p a q", a=24)
        Jrev = carve_f(128)
        t_bst, t_J = Tk(), Tk()

        P.op("pool", lambda e: e.memset(Jrev[:, :], 0.0), writes=[t_J])
        P.op("pool", lambda e: e.affine_select(out=Jrev[:, :], in_=Jrev[:, :], pattern=[[1, 128]], compare_op=ALU.not_equal, fill=1.0, base=-127,
                                               channel_multiplier=1), reads=[t_J], writes=[t_J])
        for g in range(3):
            for h in range(4):
                for blk in range(2):
                    off = (4 * g + h) * 1152 + g * 384 + (128 if blk == 0 else 0)
                    src = bass.AP(tensor=e_d.tensor, offset=off, ap=[[1, 128], [1, 128]])
                    P.op("sp", (lambda e, g=g, h=h, blk=blk, src=src: e.dma_start(out=bstage[:, (g * 4 + h) * 2 + blk, :], in_=src)),
                         reads=[t_ed], writes=[t_bst], dma=True)
        for a4 in range(6):
            i = nxt("mm")
            P.op("pe", (lambda e, a4=a4, i=i: e.matmul(ps_mm[i][:, :], lhsT=Jrev[:, :], rhs=bstage[:, 4 * a4:4 * a4 + 4, :].rearrange("p a q -> p (a q)"),
                                                         start=True, stop=True)), reads=[t_J, t_bst], writes=[t_ps_mm[i]])
            copy_op(ev_eng(), biasT.rearrange("p a b q -> p (a b q)")[:, 512 * a4:512 * (a4 + 1)], ps_mm[i][:, :], [t_ps_mm[i]], [t_biasT])
        for (Mt, ncol, base, cm) in ((M1e, 528, 16, 4), (M1o, 528, 15, 4), (M2e, 2064, 16, 16), (M2o, 2064, 15, 16)):
            P.op("pool", (lambda e, ncol=ncol: e.memset(mtmp[:, 0:ncol], 0.0)), writes=[t_mtmp])
            P.op("pool", (lambda e, ncol=ncol, base=base, cm=cm: e.affine_select(out=mtmp[:, 0:ncol], in_=mtmp[:, 0:ncol], pattern=[[-1, ncol]],
                                                                                compare_op=ALU.not_equal, fill=1.0, base=base, channel_multiplier=cm)),
                 reads=[t_mtmp], writes=[t_mtmp])
            P.op("dve", (lambda e, Mt=Mt, ncol=ncol: e.tensor_copy(out=Mt[:, :], in_=mtmp[:, 0:ncol])), reads=[t_mtmp], writes=[t_M])

        def perm_lhsT(g, r, Tq):
            if g == 1:
                Tp = Tq % 4
                if r % 2 == 0:
                    s0 = 16 + 128 * Tp - r
                    return M1e[:, s0:s0 + 128]
                s0 = 15 + 128 * Tp - r
                return M1o[:, s0:s0 + 128]
            if r % 2 == 0:
                s0 = 16 + 128 * Tq - r
                return M2e[:, s0:s0 + 128]
            s0 = 15 + 128 * Tq - r
            return M2o[:, s0:s0 + 128]

        barrier()
        apos[0] = 0
        L0 = apos[0]
        xnT = carve_bf(8 * 2048).rearrange("p (k t) -> p k t", k=8)
        xnTs = carve_bf(8 * 16).rearrange("p (k t) -> p k t", k=8)
        NW = 4
        wsl = [carve_bf(8 * 256).rearrange("p (k n) -> p k n", k=8) for _ in range(NW)]
        t_wsl = [Tk() for _ in range(NW)]
        xld = [carve_f(1024) for _ in range(2)]
        t_xld = [Tk(), Tk()]
        xnb = [carve_bf(1024) for _ in range(2)]
        t_xnb = [Tk(), Tk()]
        t_xnT = [Tk() for _ in range(NT)]
        t_xnTs = Tk()
        QT = carve_bf(2 * 2048).rearrange("p (m t) -> p m t", m=2)
        KT = carve_bf(2 * 2048).rearrange("p (m t) -> p m t", m=2)
        QmT = carve_bf(2 * 2048).rearrange("p (m t) -> p m t", m=2)
        t_QT = [[Tk() for _ in range(4)] for _ in range(2)]
        t_KT = [[Tk() for _ in range(4)] for _ in range(2)]
        t_QmT = [[Tk() for _ in range(4)] for _ in range(2)]
        QTs = carve_bf(8 * 16).rearrange("p (m t) -> p m t", m=8)
        KTs = carve_bf(6 * 16).rearrange("p (m t) -> p m t", m=6)
        t_QTs, t_KTs = Tk(), Tk()
        gate = carve_bf(16 * 512).rearrange("p (i n) -> p i n", i=16)
        t_gate = [Tk() for _ in range(NT)]
        gate_s = carve_bf(4 * 512).rearrange("p (b n) -> p b n", b=4)
        t_gate_s = Tk()
        Vg = carve_bf(16 * 4 * 80).rearrange("p (i h d) -> p i h d", i=16, h=4)
        t_Vg = [Tk() for _ in range(NT)]
        Vns = carve_bf(4 * 3 * 4 * 80).rearrange("p (b g h d) -> p b g h d", b=4, g=3, h=4)
        t_Vns = Tk()
        Og = carve_bf(48 * 260).rearrange("p (i n) -> p i n", i=48)
        t_Og = [Tk() for _ in range(48)]
        wst = [carve_f(512) for _ in range(2)]
        t_wst = [Tk(), Tk()]
        wins = carve_f(3 * 512).rearrange("p (g n) -> p g n", g=3)
        t_wins = Tk()
        sbf = [carve_f(256) for _ in range(2)]
        t_sbf = [Tk(), Tk()]
        PT = [carve_bf(256) for _ in range(2)]
        t_PT = [Tk(), Tk()]
        rr.update({"w": 0, "sb": 0, "pt": 0, "wst": 0})

        P.op("pool", lambda e: e.memset(Vg[:, :, :, 64:65], 1.0), writes=t_Vg)
        P.op("pool", lambda e: e.memset(Vns[0:4, :, :, :, 64:65], 1.0), writes=[t_Vns])

        def phaseA(src_dram, gi, lname):
            for i in range(NT + 1):
                s_ = i & 1
                np_ = 128 if i < NT else 16
                if i < NT:
                    P.op("sp", (lambda e, i=i, s_=s_: e.dma_start(out=xld[s_][:, :], in_=src_dram[0][i * 128:(i + 1) * 128, :])),
                         writes=[t_xld[s_]], dma=True)
                else:
                    P.op("sp", (lambda e, s_=s_: e.dma_start(out=xld[s_][0:16, :], in_=src_dram[1])), writes=[t_xld[s_]], dma=True)
                rmsnorm(xld[s_][0:np_, :], np_, gi, xnb[s_][0:np_, :], t_xld[s_], t_xnb[s_])

                def tr(e, s_=s_, np_=np_):
                    for k in range(8):
                        ins = e.transpose(out=ps_t[:, k * 128:k * 128 + np_], in_=xnb[s_][0:np_, k * 128:(k + 1) * 128], identity=ident[0:np_, 0:np_])
                    return ins
                P.op("pe", tr, reads=[t_xnb[s_], t_ident], writes=[t_ps_t])
                if i < NT:
                    copy_op(ev_eng(), xnT[:, :, i * 128:(i + 1) * 128], ps_t[:].rearrange("p (k m) -> p k m", k=8), [t_ps_t], [t_xnT[i]])
                else:
                    copy_op(ev_eng(), xnTs[:, :, :], ps_t[:].rearrange("p (k m) -> p k m", k=8)[:, :, 0:16], [t_ps_t], [t_xnTs])

        phaseA((x_p, x_s), G_PRE[0], "l0")

        chunk_cols = []
        for g in range(3):
            chunk_cols += [("q", g, 256 * g), ("k", g, 768 + 256 * g), ("v", g, 1536 + 256 * g)]
        chunk_cols += [("qm", 0, 2304), ("gate", 0, 2560), ("gate", 1, 2816)]
        wstate = {"loaded": 0}

        def load_w(n):
            if n >= len(chunk_cols) or n < wstate["loaded"]:
                return
            assert n == wstate["loaded"]
            wstate["loaded"] += 1
            c0 = chunk_cols[n][2]
            sl_ = n % NW
            P.op("pool", (lambda e, c0=c0, sl_=sl_: e.dma_start(out=wsl[sl_], in_=w_in_a[:, c0:c0 + 256].rearrange("(k p) n -> p k n", p=128))),
                 writes=[t_wsl[sl_]], dma=True)

        def proj_fm(sl_, dst, t_dst, sdst, t_sdst, smb0):
            for mb in range(2):
                for tg in range(4):
                    i = nxt("mm")

                    def mm(e, mb=mb, tg=tg, i=i):
                        for k in range(8):
                            ins = e.matmul(ps_mm[i][:, :], lhsT=wsl[sl_][:, k, mb * 128:(mb + 1) * 128], rhs=xnT[:, k, tg * 512:(tg + 1) * 512],
                                           start=(k == 0), stop=(k == 7))
                        return ins
                    P.op("pe", mm, reads=[t_wsl[sl_]] + t_xnT[4 * tg:4 * tg + 4], writes=[t_ps_mm[i]])
                    copy_op(ev_eng(), dst[:, mb, tg * 512:(tg + 1) * 512], ps_mm[i][:, :], [t_ps_mm[i]], [t_dst[mb][tg]])
                def mms(e, mb=mb):
                    for k in range(8):
                        ins = e.matmul(ps_x[:, 0:16], lhsT=wsl[sl_][:, k, mb * 128:(mb + 1) * 128], rhs=xnTs[:, k, :], start=(k == 0), stop=(k == 7))
                    return ins
                P.op("pe", mms, reads=[t_wsl[sl_], t_xnTs], writes=[t_ps_x])
                copy_op(ev_eng(), sdst[:, smb0 + mb, :], ps_x[:, 0:16], [t_ps_x], [t_sdst])

        def proj_tm(sl_, tok_ap_fn, reads, evac_fn):
            i = nxt("mm")

            def mm(e, i=i):
                for k in range(8):
                    ins = e.matmul(ps_mm[i][:, 0:256], lhsT=tok_ap_fn(k), rhs=wsl[sl_][:, k, :], start=(k == 0), stop=(k == 7))
                return ins
            P.op("pe", mm, reads=[t_wsl[sl_]] + list(reads), writes=[t_ps_mm[i]])
            evac_fn(ps_mm[i], t_ps_mm[i])

        def proj_tm_s(sl_, M, c0, evac_fn):
            def mm(e):
                for k in range(8):
                    ins = e.matmul(ps_x[0:M, 0:256], lhsT=xnTs[:, k, c0:c0 + M], rhs=wsl[sl_][:, k, :], start=(k == 0), stop=(k == 7))
                return ins
            P.op("pe", mm, reads=[t_wsl[sl_], t_xnTs], writes=[t_ps_x])
            evac_fn()

        def grp_tiles(g):
            d = DIL[g]
            nsb = NT // d
            return [(r, sb_) for r in range(d) for sb_ in range(nsb)]

        def tile_tok_slice(g, r, sb_):
            d = DIL[g]
            base = d * 128 * sb_ + r
            return slice(base, base + d * 127 + 1, d)

        def nat_tiles_of(g, r, sb_):
            d = DIL[g]
            return list(range(d * sb_, d * sb_ + d))

        win_out = (o_w0p, o_w1p, o_w2p)
        win_s_out = (o_w0s, o_w1s, o_w2s)
        load_w(0)
        load_w(1)
        load_w(2)
        def win_rows(g, r):
            dd = DIL[g]
            if dd == 1:
                return win_out[g]
            return win_out[g].rearrange("(j r) c -> r j c", r=dd)[r]

        def do_group(g):
            d = DIL[g]
            nsb = NT // d
            tiles = grp_tiles(g)
            n = 3 * g
            load_w(n + 3)
            proj_fm(n % NW, QT, t_QT, QTs, t_QTs, 2 * g)
            n = 3 * g + 1
            load_w(n + 3)
            proj_fm(n % NW, KT, t_KT, KTs, t_KTs, 2 * g)
            for r in range(d):
                sb_ = nsb - 1
                tsl = tile_tok_slice(g, r, sb_)

                def evk(pst, tps, r=r):
                    ws_ = nxt("wst")
                    P.op("act", lambda e: e.copy(out=wst[ws_][:, 0:256], in_=pst[:, 0:256]), reads=[tps], writes=[t_wst[ws_]])
                    rows = win_rows(g, r)
                    P.op("sp", lambda e: e.dma_start(out=rows[:, 0:256], in_=wst[ws_][:, 0:256]), reads=[t_wst[ws_]], dma=True)
                proj_tm(n % NW, (lambda k, tsl=tsl: xnT[:, k, tsl]), [t_xnT[j] for j in nat_tiles_of(g, r, sb_)], evk)

            def evks():
                P.op("act", lambda e: e.copy(out=wins[0:16, g, 0:256], in_=ps_x[0:16, 0:256]), reads=[t_ps_x], writes=[t_wins])
            proj_tm_s(n % NW, 16, 0, evks)
            n = 3 * g + 2
            load_w(n + 3)
            for gi_, (r, sb_) in enumerate(tiles):
                tsl = tile_tok_slice(g, r, sb_)
                is_win = (sb_ == nsb - 1)

                def evv(pst, tps, gi_=gi_, is_win=is_win, r=r):
                    P.op("dve", lambda e: e.tensor_copy(out=Vg[:, gi_, :, 0:64], in_=pst[:, 0:256].rearrange("p (h d) -> p h d", h=4)),
                         reads=[tps], writes=[t_Vg[gi_]])
                    if is_win:
                        ws_ = nxt("wst")
                        P.op("act", lambda e: e.copy(out=wst[ws_][:, 256:512], in_=pst[:, 0:256]), reads=[tps], writes=[t_wst[ws_]])
                        rows = win_rows(g, r)
                        P.op("sp", lambda e: e.dma_start(out=rows[:, 256:512], in_=wst[ws_][:, 256:512]), reads=[t_wst[ws_]], dma=True)
                proj_tm(n % NW, (lambda k, tsl=tsl: xnT[:, k, tsl]), [t_xnT[j] for j in nat_tiles_of(g, r, sb_)], evv)

            def evvs():
                P.op("act", lambda e: e.copy(out=wins[0:16, g, 256:512], in_=ps_x[0:16, 0:256]), reads=[t_ps_x], writes=[t_wins])
            proj_tm_s(n % NW, 16, 0, evvs)
            P.op("sp", lambda e: e.dma_start(out=win_s_out[g], in_=wins[0:16, g, :]), reads=[t_wins], dma=True)
            for bb in range(4):
                def evvn(bb=bb):
                    P.op("dve", lambda e: e.tensor_copy(out=Vns[0:4, bb, g, :, 0:64], in_=ps_x[0:4, 0:256].rearrange("p (h d) -> p h d", h=4)),
                         reads=[t_ps_x], writes=[t_Vns])
                proj_tm_s(n % NW, 4, 4 * bb, evvn)
            allqk = [x for row in t_QT for x in row] + [x for row in t_KT for x in row]
            pending = []
            for gi_, (r, sb_) in enumerate(tiles):
                qsl = tile_tok_slice(g, r, sb_)
                blocks = ([(0, (r, sb_ - 1))] if sb_ > 0 else []) + [(1, (r, sb_))]
                io = nxt("o")
                for h in range(4):
                    pr, hf = h // 2, h % 2
                    psl = slice(64 * hf, 64 * hf + 64)
                    isx = nxt("s")

                    def qk(e, isx=isx, pr=pr, psl=psl, qsl=qsl, blocks=blocks):
                        for (blk, (kr, ksb)) in blocks:
                            ksl = tile_tok_slice(g, kr, ksb)
                            ins = e.matmul(ps_s[isx][:, blk * 128:(blk + 1) * 128], lhsT=KT[psl, pr, ksl], rhs=QT[psl, pr, qsl], start=True, stop=True)
                        return ins
                    P.op("pe", qk, reads=allqk, writes=[t_ps_s[isx]])
                    b0 = blocks[0][0]
                    csl = slice(b0 * 128, 256)
                    isb = nxt("sb")
                    P.op("dve", (lambda e, isx=isx, isb=isb, csl=csl, h=h, b0=b0: e.scalar_tensor_tensor(
                        out=sbf[isb][:, csl], in0=ps_s[isx][:, csl], scalar=SCALE,
                        in1=biasT[:, g * 4 + h, b0:2, :].rearrange("p b q -> p (b q)"), op0=ALU.mult, op1=ALU.add)),
                        reads=[t_ps_s[isx], t_biasT], writes=[t_sbf[isb]])
                    ipt = nxt("pt")
                    P.op("act", (lambda e, isb=isb, ipt=ipt, csl=csl: e.activation(out=PT[ipt][:, csl], in_=sbf[isb][:, csl], func=AF.Exp)),
                         reads=[t_sbf[isb]], writes=[t_PT[ipt]])
                    if pending:
                        pending.pop(0)()

                    def later(ipt=ipt, io=io, h=h, blocks=blocks, gi_=gi_):
                        def pv(e):
                            nb = len(blocks)
                            for bi, (blk, (kr, ksb)) in enumerate(blocks):
                                kgi = kr * nsb + ksb
                                ins = e.matmul(ps_o[io][:, h * 65:(h + 1) * 65], lhsT=PT[ipt][:, blk * 128:(blk + 1) * 128], rhs=Vg[:, kgi, h, 0:65],
                                               start=(bi == 0), stop=(bi == nb - 1))
                            return ins
                        P.op("pe", pv, reads=[t_PT[ipt]] + [t_Vg[kr * nsb + ksb] for (_, (kr, ksb)) in blocks], writes=[t_ps_o[io]])
                        if h == 3:
                            copy_op(ev_eng(), Og[:, 16 * g + gi_, :], ps_o[io][:, 0:260], [t_ps_o[io]], [t_Og[16 * g + gi_]])
                    pending.append(later)
            while pending:
                pending.pop(0)()

        for g in range(3):
            do_group(g)

        n = 9
        load_w(n + 3)
        proj_fm(n % NW, QmT, t_QmT, QTs, t_QTs, 6)
        def do_gate(gc):
            n = 10 + gc
            load_w(n + 3)
            for i in range(NT):
                def evg(pst, tps, i=i, gc=gc):
                    P.op("act", lambda e: e.activation(out=gate[:, i, gc * 256:(gc + 1) * 256], in_=pst[:, 0:256], func=AF.Silu),
                         reads=[tps], writes=[t_gate[i]])
                proj_tm(n % NW, (lambda k, i=i: xnT[:, k, i * 128:(i + 1) * 128]), [t_xnT[i]], evg)
            for bb in range(4):
                def evgs(bb=bb, gc=gc):
                    P.op("act", lambda e: e.activation(out=gate_s[0:4, bb, gc * 256:(gc + 1) * 256], in_=ps_x[0:4, 0:256], func=AF.Silu),
                         reads=[t_ps_x], writes=[t_gate_s])
                proj_tm_s(n % NW, 4, 4 * bb, evgs)
        for gc in range(2):
            do_gate(gc)

        barrier()
        l0_keep = apos[0]
        apos[0] = L0
        wout = carve_bf(4 * 1024).rearrange("p (k n) -> p k n", k=4)
        t_wout = Tk()
        P.op("pool", lambda e: e.dma_start(out=wout, in_=w_out_a.rearrange("(k p) n -> p k n", p=128)), writes=[t_wout], dma=True)
        hbuf = [carve_f(512) for _ in range(2)]
        t_hbuf = [Tk(), Tk()]
        hb = [carve_bf(512) for _ in range(2)]
        t_hb = [Tk(), Tk()]
        hT = [carve_bf(512).rearrange("p (k t) -> p k t", k=4) for _ in range(2)]
        t_hT = [Tk(), Tk()]
        rcp = [carve_f(8) for _ in range(2)]
        t_rcp = [Tk(), Tk()]
        ytmp = [carve_f(1024) for _ in range(2)]
        t_ytmp = [Tk(), Tk()]
        cst = carve_f(2048).rearrange("p (t n) -> p t n", t=4)
        ckb = carve_bf(4 * 256).rearrange("p (t n) -> p t n", t=4)
        cKT = carve_bf(8 * 128).rearrange("p (i n) -> p i n", i=8)
        cV = carve_bf(4 * 4 * 80).rearrange("p (t h d) -> p t h d", t=4, h=4)
        PTz = carve_bf(16)
        PTn = carve_bf(16)
        sbn = carve_f(16)
        t_cst, t_ckb, t_cKT, t_cV, t_PTz, t_PTn, t_sbn = Tk(), Tk(), Tk(), Tk(), Tk(), Tk(), Tk()
        assert apos[0] <= L0 + 8192 + 64 + 4096, apos[0] - L0
        apos[0] = l0_keep
        xres = xld
        t_xres = t_xld
        t_x1 = [Tk() for _ in range(NT + 1)]
        rr.update({"hb": 0})
        CTX = dict(PT=PT, t_PT=t_PT, ytmp=ytmp, t_ytmp=t_ytmp, xres=xres, t_xres=t_xres,
                   cst=cst, ckb=ckb, cKT=cKT, cV=cV, t_cst=t_cst, t_ckb=t_ckb, t_cKT=t_cKT, t_cV=t_cV)
        P.op("pool", lambda e: e.memset(cV[:, :, :, 64:65], 1.0), writes=[t_cV])
        P.op("pool", lambda e: e.memset(PTz[:, :], 0.0), writes=[t_PTz])

        def cross_attn(k_ap_fn, t_k, v_ap_fn, t_v, qT_ap_fn, q_reads, nq, ps_out, t_ps_out):
            PT_, t_PT_ = CTX["PT"], CTX["t_PT"]
            for h in range(4):
                pr, hf = h // 2, h % 2
                psl = slice(64 * hf, 64 * hf + 64)
                isx = nxt("s")

                def qk(e, isx=isx, pr=pr, psl=psl):
                    for blk in range(2):
                        ins = e.matmul(ps_s[isx][:, blk * 128:blk * 128 + nq], lhsT=k_ap_fn(psl, pr, blk), rhs=qT_ap_fn(psl, pr),
                                       start=True, stop=True)
                    return ins
                P.op("pe", qk, reads=[t_k] + list(q_reads), writes=[t_ps_s[isx]])
                ipt = nxt("pt")
                P.op("act", (lambda e, isx=isx, ipt=ipt: e.activation(
                    out=PT_[ipt][:, :].rearrange("p (b q) -> p b q", b=2)[:, :, 0:nq],
                    in_=ps_s[isx][:, 0:256].rearrange("p (b q) -> p b q", b=2)[:, :, 0:nq], func=AF.Exp, scale=SCALE)),
                    reads=[t_ps_s[isx]], writes=[t_PT_[ipt]])

                def pv(e, ipt=ipt, h=h):
                    for blk in range(2):
                        ins = e.matmul(ps_out[0:nq, h * 65:(h + 1) * 65], lhsT=PT_[ipt][:, blk * 128:blk * 128 + nq], rhs=v_ap_fn(blk, h),
                                       start=(blk == 0), stop=(blk == 1))
                    return ins
                P.op("pe", pv, reads=[t_PT_[ipt], t_v], writes=[t_ps_out])

        def normalize_o(ps_in, t_ps_in, nq, dst_ap, t_dst, t_rc, rc):
            v = ps_in[0:nq, 0:260].rearrange("p (h d) -> p h d", h=4)
            P.op("dve", lambda e: e.reciprocal(out=rc[0:nq, 0:4], in_=v[:, :, 64]), reads=[t_ps_in], writes=[t_rc])
            P.op("dve", lambda e: e.tensor_tensor(out=dst_ap.rearrange("p (h d) -> p h d", h=4), in0=v[:, :, 0:64],
                                                  in1=rc[0:nq, 0:4].unsqueeze(2).to_broadcast([nq, 4, 64]), op=ALU.mult),
                 reads=[t_ps_in, t_rc], writes=[t_dst])

        def post_a(nq, hbuf_ap, t_hb_in, gate_ap, t_gate_in, nk, hb_t, t_hb_t, hT_t, t_hT_t, col0):
            P.op("dve", lambda e: e.tensor_tensor(out=hb_t[0:nq, 0:nk * 128], in0=hbuf_ap, in1=gate_ap, op=ALU.mult),
                 reads=[t_hb_in, t_gate_in], writes=[t_hb_t])

            def tr(e):
                for k in range(nk):
                    ins = e.transpose(out=ps_t[:, k * 128:k * 128 + nq], in_=hb_t[0:nq, k * 128:(k + 1) * 128], identity=ident[0:nq, 0:nq])
                return ins
            P.op("pe", tr, reads=[t_hb_t, t_ident], writes=[t_ps_t])
            copy_op(ev_eng(), hT_t[:, 0:nk, col0:col0 + nq], ps_t[:].rearrange("p (k m) -> p k m", k=8)[:, 0:nk, 0:nq], [t_ps_t], [t_hT_t])

        def post_b(gi_post, nq, nk, hT_t, t_hT_t, wout_t, t_wout_t, ys_, x_src_fn, x_dst_fn):
            ytmp_, t_ytmp_, xres_, t_xres_ = CTX["ytmp"], CTX["t_ytmp"], CTX["xres"], CTX["t_xres"]
            for nb in range(2):
                def mm(e, nb=nb):
                    for k in range(nk):
                        ins = e.matmul(ps_mm[nb][0:nq, :], lhsT=hT_t[:, k, 0:nq], rhs=wout_t[:, k, nb * 512:(nb + 1) * 512], start=(k == 0), stop=(k == nk - 1))
                    return ins
                P.op("pe", mm, reads=[t_hT_t, t_wout_t], writes=[t_ps_mm[nb]])
                copy_op("act" if nb == 0 else "dve", ytmp_[ys_][0:nq, nb * 512:(nb + 1) * 512], ps_mm[nb][0:nq, :], [t_ps_mm[nb]], [t_ytmp_[ys_]])
            x_src_fn(ys_)
            rmsnorm(ytmp_[ys_][0:nq, :], nq, gi_post, ytmp_[ys_][0:nq, :], t_ytmp_[ys_], t_ytmp_[ys_])
            P.op("dve", lambda e: e.tensor_tensor(out=xres_[ys_][0:nq, :], in0=xres_[ys_][0:nq, :], in1=ytmp_[ys_][0:nq, :], op=ALU.add),
                 reads=[t_ytmp_[ys_], t_xres_[ys_]], writes=[t_xres_[ys_]])
            x_dst_fn(ys_)

        tile_hs = {}

        def tile_X(Tq):
            hs_ = nxt("hb")
            tile_hs[Tq] = hs_
            io = nxt("o")
            srcs = [(0, 0, Tq, ident[:, :])]
            srcs += [(1, r, Tq // 4, perm_lhsT(1, r, Tq)) for r in range(4)]
            srcs += [(2, r, 0, perm_lhsT(2, r, Tq)) for r in range(16)]

            def comb(e):
                ns = len(srcs)
                for si, (g, r, sb_, lt) in enumerate(srcs):
                    gi_ = r * (NT // DIL[g]) + sb_
                    ins = e.matmul(ps_o[io][:, 0:260], lhsT=lt, rhs=Og[:, 16 * g + gi_, :], start=(si == 0), stop=(si == ns - 1))
                return ins
            P.op("pe", comb, reads=[t_ident, t_M] + [t_Og[16 * g + r * (NT // DIL[g]) + sb_] for (g, r, sb_, _) in srcs], writes=[t_ps_o[io]])
            normalize_o(ps_o[io], t_ps_o[io], 128, hbuf[hs_][:, 0:256], t_hbuf[hs_], t_rcp[hs_], rcp[hs_])
            io2 = nxt("o")
            cross_attn((lambda psl, pr, blk: KmT[psl, 0, pr, blk * 128:(blk + 1) * 128]), t_KmT[0],
                       (lambda blk, h: Vm[:, 0, blk, h, 0:65]), t_Vm[0],
                       (lambda psl, pr: QmT[psl, pr, Tq * 128:(Tq + 1) * 128]), [x for row in t_QmT for x in row],
                       128, ps_o[io2], t_ps_o[io2])
            normalize_o(ps_o[io2], t_ps_o[io2], 128, hbuf[hs_][:, 256:512], t_hbuf[hs_], t_rcp[hs_], rcp[hs_])

        def tile_Y(Tq):
            hs_ = tile_hs[Tq]
            post_a(128, hbuf[hs_][:, :], t_hbuf[hs_], gate[:, Tq, :], t_gate[Tq], 4, hb[hs_], t_hb[hs_], hT[hs_], t_hT[hs_], 0)

            def xsrc(ys_):
                P.op("sp", lambda e: e.dma_start(out=xres[ys_][:, :], in_=x_p[Tq * 128:(Tq + 1) * 128, :]), writes=[t_xres[ys_]], dma=True)

            def xdst(ys_):
                P.op("sp", lambda e: e.dma_start(out=x1_d[Tq * 128:(Tq + 1) * 128, :], in_=xres[ys_][:, :]), reads=[t_xres[ys_]], writes=[t_x1[Tq]],
                     dma=True, sem_tk=t_xres[ys_])
                if "l0out" in DBG:
                    P.op("sp", lambda e: e.dma_start(out=y_p[Tq * 128:(Tq + 1) * 128, :], in_=xres[ys_][:, :]), reads=[t_xres[ys_]], dma=True)
            post_b(G_POST[0], 128, 4, hT[hs_], t_hT[hs_], wout, t_wout, hs_, xsrc, xdst)

        tile_X(0)
        for Tq in range(1, NT):
            tile_X(Tq)
            tile_Y(Tq - 1)
        tile_Y(NT - 1)

        def load_piece(src_ap, nt_):
            cst, ckb, cKT, cV = CTX["cst"], CTX["ckb"], CTX["cKT"], CTX["cV"]
            t_cst, t_ckb, t_cKT, t_cV = CTX["t_cst"], CTX["t_ckb"], CTX["t_cKT"], CTX["t_cV"]
            P.op("sp", lambda e: e.dma_start(out=cst[:, 0:nt_, :], in_=src_ap), writes=[t_cst], dma=True)
            P.op("dve", lambda e: e.tensor_copy(out=ckb[:, 0:nt_, :], in_=cst[:, 0:nt_, 0:256]), reads=[t_cst], writes=[t_ckb])
            P.op("act", lambda e: e.copy(out=cV[:, 0:nt_, :, 0:64], in_=cst[:, 0:nt_, 256:512].rearrange("p t (h d) -> p t h d", h=4)),
                 reads=[t_cst], writes=[t_cV])

            def tr(e):
                for t_ in range(nt_):
                    for pr in range(2):
                        idx = t_ * 2 + pr
                        ins = e.transpose(out=ps_t[:, idx * 128:(idx + 1) * 128], in_=ckb[:, t_, pr * 128:(pr + 1) * 128], identity=ident[:, :])
                return ins
            P.op("pe", tr, reads=[t_ckb, t_ident], writes=[t_ps_t])
            copy_op(ev_eng(), cKT[:, 0:2 * nt_, :], ps_t[:].rearrange("p (k m) -> p k m", k=8)[:, 0:2 * nt_, :], [t_ps_t], [t_cKT])

        def sample_batch(bb):
            io = nxt("o")
            pso, tpso = ps_o[io], t_ps_o[io]
            first = [True]

            def pv_mm(e, out_ap, lhsT, rhs):
                ins = e.matmul(out_ap, lhsT=lhsT, rhs=rhs, start=first[0], stop=False, skip_group_check=True)
                first[0] = False
                return ins
            qs = slice(4 * bb, 4 * bb + 4)
            for g in range(3):
                d = DIL[g]
                if g == 0:
                    load_piece(c_w0[bb].rearrange("(j o) c -> j o c", o=1), 1)
                elif g == 1:
                    load_piece(c_w1[bb].rearrange("(j r) c -> j r c", r=4), 4)
                else:
                    load_piece(c_w2[bb].rearrange("(j r) c -> j r c", r=16)[:, 0:4, :], 4)
                for h in range(4):
                    pr, hf = h // 2, h % 2
                    psl = slice(64 * hf, 64 * hf + 64)
                    isx = nxt("s")
                    if g == 0:
                        def qk(e, isx=isx, pr=pr, psl=psl):
                            e.matmul(ps_s[isx][:, 0:4], lhsT=cKT[psl, pr, :], rhs=QTs[psl, pr, qs], start=True, stop=True)
                            return e.matmul(ps_s[isx][0:4, 8:12], lhsT=KTs[psl, pr, qs], rhs=QTs[psl, pr, qs], start=True, stop=True)
                        P.op("pe", qk, reads=[t_cKT, t_QTs, t_KTs], writes=[t_ps_s[isx]])
                        isb = nxt("sb")
                        P.op("dve", (lambda e, isx=isx, isb=isb, h=h: e.scalar_tensor_tensor(
                            out=sbf[isb][:, 0:4], in0=ps_s[isx][:, 0:4], scalar=SCALE, in1=biasT[:, h, 0, 0:4], op0=ALU.mult, op1=ALU.add)),
                            reads=[t_ps_s[isx], t_biasT], writes=[t_sbf[isb]])
                        P.op("dve", (lambda e, isx=isx, h=h: e.scalar_tensor_tensor(
                            out=sbn[0:4, 0:4], in0=ps_s[isx][0:4, 8:12], scalar=SCALE, in1=biasT[0:4, h, 1, 0:4], op0=ALU.mult, op1=ALU.add)),
                            reads=[t_ps_s[isx], t_biasT], writes=[t_sbn])
                        ipt = nxt("pt")
                        P.op("act", (lambda e, isb=isb, ipt=ipt: e.activation(out=PT[ipt][:, 0:4], in_=sbf[isb][:, 0:4], func=AF.Exp)),
                             reads=[t_sbf[isb]], writes=[t_PT[ipt]])
                        P.op("act", lambda e: e.activation(out=PTn[0:4, 0:4], in_=sbn[0:4, 0:4], func=AF.Exp), reads=[t_sbn], writes=[t_PTn])

                        def pv(e, ipt=ipt, h=h):
                            pv_mm(e, pso[0:4, h * 65:(h + 1) * 65], PT[ipt][:, 0:4], cV[:, 0, h, 0:65])
                            return pv_mm(e, pso[0:4, h * 65:(h + 1) * 65], PTn[0:4, 0:4], Vns[0:4, bb, 0, h, 0:65])
                        P.op("pe", pv, reads=[t_PT[ipt], t_PTn, t_cV, t_Vns], writes=[tpso])
                    else:
                        def qk(e, isx=isx, pr=pr, psl=psl, g=g):
                            for t_ in range(4):
                                e.matmul(ps_s[isx][:, t_:t_ + 1], lhsT=cKT[psl, 2 * t_ + pr, :], rhs=QTs[psl, 2 * g + pr, 4 * bb + t_:4 * bb + t_ + 1],
                                         start=True, stop=True)
                            return e.matmul(ps_s[isx][0:4, 8:12], lhsT=KTs[psl, 2 * g + pr, qs], rhs=QTs[psl, 2 * g + pr, qs], start=True, stop=True)
                        P.op("pe", qk, reads=[t_cKT, t_QTs, t_KTs], writes=[t_ps_s[isx]])
                        P.op("act", (lambda e, isx=isx, h=h, g=g: e.activation(out=PTz[:, 0:16:5], in_=ps_s[isx][:, 0:4], func=AF.Exp,
                                                                              bias=biasT[:, g * 4 + h, 0, 0:1], scale=SCALE)),
                             reads=[t_ps_s[isx], t_biasT], writes=[t_PTz])
                        P.op("dve", (lambda e, isx=isx, h=h, g=g: e.scalar_tensor_tensor(
                            out=sbn[0:4, 0:4], in0=ps_s[isx][0:4, 8:12], scalar=SCALE, in1=biasT[0:4, g * 4 + h, 1, 0:4], op0=ALU.mult, op1=ALU.add)),
                            reads=[t_ps_s[isx], t_biasT], writes=[t_sbn])
                        P.op("act", lambda e: e.activation(out=sbn[0:4, 4:8], in_=sbn[0:4, 0:4], func=AF.Exp), reads=[t_sbn], writes=[t_sbn])
                        P.op("dve", lambda e: e.tensor_tensor(out=PTn[0:4, 0:4], in0=sbn[0:4, 4:8], in1=identf[0:4, 0:4], op=ALU.mult),
                             reads=[t_sbn, t_ident], writes=[t_PTn])

                        def pv(e, h=h, g=g):
                            for t_ in range(4):
                                pv_mm(e, pso[0:4, h * 65:(h + 1) * 65], PTz[:, 4 * t_:4 * t_ + 4], cV[:, t_, h, 0:65])
                            return pv_mm(e, pso[0:4, h * 65:(h + 1) * 65], PTn[0:4, 0:4], Vns[0:4, bb, g, h, 0:65])
                        P.op("pe", pv, reads=[t_PTz, t_PTn, t_cV, t_Vns], writes=[tpso])
            hs_ = 0
            normalize_o(pso, tpso, 4, hbuf[hs_][0:4, 0:256], t_hbuf[hs_], t_rcp[hs_], rcp[hs_])
            load_piece(c_mem[0, bb].rearrange("(b j) c -> j b c", b=2), 2)
            io2 = nxt("o")
            cross_attn((lambda psl, pr, blk: cKT[psl, 2 * blk + pr, :]), t_cKT, (lambda blk, h: cV[:, blk, h, 0:65]), t_cV,
                       (lambda psl, pr: QTs[psl, 6 + pr, qs]), [t_QTs], 4, ps_o[io2], t_ps_o[io2])
            normalize_o(ps_o[io2], t_ps_o[io2], 4, hbuf[hs_][0:4, 256:512], t_hbuf[hs_], t_rcp[hs_], rcp[hs_])
            post_a(4, hbuf[hs_][0:4, :], t_hbuf[hs_], gate_s[0:4, bb, :], t_gate_s, 4, hb[hs_], t_hb[hs_], hT[1], t_hT[1], 4 * bb)

        for bb in range(0 if "nosample" in DBG else 4):
            sample_batch(bb)

        def xsrc_s(ys_):
            P.op("sp", lambda e: e.dma_start(out=xres[ys_][0:16, :], in_=x_s), writes=[t_xres[ys_]], dma=True)

        def xdst_s(ys_):
            P.op("sp", lambda e: e.dma_start(out=x1s_d, in_=xres[ys_][0:16, :]), reads=[t_xres[ys_]], writes=[t_x1[NT]], dma=True, sem_tk=t_xres[ys_])
            if "l0out" in DBG:
                P.op("sp", lambda e: e.dma_start(out=y_s, in_=xres[ys_][0:16, :]), reads=[t_xres[ys_]], dma=True)
        if "nosample" not in DBG:
            post_b(G_POST[0], 16, 4, hT[1], t_hT[1], wout, t_wout, 1, xsrc_s, xdst_s)
        print('n_ops', len(P.ops))
        if 'dump' in DBG:
            for _i, _o in enumerate(P.ops):
                print('OP', _i, _o.eng, _o.line, 'dma' if _o.is_dma else '')

        if STAGE >= 2:
            barrier()
            apos[0] = 0
            atop[0] = ARENA_WORDS
            load_gains([(0, norm_pre, 1), (1, norm_post, 1)])
            Wb = carve_bf(8 * B_IN).rearrange("p (k n) -> p k n", k=8)
            t_Wb = Tk()
            for c0 in range(0, B_IN, 464):
                P.op("pool", (lambda e, c0=c0: e.dma_start(out=Wb[:, :, c0:c0 + 464], in_=w_in_b[:, c0:c0 + 464].rearrange("(k p) n -> p k n", p=128))),
                     writes=[t_Wb], dma=True)
            Wo = carve_bf(8 * 1024).rearrange("p (k n) -> p k n", k=8)
            t_Wo = Tk()
            P.op("pool", lambda e: e.dma_start(out=Wo, in_=w_out_b.rearrange("(k p) n -> p k n", p=128)), writes=[t_Wo], dma=True)
            wup = carve_bf(768)
            aup = carve_bf(768)
            t_lora = Tk()
            P.op("pool", lambda e: e.dma_start(out=wup[0:64, :], in_=r_wup), writes=[t_lora], dma=True)
            P.op("pool", lambda e: e.dma_start(out=aup[64:128, :], in_=r_aup), writes=[t_lora], dma=True)
            par = carve_f(64)
            t_par = Tk()
            P_MU, P_W0, P_A0, P_KK, P_KA, P_OMKA, P_RK = 0, 19, 25, 31, 37, 43, 49
            for (src, c0, nm) in ((r_mu, P_MU, 19), (r_w0, P_W0, 6), (r_a0, P_A0, 6), (r_kk, P_KK, 6), (r_ka, P_KA, 6), (r_rk, P_RK, 6)):
                P.op("sp", (lambda e, src=src, c0=c0, nm=nm: e.dma_start(out=par[:, c0:c0 + nm], in_=src.rearrange("o (m p) -> p (o m)", p=128),
                                                                         allow_slow_non_contiguous=True)), writes=[t_par], dma=True)
            P.op("dve", lambda e: e.tensor_scalar(out=par[:, P_OMKA:P_OMKA + 6], in0=par[:, P_KA:P_KA + 6], scalar1=-1.0, scalar2=1.0, op0=ALU.mult, op1=ALU.add),
                 reads=[t_par], writes=[t_par])
            lnw_b = carve_f(768)
            lnb_b = carve_f(768)
            t_ln = Tk()
            for (dst, src) in ((lnw_b, r_lnw), (lnb_b, r_lnb)):
                apb = bass.AP(tensor=src.tensor, offset=0, ap=[[0, 128], [1, 768]])
                P.op("sp", (lambda e, dst=dst, apb=apb: e.dma_start(out=dst, in_=apb)), writes=[t_ln], dma=True)
            SA, SB, SC, SD, SE, SF, SG = [carve_f(768).rearrange("p (m j) -> p m j", m=6) for _ in range(7)]
            t_S = {k_: Tk() for k_ in "ABCDEFG"}
            t_mt1 = t_S["A"]
            mtmp1 = SA[:, :, :].rearrange("p m j -> p (m j)")
            ML = carve_bf(512).rearrange("p (h i) -> p h i", h=4)
            MU = carve_bf(512).rearrange("p (h i) -> p h i", h=4)
            MUi = carve_bf(512).rearrange("p (h i) -> p h i", h=4)
            Irep = carve_bf(512).rearrange("p (h i) -> p h i", h=4)
            rmask = carve_bf(768).rearrange("p (m j) -> p m j", m=6)
            bd = carve_f(128)
            hsel = carve_bf(2)
            t_cst1 = Tk()
            for (Mt, cm, step, cmp_) in ((ML, 1, -1, ALU.is_gt), (MU, -1, 1, ALU.is_gt), (MUi, -1, 1, ALU.is_ge), (Irep, 1, -1, ALU.is_equal)):
                P.op("pool", lambda e: e.memset(mtmp1[:, 0:512], 1.0), writes=[t_mt1])

                def mk(e, cm=cm, step=step, cmp_=cmp_):
                    v = mtmp1[:, 0:512].rearrange("p (h i) -> p h i", h=4)
                    return e.affine_select(out=v, in_=v, pattern=[[0, 4], [step, 128]], compare_op=cmp_, fill=0.0, base=0, channel_multiplier=cm)
                P.op("pool", mk, reads=[t_mt1], writes=[t_mt1])
                P.op("dve", (lambda e, Mt=Mt: e.tensor_copy(out=Mt, in_=mtmp1[:, 0:512].rearrange("p (h i) -> p h i", h=4))), reads=[t_mt1], writes=[t_cst1])

            def mk2(e):
                e.memset(rmask[:, :, :], 1.0)
                e.memset(rmask[:, :, 0:1], 0.0)
                e.memset(bd[:, :], 0.0)
                e.memset(bd[0:64, 0:64], 1.0)
                e.memset(bd[64:128, 64:128], 1.0)
                e.memset(hsel[:, :], 0.0)
                e.memset(hsel[0:64, 0:1], 1.0)
                return e.memset(hsel[64:128, 1:2], 1.0)
            P.op("pool", mk2, writes=[t_cst1])

            xld1 = carve_f(1024)
            xnb1 = carve_bf(1024)
            xnTc = carve_bf(8 * 128).rearrange("p (k t) -> p k t", k=8)
            xnTs1 = carve_bf(8 * 16).rearrange("p (k t) -> p k t", k=8)
            t_xld1, t_xnb1, t_xnTc, t_xnTs1 = Tk(), Tk(), Tk(), Tk()
            cols = carve_f(19 * 129).rearrange("p (m j) -> p m j", m=19)
            t_cols = Tk()
            lastc = carve_f(20)
            t_xs = t_cols
            QmTc = carve_bf(2 * 128).rearrange("p (m j) -> p m j", m=2)
            t_QmTc = Tk()
            gt = carve_bf(1024)
            t_gt = Tk()
            lw = carve_bf(128)
            t_lw = Tk()
            outs_w0 = apos[0]
            t_o = {k_: Tk() for k_ in ("AT", "BT", "KT", "KH", "BH", "RT", "RKT", "XV")}
            BT, KT1, KH, BH, RKT, XV = [carve_bf(768).rearrange("p (m j) -> p m j", m=6) for _ in range(6)]
            AT2 = carve_bf(12 * 128).rearrange("p (h j) -> p h j", h=12)
            RT2 = carve_bf(12 * 128).rearrange("p (h j) -> p h j", h=12)
            P.op("pool", lambda e: e.memset(AT2[:, :, :], 0.0), writes=[t_o["AT"]])
            P.op("pool", lambda e: e.memset(RT2[:, :, :], 0.0), writes=[t_o["RT"]])
            Vt = carve_bf(768)
            Kh = carve_bf(768)
            Bh = carve_bf(768)
            t_Vt, t_Kh, t_Bh = Tk(), Tk(), Tk()
            Lh = [carve_bf(512).rearrange("p (h i) -> p h i", h=4) for _ in range(3)]
            Xh = [carve_bf(512).rearrange("p (h i) -> p h i", h=4) for _ in range(3)]
            Qh = [carve_bf(512).rearrange("p (h i) -> p h i", h=4) for _ in range(3)]
            t_Lh, t_Xh, t_Qh = [Tk() for _ in range(3)], [Tk() for _ in range(3)], [Tk() for _ in range(3)]
            Qf = carve_bf(12 * 128).rearrange("p (h i) -> p h i", h=12)
            Aak = carve_bf(12 * 128).rearrange("p (h i) -> p h i", h=12)
            Ark = carve_bf(12 * 128).rearrange("p (h i) -> p h i", h=12)
            Arb = carve_bf(12 * 128).rearrange("p (h i) -> p h i", h=12)
            t_Qf, t_Aak, t_Ark, t_Arb = Tk(), Tk(), Tk(), Tk()
            zb_w0 = apos[0]
            Zb = carve_bf(768)
            Ub = carve_bf(768)
            t_Zb, t_Ub = Tk(), Tk()
            Yf = SD[:, :, :].rearrange("p m j -> p (m j)")
            t_Yf = t_S["D"]
            St = carve_f(384).rearrange("p (m v) -> p m v", m=6)
            Sbf = carve_bf(384).rearrange("p (m v) -> p m v", m=6)
            t_St = Tk()
            wc = carve_f(8)
            t_wc = Tk()
            gsm = carve_f(96)
            t_gsm = Tk()
            hbuf1 = carve_f(1024)
            hb1 = carve_bf(1024)
            hT1 = carve_bf(8 * 128).rearrange("p (k t) -> p k t", k=8)
            t_hbuf1, t_hb1, t_hT1 = Tk(), Tk(), Tk()
            rcp1 = carve_f(8)
            t_rcp1 = Tk()
            xres1 = carve_f(1024)
            t_xres1 = Tk()
            PT1 = [carve_bf(256) for _ in range(2)]
            t_PT1 = [Tk(), Tk()]
            svst = arena[:, zb_w0:zb_w0 + 768].rearrange("p (h k) -> p h k", h=12)
            t_svst = Tk()
            print("L1 arena words", apos[0])
            C0 = 0.6065306597126334
            psq = [ps_s[0], ps_s[1], ps_o[0], ps_o[1]]
            t_psq = [t_ps_s[0], t_ps_s[1], t_ps_o[0], t_ps_o[1]]
            rr.update({"q": 0})

            def nq4():
                rr["q"] += 1
                return rr["q"] % 4

            def tt(eng, out, in0, in1, op, reads, writes):
                P.op(eng, lambda e: e.tensor_tensor(out=out, in0=in0, in1=in1, op=op), reads=reads, writes=writes)

            def bc6(col0, C):
                return par[:, col0:col0 + 6].unsqueeze(2).to_broadcast([128, 6, C])

            def inproj_group(C, xn_ap, t_xn, m0, nm):
                if True:
                    i = nxt("mm")

                    def mm(e, m0=m0, nm=nm, i=i):
                        for mi in range(nm):
                            m = m0 + mi
                            for k in range(8):
                                ins = e.matmul(ps_mm[i][:, mi * 128:mi * 128 + C], lhsT=Wb[:, k, m * 128:(m + 1) * 128], rhs=xn_ap(k),
                                               start=(k == 0), stop=(k == 7))
                        return ins
                    P.op("pe", mm, reads=[t_Wb, t_xn], writes=[t_ps_mm[i]])
                    src = ps_mm[i][:, 0:nm * 128].rearrange("p (m j) -> p m j", m=nm)[:, :, 0:C]
                    if m0 < 19:
                        copy_op("act", cols[:, m0:m0 + nm, 1:1 + C], src, [t_ps_mm[i]], [t_cols])
                    else:
                        copy_op("act", QmTc[:, :, 0:C], src, [t_ps_mm[i]], [t_QmTc])

            COLS_GROUPS = [(16, 3), (0, 4), (4, 4), (8, 4), (12, 4)]

            def rwkv_chunk(C, xn_ap, t_xn, mode, idx, pre_done=False, early_fn=None, interleave=()):
                first = (idx == 0) if mode == "p" else True
                last = (idx == NT - 1) if mode == "p" else True
                interleave = list(interleave)
                if not pre_done:
                    for (m0, nm) in COLS_GROUPS:
                        inproj_group(C, xn_ap, t_xn, m0, nm)
                inproj_group(C, xn_ap, t_xn, 19, 2)
                for gc in range(2):
                    i = nxt("mm")

                    def mmg(e, gc=gc, i=i):
                        for k in range(8):
                            ins = e.matmul(ps_mm[i][0:C, :], lhsT=xn_ap(k), rhs=Wb[:, k, 2688 + gc * 512:2688 + (gc + 1) * 512], start=(k == 0), stop=(k == 7))
                        return ins
                    P.op("pe", mmg, reads=[t_Wb, t_xn], writes=[t_ps_mm[i]])
                    P.op("act", (lambda e, gc=gc, i=i: e.activation(out=gt[0:C, gc * 512:(gc + 1) * 512], in_=ps_mm[i][0:C, :], func=AF.Silu)),
                         reads=[t_ps_mm[i]], writes=[t_gt])
                t_lastc = Tk()
                P.op("act", lambda e: e.copy(out=lastc[:, 0:19], in_=cols[:, :, C]), reads=[t_cols], writes=[t_lastc])
                if last:
                    dst = (o_shp if mode == "p" else o_shs[idx:idx + 1, :]).rearrange("o (m p) -> p (o m)", p=128)
                    P.op("sp", lambda e: e.dma_start(out=dst, in_=lastc[:, 0:19], allow_slow_non_contiguous=True), reads=[t_lastc], dma=True)
                for (m0, nm) in ((0, 6), (6, 6), (12, 6), (18, 1)):
                    cur = cols[:, m0:m0 + nm, 1:1 + C]
                    prv = cols[:, m0:m0 + nm, 0:C]
                    tmp = SG[:, 0:nm, 0:C]
                    tt("dve", tmp, prv, cur, ALU.subtract, [t_cols], [t_S["G"]])
                    tt("dve", tmp, tmp, par[:, P_MU + m0:P_MU + m0 + nm].unsqueeze(2).to_broadcast([128, nm, C]), ALU.mult, [t_S["G"], t_par], [t_S["G"]])
                    tt("dve", cur, tmp, cur, ALU.add, [t_S["G"], t_cols], [t_cols])
                P.op("act", lambda e: e.copy(out=cols[:, :, 0], in_=lastc[:, 0:19]), reads=[t_lastc, t_cols], writes=[t_cols])

                class _XS:
                    def __getitem__(self, key):
                        p_, m_, j_ = key
                        assert j_ == slice(0, C)
                        return cols[p_, m_, 1:1 + C]
                xs = _XS()
                xr, xk, xv_ = xs[:, 0:6, 0:C], xs[:, 6:12, 0:C], xs[:, 12:18, 0:C]
                P.op("act", lambda e: e.activation(out=lw[0:64, 0:C], in_=xs[0:64, 18, 0:C], func=AF.Tanh), reads=[t_xs], writes=[t_lw])
                P.op("dve", lambda e: e.tensor_copy(out=lw[64:128, 0:C], in_=xs[64:128, 18, 0:C]), reads=[t_xs], writes=[t_lw])
                sigw, asig = SA[:, :, 0:C], SB[:, :, 0:C]
                for (which, wt, rows, pcol, dstS, tS) in (("w", wup, slice(0, 64), P_W0, SA, "A"), ("a", aup, slice(64, 128), P_A0, SB, "B")):
                    for (p0, np_) in ((0, 4), (4, 2)):
                        q = nq4()

                        def mml(e, wt=wt, rows=rows, p0=p0, np_=np_, q=q):
                            for pi in range(np_):
                                p = p0 + pi
                                ins = e.matmul(psq[q][:, pi * 128:pi * 128 + C], lhsT=wt[rows, p * 128:(p + 1) * 128], rhs=lw[rows, 0:C], start=True, stop=True)
                            return ins
                        P.op("pe", mml, reads=[t_lora, t_lw], writes=[t_psq[q]])
                        for pi in range(np_):
                            p = p0 + pi
                            P.op("act", (lambda e, pi=pi, p=p, q=q, pcol=pcol, dstS=dstS: e.activation(
                                out=dstS[:, p, 0:C], in_=psq[q][:, pi * 128:pi * 128 + C], func=AF.Sigmoid, bias=par[:, pcol + p:pcol + p + 1], scale=1.0)),
                                reads=[t_psq[q], t_par], writes=[t_S[tS]])
                cs = SC[:, :, 0:C]
                if C == 128:
                    P.op("dve", lambda e: e.tensor_tensor_scan(out=SC[:, :, :].rearrange("p m j -> p (m j)"), data0=rmask[:, :, :].rearrange("p m j -> p (m j)"),
                                                               data1=SA[:, :, :].rearrange("p m j -> p (m j)"), initial=0.0, op0=ALU.mult, op1=ALU.add),
                         reads=[t_S["A"], t_cst1], writes=[t_S["C"]])
                else:
                    for p in range(6):
                        P.op("dve", (lambda e, p=p: e.tensor_tensor_scan(out=SC[:, p, 0:C], data0=rmask[:, p, 0:C], data1=SA[:, p, 0:C], initial=0.0,
                                                                         op0=ALU.mult, op1=ALU.add)), reads=[t_S["A"], t_cst1], writes=[t_S["C"]])
                csC = SC[:, :, C - 1:C]
                eP, eH, eA, eN = SE[:, :, 0:C], SD[:, :, 0:C], SA[:, :, 0:C], SC[:, :, 0:C]
                P.op("act", lambda e: e.activation(out=eP, in_=cs, func=AF.Exp, scale=-C0), reads=[t_S["C"]], writes=[t_S["E"]])
                tt("dve", eH, csC.to_broadcast([128, 6, C]), cs, ALU.subtract, [t_S["C"]], [t_S["D"]])
                P.op("act", lambda e: e.activation(out=eH, in_=eH, func=AF.Exp, scale=-C0), reads=[t_S["D"]], writes=[t_S["D"]])
                P.op("act", lambda e: e.activation(out=wc[:, 0:6], in_=SC[:, :, C - 1], func=AF.Exp, scale=-C0), reads=[t_S["C"]], writes=[t_wc])
                tt("dve", eA, cs, sigw, ALU.subtract, [t_S["C"], t_S["A"]], [t_S["A"]])
                P.op("act", lambda e: e.activation(out=eA, in_=eA, func=AF.Exp, scale=-C0), reads=[t_S["A"]], writes=[t_S["A"]])
                P.op("act", lambda e: e.activation(out=eN, in_=cs, func=AF.Exp, scale=C0), reads=[t_S["C"], t_S["D"], t_S["E"], t_S["A"], t_wc], writes=[t_S["C"]])
                kk, g_ = SF[:, :, 0:C], SG[:, :, 0:C]
                tt("dve", kk, xk, bc6(P_KK, C), ALU.mult, [t_xs, t_par], [t_S["F"]])
                tt("dve", g_, kk, kk, ALU.mult, [t_S["F"]], [t_S["G"]])
                qa, qb_ = nq4(), nq4()

                def mmn(e):
                    e.matmul(psq[qa][:, 0:4 * 128].rearrange("p (m j) -> p m j", m=4)[:, :, 0:C], lhsT=bd[:, :], rhs=SG[:, 0:4, 0:C], start=True, stop=True)
                    return e.matmul(psq[qb_][:, 0:2 * 128].rearrange("p (m j) -> p m j", m=2)[:, :, 0:C], lhsT=bd[:, :], rhs=SG[:, 4:6, 0:C], start=True, stop=True)
                P.op("pe", mmn, reads=[t_S["G"], t_cst1], writes=[t_psq[qa], t_psq[qb_]])
                P.op("act", lambda e: e.activation(out=SG[:, 0:4, 0:C], in_=psq[qa][:, 0:512].rearrange("p (m j) -> p m j", m=4)[:, :, 0:C], func=AF.Sqrt),
                     reads=[t_psq[qa]], writes=[t_S["G"]])
                P.op("act", lambda e: e.activation(out=SG[:, 4:6, 0:C], in_=psq[qb_][:, 0:256].rearrange("p (m j) -> p m j", m=2)[:, :, 0:C], func=AF.Sqrt),
                     reads=[t_psq[qb_]], writes=[t_S["G"]])
                P.op("dve", lambda e: e.tensor_scalar_max(out=g_, in0=g_, scalar1=1e-12), reads=[t_S["G"]], writes=[t_S["G"]])
                P.op("dve", lambda e: e.reciprocal(out=g_, in_=g_), reads=[t_S["G"]], writes=[t_S["G"]])
                tt("dve", kk, kk, g_, ALU.mult, [t_S["F"], t_S["G"]], [t_S["F"]])
                for hf_ in range(2):
                    rs_ = slice(64 * hf_, 64 * hf_ + 64)
                    P.op("dve", (lambda e, hf_=hf_, rs_=rs_: e.scalar_tensor_tensor(out=AT2[rs_, hf_:12:2, 0:C], in0=SF[rs_, :, 0:C], scalar=-1.0,
                                                                                   in1=SA[rs_, :, 0:C], op0=ALU.mult, op1=ALU.mult)),
                         reads=[t_S["F"], t_S["A"]], writes=[t_o["AT"]])
                tt("dve", g_, kk, asig, ALU.mult, [t_S["F"], t_S["B"]], [t_S["G"]])
                tt("dve", BT[:, :, 0:C], g_, eN, ALU.mult, [t_S["G"], t_S["C"]], [t_o["BT"]])
                tt("dve", BH[:, :, 0:C], g_, eH, ALU.mult, [t_S["G"], t_S["D"]], [t_o["BH"]])
                km = hbuf1[:, 0:768].rearrange("p (m j) -> p m j", m=6)[:, :, 0:C]
                tt("pool", km, asig, bc6(P_KA, C), ALU.mult, [t_S["B"], t_par], [t_hbuf1])
                tt("pool", km, km, bc6(P_OMKA, C), ALU.add, [t_hbuf1, t_par], [t_hbuf1])
                tt("pool", km, km, xk, ALU.mult, [t_hbuf1, t_xs], [t_hbuf1])
                tt("pool", KT1[:, :, 0:C], km, eN, ALU.mult, [t_hbuf1, t_S["C"]], [t_o["KT"]])
                tt("pool", KH[:, :, 0:C], km, eH, ALU.mult, [t_hbuf1, t_S["D"]], [t_o["KH"]])
                tt("pool", km, km, bc6(P_RK, C), ALU.mult, [t_hbuf1, t_par], [t_hbuf1])
                tt("pool", RKT[:, :, 0:C], km, xr, ALU.mult, [t_hbuf1, t_xs], [t_o["RKT"]])
                for hf_ in range(2):
                    rs_ = slice(64 * hf_, 64 * hf_ + 64)
                    tt("pool", RT2[rs_, hf_:12:2, 0:C], cols[rs_, 0:6, 1:1 + C], SE[rs_, :, 0:C], ALU.mult, [t_xs, t_S["E"]], [t_o["RT"]])
                P.op("act", lambda e: e.copy(out=XV[:, :, 0:C], in_=xv_), reads=[t_xs], writes=[t_o["XV"]])
                for (srcT, tsrc, dstT, tdst) in ((XV, "XV", Vt, t_Vt), (KH, "KH", Kh, t_Kh), (BH, "BH", Bh, t_Bh)):
                    def tr(e, srcT=srcT):
                        for p in range(6):
                            ins = e.transpose(out=ps_t[0:C, p * 128:(p + 1) * 128], in_=srcT[:, p, 0:C], identity=ident[:, :])
                        return ins
                    P.op("pe", tr, reads=[t_o[tsrc], t_ident], writes=[t_ps_t])
                    copy_op(ev_eng(), dstT[0:C, :], ps_t[0:C, 0:768], [t_ps_t], [tdst])
                nlev = max(1, int(math.ceil(math.log2(C))))

                def pair_mm(q, l_fn, r_fn, hg, reads):
                    def f(e):
                        for hi in range(4):
                            h = 4 * hg + hi
                            ins = e.matmul(psq[q][0:C, hi * 128:hi * 128 + C], lhsT=l_fn(h), rhs=r_fn(h), start=True, stop=True)
                        return ins
                    P.op("pe", f, reads=reads, writes=[t_psq[q]])

                A2 = lambda h: AT2[:, h, 0:C]
                R2 = lambda h: RT2[:, h, 0:C]
                Bp = lambda h: BT[:, h // 2, 0:C]
                Kp = lambda h: KT1[:, h // 2, 0:C]

                def pv4(q):
                    return psq[q][0:C, :].rearrange("p (h i) -> p h i", h=4)[:, :, 0:C]

                def sq_mm(q, lT, rT, reads, acc_ident_rhs=None):
                    def f(e):
                        for hi in range(4):
                            if acc_ident_rhs is not None:
                                e.matmul(psq[q][0:C, hi * 128:hi * 128 + C], lhsT=ident[0:C, 0:C], rhs=acc_ident_rhs[0:C, hi, 0:C], start=True, stop=False)
                            ins = e.matmul(psq[q][0:C, hi * 128:hi * 128 + C], lhsT=lT[0:C, hi, 0:C], rhs=rT[0:C, hi, 0:C],
                                           start=(acc_ident_rhs is None), stop=True)
                        return ins
                    P.op("pe", f, reads=reads + [t_ident], writes=[t_psq[q]])

                if early_fn is not None:
                    early_fn()
                for hg in range(3):
                    q = nq4()
                    pair_mm(q, A2, Bp, hg, [t_o["AT"], t_o["BT"]])
                    tt("dve", Lh[hg][0:C, :, 0:C], pv4(q), ML[0:C, :, 0:C], ALU.mult, [t_psq[q], t_cst1], [t_Lh[hg]])
                    q = nq4()
                    pair_mm(q, Bp, A2, hg, [t_o["AT"], t_o["BT"]])
                    tt("dve", Xh[hg][0:C, :, 0:C], pv4(q), MU[0:C, :, 0:C], ALU.mult, [t_psq[q], t_cst1], [t_Xh[hg]])
                    tt("pool", Qh[hg][0:C, :, 0:C], Xh[hg][0:C, :, 0:C], Irep[0:C, :, 0:C], ALU.add, [t_Xh[hg], t_cst1], [t_Qh[hg]])
                for j in range(nlev - 1):
                    need_x = (j + 1 < nlev - 1)
                    if interleave:
                        interleave.pop(0)()
                    for hg in range(3):
                        q = nq4()
                        sq_mm(q, Xh[hg], Lh[hg], [t_Xh[hg], t_Lh[hg]])
                        qx = None
                        if need_x:
                            qx = nq4()
                            sq_mm(qx, Lh[hg], Xh[hg], [t_Xh[hg], t_Lh[hg]])
                        copy_op("act", Lh[hg][0:C, :, 0:C], pv4(q), [t_psq[q]], [t_Lh[hg]])
                        if need_x:
                            copy_op("dve", Xh[hg][0:C, :, 0:C], pv4(qx), [t_psq[qx]], [t_Xh[hg]])
                    for hg in range(3):
                        q = nq4()
                        sq_mm(q, Lh[hg], Qh[hg], [t_Lh[hg], t_Qh[hg]], acc_ident_rhs=Qh[hg])
                        if j == nlev - 2:
                            copy_op("act", Qf[0:C, 4 * hg:4 * hg + 4, 0:C], pv4(q), [t_psq[q]], [t_Qf])
                        else:
                            copy_op("act", Qh[hg][0:C, :, 0:C], pv4(q), [t_psq[q]], [t_Qh[hg]])
                while interleave:
                    interleave.pop(0)()
                for hg in range(3):
                    if nlev == 1:
                        copy_op("act", Qf[0:C, 4 * hg:4 * hg + 4, 0:C], Qh[hg][0:C, :, 0:C], [t_Qh[hg]], [t_Qf])
                    q = nq4()
                    pair_mm(q, Kp, A2, hg, [t_o["KT"], t_o["AT"]])
                    tt("dve", Aak[0:C, 4 * hg:4 * hg + 4, 0:C], pv4(q), MU[0:C, :, 0:C], ALU.mult, [t_psq[q], t_cst1], [t_Aak])
                    q = nq4()
                    pair_mm(q, Kp, R2, hg, [t_o["KT"], t_o["RT"]])
                    tt("dve", Ark[0:C, 4 * hg:4 * hg + 4, 0:C], pv4(q), MUi[0:C, :, 0:C], ALU.mult, [t_psq[q], t_cst1], [t_Ark])
                    q = nq4()
                    pair_mm(q, Bp, R2, hg, [t_o["BT"], t_o["RT"]])
                    tt("dve", Arb[0:C, 4 * hg:4 * hg + 4, 0:C], pv4(q), MUi[0:C, :, 0:C], ALU.mult, [t_psq[q], t_cst1], [t_Arb])
                if first:
                    if mode == "p":
                        P.op("pool", lambda e: e.memset(St[:, :, :], 0.0), writes=[t_St])
                        P.op("pool", lambda e: e.memset(Sbf[:, :, :], 0.0), writes=[t_St])
                    else:
                        P.op("sp", lambda e: e.dma_start(out=svst[0:64, :, :], in_=s_wkv[idx].rearrange("h v k -> v h k")), writes=[t_svst, t_Zb, t_Ub], dma=True, sem_tk=t_svst)

                        def trs(e):
                            for p in range(6):
                                ins = e.transpose(out=ps_x[:, p * 64:(p + 1) * 64], in_=svst[0:64, 2 * p:2 * p + 2, :].rearrange("v h k -> v (h k)"),
                                                  identity=identf[0:64, 0:64])
                            return ins
                        P.op("pe", trs, reads=[t_svst, t_ident], writes=[t_ps_x])
                        P.op("act", lambda e: e.copy(out=St[:, :, :], in_=ps_x[:, 0:384].rearrange("p (m v) -> p m v", m=6)), reads=[t_ps_x], writes=[t_St])
                        P.op("dve", lambda e: e.tensor_copy(out=Sbf[:, :, :], in_=ps_x[:, 0:384].rearrange("p (m v) -> p m v", m=6)), reads=[t_ps_x], writes=[t_St])
                def head_cols(h):
                    return slice(h * 64, (h + 1) * 64)

                def seq_mm(name, fn_terms, reads, dst_banks):
                    def f(e):
                        for h in range(12):
                            bank, hc = (dst_banks[0], h) if h < 8 else (dst_banks[1], h - 8)
                            terms = fn_terms(h)
                            for ti, (lT, r_) in enumerate(terms):
                                ins = e.matmul(psq[bank][0:C, hc * 64:(hc + 1) * 64], lhsT=lT, rhs=r_, start=(ti == 0), stop=(ti == len(terms) - 1))
                        return ins
                    P.op("pe", f, reads=reads, writes=[t_psq[dst_banks[0]], t_psq[dst_banks[1]]])

                def hsl(h):
                    p, hf = h // 2, h % 2
                    return slice(64 * hf, 64 * hf + 64), p

                def evac768(dst, banks, tdst, as_f32=False):
                    copy_op("act", dst[0:C, 0:512], psq[banks[0]][0:C, 0:512], [t_psq[banks[0]]], [tdst])
                    copy_op("dve", dst[0:C, 512:768], psq[banks[1]][0:C, 0:256], [t_psq[banks[1]]], [tdst])

                seq_mm("Z", lambda h: [(AT2[:, h, 0:C], Sbf[:, h // 2, :]), (Aak[0:C, h, 0:C], Vt[0:C, head_cols(h)])],
                       [t_o["AT"], t_St, t_Aak, t_Vt], (0, 1))
                evac768(Zb, (0, 1), t_Zb)
                seq_mm("U", lambda h: [(Qf[0:C, h, 0:C], Zb[0:C, head_cols(h)])], [t_Qf, t_Zb], (2, 3))
                evac768(Ub, (2, 3), t_Ub)
                seq_mm("Y", lambda h: [(RT2[:, h, 0:C], Sbf[:, h // 2, :]), (Ark[0:C, h, 0:C], Vt[0:C, head_cols(h)]),
                                       (Arb[0:C, h, 0:C], Ub[0:C, head_cols(h)])],
                       [t_o["RT"], t_St, t_Ark, t_Vt, t_Arb, t_Ub], (0, 1))
                evac768(Yf, (0, 1), t_Yf)

                def snew(e):
                    for h in range(12):
                        psl, p = hsl(h)
                        e.matmul(ps_x[psl, p * 64:(p + 1) * 64], lhsT=Kh[0:C, head_cols(h)], rhs=Vt[0:C, head_cols(h)], start=True, stop=False)
                        ins = e.matmul(ps_x[psl, p * 64:(p + 1) * 64], lhsT=Bh[0:C, head_cols(h)], rhs=Ub[0:C, head_cols(h)], start=False, stop=True)
                    return ins
                P.op("pe", snew, reads=[t_Kh, t_Vt, t_Bh, t_Ub], writes=[t_ps_x])
                tt("dve", St[:, :, :], St[:, :, :], wc[:, 0:6].unsqueeze(2).to_broadcast([128, 6, 64]), ALU.mult, [t_St, t_wc], [t_St])
                tt("dve", St[:, :, :], St[:, :, :], ps_x[:, 0:384].rearrange("p (m v) -> p m v", m=6), ALU.add, [t_St, t_ps_x], [t_St])
                P.op("dve", lambda e: e.tensor_copy(out=Sbf[:, :, :], in_=St[:, :, :]), reads=[t_St], writes=[t_St])
                if last:
                    def trs2(e):
                        for p in range(6):
                            ins = e.transpose(out=ps_x[0:64, p * 128:(p + 1) * 128] if False else ps_mm[0][0:64, p * 64:(p + 1) * 64], in_=St[:, p, :], identity=identf[:, :])
                        return ins
                    def trs3(e):
                        for p in range(6):
                            bank = ps_mm[0] if p < 4 else ps_mm[1]
                            pc = p if p < 4 else p - 4
                            ins = e.transpose(out=bank[0:64, pc * 128:(pc + 1) * 128], in_=St[:, p, :], identity=identf[:, :])
                        return ins
                    P.op("pe", trs3, reads=[t_St, t_ident], writes=[t_ps_mm[0], t_ps_mm[1]])
                    P.op("act", lambda e: e.copy(out=svst[0:64, 0:8, :].rearrange("v h k -> v (h k)"), in_=ps_mm[0][0:64, 0:512]), reads=[t_ps_mm[0]], writes=[t_svst, t_Zb, t_Ub])
                    P.op("act", lambda e: e.copy(out=svst[0:64, 8:12, :].rearrange("v h k -> v (h k)"), in_=ps_mm[1][0:64, 0:256]), reads=[t_ps_mm[1]], writes=[t_svst, t_Zb, t_Ub])
                    dsto = (o_wkvp if mode == "p" else o_wkvs[idx]).rearrange("h v k -> v h k")
                    P.op("sp", lambda e: e.dma_start(out=dsto, in_=svst[0:64, :, :]), reads=[t_svst, t_Zb, t_Ub], dma=True, sem_tk=t_svst)
                Y3 = Yf[0:C, :].rearrange("p (h d) -> p h d", h=12)
                sqv = SF[0:C, :, :].rearrange("p m j -> p (m j)").rearrange("p (h d) -> p h d", h=12)
                P.op("dve", lambda e: e.reduce_sum(out=gsm[0:C, 0:12], in_=Y3, axis=AX.X), reads=[t_Yf], writes=[t_gsm])
                P.op("act", lambda e: e.activation(out=sqv, in_=Y3, func=AF.Square), reads=[t_Yf, t_o["RKT"]], writes=[t_S["F"]])
                P.op("dve", lambda e: e.reduce_sum(out=gsm[0:C, 12:24], in_=sqv, axis=AX.X), reads=[t_S["F"]], writes=[t_gsm])
                P.op("dve", lambda e: e.tensor_scalar(out=gsm[0:C, 24:36], in0=gsm[0:C, 0:12], scalar1=1.0 / 64, scalar2=None, op0=ALU.mult),
                     reads=[t_gsm], writes=[t_gsm])
                tt("dve", gsm[0:C, 36:48], gsm[0:C, 24:36], gsm[0:C, 24:36], ALU.mult, [t_gsm], [t_gsm])
                P.op("dve", lambda e: e.scalar_tensor_tensor(out=gsm[0:C, 48:60], in0=gsm[0:C, 12:24], scalar=1.0 / 64, in1=gsm[0:C, 36:48],
                                                             op0=ALU.mult, op1=ALU.subtract), reads=[t_gsm], writes=[t_gsm])
                P.op("dve", lambda e: e.tensor_scalar(out=gsm[0:C, 48:60], in0=gsm[0:C, 48:60], scalar1=64e-5, scalar2=None, op0=ALU.add),
                     reads=[t_gsm], writes=[t_gsm])
                P.op("act", lambda e: e.activation(out=gsm[0:C, 60:72], in_=gsm[0:C, 48:60], func=AF.Sqrt), reads=[t_gsm], writes=[t_gsm])
                P.op("dve", lambda e: e.reciprocal(out=gsm[0:C, 72:84], in_=gsm[0:C, 60:72]), reads=[t_gsm], writes=[t_gsm])
                hb3 = hbuf1[0:C, 0:768].rearrange("p (h d) -> p h d", h=12)
                tt("dve", hb3, Y3, gsm[0:C, 24:36].unsqueeze(2).to_broadcast([C, 12, 64]), ALU.subtract, [t_Yf, t_gsm], [t_hbuf1])
                tt("dve", hb3, hb3, gsm[0:C, 72:84].unsqueeze(2).to_broadcast([C, 12, 64]), ALU.mult, [t_hbuf1, t_gsm], [t_hbuf1])
                tt("dve", hbuf1[0:C, 0:768], hbuf1[0:C, 0:768], lnw_b[0:C, :], ALU.mult, [t_hbuf1, t_ln], [t_hbuf1])
                tt("dve", hbuf1[0:C, 0:768], hbuf1[0:C, 0:768], lnb_b[0:C, :], ALU.add, [t_hbuf1, t_ln], [t_hbuf1])

                def bon(e):
                    for p in range(6):
                        ins = e.matmul(ps_x[0:C, 400 + 2 * p:402 + 2 * p], lhsT=RKT[:, p, 0:C], rhs=hsel[:, 0:2], start=True, stop=True)
                    return ins
                P.op("pe", bon, reads=[t_o["RKT"], t_cst1], writes=[t_ps_x])
                P.op("act", lambda e: e.copy(out=gsm[0:C, 84:96], in_=ps_x[0:C, 400:412]), reads=[t_ps_x], writes=[t_gsm])
                tt("dve", sqv, Vt[0:C, :].rearrange("p (h d) -> p h d", h=12), gsm[0:C, 84:96].unsqueeze(2).to_broadcast([C, 12, 64]), ALU.mult,
                   [t_Vt, t_gsm, t_S["F"]], [t_S["F"]])
                tt("dve", hb3, hb3, sqv, ALU.add, [t_hbuf1, t_S["F"]], [t_hbuf1])
                io2 = nxt("o")
                if mode == "p":
                    cross_attn((lambda psl, pr, blk: KmT[psl, 1, pr, blk * 128:(blk + 1) * 128]), t_KmT[1],
                               (lambda blk, h: Vm[:, 1, blk, h, 0:65]), t_Vm[1],
                               (lambda psl, pr: QmTc[psl, pr, 0:C]), [t_QmTc], C, ps_o[io2], t_ps_o[io2])
                else:
                    barrier()
                    P.op("pool", lambda e: e.memset(cV1[:, :, :, 64:65], 1.0), writes=[t_cV1])
                    load_piece(c_mem[1, idx].rearrange("(b j) c -> j b c", b=2), 2)
                    cross_attn((lambda psl, pr, blk: cKT1[psl, 2 * blk + pr, :]), t_cKT1, (lambda blk, h: cV1[:, blk, h, 0:65]), t_cV1,
                               (lambda psl, pr: QmTc[psl, pr, 0:C]), [t_QmTc], C, ps_o[io2], t_ps_o[io2])
                normalize_o(ps_o[io2], t_ps_o[io2], C, hbuf1[0:C, 768:1024], t_hbuf1, t_rcp1, rcp1)
                col0 = 0 if mode == "p" else 4 * idx
                post_a(C, hbuf1[0:C, :], t_hbuf1, gt[0:C, :], t_gt, 8, hb1, t_hb1, hT1, t_hT1, col0)
                if mode == "p":
                    def xsrc(ys_):
                        P.op("sp", lambda e: e.dma_start(out=xres1[:, :], in_=x1_d[idx * 128:(idx + 1) * 128, :]), reads=[t_x1[idx]], writes=[t_xres1], dma=True)

                    def xdst(ys_):
                        P.op("sp", lambda e: e.dma_start(out=y_p[idx * 128:(idx + 1) * 128, :], in_=xres1[:, :]), reads=[t_xres1], dma=True)
                    post_b(G_POST[1], 128, 8, hT1, t_hT1, Wo, t_Wo, 0, xsrc, xdst)
                else:
                    barrier()

            keep1 = apos[0]
            apos[0] = outs_w0
            cst1 = carve_f(1024).rearrange("p (t n) -> p t n", t=2)
            ckb1 = carve_bf(2 * 256).rearrange("p (t n) -> p t n", t=2)
            cKT1 = carve_bf(4 * 128).rearrange("p (i n) -> p i n", i=4)
            cV1 = carve_bf(2 * 4 * 80).rearrange("p (t h d) -> p t h d", t=2, h=4)
            t_cst1b, t_ckb1, t_cKT1, t_cV1 = Tk(), Tk(), Tk(), Tk()
            apos[0] = keep1
            CTX.update(PT=PT1, t_PT=t_PT1, ytmp=[hbuf1] * 2, t_ytmp=[t_hbuf1] * 2, xres=[xres1] * 2, t_xres=[t_xres1] * 2,
                       cst=cst1, ckb=ckb1, cKT=cKT1, cV=cV1, t_cst=t_cst1b, t_ckb=t_ckb1, t_cKT=t_cKT1, t_cV=t_cV1)
            print("L1 arena words (final)", apos[0])

            def prompt_early(c):
                P.op("sp", lambda e: e.dma_start(out=xld1[:, :], in_=x1_d[c * 128:(c + 1) * 128, :]), reads=[t_x1[c]], writes=[t_xld1], dma=True)
                rmsnorm(xld1[:, :], 128, G_PRE[1], xnb1[:, :], t_xld1, t_xnb1)

                def tr(e):
                    for k in range(8):
                        ins = e.transpose(out=ps_t[:, k * 128:(k + 1) * 128], in_=xnb1[:, k * 128:(k + 1) * 128], identity=ident[:, :])
                    return ins
                P.op("pe", tr, reads=[t_xnb1, t_ident], writes=[t_ps_t])
                copy_op(ev_eng(), xnTc[:, :, :], ps_t[:].rearrange("p (k m) -> p k m", k=8), [t_ps_t], [t_xnTc])

            xn_fn = (lambda k: xnTc[:, k, :])
            n_chunks = NT if "l1few" not in DBG else 2
            prompt_early(0)
            P.op("pool", lambda e: e.memset(cols[:, :, 0:1], 0.0), writes=[t_cols])
            for c in range(n_chunks):
                if c + 1 < n_chunks and "nopipe1" not in DBG:
                    ef = (lambda c=c: prompt_early(c + 1))
                    il = [(lambda m0=m0, nm=nm: inproj_group(128, xn_fn, t_xnTc, m0, nm)) for (m0, nm) in COLS_GROUPS]
                else:
                    ef, il = None, ()
                rwkv_chunk(128, xn_fn, t_xnTc, "p", c, pre_done=(c > 0 and "nopipe1" not in DBG), early_fn=ef, interleave=il)
                if c + 1 < n_chunks and "nopipe1" in DBG:
                    prompt_early(c + 1)

            def sample_l1():
                P.op("sp", lambda e: e.dma_start(out=xld1[0:16, :], in_=x1s_d), reads=[t_x1[NT]], writes=[t_xld1], dma=True)
                rmsnorm(xld1[0:16, :], 16, G_PRE[1], xnb1[0:16, :], t_xld1, t_xnb1)

                def tr(e):
                    for k in range(8):
                        ins = e.transpose(out=ps_t[:, k * 128:k * 128 + 16], in_=xnb1[0:16, k * 128:(k + 1) * 128], identity=ident[0:16, 0:16])
                    return ins
                P.op("pe", tr, reads=[t_xnb1, t_ident], writes=[t_ps_t])
                copy_op(ev_eng(), xnTs1[:, :, :], ps_t[:].rearrange("p (k m) -> p k m", k=8)[:, :, 0:16], [t_ps_t], [t_xnTs1])
                for bb in range(4):
                    P.op("sp", (lambda e, bb=bb: e.dma_start(out=cols[:, :, 0], in_=s_shift[bb:bb + 1, :].rearrange("o (m p) -> p (o m)", p=128),
                                                             allow_slow_non_contiguous=True)), writes=[t_cols], dma=True)
                    rwkv_chunk(4, (lambda k, bb=bb: xnTs1[:, k, 4 * bb:4 * bb + 4]), t_xnTs1, "s", bb)

                def xsrc_s(ys_):
                    P.op("sp", lambda e: e.dma_start(out=xres1[0:16, :], in_=x1s_d), reads=[t_x1[NT]], writes=[t_xres1], dma=True)

                def xdst_s(ys_):
                    P.op("sp", lambda e: e.dma_start(out=y_s, in_=xres1[0:16, :]), reads=[t_xres1], dma=True)
                post_b(G_POST[1], 16, 8, hT1, t_hT1, Wo, t_Wo, 0, xsrc_s, xdst_s)
            if "nosample1" not in DBG:
                sample_l1()

        print('total_ops', len(P.ops))
        if 'dump2' in DBG:
            for _i, _o in enumerate(P.ops):
                print('OP', _i, _o.eng, _o.line, 'dma' if _o.is_dma else '')
        P.finalize_and_emit(st)
    return nc


def layer0(env):
    pass


_CACHE = {}


def kernel(x_prompt, x_sample, mem_prompt, cache_mem_kv, cache_win0, cache_win1, cache_win2, state_wkv,
           state_shift, norm_pre, norm_post, norm_mem, w_mem_kv, rel_bias, w_in_a, w_out_a, w_in_b, w_out_b,
           rwkv_mu, rwkv_w0, rwkv_w_up, rwkv_a0, rwkv_a_up, rwkv_k_k, rwkv_k_a, rwkv_r_k, rwkv_ln_w, rwkv_ln_b):
    f = lambda a: np.ascontiguousarray(np.asarray(a, dtype=np.float32))
    if "nc" not in _CACHE:
        _CACHE["nc"] = build_program()
    nc = _CACHE["nc"]
    oh = _onehot_const()
    in_maps = []
    for c in range(NCORES):
        sl = slice(4 * c, 4 * c + 4)
        in_maps.append({
            "x_p": f(x_prompt[c]),
            "x_s": f(x_sample[sl]).reshape(16, D),
            "mem_p": f(mem_prompt[c]),
            "c_mem": f(cache_mem_kv[:, sl]).reshape(2, 4, 256, 512),
            "c_w0": f(cache_win0[0, sl]).reshape(4, 128, 512),
            "c_w1": f(cache_win1[0, sl]).reshape(4, 512, 512),
            "c_w2": f(cache_win2[0, sl]).reshape(4, 2048, 512),
            "s_wkv": f(state_wkv[0, sl]),
            "s_shift": f(state_shift[0, sl]),
            "norm_pre": f(norm_pre), "norm_post": f(norm_post), "norm_mem": f(norm_mem),
            "w_mem": f(w_mem_kv), "rel_bias": f(rel_bias),
            "w_in_a": f(w_in_a[0]), "w_out_a": f(w_out_a[0]), "w_in_b": f(w_in_b[0]), "w_out_b": f(w_out_b[0]),
            "c_onehot": oh,
            "r_mu": f(rwkv_mu).reshape(1, C_SHIFT), "r_w0": f(rwkv_w0).reshape(1, 768), "r_wup": f(rwkv_w_up).reshape(64, 768),
            "r_a0": f(rwkv_a0).reshape(1, 768), "r_aup": f(rwkv_a_up).reshape(64, 768), "r_kk": f(rwkv_k_k).reshape(1, 768),
            "r_ka": f(rwkv_k_a).reshape(1, 768), "r_rk": f(rwkv_r_k).reshape(1, 768), "r_lnw": f(rwkv_ln_w).reshape(1, 768),
            "r_lnb": f(rwkv_ln_b).reshape(1, 768),
        })
    res = run_bass_kernel_spmd(nc, in_maps, core_ids=list(range(NCORES)))
    R = res.results
    cat = lambda k: np.stack([np.asarray(R[c][k]) for c in range(NCORES)])
    y_prompt = cat("y_p")
    y_sample = cat("y_s").reshape(32, 4, D)
    new_mem = cat("o_mem").transpose(1, 0, 2, 3).reshape(2, 8, 256, 2, 4, 64)
    w0p = cat("o_w0p").reshape(1, 8, 128, 2, 4, 64)
    w1p = cat("o_w1p").reshape(1, 8, 512, 2, 4, 64)
    w2p = cat("o_w2p").reshape(1, 8, 2048, 2, 4, 64)
    w0s = cat("o_w0s").reshape(1, 32, 4, 2, 4, 64)
    w1s = cat("o_w1s").reshape(1, 32, 4, 2, 4, 64)
    w2s = cat("o_w2s").reshape(1, 32, 4, 2, 4, 64)
    wkvp = cat("o_wkvp").reshape(1, 8, 12, 64, 64)
    wkvs = cat("o_wkvs").reshape(1, 32, 12, 64, 64)
    shp = cat("o_shp").reshape(1, 8, C_SHIFT)
    shs = cat("o_shs").reshape(1, 32, C_SHIFT)
    outs = (y_prompt, y_sample, new_mem, w0p, w1p, w2p, w0s, w1s, w2s, wkvp, wkvs, shp, shs)
    return tuple(np.ascontiguousarray(o.astype(np.float32)) for o in outs)
```

```python
import math
from contextlib import ExitStack
import numpy as np
import concourse.bass as bass
import concourse.mybir as mybir
from concourse.bass_utils import run_bass_kernel_spmd

F32 = mybir.dt.float32
BF16 = mybir.dt.bfloat16
AF = mybir.ActivationFunctionType
ALU = mybir.AluOpType
AX = mybir.AxisListType

ENGS = ("pe", "act", "dve", "pool", "sp")
NCORES = 8
T = 2048
D = 1024
NT = 16
NEG = -30000.0
SCALE = 0.125
DIL = (1, 4, 16)
RMS_EPS = 1e-6
C_SHIFT = 2432
A_IN = 3072
B_IN = 3712


class Tk:
    __slots__ = ("name", "lw", "rd", "dsem", "dcount", "excl")

    def __init__(self, name="", excl=False):
        self.name = name
        self.excl = excl
        self.lw = None
        self.rd = []
        self.dsem = None
        self.dcount = 0


class Op:
    __slots__ = ("eng", "fn", "deps", "is_dma", "pos", "signal", "cnt", "sem_tk", "waits", "line")


class Prog:
    def __init__(self, nc):
        self.nc = nc
        self.ops = []
        self.eng_ops = {e: [] for e in ENGS}
        self.last_dma = {}

    def op(self, eng, fn, reads=(), writes=(), dma=False, sem_tk=None, extra_deps=()):
        if len(self.ops) >= getattr(self, "maxops", 10 ** 9):
            return None
        o = Op()
        import sys as _sys
        o.line = _sys._getframe(1).f_lineno
        o.eng = eng
        o.fn = fn
        o.is_dma = dma
        o.signal = dma
        o.cnt = 0
        o.waits = []
        deps = []
        for r in reads:
            if r.lw is not None:
                deps.append((r.lw, "raw"))
            if r.excl:
                for rr in r.rd:
                    deps.append((rr, "war"))
        for w in writes:
            if w.lw is not None:
                deps.append((w.lw, "waw"))
            for rr in w.rd:
                deps.append((rr, "war"))
        for xd in extra_deps:
            deps.append((xd, "raw"))
        fdeps = []
        seen = set()
        for d, kind in deps:
            if d is o:
                continue
            if (not d.is_dma) and d.eng == eng and (not dma):
                if eng == "pe" or kind != "raw":
                    continue
            if id(d) in seen:
                continue
            seen.add(id(d))
            fdeps.append(d)
        o.deps = fdeps
        for r in (reads if fn is not None else ()):
            if r.excl:
                r.rd = [o]
            else:
                r.rd.append(o)
        for w in (writes if fn is not None else ()):
            w.lw = o
            w.rd = []
        if dma:
            if sem_tk is None:
                sem_tk = (list(writes) + list(reads))[0]
            o.sem_tk = sem_tk
            sem_tk.dcount += 1
            o.cnt = sem_tk.dcount * 16
            self.last_dma[id(sem_tk)] = o
        else:
            o.sem_tk = None
        o.pos = len(self.eng_ops[eng])
        self.eng_ops[eng].append(o)
        self.ops.append(o)
        return o

    def finalize_and_emit(self, stack):
        nc = self.nc
        waited_pos = {e: {p: -1 for p in ENGS} for e in ENGS}
        waited_dma = {e: {} for e in ENGS}
        for o in self.ops:
            e = o.eng
            for d in o.deps:
                if d.is_dma:
                    key = id(d.sem_tk)
                    if waited_dma[e].get(key, 0) >= d.cnt:
                        continue
                    waited_dma[e][key] = d.cnt
                    o.waits.append(d)
                else:
                    if waited_pos[e][d.eng] >= d.pos:
                        continue
                    waited_pos[e][d.eng] = d.pos
                    d.signal = True
                    o.waits.append(d)
        for e in ENGS:
            c = 0
            for o in self.eng_ops[e]:
                if not o.is_dma and o.signal:
                    c += 1
                    o.cnt = c
        esem = {e: stack.enter_context(nc.semaphore("es_" + e)) for e in ENGS}
        nd = 0
        for o in self.ops:
            if o.is_dma and o.sem_tk.dsem is None:
                o.sem_tk.dsem = stack.enter_context(nc.semaphore("ds%d" % nd))
                nd += 1
        self.n_dma_sems = nd
        block = stack.enter_context(nc.Block())

        def emit(engobj, elist):
            for o in elist:
                for d in o.waits:
                    if d.is_dma:
                        engobj.wait_ge(d.sem_tk.dsem, d.cnt)
                    else:
                        engobj.wait_ge(esem[d.eng], d.cnt)
                if o.fn is None:
                    continue
                ins = o.fn(engobj)
                if o.is_dma:
                    ins.then_inc(o.sem_tk.dsem, 16)
                elif o.signal:
                    ins.then_inc(esem[o.eng], 1)

        @block.tensor
        def _(pe):
            emit(pe, self.eng_ops["pe"])

        @block.scalar
        def _(act):
            emit(act, self.eng_ops["act"])

        @block.vector
        def _(dve):
            emit(dve, self.eng_ops["dve"])

        @block.gpsimd
        def _(pool):
            emit(pool, self.eng_ops["pool"])

        @block.sync
        def _(sp):
            emit(sp, self.eng_ops["sp"])
            done = set()
            for o in self.ops:
                if o.is_dma and id(o.sem_tk) not in done:
                    done.add(id(o.sem_tk))
                    sp.wait_ge(o.sem_tk.dsem, 16 * o.sem_tk.dcount)


def _t5_bucket_np(dist):
    dist = np.asarray(dist, dtype=np.int32)
    d = np.maximum(dist, 1).astype(np.float32)
    large = 16 + (np.log(d / np.float32(16.0)) / np.float32(math.log(2048 / 16)) * np.float32(16.0)).astype(np.int32)
    large = np.minimum(large, 31)
    return np.where(dist < 16, dist, large)


def _onehot_const():
    oh = np.zeros((33, 3, 384), np.float32)
    for g in range(3):
        for u in range(383):
            dist = u - 127
            if 0 <= dist <= 128:
                b = int(_t5_bucket_np(DIL[g] * dist))
                oh[b, g, u] = 1.0
            else:
                oh[32, g, u] = NEG
        oh[32, g, 383] = NEG
    return oh.reshape(33, 3 * 384)


STAGE = 2
DBG = set()


def build_program():
    nc = bass.Bass("TRN2", target_bir_lowering=False)
    try:
        nc.allow_low_precision("bf16 matmul operands with fp32 accumulation (matches problem tolerance)")
    except Exception:
        pass
    try:
        nc.allow_non_contiguous_dma("strided window / toeplitz accesses")
    except Exception:
        pass
    P = Prog(nc)
    for _d in DBG:
        if _d.startswith('maxops='):
            P.maxops = int(_d.split('=')[1])

    def din(name, shape):
        return nc.dram_tensor(name, list(shape), F32, kind="ExternalInput").ap()

    def dout(name, shape):
        return nc.dram_tensor(name, list(shape), F32, kind="ExternalOutput").ap()

    x_p = din("x_p", [T, D])
    x_s = din("x_s", [16, D])
    mem_p = din("mem_p", [256, D])
    c_mem = din("c_mem", [2, 4, 256, 512])
    c_w0 = din("c_w0", [4, 128, 512])
    c_w1 = din("c_w1", [4, 512, 512])
    c_w2 = din("c_w2", [4, 2048, 512])
    s_wkv = din("s_wkv", [4, 12, 64, 64])
    s_shift = din("s_shift", [4, C_SHIFT])
    norm_pre = din("norm_pre", [2, D])
    norm_post = din("norm_post", [2, D])
    norm_mem = din("norm_mem", [2, D])
    w_mem = din("w_mem", [2, D, 512])
    rel_bias = din("rel_bias", [32, 12])
    w_in_a = din("w_in_a", [D, A_IN])
    w_out_a = din("w_out_a", [512, D])
    w_in_b = din("w_in_b", [D, B_IN])
    w_out_b = din("w_out_b", [D, D])
    onehot = din("c_onehot", [33, 3 * 384])
    r_mu = din("r_mu", [1, C_SHIFT])
    r_w0 = din("r_w0", [1, 768])
    r_wup = din("r_wup", [64, 768])
    r_a0 = din("r_a0", [1, 768])
    r_aup = din("r_aup", [64, 768])
    r_kk = din("r_kk", [1, 768])
    r_ka = din("r_ka", [1, 768])
    r_rk = din("r_rk", [1, 768])
    r_lnw = din("r_lnw", [1, 768])
    r_lnb = din("r_lnb", [1, 768])
    y_p = dout("y_p", [T, D])
    y_s = dout("y_s", [16, D])
    o_mem = dout("o_mem", [2, 256, 512])
    o_w0p = dout("o_w0p", [128, 512])
    o_w1p = dout("o_w1p", [512, 512])
    o_w2p = dout("o_w2p", [2048, 512])
    o_w0s = dout("o_w0s", [16, 512])
    o_w1s = dout("o_w1s", [16, 512])
    o_w2s = dout("o_w2s", [16, 512])
    o_wkvp = dout("o_wkvp", [12, 64, 64])
    o_wkvs = dout("o_wkvs", [4, 12, 64, 64])
    o_shp = dout("o_shp", [1, C_SHIFT])
    o_shs = dout("o_shs", [4, C_SHIFT])
    x1_d = nc.dram_tensor("x1_d", [T, D], F32, kind="Internal").ap()
    x1s_d = nc.dram_tensor("x1s_d", [16, D], F32, kind="Internal").ap()
    e_d = nc.dram_tensor("e_d", [12, 3 * 384], F32, kind="Internal").ap()
    out_tokens = []

    with ExitStack() as st:
        def sb(name, shape, dt):
            return st.enter_context(nc.sbuf_tensor(name, list(shape), dt))

        def psum(name, shape, dt):
            return st.enter_context(nc.psum_tensor(name, list(shape), dt))

        ps_mm = [psum("ps_mm%d" % i, [128, 512], F32) for i in range(2)]
        ps_s = [psum("ps_s%d" % i, [128, 512], F32) for i in range(2)]
        ps_o = [psum("ps_o%d" % i, [128, 512], F32) for i in range(2)]
        ps_t = psum("ps_t", [128, 1024], BF16)
        ps_x = psum("ps_x", [128, 512], F32)
        t_ps_mm = [Tk("ps_mm0", True), Tk("ps_mm1", True)]
        t_ps_s = [Tk("ps_s0", True), Tk("ps_s1", True)]
        t_ps_o = [Tk("ps_o0", True), Tk("ps_o1", True)]
        t_ps_t = Tk("ps_t", True)
        t_ps_x = Tk("ps_x", True)
        rr = {"mm": 0, "s": 0, "o": 0, "ev": 0}

        def nxt(k):
            rr[k] += 1
            return rr[k] & 1

        def ev_eng():
            rr["ev"] += 1
            return "act" if (rr["ev"] & 1) else "dve"

        def copy_op(eng, out, in_, reads, writes):
            if eng == "act":
                P.op("act", lambda e: e.copy(out=out, in_=in_), reads=reads, writes=writes)
            else:
                P.op(eng, lambda e: e.tensor_copy(out=out, in_=in_), reads=reads, writes=writes)

        identf = sb("identf", [128, 128], F32)
        ident = sb("ident", [128, 128], BF16)
        t_ident = Tk()

        P.op("pool", lambda e: e.memset(identf[:], 0.0), writes=[t_ident])
        P.op("pool", lambda e: e.affine_select(out=identf[:], in_=identf[:], pattern=[[-1, 128]], compare_op=ALU.not_equal,
                                               fill=1.0, base=0, channel_multiplier=1), reads=[t_ident], writes=[t_ident])
        P.op("dve", lambda e: e.tensor_copy(out=ident[:], in_=identf[:]), reads=[t_ident], writes=[t_ident])

        ARENA_WORDS = 49000
        arena = sb("arena", [128, ARENA_WORDS], F32)
        gains4 = sb("gains", [128, 2, 1024], F32)
        gmem = arena[:, 0:2048].rearrange("p (a d) -> p a d", a=2)
        barsb = sb("barsb", [128, 16], F32)
        t_bar = {e: Tk() for e in ("pe", "act", "dve", "pool")}
        t_barinit = Tk()
        P.op("pool", lambda e: e.memset(barsb[:, :], 0.0), writes=[t_barinit])

        def barrier():
            dmas = list(P.last_dma.values())
            P.op("pe", lambda e: e.transpose(out=ps_t[0:16, 0:16], in_=ident[0:16, 0:16], identity=ident[0:16, 0:16]),
                 reads=[t_ident], writes=[t_ps_t, t_bar["pe"]], extra_deps=dmas)
            P.op("act", lambda e: e.copy(out=barsb[:, 0:1], in_=barsb[:, 8:9]), reads=[t_barinit], writes=[t_bar["act"]], extra_deps=dmas)
            P.op("dve", lambda e: e.tensor_copy(out=barsb[:, 1:2], in_=barsb[:, 9:10]), reads=[t_barinit], writes=[t_bar["dve"]], extra_deps=dmas)
            P.op("pool", lambda e: e.memset(barsb[:, 2:3], 0.0), writes=[t_bar["pool"]], extra_deps=dmas)
            allb = list(t_bar.values())
            P.op("pe", lambda e: e.transpose(out=ps_t[0:16, 0:16], in_=ident[0:16, 0:16], identity=ident[0:16, 0:16]),
                 reads=[t_ident] + allb, writes=[t_ps_t])
            P.op("act", lambda e: e.copy(out=barsb[:, 3:4], in_=barsb[:, 8:9]), reads=allb)
            P.op("dve", lambda e: e.tensor_copy(out=barsb[:, 4:5], in_=barsb[:, 9:10]), reads=allb)
            P.op("pool", lambda e: e.memset(barsb[:, 5:6], 0.0), reads=allb)
            P.op("sp", None, reads=allb)

        class _G:
            def __getitem__(self, key):
                p, gi, c = key
                return gains4[p, gi, c] if gi < 2 else gmem[p, gi - 4, c]
        gains = _G()
        t_gains = Tk()
        def load_gains(items):
            for (i, src, l) in items:
                ap_b = bass.AP(tensor=src.tensor, offset=l * D, ap=[[0, 128], [1, D]])
                P.op("sp", (lambda e, i=i, ap_b=ap_b: e.dma_start(out=gains[:, i, :], in_=ap_b)), writes=[t_gains], dma=True)
        load_gains([(0, norm_pre, 0), (1, norm_post, 0), (4, norm_mem, 0), (5, norm_mem, 1)])
        G_PRE, G_POST, G_MEM = (0, 0), (1, 1), (4, 5)

        ss = sb("ss", [128, 8], F32)
        junk = sb("junk", [128, 1024], BF16)
        t_junk = Tk()

        def rmsnorm(src_ap, np_, gi, dst_ap, t_src, t_dst, extra_reads=()):
            t_ss = Tk()
            P.op("act", lambda e: e.activation(out=junk[:np_, :], in_=src_ap, func=AF.Square),
                 reads=[t_src], writes=[t_junk])
            P.op("dve", lambda e: e.reduce_sum(out=ss[:np_, 0:1], in_=junk[:np_, :], axis=AX.X),
                 reads=[t_junk], writes=[t_ss])
            P.op("dve", lambda e: e.tensor_scalar(out=ss[:np_, 1:2], in0=ss[:np_, 0:1], scalar1=1.0 / D, scalar2=RMS_EPS,
                                                  op0=ALU.mult, op1=ALU.add), reads=[t_ss], writes=[t_ss])
            P.op("act", lambda e: e.activation(out=ss[:np_, 2:3], in_=ss[:np_, 1:2], func=AF.Sqrt), reads=[t_ss], writes=[t_ss])
            P.op("dve", lambda e: e.reciprocal(out=ss[:np_, 3:4], in_=ss[:np_, 2:3]), reads=[t_ss], writes=[t_ss])
            P.op("dve", lambda e: e.scalar_tensor_tensor(out=dst_ap, in0=src_ap, scalar=ss[:np_, 3:4], in1=gains[:np_, gi, :],
                                                         op0=ALU.mult, op1=ALU.mult),
                 reads=[t_src, t_ss, t_gains] + list(extra_reads), writes=[t_dst])

        apos = [0]
        atop = [ARENA_WORDS]

        def carve_top(nwords):
            atop[0] -= nwords
            assert atop[0] >= apos[0]
            return arena[:, atop[0]:atop[0] + nwords]

        def carve(nbytes):
            w0 = apos[0]
            nw = (nbytes + 3) // 4
            apos[0] += nw
            assert apos[0] <= atop[0], ("arena overflow", apos[0], atop[0])
            return arena[:, w0:w0 + nw]

        def carve_bf(n):
            return carve(2 * n).bitcast(BF16)

        def carve_f(n):
            return carve(4 * n)

        KmT = sb("KmT", [128, 2, 2, 256], BF16)
        Vm = sb("Vm", [128, 2, 2, 4, 80], BF16)
        t_KmT = [Tk(), Tk()]
        t_Vm = [Tk(), Tk()]
        mark = apos[0]
        carve_f(2048)
        memx = carve_f(2 * 1024).rearrange("p (b d) -> p b d", b=2)
        memn = carve_bf(2 * 1024).rearrange("p (b d) -> p b d", b=2)
        memnT = carve_bf(8 * 256).rearrange("p (k m) -> p k m", k=8)
        wm = carve_bf(8 * 512).rearrange("p (k n) -> p k n", k=8)
        kvst = carve_f(2 * 512).rearrange("p (b n) -> p b n", b=2)
        t_memx, t_memn, t_memnT, t_wm, t_kvst = Tk(), Tk(), Tk(), Tk(), [Tk(), Tk()]
        P.op("sp", lambda e: e.dma_start(out=memx, in_=mem_p.rearrange("(b p) d -> p b d", p=128)), writes=[t_memx], dma=True)
        if "novm" not in DBG:
            P.op("pool", lambda e: e.memset(Vm[:, :, :, :, 64:65], 1.0), writes=t_Vm)
        for l in range(0 if "nomem" in DBG else 2):
            P.op("pool", (lambda e, l=l: e.dma_start(out=wm, in_=w_mem[l].rearrange("(k p) n -> p k n", p=128))),
                 writes=[t_wm], dma=True)
            for b in range(2):
                rmsnorm(memx[:, b, :], 128, G_MEM[l], memn[:, b, :], t_memx, t_memn)

                def tr(e, b=b):
                    for k in range(8):
                        ins = e.transpose(out=ps_t[:, k * 128:(k + 1) * 128], in_=memn[:, b, k * 128:(k + 1) * 128], identity=ident[:])
                    return ins
                P.op("pe", tr, reads=[t_memn, t_ident], writes=[t_ps_t])
                copy_op(ev_eng(), memnT[:, :, b * 128:(b + 1) * 128], ps_t[:].rearrange("p (k m) -> p k m", k=8), [t_ps_t], [t_memnT])
            for b in range(2):
                i = nxt("mm")

                def mm(e, b=b, i=i):
                    for k in range(8):
                        ins = e.matmul(ps_mm[i][:, :], lhsT=memnT[:, k, b * 128:(b + 1) * 128], rhs=wm[:, k, :], start=(k == 0), stop=(k == 7))
                    return ins
                P.op("pe", mm, reads=[t_memnT, t_wm], writes=[t_ps_mm[i]])
                P.op("act", (lambda e, b=b, i=i: e.copy(out=kvst[:, b, :], in_=ps_mm[i][:, :])), reads=[t_ps_mm[i]], writes=[t_kvst[b]])
                P.op("dve", (lambda e, b=b, i=i, l=l: e.tensor_copy(out=Vm[:, l, b, :, 0:64],
                                                                    in_=ps_mm[i][:, 256:512].rearrange("p (h d) -> p h d", h=4))),
                     reads=[t_ps_mm[i]], writes=[t_Vm[l]])
                P.op("sp", (lambda e, b=b, l=l: e.dma_start(out=o_mem[l, b * 128:(b + 1) * 128, :], in_=kvst[:, b, :])),
                     reads=[t_kvst[b]], dma=True)
                out_tokens.append(t_kvst[b])
            for pr in range(2):
                i = nxt("mm")

                def mmk(e, pr=pr, i=i):
                    for k in range(8):
                        ins = e.matmul(ps_mm[i][:, 0:256], lhsT=wm[:, k, pr * 128:(pr + 1) * 128], rhs=memnT[:, k, :], start=(k == 0), stop=(k == 7))
                    return ins
                P.op("pe", mmk, reads=[t_memnT, t_wm], writes=[t_ps_mm[i]])
                copy_op(ev_eng(), KmT[:, l, pr, :], ps_mm[i][:, 0:256], [t_ps_mm[i]], [t_KmT[l]])


        biasT = carve_top(12 * 2 * 128).rearrange("p (a b q) -> p a b q", a=12, b=2)
        t_biasT = Tk()
        M1e = carve_top(264).bitcast(BF16)
        M1o = carve_top(264).bitcast(BF16)
        M2e = carve_top(1032).bitcast(BF16)
        M2o = carve_top(1032).bitcast(BF16)
        t_M = Tk()
        mark = apos[0]
        rb33 = carve_f(12)
        oh33 = carve_f(1152)
        esb = carve_f(1152)
        mtmp = carve_f(2064)
        t_rb, t_oh, t_esb, t_mtmp, t_ed = Tk(), Tk(), Tk(), Tk(), Tk()
        P.op("pool", lambda e: e.memset(rb33[32:33, :], 1.0), writes=[t_rb])
        P.op("sp", lambda e: e.dma_start(out=rb33[0:32, :], in_=rel_bias), writes=[t_rb], dma=True)
        P.op("sp", lambda e: e.dma_start(out=oh33[0:33, :], in_=onehot), writes=[t_oh], dma=True)
        for g in range(3):
            P.op("pe", (lambda e, g=g: e.matmul(ps_x[0:12, 0:384], lhsT=rb33[0:33, :], rhs=oh33[0:33, g * 384:(g + 1) * 384], start=True, stop=True)),
                 reads=[t_rb, t_oh], writes=[t_ps_x])
            P.op("act", (lambda e, g=g: e.copy(out=esb[0:12, g * 384:(g + 1) * 384], in_=ps_x[0:12, 0:384])), reads=[t_ps_x], writes=[t_esb])
        P.op("sp", lambda e: e.dma_start(out=e_d, in_=esb[0:12, :]), reads=[t_esb], writes=[t_ed], dma=True, sem_tk=t_esb)
        bstage = carve_f(24 * 128).rearrange("p (a q) -> p a q", a=24)
        Jrev = carve_f(128)
        t_bst, t_J = Tk(), Tk()

        P.op("pool", lambda e: e.memset(Jrev[:, :], 0.0), writes=[t_J])
        P.op("pool", lambda e: e.affine_select(out=Jrev[:, :], in_=Jrev[:, :], pattern=[[1, 128]], compare_op=ALU.not_equal, fill=1.0, base=-127,
                                               channel_multiplier=1), reads=[t_J], writes=[t_J])
        for g in range(3):
            for h in range(4):
                for blk in range(2):
                    off = (4 * g + h) * 1152 + g * 384 + (128 if blk == 0 else 0)
                    src = bass.AP(tensor=e_d.tensor, offset=off, ap=[[1, 128], [1, 128]])
                    P.op("sp", (lambda e, g=g, h=h, blk=blk, src=src: e.dma_start(out=bstage[:, (g * 4 + h) * 2 + blk, :], in_=src)),
                         reads=[t_ed], writes=[t_bst], dma=True)
        for a4 in range(6):
            i = nxt("mm")
            P.op("pe", (lambda e, a4=a4, i=i: e.matmul(ps_mm[i][:, :], lhsT=Jrev[:, :], rhs=bstage[:, 4 * a4:4 * a4 + 4, :].rearrange("p a q -> p (a q)"),
                                                         start=True, stop=True)), reads=[t_J, t_bst], writes=[t_ps_mm[i]])
            copy_op(ev_eng(), biasT.rearrange("p a b q -> p (a b q)")[:, 512 * a4:512 * (a4 + 1)], ps_mm[i][:, :], [t_ps_mm[i]], [t_biasT])
        for (Mt, ncol, base, cm) in ((M1e, 528, 16, 4), (M1o, 528, 15, 4), (M2e, 2064, 16, 16), (M2o, 2064, 15, 16)):
            P.op("pool", (lambda e, ncol=ncol: e.memset(mtmp[:, 0:ncol], 0.0)), writes=[t_mtmp])
            P.op("pool", (lambda e, ncol=ncol, base=base, cm=cm: e.affine_select(out=mtmp[:, 0:ncol], in_=mtmp[:, 0:ncol], pattern=[[-1, ncol]],
                                                                                compare_op=ALU.not_equal, fill=1.0, base=base, channel_multiplier=cm)),
                 reads=[t_mtmp], writes=[t_mtmp])
            P.op("dve", (lambda e, Mt=Mt, ncol=ncol: e.tensor_copy(out=Mt[:, :], in_=mtmp[:, 0:ncol])), reads=[t_mtmp], writes=[t_M])

        def perm_lhsT(g, r, Tq):
            if g == 1:
                Tp = Tq % 4
                if r % 2 == 0:
                    s0 = 16 + 128 * Tp - r
                    return M1e[:, s0:s0 + 128]
                s0 = 15 + 128 * Tp - r
                return M1o[:, s0:s0 + 128]
            if r % 2 == 0:
                s0 = 16 + 128 * Tq - r
                return M2e[:, s0:s0 + 128]
            s0 = 15 + 128 * Tq - r
            return M2o[:, s0:s0 + 128]

        barrier()
        apos[0] = 0
        L0 = apos[0]
        xnT = carve_bf(8 * 2048).rearrange("p (k t) -> p k t", k=8)
        xnTs = carve_bf(8 * 16).rearrange("p (k t) -> p k t", k=8)
        NW = 4
        wsl = [carve_bf(8 * 256).rearrange("p (k n) -> p k n", k=8) for _ in range(NW)]
        t_wsl = [Tk() for _ in range(NW)]
        xld = [carve_f(1024) for _ in range(2)]
        t_xld = [Tk(), Tk()]
        xnb = [carve_bf(1024) for _ in range(2)]
        t_xnb = [Tk(), Tk()]
        t_xnT = [Tk() for _ in range(NT)]
        t_xnTs = Tk()
        QT = carve_bf(2 * 2048).rearrange("p (m t) -> p m t", m=2)
        KT = carve_bf(2 * 2048).rearrange("p (m t) -> p m t", m=2)
        QmT = carve_bf(2 * 2048).rearrange("p (m t) -> p m t", m=2)
        t_QT = [[Tk() for _ in range(4)] for _ in range(2)]
        t_KT = [[Tk() for _ in range(4)] for _ in range(2)]
        t_QmT = [[Tk() for _ in range(4)] for _ in range(2)]
        QTs = carve_bf(8 * 16).rearrange("p (m t) -> p m t", m=8)
        KTs = carve_bf(6 * 16).rearrange("p (m t) -> p m t", m=6)
        t_QTs, t_KTs = Tk(), Tk()
        gate = carve_bf(16 * 512).rearrange("p (i n) -> p i n", i=16)
        t_gate = [Tk() for _ in range(NT)]
        gate_s = carve_bf(4 * 512).rearrange("p (b n) -> p b n", b=4)
        t_gate_s = Tk()
        Vg = carve_bf(16 * 4 * 80).rearrange("p (i h d) -> p i h d", i=16, h=4)
        t_Vg = [Tk() for _ in range(NT)]
        Vns = carve_bf(4 * 3 * 4 * 80).rearrange("p (b g h d) -> p b g h d", b=4, g=3, h=4)
        t_Vns = Tk()
        Og = carve_bf(48 * 260).rearrange("p (i n) -> p i n", i=48)
        t_Og = [Tk() for _ in range(48)]
        wst = [carve_f(512) for _ in range(2)]
        t_wst = [Tk(), Tk()]
        wins = carve_f(3 * 512).rearrange("p (g n) -> p g n", g=3)
        t_wins = Tk()
        sbf = [carve_f(256) for _ in range(2)]
        t_sbf = [Tk(), Tk()]
        PT = [carve_bf(256) for _ in range(2)]
        t_PT = [Tk(), Tk()]
        rr.update({"w": 0, "sb": 0, "pt": 0, "wst": 0})

        P.op("pool", lambda e: e.memset(Vg[:, :, :, 64:65], 1.0), writes=t_Vg)
        P.op("pool", lambda e: e.memset(Vns[0:4, :, :, :, 64:65], 1.0), writes=[t_Vns])

        def phaseA(src_dram, gi, lname):
            for i in range(NT + 1):
                s_ = i & 1
                np_ = 128 if i < NT else 16
                if i < NT:
                    P.op("sp", (lambda e, i=i, s_=s_: e.dma_start(out=xld[s_][:, :], in_=src_dram[0][i * 128:(i + 1) * 128, :])),
                         writes=[t_xld[s_]], dma=True)
                else:
                    P.op("sp", (lambda e, s_=s_: e.dma_start(out=xld[s_][0:16, :], in_=src_dram[1])), writes=[t_xld[s_]], dma=True)
                rmsnorm(xld[s_][0:np_, :], np_, gi, xnb[s_][0:np_, :], t_xld[s_], t_xnb[s_])

                def tr(e, s_=s_, np_=np_):
                    for k in range(8):
                        ins = e.transpose(out=ps_t[:, k * 128:k * 128 + np_], in_=xnb[s_][0:np_, k * 128:(k + 1) * 128], identity=ident[0:np_, 0:np_])
                    return ins
                P.op("pe", tr, reads=[t_xnb[s_], t_ident], writes=[t_ps_t])
                if i < NT:
                    copy_op(ev_eng(), xnT[:, :, i * 128:(i + 1) * 128], ps_t[:].rearrange("p (k m) -> p k m", k=8), [t_ps_t], [t_xnT[i]])
                else:
                    copy_op(ev_eng(), xnTs[:, :, :], ps_t[:].rearrange("p (k m) -> p k m", k=8)[:, :, 0:16], [t_ps_t], [t_xnTs])

        phaseA((x_p, x_s), G_PRE[0], "l0")

        chunk_cols = []
        for g in range(3):
            chunk_cols += [("q", g, 256 * g), ("k", g, 768 + 256 * g), ("v", g, 1536 + 256 * g)]
        chunk_cols += [("qm", 0, 2304), ("gate", 0, 2560), ("gate", 1, 2816)]
        wstate = {"loaded": 0}

        def load_w(n):
            if n >= len(chunk_cols) or n < wstate["loaded"]:
                return
            assert n == wstate["loaded"]
            wstate["loaded"] += 1
            c0 = chunk_cols[n][2]
            sl_ = n % NW
            P.op("pool", (lambda e, c0=c0, sl_=sl_: e.dma_start(out=wsl[sl_], in_=w_in_a[:, c0:c0 + 256].rearrange("(k p) n -> p k n", p=128))),
                 writes=[t_wsl[sl_]], dma=True)

        def proj_fm(sl_, dst, t_dst, sdst, t_sdst, smb0):
            for mb in range(2):
                for tg in range(4):
                    i = nxt("mm")

                    def mm(e, mb=mb, tg=tg, i=i):
                        for k in range(8):
                            ins = e.matmul(ps_mm[i][:, :], lhsT=wsl[sl_][:, k, mb * 128:(mb + 1) * 128], rhs=xnT[:, k, tg * 512:(tg + 1) * 512],
                                           start=(k == 0), stop=(k == 7))
                        return ins
                    P.op("pe", mm, reads=[t_wsl[sl_]] + t_xnT[4 * tg:4 * tg + 4], writes=[t_ps_mm[i]])
                    copy_op(ev_eng(), dst[:, mb, tg * 512:(tg + 1) * 512], ps_mm[i][:, :], [t_ps_mm[i]], [t_dst[mb][tg]])
                def mms(e, mb=mb):
                    for k in range(8):
                        ins = e.matmul(ps_x[:, 0:16], lhsT=wsl[sl_][:, k, mb * 128:(mb + 1) * 128], rhs=xnTs[:, k, :], start=(k == 0), stop=(k == 7))
                    return ins
                P.op("pe", mms, reads=[t_wsl[sl_], t_xnTs], writes=[t_ps_x])
                copy_op(ev_eng(), sdst[:, smb0 + mb, :], ps_x[:, 0:16], [t_ps_x], [t_sdst])

        def proj_tm(sl_, tok_ap_fn, reads, evac_fn):
            i = nxt("mm")

            def mm(e, i=i):
                for k in range(8):
                    ins = e.matmul(ps_mm[i][:, 0:256], lhsT=tok_ap_fn(k), rhs=wsl[sl_][:, k, :], start=(k == 0), stop=(k == 7))
                return ins
            P.op("pe", mm, reads=[t_wsl[sl_]] + list(reads), writes=[t_ps_mm[i]])
            evac_fn(ps_mm[i], t_ps_mm[i])

        def proj_tm_s(sl_, M, c0, evac_fn):
            def mm(e):
                for k in range(8):
                    ins = e.matmul(ps_x[0:M, 0:256], lhsT=xnTs[:, k, c0:c0 + M], rhs=wsl[sl_][:, k, :], start=(k == 0), stop=(k == 7))
                return ins
            P.op("pe", mm, reads=[t_wsl[sl_], t_xnTs], writes=[t_ps_x])
            evac_fn()

        def grp_tiles(g):
            d = DIL[g]
            nsb = NT // d
            return [(r, sb_) for r in range(d) for sb_ in range(nsb)]

        def tile_tok_slice(g, r, sb_):
            d = DIL[g]
            base = d * 128 * sb_ + r
            return slice(base, base + d * 127 + 1, d)

        def nat_tiles_of(g, r, sb_):
            d = DIL[g]
            return list(range(d * sb_, d * sb_ + d))

        win_out = (o_w0p, o_w1p, o_w2p)
        win_s_out = (o_w0s, o_w1s, o_w2s)
        load_w(0)
        load_w(1)
        load_w(2)
        def win_rows(g, r):
            dd = DIL[g]
            if dd == 1:
                return win_out[g]
            return win_out[g].rearrange("(j r) c -> r j c", r=dd)[r]

        def do_group(g):
            d = DIL[g]
            nsb = NT // d
            tiles = grp_tiles(g)
            n = 3 * g
            load_w(n + 3)
            proj_fm(n % NW, QT, t_QT, QTs, t_QTs, 2 * g)
            n = 3 * g + 1
            load_w(n + 3)
            proj_fm(n % NW, KT, t_KT, KTs, t_KTs, 2 * g)
            for r in range(d):
                sb_ = nsb - 1
                tsl = tile_tok_slice(g, r, sb_)

                def evk(pst, tps, r=r):
                    ws_ = nxt("wst")
                    P.op("act", lambda e: e.copy(out=wst[ws_][:, 0:256], in_=pst[:, 0:256]), reads=[tps], writes=[t_wst[ws_]])
                    rows = win_rows(g, r)
                    P.op("sp", lambda e: e.dma_start(out=rows[:, 0:256], in_=wst[ws_][:, 0:256]), reads=[t_wst[ws_]], dma=True)
                proj_tm(n % NW, (lambda k, tsl=tsl: xnT[:, k, tsl]), [t_xnT[j] for j in nat_tiles_of(g, r, sb_)], evk)

            def evks():
                P.op("act", lambda e: e.copy(out=wins[0:16, g, 0:256], in_=ps_x[0:16, 0:256]), reads=[t_ps_x], writes=[t_wins])
            proj_tm_s(n % NW, 16, 0, evks)
            n = 3 * g + 2
            load_w(n + 3)
            for gi_, (r, sb_) in enumerate(tiles):
                tsl = tile_tok_slice(g, r, sb_)
                is_win = (sb_ == nsb - 1)

                def evv(pst, tps, gi_=gi_, is_win=is_win, r=r):
                    P.op("dve", lambda e: e.tensor_copy(out=Vg[:, gi_, :, 0:64], in_=pst[:, 0:256].rearrange("p (h d) -> p h d", h=4)),
                         reads=[tps], writes=[t_Vg[gi_]])
                    if is_win:
                        ws_ = nxt("wst")
                        P.op("act", lambda e: e.copy(out=wst[ws_][:, 256:512], in_=pst[:, 0:256]), reads=[tps], writes=[t_wst[ws_]])
                        rows = win_rows(g, r)
                        P.op("sp", lambda e: e.dma_start(out=rows[:, 256:512], in_=wst[ws_][:, 256:512]), reads=[t_wst[ws_]], dma=True)
                proj_tm(n % NW, (lambda k, tsl=tsl: xnT[:, k, tsl]), [t_xnT[j] for j in nat_tiles_of(g, r, sb_)], evv)

            def evvs():
                P.op("act", lambda e: e.copy(out=wins[0:16, g, 256:512], in_=ps_x[0:16, 0:256]), reads=[t_ps_x], writes=[t_wins])
            proj_tm_s(n % NW, 16, 0, evvs)
            P.op("sp", lambda e: e.dma_start(out=win_s_out[g], in_=wins[0:16, g, :]), reads=[t_wins], dma=True)
            for bb in range(4):
                def evvn(bb=bb):
                    P.op("dve", lambda e: e.tensor_copy(out=Vns[0:4, bb, g, :, 0:64], in_=ps_x[0:4, 0:256].rearrange("p (h d) -> p h d", h=4)),
                         reads=[t_ps_x], writes=[t_Vns])
                proj_tm_s(n % NW, 4, 4 * bb, evvn)
            allqk = [x for row in t_QT for x in row] + [x for row in t_KT for x in row]
            pending = []
            for gi_, (r, sb_) in enumerate(tiles):
                qsl = tile_tok_slice(g, r, sb_)
                blocks = ([(0, (r, sb_ - 1))] if sb_ > 0 else []) + [(1, (r, sb_))]
                io = nxt("o")
                for h in range(4):
                    pr, hf = h // 2, h % 2
                    psl = slice(64 * hf, 64 * hf + 64)
                    isx = nxt("s")

                    def qk(e, isx=isx, pr=pr, psl=psl, qsl=qsl, blocks=blocks):
                        for (blk, (kr, ksb)) in blocks:
                            ksl = tile_tok_slice(g, kr, ksb)
                            ins = e.matmul(ps_s[isx][:, blk * 128:(blk + 1) * 128], lhsT=KT[psl, pr, ksl], rhs=QT[psl, pr, qsl], start=True, stop=True)
                        return ins
                    P.op("pe", qk, reads=allqk, writes=[t_ps_s[isx]])
                    b0 = blocks[0][0]
                    csl = slice(b0 * 128, 256)
                    isb = nxt("sb")
                    P.op("dve", (lambda e, isx=isx, isb=isb, csl=csl, h=h, b0=b0: e.scalar_tensor_tensor(
                        out=sbf[isb][:, csl], in0=ps_s[isx][:, csl], scalar=SCALE,
                        in1=biasT[:, g * 4 + h, b0:2, :].rearrange("p b q -> p (b q)"), op0=ALU.mult, op1=ALU.add)),
                        reads=[t_ps_s[isx], t_biasT], writes=[t_sbf[isb]])
                    ipt = nxt("pt")
                    P.op("act", (lambda e, isb=isb, ipt=ipt, csl=csl: e.activation(out=PT[ipt][:, csl], in_=sbf[isb][:, csl], func=AF.Exp)),
                         reads=[t_sbf[isb]], writes=[t_PT[ipt]])
                    if pending:
                        pending.pop(0)()

                    def later(ipt=ipt, io=io, h=h, blocks=blocks, gi_=gi_):
                        def pv(e):
                            nb = len(blocks)
                            for bi, (blk, (kr, ksb)) in enumerate(blocks):
                                kgi = kr * nsb + ksb
                                ins = e.matmul(ps_o[io][:, h * 65:(h + 1) * 65], lhsT=PT[ipt][:, blk * 128:(blk + 1) * 128], rhs=Vg[:, kgi, h, 0:65],
                                               start=(bi == 0), stop=(bi == nb - 1))
                            return ins
                        P.op("pe", pv, reads=[t_PT[ipt]] + [t_Vg[kr * nsb + ksb] for (_, (kr, ksb)) in blocks], writes=[t_ps_o[io]])
                        if h == 3:
                            copy_op(ev_eng(), Og[:, 16 * g + gi_, :], ps_o[io][:, 0:260], [t_ps_o[io]], [t_Og[16 * g + gi_]])
                    pending.append(later)
            while pending:
                pending.pop(0)()

        for g in range(3):
            do_group(g)

        n = 9
        load_w(n + 3)
        proj_fm(n % NW, QmT, t_QmT, QTs, t_QTs, 6)
        def do_gate(gc):
            n = 10 + gc
            load_w(n + 3)
            for i in range(NT):
                def evg(pst, tps, i=i, gc=gc):
                    P.op("act", lambda e: e.activation(out=gate[:, i, gc * 256:(gc + 1) * 256], in_=pst[:, 0:256], func=AF.Silu),
                         reads=[tps], writes=[t_gate[i]])
                proj_tm(n % NW, (lambda k, i=i: xnT[:, k, i * 128:(i + 1) * 128]), [t_xnT[i]], evg)
            for bb in range(4):
                def evgs(bb=bb, gc=gc):
                    P.op("act", lambda e: e.activation(out=gate_s[0:4, bb, gc * 256:(gc + 1) * 256], in_=ps_x[0:4, 0:256], func=AF.Silu),
                         reads=[t_ps_x], writes=[t_gate_s])
                proj_tm_s(n % NW, 4, 4 * bb, evgs)
        for gc in range(2):
            do_gate(gc)

        barrier()
        l0_keep = apos[0]
        apos[0] = L0
        wout = carve_bf(4 * 1024).rearrange("p (k n) -> p k n", k=4)
        t_wout = Tk()
        P.op("pool", lambda e: e.dma_start(out=wout, in_=w_out_a.rearrange("(k p) n -> p k n", p=128)), writes=[t_wout], dma=True)
        hbuf = [carve_f(512) for _ in range(2)]
        t_hbuf = [Tk(), Tk()]
        hb = [carve_bf(512) for _ in range(2)]
        t_hb = [Tk(), Tk()]
        hT = [carve_bf(512).rearrange("p (k t) -> p k t", k=4) for _ in range(2)]
        t_hT = [Tk(), Tk()]
        rcp = [carve_f(8) for _ in range(2)]
        t_rcp = [Tk(), Tk()]
        ytmp = [carve_f(1024) for _ in range(2)]
        t_ytmp = [Tk(), Tk()]
        cst = carve_f(2048).rearrange("p (t n) -> p t n", t=4)
        ckb = carve_bf(4 * 256).rearrange("p (t n) -> p t n", t=4)
        cKT = carve_bf(8 * 128).rearrange("p (i n) -> p i n", i=8)
        cV = carve_bf(4 * 4 * 80).rearrange("p (t h d) -> p t h d", t=4, h=4)
        PTz = carve_bf(16)
        PTn = carve_bf(16)
        sbn = carve_f(16)
        t_cst, t_ckb, t_cKT, t_cV, t_PTz, t_PTn, t_sbn = Tk(), Tk(), Tk(), Tk(), Tk(), Tk(), Tk()
        assert apos[0] <= L0 + 8192 + 64 + 4096, apos[0] - L0
        apos[0] = l0_keep
        xres = xld
        t_xres = t_xld
        t_x1 = [Tk() for _ in range(NT + 1)]
        rr.update({"hb": 0})
        CTX = dict(PT=PT, t_PT=t_PT, ytmp=ytmp, t_ytmp=t_ytmp, xres=xres, t_xres=t_xres,
                   cst=cst, ckb=ckb, cKT=cKT, cV=cV, t_cst=t_cst, t_ckb=t_ckb, t_cKT=t_cKT, t_cV=t_cV)
        P.op("pool", lambda e: e.memset(cV[:, :, :, 64:65], 1.0), writes=[t_cV])
        P.op("pool", lambda e: e.memset(PTz[:, :], 0.0), writes=[t_PTz])

        def cross_attn(k_ap_fn, t_k, v_ap_fn, t_v, qT_ap_fn, q_reads, nq, ps_out, t_ps_out):
            PT_, t_PT_ = CTX["PT"], CTX["t_PT"]
            for h in range(4):
                pr, hf = h // 2, h % 2
                psl = slice(64 * hf, 64 * hf + 64)
                isx = nxt("s")

                def qk(e, isx=isx, pr=pr, psl=psl):
                    for blk in range(2):
                        ins = e.matmul(ps_s[isx][:, blk * 128:blk * 128 + nq], lhsT=k_ap_fn(psl, pr, blk), rhs=qT_ap_fn(psl, pr),
                                       start=True, stop=True)
                    return ins
                P.op("pe", qk, reads=[t_k] + list(q_reads), writes=[t_ps_s[isx]])
                ipt = nxt("pt")
                P.op("act", (lambda e, isx=isx, ipt=ipt: e.activation(
                    out=PT_[ipt][:, :].rearrange("p (b q) -> p b q", b=2)[:, :, 0:nq],
                    in_=ps_s[isx][:, 0:256].rearrange("p (b q) -> p b q", b=2)[:, :, 0:nq], func=AF.Exp, scale=SCALE)),
                    reads=[t_ps_s[isx]], writes=[t_PT_[ipt]])

                def pv(e, ipt=ipt, h=h):
                    for blk in range(2):
                        ins = e.matmul(ps_out[0:nq, h * 65:(h + 1) * 65], lhsT=PT_[ipt][:, blk * 128:blk * 128 + nq], rhs=v_ap_fn(blk, h),
                                       start=(blk == 0), stop=(blk == 1))
                    return ins
                P.op("pe", pv, reads=[t_PT_[ipt], t_v], writes=[t_ps_out])

        def normalize_o(ps_in, t_ps_in, nq, dst_ap, t_dst, t_rc, rc):
            v = ps_in[0:nq, 0:260].rearrange("p (h d) -> p h d", h=4)
            P.op("dve", lambda e: e.reciprocal(out=rc[0:nq, 0:4], in_=v[:, :, 64]), reads=[t_ps_in], writes=[t_rc])
            P.op("dve", lambda e: e.tensor_tensor(out=dst_ap.rearrange("p (h d) -> p h d", h=4), in0=v[:, :, 0:64],
                                                  in1=rc[0:nq, 0:4].unsqueeze(2).to_broadcast([nq, 4, 64]), op=ALU.mult),
                 reads=[t_ps_in, t_rc], writes=[t_dst])

        def post_a(nq, hbuf_ap, t_hb_in, gate_ap, t_gate_in, nk, hb_t, t_hb_t, hT_t, t_hT_t, col0):
            P.op("dve", lambda e: e.tensor_tensor(out=hb_t[0:nq, 0:nk * 128], in0=hbuf_ap, in1=gate_ap, op=ALU.mult),
                 reads=[t_hb_in, t_gate_in], writes=[t_hb_t])

            def tr(e):
                for k in range(nk):
                    ins = e.transpose(out=ps_t[:, k * 128:k * 128 + nq], in_=hb_t[0:nq, k * 128:(k + 1) * 128], identity=ident[0:nq, 0:nq])
                return ins
            P.op("pe", tr, reads=[t_hb_t, t_ident], writes=[t_ps_t])
            copy_op(ev_eng(), hT_t[:, 0:nk, col0:col0 + nq], ps_t[:].rearrange("p (k m) -> p k m", k=8)[:, 0:nk, 0:nq], [t_ps_t], [t_hT_t])

        def post_b(gi_post, nq, nk, hT_t, t_hT_t, wout_t, t_wout_t, ys_, x_src_fn, x_dst_fn):
            ytmp_, t_ytmp_, xres_, t_xres_ = CTX["ytmp"], CTX["t_ytmp"], CTX["xres"], CTX["t_xres"]
            for nb in range(2):
                def mm(e, nb=nb):
                    for k in range(nk):
                        ins = e.matmul(ps_mm[nb][0:nq, :], lhsT=hT_t[:, k, 0:nq], rhs=wout_t[:, k, nb * 512:(nb + 1) * 512], start=(k == 0), stop=(k == nk - 1))
                    return ins
                P.op("pe", mm, reads=[t_hT_t, t_wout_t], writes=[t_ps_mm[nb]])
                copy_op("act" if nb == 0 else "dve", ytmp_[ys_][0:nq, nb * 512:(nb + 1) * 512], ps_mm[nb][0:nq, :], [t_ps_mm[nb]], [t_ytmp_[ys_]])
            x_src_fn(ys_)
            rmsnorm(ytmp_[ys_][0:nq, :], nq, gi_post, ytmp_[ys_][0:nq, :], t_ytmp_[ys_], t_ytmp_[ys_])
            P.op("dve", lambda e: e.tensor_tensor(out=xres_[ys_][0:nq, :], in0=xres_[ys_][0:nq, :], in1=ytmp_[ys_][0:nq, :], op=ALU.add),
                 reads=[t_ytmp_[ys_], t_xres_[ys_]], writes=[t_xres_[ys_]])
            x_dst_fn(ys_)

        tile_hs = {}

        def tile_X(Tq):
            hs_ = nxt("hb")
            tile_hs[Tq] = hs_
            io = nxt("o")
            srcs = [(0, 0, Tq, ident[:, :])]
            srcs += [(1, r, Tq // 4, perm_lhsT(1, r, Tq)) for r in range(4)]
            srcs += [(2, r, 0, perm_lhsT(2, r, Tq)) for r in range(16)]

            def comb(e):
                ns = len(srcs)
                for si, (g, r, sb_, lt) in enumerate(srcs):
                    gi_ = r * (NT // DIL[g]) + sb_
                    ins = e.matmul(ps_o[io][:, 0:260], lhsT=lt, rhs=Og[:, 16 * g + gi_, :], start=(si == 0), stop=(si == ns - 1))
                return ins
            P.op("pe", comb, reads=[t_ident, t_M] + [t_Og[16 * g + r * (NT // DIL[g]) + sb_] for (g, r, sb_, _) in srcs], writes=[t_ps_o[io]])
            normalize_o(ps_o[io], t_ps_o[io], 128, hbuf[hs_][:, 0:256], t_hbuf[hs_], t_rcp[hs_], rcp[hs_])
            io2 = nxt("o")
            cross_attn((lambda psl, pr, blk: KmT[psl, 0, pr, blk * 128:(blk + 1) * 128]), t_KmT[0],
                       (lambda blk, h: Vm[:, 0, blk, h, 0:65]), t_Vm[0],
                       (lambda psl, pr: QmT[psl, pr, Tq * 128:(Tq + 1) * 128]), [x for row in t_QmT for x in row],
                       128, ps_o[io2], t_ps_o[io2])
            normalize_o(ps_o[io2], t_ps_o[io2], 128, hbuf[hs_][:, 256:512], t_hbuf[hs_], t_rcp[hs_], rcp[hs_])

        def tile_Y(Tq):
            hs_ = tile_hs[Tq]
            post_a(128, hbuf[hs_][:, :], t_hbuf[hs_], gate[:, Tq, :], t_gate[Tq], 4, hb[hs_], t_hb[hs_], hT[hs_], t_hT[hs_], 0)

            def xsrc(ys_):
                P.op("sp", lambda e: e.dma_start(out=xres[ys_][:, :], in_=x_p[Tq * 128:(Tq + 1) * 128, :]), writes=[t_xres[ys_]], dma=True)

            def xdst(ys_):
                P.op("sp", lambda e: e.dma_start(out=x1_d[Tq * 128:(Tq + 1) * 128, :], in_=xres[ys_][:, :]), reads=[t_xres[ys_]], writes=[t_x1[Tq]],
                     dma=True, sem_tk=t_xres[ys_])
                if "l0out" in DBG:
                    P.op("sp", lambda e: e.dma_start(out=y_p[Tq * 128:(Tq + 1) * 128, :], in_=xres[ys_][:, :]), reads=[t_xres[ys_]], dma=True)
            post_b(G_POST[0], 128, 4, hT[hs_], t_hT[hs_], wout, t_wout, hs_, xsrc, xdst)

        tile_X(0)
        for Tq in range(1, NT):
            tile_X(Tq)
            tile_Y(Tq - 1)
        tile_Y(NT - 1)

        def load_piece(src_ap, nt_):
            cst, ckb, cKT, cV = CTX["cst"], CTX["ckb"], CTX["cKT"], CTX["cV"]
            t_cst, t_ckb, t_cKT, t_cV = CTX["t_cst"], CTX["t_ckb"], CTX["t_cKT"], CTX["t_cV"]
            P.op("sp", lambda e: e.dma_start(out=cst[:, 0:nt_, :], in_=src_ap), writes=[t_cst], dma=True)
            P.op("dve", lambda e: e.tensor_copy(out=ckb[:, 0:nt_, :], in_=cst[:, 0:nt_, 0:256]), reads=[t_cst], writes=[t_ckb])
            P.op("act", lambda e: e.copy(out=cV[:, 0:nt_, :, 0:64], in_=cst[:, 0:nt_, 256:512].rearrange("p t (h d) -> p t h d", h=4)),
                 reads=[t_cst], writes=[t_cV])

            def tr(e):
                for t_ in range(nt_):
                    for pr in range(2):
                        idx = t_ * 2 + pr
                        ins = e.transpose(out=ps_t[:, idx * 128:(idx + 1) * 128], in_=ckb[:, t_, pr * 128:(pr + 1) * 128], identity=ident[:, :])
                return ins
            P.op("pe", tr, reads=[t_ckb, t_ident], writes=[t_ps_t])
            copy_op(ev_eng(), cKT[:, 0:2 * nt_, :], ps_t[:].rearrange("p (k m) -> p k m", k=8)[:, 0:2 * nt_, :], [t_ps_t], [t_cKT])

        def sample_batch(bb):
            io = nxt("o")
            pso, tpso = ps_o[io], t_ps_o[io]
            first = [True]

            def pv_mm(e, out_ap, lhsT, rhs):
                ins = e.matmul(out_ap, lhsT=lhsT, rhs=rhs, start=first[0], stop=False, skip_group_check=True)
                first[0] = False
                return ins
            qs = slice(4 * bb, 4 * bb + 4)
            for g in range(3):
                d = DIL[g]
                if g == 0:
                    load_piece(c_w0[bb].rearrange("(j o) c -> j o c", o=1), 1)
                elif g == 1:
                    load_piece(c_w1[bb].rearrange("(j r) c -> j r c", r=4), 4)
                else:
                    load_piece(c_w2[bb].rearrange("(j r) c -> j r c", r=16)[:, 0:4, :], 4)
                for h in range(4):
                    pr, hf = h // 2, h % 2
                    psl = slice(64 * hf, 64 * hf + 64)
                    isx = nxt("s")
                    if g == 0:
                        def qk(e, isx=isx, pr=pr, psl=psl):
                            e.matmul(ps_s[isx][:, 0:4], lhsT=cKT[psl, pr, :], rhs=QTs[psl, pr, qs], start=True, stop=True)
                            return e.matmul(ps_s[isx][0:4, 8:12], lhsT=KTs[psl, pr, qs], rhs=QTs[psl, pr, qs], start=True, stop=True)
                        P.op("pe", qk, reads=[t_cKT, t_QTs, t_KTs], writes=[t_ps_s[isx]])
                        isb = nxt("sb")
                        P.op("dve", (lambda e, isx=isx, isb=isb, h=h: e.scalar_tensor_tensor(
                            out=sbf[isb][:, 0:4], in0=ps_s[isx][:, 0:4], scalar=SCALE, in1=biasT[:, h, 0, 0:4], op0=ALU.mult, op1=ALU.add)),
                            reads=[t_ps_s[isx], t_biasT], writes=[t_sbf[isb]])
                        P.op("dve", (lambda e, isx=isx, h=h: e.scalar_tensor_tensor(
                            out=sbn[0:4, 0:4], in0=ps_s[isx][0:4, 8:12], scalar=SCALE, in1=biasT[0:4, h, 1, 0:4], op0=ALU.mult, op1=ALU.add)),
                            reads=[t_ps_s[isx], t_biasT], writes=[t_sbn])
                        ipt = nxt("pt")
                        P.op("act", (lambda e, isb=isb, ipt=ipt: e.activation(out=PT[ipt][:, 0:4], in_=sbf[isb][:, 0:4], func=AF.Exp)),
                             reads=[t_sbf[isb]], writes=[t_PT[ipt]])
                        P.op("act", lambda e: e.activation(out=PTn[0:4, 0:4], in_=sbn[0:4, 0:4], func=AF.Exp), reads=[t_sbn], writes=[t_PTn])

                        def pv(e, ipt=ipt, h=h):
                            pv_mm(e, pso[0:4, h * 65:(h + 1) * 65], PT[ipt][:, 0:4], cV[:, 0, h, 0:65])
                            return pv_mm(e, pso[0:4, h * 65:(h + 1) * 65], PTn[0:4, 0:4], Vns[0:4, bb, 0, h, 0:65])
                        P.op("pe", pv, reads=[t_PT[ipt], t_PTn, t_cV, t_Vns], writes=[tpso])
                    else:
                        def qk(e, isx=isx, pr=pr, psl=psl, g=g):
                            for t_ in range(4):
                                e.matmul(ps_s[isx][:, t_:t_ + 1], lhsT=cKT[psl, 2 * t_ + pr, :], rhs=QTs[psl, 2 * g + pr, 4 * bb + t_:4 * bb + t_ + 1],
                                         start=True, stop=True)
                            return e.matmul(ps_s[isx][0:4, 8:12], lhsT=KTs[psl, 2 * g + pr, qs], rhs=QTs[psl, 2 * g + pr, qs], start=True, stop=True)
                        P.op("pe", qk, reads=[t_cKT, t_QTs, t_KTs], writes=[t_ps_s[isx]])
                        P.op("act", (lambda e, isx=isx, h=h, g=g: e.activation(out=PTz[:, 0:16:5], in_=ps_s[isx][:, 0:4], func=AF.Exp,
                                                                              bias=biasT[:, g * 4 + h, 0, 0:1], scale=SCALE)),
                             reads=[t_ps_s[isx], t_biasT], writes=[t_PTz])
                        P.op("dve", (lambda e, isx=isx, h=h, g=g: e.scalar_tensor_tensor(
                            out=sbn[0:4, 0:4], in0=ps_s[isx][0:4, 8:12], scalar=SCALE, in1=biasT[0:4, g * 4 + h, 1, 0:4], op0=ALU.mult, op1=ALU.add)),
                            reads=[t_ps_s[isx], t_biasT], writes=[t_sbn])
                        P.op("act", lambda e: e.activation(out=sbn[0:4, 4:8], in_=sbn[0:4, 0:4], func=AF.Exp), reads=[t_sbn], writes=[t_sbn])
                        P.op("dve", lambda e: e.tensor_tensor(out=PTn[0:4, 0:4], in0=sbn[0:4, 4:8], in1=identf[0:4, 0:4], op=ALU.mult),
                             reads=[t_sbn, t_ident], writes=[t_PTn])

                        def pv(e, h=h, g=g):
                            for t_ in range(4):
                                pv_mm(e, pso[0:4, h * 65:(h + 1) * 65], PTz[:, 4 * t_:4 * t_ + 4], cV[:, t_, h, 0:65])
                            return pv_mm(e, pso[0:4, h * 65:(h + 1) * 65], PTn[0:4, 0:4], Vns[0:4, bb, g, h, 0:65])
                        P.op("pe", pv, reads=[t_PTz, t_PTn, t_cV, t_Vns], writes=[tpso])
            hs_ = 0
            normalize_o(pso, tpso, 4, hbuf[hs_][0:4, 0:256], t_hbuf[hs_], t_rcp[hs_], rcp[hs_])
            load_piece(c_mem[0, bb].rearrange("(b j) c -> j b c", b=2), 2)
            io2 = nxt("o")
            cross_attn((lambda psl, pr, blk: cKT[psl, 2 * blk + pr, :]), t_cKT, (lambda blk, h: cV[:, blk, h, 0:65]), t_cV,
                       (lambda psl, pr: QTs[psl, 6 + pr, qs]), [t_QTs], 4, ps_o[io2], t_ps_o[io2])
            normalize_o(ps_o[io2], t_ps_o[io2], 4, hbuf[hs_][0:4, 256:512], t_hbuf[hs_], t_rcp[hs_], rcp[hs_])
            post_a(4, hbuf[hs_][0:4, :], t_hbuf[hs_], gate_s[0:4, bb, :], t_gate_s, 4, hb[hs_], t_hb[hs_], hT[1], t_hT[1], 4 * bb)

        for bb in range(0 if "nosample" in DBG else 4):
            sample_batch(bb)

        def xsrc_s(ys_):
            P.op("sp", lambda e: e.dma_start(out=xres[ys_][0:16, :], in_=x_s), writes=[t_xres[ys_]], dma=True)

        def xdst_s(ys_):
            P.op("sp", lambda e: e.dma_start(out=x1s_d, in_=xres[ys_][0:16, :]), reads=[t_xres[ys_]], writes=[t_x1[NT]], dma=True, sem_tk=t_xres[ys_])
            if "l0out" in DBG:
                P.op("sp", lambda e: e.dma_start(out=y_s, in_=xres[ys_][0:16, :]), reads=[t_xres[ys_]], dma=True)
        if "nosample" not in DBG:
            post_b(G_POST[0], 16, 4, hT[1], t_hT[1], wout, t_wout, 1, xsrc_s, xdst_s)
        print('n_ops', len(P.ops))
        if 'dump' in DBG:
            for _i, _o in enumerate(P.ops):
                print('OP', _i, _o.eng, _o.line, 'dma' if _o.is_dma else '')

        if STAGE >= 2:
            barrier()
            apos[0] = 0
            atop[0] = ARENA_WORDS
            load_gains([(0, norm_pre, 1), (1, norm_post, 1)])
            Wb = carve_bf(8 * B_IN).rearrange("p (k n) -> p k n", k=8)
            t_Wb = Tk()
            for c0 in range(0, B_IN, 464):
                P.op("pool", (lambda e, c0=c0: e.dma_start(out=Wb[:, :, c0:c0 + 464], in_=w_in_b[:, c0:c0 + 464].rearrange("(k p) n -> p k n", p=128))),
                     writes=[t_Wb], dma=True)
            Wo = carve_bf(8 * 1024).rearrange("p (k n) -> p k n", k=8)
            t_Wo = Tk()
            P.op("pool", lambda e: e.dma_start(out=Wo, in_=w_out_b.rearrange("(k p) n -> p k n", p=128)), writes=[t_Wo], dma=True)
            wup = carve_bf(768)
            aup = carve_bf(768)
            t_lora = Tk()
            P.op("pool", lambda e: e.dma_start(out=wup[0:64, :], in_=r_wup), writes=[t_lora], dma=True)
            P.op("pool", lambda e: e.dma_start(out=aup[64:128, :], in_=r_aup), writes=[t_lora], dma=True)
            par = carve_f(64)
            t_par = Tk()
            P_MU, P_W0, P_A0, P_KK, P_KA, P_OMKA, P_RK = 0, 19, 25, 31, 37, 43, 49
            for (src, c0, nm) in ((r_mu, P_MU, 19), (r_w0, P_W0, 6), (r_a0, P_A0, 6), (r_kk, P_KK, 6), (r_ka, P_KA, 6), (r_rk, P_RK, 6)):
                P.op("sp", (lambda e, src=src, c0=c0, nm=nm: e.dma_start(out=par[:, c0:c0 + nm], in_=src.rearrange("o (m p) -> p (o m)", p=128),
                                                                         allow_slow_non_contiguous=True)), writes=[t_par], dma=True)
            P.op("dve", lambda e: e.tensor_scalar(out=par[:, P_OMKA:P_OMKA + 6], in0=par[:, P_KA:P_KA + 6], scalar1=-1.0, scalar2=1.0, op0=ALU.mult, op1=ALU.add),
                 reads=[t_par], writes=[t_par])
            lnw_b = carve_f(768)
            lnb_b = carve_f(768)
            t_ln = Tk()
            for (dst, src) in ((lnw_b, r_lnw), (lnb_b, r_lnb)):
                apb = bass.AP(tensor=src.tensor, offset=0, ap=[[0, 128], [1, 768]])
                P.op("sp", (lambda e, dst=dst, apb=apb: e.dma_start(out=dst, in_=apb)), writes=[t_ln], dma=True)
            SA, SB, SC, SD, SE, SF, SG = [carve_f(768).rearrange("p (m j) -> p m j", m=6) for _ in range(7)]
            t_S = {k_: Tk() for k_ in "ABCDEFG"}
            t_mt1 = t_S["A"]
            mtmp1 = SA[:, :, :].rearrange("p m j -> p (m j)")
            ML = carve_bf(512).rearrange("p (h i) -> p h i", h=4)
            MU = carve_bf(512).rearrange("p (h i) -> p h i", h=4)
            MUi = carve_bf(512).rearrange("p (h i) -> p h i", h=4)
            Irep = carve_bf(512).rearrange("p (h i) -> p h i", h=4)
            rmask = carve_bf(768).rearrange("p (m j) -> p m j", m=6)
            bd = carve_f(128)
            hsel = carve_bf(2)
            t_cst1 = Tk()
            for (Mt, cm, step, cmp_) in ((ML, 1, -1, ALU.is_gt), (MU, -1, 1, ALU.is_gt), (MUi, -1, 1, ALU.is_ge), (Irep, 1, -1, ALU.is_equal)):
                P.op("pool", lambda e: e.memset(mtmp1[:, 0:512], 1.0), writes=[t_mt1])

                def mk(e, cm=cm, step=step, cmp_=cmp_):
                    v = mtmp1[:, 0:512].rearrange("p (h i) -> p h i", h=4)
                    return e.affine_select(out=v, in_=v, pattern=[[0, 4], [step, 128]], compare_op=cmp_, fill=0.0, base=0, channel_multiplier=cm)
                P.op("pool", mk, reads=[t_mt1], writes=[t_mt1])
                P.op("dve", (lambda e, Mt=Mt: e.tensor_copy(out=Mt, in_=mtmp1[:, 0:512].rearrange("p (h i) -> p h i", h=4))), reads=[t_mt1], writes=[t_cst1])

            def mk2(e):
                e.memset(rmask[:, :, :], 1.0)
                e.memset(rmask[:, :, 0:1], 0.0)
                e.memset(bd[:, :], 0.0)
                e.memset(bd[0:64, 0:64], 1.0)
                e.memset(bd[64:128, 64:128], 1.0)
                e.memset(hsel[:, :], 0.0)
                e.memset(hsel[0:64, 0:1], 1.0)
                return e.memset(hsel[64:128, 1:2], 1.0)
            P.op("pool", mk2, writes=[t_cst1])

            xld1 = carve_f(1024)
            xnb1 = carve_bf(1024)
            xnTc = carve_bf(8 * 128).rearrange("p (k t) -> p k t", k=8)
            xnTs1 = carve_bf(8 * 16).rearrange("p (k t) -> p k t", k=8)
            t_xld1, t_xnb1, t_xnTc, t_xnTs1 = Tk(), Tk(), Tk(), Tk()
            cols = carve_f(19 * 129).rearrange("p (m j) -> p m j", m=19)
            t_cols = Tk()
            lastc = carve_f(20)
            t_lastc_g = Tk()
            t_xs = t_cols
            QmTc = carve_bf(2 * 128).rearrange("p (m j) -> p m j", m=2)
            t_QmTc = Tk()
            gt = carve_bf(1024)
            t_gt = Tk()
            lw = carve_bf(128)
            t_lw = Tk()
            outs_w0 = apos[0]
            t_o = {k_: Tk() for k_ in ("AT", "BT", "KT", "KH", "BH", "RT", "RKT", "XV")}
            BT, KT1, KH, BH, RKT, XV = [carve_bf(768).rearrange("p (m j) -> p m j", m=6) for _ in range(6)]
            AT2 = carve_bf(12 * 128).rearrange("p (h j) -> p h j", h=12)
            RT2 = carve_bf(12 * 128).rearrange("p (h j) -> p h j", h=12)
            P.op("pool", lambda e: e.memset(AT2[:, :, :], 0.0), writes=[t_o["AT"]])
            P.op("pool", lambda e: e.memset(RT2[:, :, :], 0.0), writes=[t_o["RT"]])
            Vt = carve_bf(768)
            Kh = carve_bf(768)
            Bh = carve_bf(768)
            t_Vt, t_Kh, t_Bh = Tk(), Tk(), Tk()
            Lh = [carve_bf(512).rearrange("p (h i) -> p h i", h=4) for _ in range(3)]
            Xh = [carve_bf(512).rearrange("p (h i) -> p h i", h=4) for _ in range(3)]
            Qh = [carve_bf(512).rearrange("p (h i) -> p h i", h=4) for _ in range(3)]
            t_Lh, t_Xh, t_Qh = [Tk() for _ in range(3)], [Tk() for _ in range(3)], [Tk() for _ in range(3)]
            Qf = carve_bf(12 * 128).rearrange("p (h i) -> p h i", h=12)
            Aak = carve_bf(12 * 128).rearrange("p (h i) -> p h i", h=12)
            Ark = carve_bf(12 * 128).rearrange("p (h i) -> p h i", h=12)
            Arb = carve_bf(12 * 128).rearrange("p (h i) -> p h i", h=12)
            t_Qf, t_Aak, t_Ark, t_Arb = Tk(), Tk(), Tk(), Tk()
            zb_w0 = apos[0]
            Zb = carve_bf(768)
            Ub = carve_bf(768)
            t_Zb, t_Ub = Tk(), Tk()
            Yf = SD[:, :, :].rearrange("p m j -> p (m j)")
            t_Yf = t_S["D"]
            St = carve_f(384).rearrange("p (m v) -> p m v", m=6)
            Sbf = carve_bf(384).rearrange("p (m v) -> p m v", m=6)
            t_St = Tk()
            wc = carve_f(8)
            t_wc = Tk()
            gsm = carve_f(96)
            t_gsm = Tk()
            hbuf1 = carve_f(1024)
            hb1 = carve_bf(1024)
            hT1 = carve_bf(8 * 128).rearrange("p (k t) -> p k t", k=8)
            t_hbuf1, t_hb1, t_hT1 = Tk(), Tk(), Tk()
            rcp1 = carve_f(8)
            t_rcp1 = Tk()
            xres1 = carve_f(1024)
            t_xres1 = Tk()
            PT1 = [carve_bf(256) for _ in range(2)]
            t_PT1 = [Tk(), Tk()]
            svst = arena[:, zb_w0:zb_w0 + 768].rearrange("p (h k) -> p h k", h=12)
            t_svst = Tk()
            print("L1 arena words", apos[0])
            C0 = 0.6065306597126334
            psq = [ps_s[0], ps_s[1], ps_o[0], ps_o[1]]
            t_psq = [t_ps_s[0], t_ps_s[1], t_ps_o[0], t_ps_o[1]]
            rr.update({"q": 0})

            def nq4():
                rr["q"] += 1
                return rr["q"] % 4

            def tt(eng, out, in0, in1, op, reads, writes):
                P.op(eng, lambda e: e.tensor_tensor(out=out, in0=in0, in1=in1, op=op), reads=reads, writes=writes)

            def bc6(col0, C):
                return par[:, col0:col0 + 6].unsqueeze(2).to_broadcast([128, 6, C])

            def inproj_group(C, xn_ap, t_xn, m0, nm):
                if True:
                    i = nxt("mm")

                    def mm(e, m0=m0, nm=nm, i=i):
                        for mi in range(nm):
                            m = m0 + mi
                            for k in range(8):
                                ins = e.matmul(ps_mm[i][:, mi * 128:mi * 128 + C], lhsT=Wb[:, k, m * 128:(m + 1) * 128], rhs=xn_ap(k),
                                               start=(k == 0), stop=(k == 7))
                        return ins
                    P.op("pe", mm, reads=[t_Wb, t_xn], writes=[t_ps_mm[i]])
                    src = ps_mm[i][:, 0:nm * 128].rearrange("p (m j) -> p m j", m=nm)[:, :, 0:C]
                    if m0 < 19:
                        copy_op("act", cols[:, m0:m0 + nm, 1:1 + C], src, [t_ps_mm[i]], [t_cols])
                    else:
                        copy_op("act", QmTc[:, :, 0:C], src, [t_ps_mm[i]], [t_QmTc])

            COLS_GROUPS = [(16, 3), (0, 4), (4, 4), (8, 4), (12, 4)]

            def rwkv_chunk(C, xn_ap, t_xn, mode, idx, pre_done=False, early_fn=None, interleave=()):
                first = (idx == 0) if mode == "p" else True
                last = (idx == NT - 1) if mode == "p" else True
                interleave = list(interleave)
                if not pre_done:
                    for (m0, nm) in COLS_GROUPS:
                        inproj_group(C, xn_ap, t_xn, m0, nm)
                inproj_group(C, xn_ap, t_xn, 19, 2)
                for gc in range(2):
                    i = nxt("mm")

                    def mmg(e, gc=gc, i=i):
                        for k in range(8):
                            ins = e.matmul(ps_mm[i][0:C, :], lhsT=xn_ap(k), rhs=Wb[:, k, 2688 + gc * 512:2688 + (gc + 1) * 512], start=(k == 0), stop=(k == 7))
                        return ins
                    P.op("pe", mmg, reads=[t_Wb, t_xn], writes=[t_ps_mm[i]])
                    P.op("act", (lambda e, gc=gc, i=i: e.activation(out=gt[0:C, gc * 512:(gc + 1) * 512], in_=ps_mm[i][0:C, :], func=AF.Silu)),
                         reads=[t_ps_mm[i]], writes=[t_gt])
                t_lastc = t_lastc_g
                P.op("act", lambda e: e.copy(out=lastc[:, 0:19], in_=cols[:, :, C]), reads=[t_cols], writes=[t_lastc])
                if last:
                    dst = (o_shp if mode == "p" else o_shs[idx:idx + 1, :]).rearrange("o (m p) -> p (o m)", p=128)
                    P.op("sp", lambda e: e.dma_start(out=dst, in_=lastc[:, 0:19], allow_slow_non_contiguous=True), reads=[t_lastc], dma=True)
                for (m0, nm) in ((0, 6), (6, 6), (12, 6), (18, 1)):
                    cur = cols[:, m0:m0 + nm, 1:1 + C]
                    prv = cols[:, m0:m0 + nm, 0:C]
                    tmp = SG[:, 0:nm, 0:C]
                    tt("dve", tmp, prv, cur, ALU.subtract, [t_cols], [t_S["G"]])
                    tt("dve", tmp, tmp, par[:, P_MU + m0:P_MU + m0 + nm].unsqueeze(2).to_broadcast([128, nm, C]), ALU.mult, [t_S["G"], t_par], [t_S["G"]])
                    tt("dve", cur, tmp, cur, ALU.add, [t_S["G"], t_cols], [t_cols])
                P.op("act", lambda e: e.copy(out=cols[:, :, 0], in_=lastc[:, 0:19]), reads=[t_lastc, t_cols], writes=[t_cols])

                class _XS:
                    def __getitem__(self, key):
                        p_, m_, j_ = key
                        assert j_ == slice(0, C)
                        return cols[p_, m_, 1:1 + C]
                xs = _XS()
                xr, xk, xv_ = xs[:, 0:6, 0:C], xs[:, 6:12, 0:C], xs[:, 12:18, 0:C]
                P.op("act", lambda e: e.activation(out=lw[0:64, 0:C], in_=xs[0:64, 18, 0:C], func=AF.Tanh), reads=[t_xs], writes=[t_lw])
                P.op("dve", lambda e: e.tensor_copy(out=lw[64:128, 0:C], in_=xs[64:128, 18, 0:C]), reads=[t_xs], writes=[t_lw])
                sigw, asig = SA[:, :, 0:C], SB[:, :, 0:C]
                for (which, wt, rows, pcol, dstS, tS) in (("w", wup, slice(0, 64), P_W0, SA, "A"), ("a", aup, slice(64, 128), P_A0, SB, "B")):
                    for (p0, np_) in ((0, 4), (4, 2)):
                        q = nq4()

                        def mml(e, wt=wt, rows=rows, p0=p0, np_=np_, q=q):
                            for pi in range(np_):
                                p = p0 + pi
                                ins = e.matmul(psq[q][:, pi * 128:pi * 128 + C], lhsT=wt[rows, p * 128:(p + 1) * 128], rhs=lw[rows, 0:C], start=True, stop=True)
                            return ins
                        P.op("pe", mml, reads=[t_lora, t_lw], writes=[t_psq[q]])
                        for pi in range(np_):
                            p = p0 + pi
                            P.op("act", (lambda e, pi=pi, p=p, q=q, pcol=pcol, dstS=dstS: e.activation(
                                out=dstS[:, p, 0:C], in_=psq[q][:, pi * 128:pi * 128 + C], func=AF.Sigmoid, bias=par[:, pcol + p:pcol + p + 1], scale=1.0)),
                                reads=[t_psq[q], t_par], writes=[t_S[tS]])
                cs = SC[:, :, 0:C]
                if C == 128:
                    P.op("dve", lambda e: e.tensor_tensor_scan(out=SC[:, :, :].rearrange("p m j -> p (m j)"), data0=rmask[:, :, :].rearrange("p m j -> p (m j)"),
                                                               data1=SA[:, :, :].rearrange("p m j -> p (m j)"), initial=0.0, op0=ALU.mult, op1=ALU.add),
                         reads=[t_S["A"], t_cst1], writes=[t_S["C"]])
                else:
                    for p in range(6):
                        P.op("dve", (lambda e, p=p: e.tensor_tensor_scan(out=SC[:, p, 0:C], data0=rmask[:, p, 0:C], data1=SA[:, p, 0:C], initial=0.0,
                                                                         op0=ALU.mult, op1=ALU.add)), reads=[t_S["A"], t_cst1], writes=[t_S["C"]])
                csC = SC[:, :, C - 1:C]
                eP, eH, eA, eN = SE[:, :, 0:C], SD[:, :, 0:C], SA[:, :, 0:C], SC[:, :, 0:C]
                P.op("act", lambda e: e.activation(out=eP, in_=cs, func=AF.Exp, scale=-C0), reads=[t_S["C"]], writes=[t_S["E"]])
                tt("dve", eH, csC.to_broadcast([128, 6, C]), cs, ALU.subtract, [t_S["C"]], [t_S["D"]])
                P.op("act", lambda e: e.activation(out=eH, in_=eH, func=AF.Exp, scale=-C0), reads=[t_S["D"]], writes=[t_S["D"]])
                P.op("act", lambda e: e.activation(out=wc[:, 0:6], in_=SC[:, :, C - 1], func=AF.Exp, scale=-C0), reads=[t_S["C"]], writes=[t_wc])
                tt("dve", eA, cs, sigw, ALU.subtract, [t_S["C"], t_S["A"]], [t_S["A"]])
                P.op("act", lambda e: e.activation(out=eA, in_=eA, func=AF.Exp, scale=-C0), reads=[t_S["A"]], writes=[t_S["A"]])
                P.op("act", lambda e: e.activation(out=eN, in_=cs, func=AF.Exp, scale=C0), reads=[t_S["C"], t_S["D"], t_S["E"], t_S["A"], t_wc], writes=[t_S["C"]])
                kk, g_ = SF[:, :, 0:C], SG[:, :, 0:C]
                tt("dve", kk, xk, bc6(P_KK, C), ALU.mult, [t_xs, t_par], [t_S["F"]])
                tt("dve", g_, kk, kk, ALU.mult, [t_S["F"]], [t_S["G"]])
                qa, qb_ = nq4(), nq4()

                def mmn(e):
                    e.matmul(psq[qa][:, 0:4 * 128].rearrange("p (m j) -> p m j", m=4)[:, :, 0:C], lhsT=bd[:, :], rhs=SG[:, 0:4, 0:C], start=True, stop=True)
                    return e.matmul(psq[qb_][:, 0:2 * 128].rearrange("p (m j) -> p m j", m=2)[:, :, 0:C], lhsT=bd[:, :], rhs=SG[:, 4:6, 0:C], start=True, stop=True)
                P.op("pe", mmn, reads=[t_S["G"], t_cst1], writes=[t_psq[qa], t_psq[qb_]])
                P.op("act", lambda e: e.activation(out=SG[:, 0:4, 0:C], in_=psq[qa][:, 0:512].rearrange("p (m j) -> p m j", m=4)[:, :, 0:C], func=AF.Sqrt),
                     reads=[t_psq[qa]], writes=[t_S["G"]])
                P.op("act", lambda e: e.activation(out=SG[:, 4:6, 0:C], in_=psq[qb_][:, 0:256].rearrange("p (m j) -> p m j", m=2)[:, :, 0:C], func=AF.Sqrt),
                     reads=[t_psq[qb_]], writes=[t_S["G"]])
                P.op("dve", lambda e: e.tensor_scalar_max(out=g_, in0=g_, scalar1=1e-12), reads=[t_S["G"]], writes=[t_S["G"]])
                P.op("dve", lambda e: e.reciprocal(out=g_, in_=g_), reads=[t_S["G"]], writes=[t_S["G"]])
                tt("dve", kk, kk, g_, ALU.mult, [t_S["F"], t_S["G"]], [t_S["F"]])
                for hf_ in range(2):
                    rs_ = slice(64 * hf_, 64 * hf_ + 64)
                    P.op("dve", (lambda e, hf_=hf_, rs_=rs_: e.scalar_tensor_tensor(out=AT2[rs_, hf_:12:2, 0:C], in0=SF[rs_, :, 0:C], scalar=-1.0,
                                                                                   in1=SA[rs_, :, 0:C], op0=ALU.mult, op1=ALU.mult)),
                         reads=[t_S["F"], t_S["A"]], writes=[t_o["AT"]])
                tt("dve", g_, kk, asig, ALU.mult, [t_S["F"], t_S["B"]], [t_S["G"]])
                tt("dve", BT[:, :, 0:C], g_, eN, ALU.mult, [t_S["G"], t_S["C"]], [t_o["BT"]])
                tt("dve", BH[:, :, 0:C], g_, eH, ALU.mult, [t_S["G"], t_S["D"]], [t_o["BH"]])
                km = hbuf1[:, 0:768].rearrange("p (m j) -> p m j", m=6)[:, :, 0:C]
                tt("pool", km, asig, bc6(P_KA, C), ALU.mult, [t_S["B"], t_par], [t_hbuf1])
                tt("pool", km, km, bc6(P_OMKA, C), ALU.add, [t_hbuf1, t_par], [t_hbuf1])
                tt("pool", km, km, xk, ALU.mult, [t_hbuf1, t_xs], [t_hbuf1])
                tt("pool", KT1[:, :, 0:C], km, eN, ALU.mult, [t_hbuf1, t_S["C"]], [t_o["KT"]])
                tt("pool", KH[:, :, 0:C], km, eH, ALU.mult, [t_hbuf1, t_S["D"]], [t_o["KH"]])
                tt("pool", km, km, bc6(P_RK, C), ALU.mult, [t_hbuf1, t_par], [t_hbuf1])
                tt("pool", RKT[:, :, 0:C], km, xr, ALU.mult, [t_hbuf1, t_xs], [t_o["RKT"]])
                for hf_ in range(2):
                    rs_ = slice(64 * hf_, 64 * hf_ + 64)
                    tt("pool", RT2[rs_, hf_:12:2, 0:C], cols[rs_, 0:6, 1:1 + C], SE[rs_, :, 0:C], ALU.mult, [t_xs, t_S["E"]], [t_o["RT"]])
                P.op("act", lambda e: e.copy(out=XV[:, :, 0:C], in_=xv_), reads=[t_xs], writes=[t_o["XV"]])
                for (srcT, tsrc, dstT, tdst) in ((XV, "XV", Vt, t_Vt), (KH, "KH", Kh, t_Kh), (BH, "BH", Bh, t_Bh)):
                    def tr(e, srcT=srcT):
                        for p in range(6):
                            ins = e.transpose(out=ps_t[0:C, p * 128:(p + 1) * 128], in_=srcT[:, p, 0:C], identity=ident[:, :])
                        return ins
                    P.op("pe", tr, reads=[t_o[tsrc], t_ident], writes=[t_ps_t])
                    copy_op(ev_eng(), dstT[0:C, :], ps_t[0:C, 0:768], [t_ps_t], [tdst])
                nlev = max(1, int(math.ceil(math.log2(C))))

                def pair_mm(q, l_fn, r_fn, hg, reads):
                    def f(e):
                        for hi in range(4):
                            h = 4 * hg + hi
                            ins = e.matmul(psq[q][0:C, hi * 128:hi * 128 + C], lhsT=l_fn(h), rhs=r_fn(h), start=True, stop=True)
                        return ins
                    P.op("pe", f, reads=reads, writes=[t_psq[q]])

                A2 = lambda h: AT2[:, h, 0:C]
                R2 = lambda h: RT2[:, h, 0:C]
                Bp = lambda h: BT[:, h // 2, 0:C]
                Kp = lambda h: KT1[:, h // 2, 0:C]

                def pv4(q):
                    return psq[q][0:C, :].rearrange("p (h i) -> p h i", h=4)[:, :, 0:C]

                def sq_mm(q, lT, rT, reads, acc_ident_rhs=None):
                    def f(e):
                        for hi in range(4):
                            if acc_ident_rhs is not None:
                                e.matmul(psq[q][0:C, hi * 128:hi * 128 + C], lhsT=ident[0:C, 0:C], rhs=acc_ident_rhs[0:C, hi, 0:C], start=True, stop=False)
                            ins = e.matmul(psq[q][0:C, hi * 128:hi * 128 + C], lhsT=lT[0:C, hi, 0:C], rhs=rT[0:C, hi, 0:C],
                                           start=(acc_ident_rhs is None), stop=True)
                        return ins
                    P.op("pe", f, reads=reads + [t_ident], writes=[t_psq[q]])

                if early_fn is not None:
                    early_fn()
                for hg in range(3):
                    q = nq4()
                    pair_mm(q, A2, Bp, hg, [t_o["AT"], t_o["BT"]])
                    tt("dve", Lh[hg][0:C, :, 0:C], pv4(q), ML[0:C, :, 0:C], ALU.mult, [t_psq[q], t_cst1], [t_Lh[hg]])
                    q = nq4()
                    pair_mm(q, Bp, A2, hg, [t_o["AT"], t_o["BT"]])
                    tt("dve", Xh[hg][0:C, :, 0:C], pv4(q), MU[0:C, :, 0:C], ALU.mult, [t_psq[q], t_cst1], [t_Xh[hg]])
                    tt("pool", Qh[hg][0:C, :, 0:C], Xh[hg][0:C, :, 0:C], Irep[0:C, :, 0:C], ALU.add, [t_Xh[hg], t_cst1], [t_Qh[hg]])
                for j in range(nlev - 1):
                    need_x = (j + 1 < nlev - 1)
                    if interleave:
                        interleave.pop(0)()
                    for hg in range(3):
                        q = nq4()
                        sq_mm(q, Xh[hg], Lh[hg], [t_Xh[hg], t_Lh[hg]])
                        qx = None
                        if need_x:
                            qx = nq4()
                            sq_mm(qx, Lh[hg], Xh[hg], [t_Xh[hg], t_Lh[hg]])
                        copy_op("act", Lh[hg][0:C, :, 0:C], pv4(q), [t_psq[q]], [t_Lh[hg]])
                        if need_x:
                            copy_op("dve", Xh[hg][0:C, :, 0:C], pv4(qx), [t_psq[qx]], [t_Xh[hg]])
                    for hg in range(3):
                        q = nq4()
                        sq_mm(q, Lh[hg], Qh[hg], [t_Lh[hg], t_Qh[hg]], acc_ident_rhs=Qh[hg])
                        if j == nlev - 2:
                            copy_op("act", Qf[0:C, 4 * hg:4 * hg + 4, 0:C], pv4(q), [t_psq[q]], [t_Qf])
                        else:
                            copy_op("act", Qh[hg][0:C, :, 0:C], pv4(q), [t_psq[q]], [t_Qh[hg]])
                while interleave:
                    interleave.pop(0)()
                for hg in range(3):
                    if nlev == 1:
                        copy_op("act", Qf[0:C, 4 * hg:4 * hg + 4, 0:C], Qh[hg][0:C, :, 0:C], [t_Qh[hg]], [t_Qf])
                    q = nq4()
                    pair_mm(q, Kp, A2, hg, [t_o["KT"], t_o["AT"]])
                    tt("dve", Aak[0:C, 4 * hg:4 * hg + 4, 0:C], pv4(q), MU[0:C, :, 0:C], ALU.mult, [t_psq[q], t_cst1], [t_Aak])
                    q = nq4()
                    pair_mm(q, Kp, R2, hg, [t_o["KT"], t_o["RT"]])
                    tt("dve", Ark[0:C, 4 * hg:4 * hg + 4, 0:C], pv4(q), MUi[0:C, :, 0:C], ALU.mult, [t_psq[q], t_cst1], [t_Ark])
                    q = nq4()
                    pair_mm(q, Bp, R2, hg, [t_o["BT"], t_o["RT"]])
                    tt("dve", Arb[0:C, 4 * hg:4 * hg + 4, 0:C], pv4(q), MUi[0:C, :, 0:C], ALU.mult, [t_psq[q], t_cst1], [t_Arb])
                if first:
                    if mode == "p":
                        P.op("pool", lambda e: e.memset(St[:, :, :], 0.0), writes=[t_St])
                        P.op("pool", lambda e: e.memset(Sbf[:, :, :], 0.0), writes=[t_St])
                    else:
                        P.op("sp", lambda e: e.dma_start(out=svst[0:64, :, :], in_=s_wkv[idx].rearrange("h v k -> v h k")), writes=[t_svst, t_Zb, t_Ub], dma=True, sem_tk=t_svst)

                        def trs(e):
                            for p in range(6):
                                ins = e.transpose(out=ps_x[:, p * 64:(p + 1) * 64], in_=svst[0:64, 2 * p:2 * p + 2, :].rearrange("v h k -> v (h k)"),
                                                  identity=identf[0:64, 0:64])
                            return ins
                        P.op("pe", trs, reads=[t_svst, t_ident], writes=[t_ps_x])
                        P.op("act", lambda e: e.copy(out=St[:, :, :], in_=ps_x[:, 0:384].rearrange("p (m v) -> p m v", m=6)), reads=[t_ps_x], writes=[t_St])
                        P.op("dve", lambda e: e.tensor_copy(out=Sbf[:, :, :], in_=ps_x[:, 0:384].rearrange("p (m v) -> p m v", m=6)), reads=[t_ps_x], writes=[t_St])
                def head_cols(h):
                    return slice(h * 64, (h + 1) * 64)

                def seq_mm(name, fn_terms, reads, dst_banks):
                    def f(e):
                        for h in range(12):
                            bank, hc = (dst_banks[0], h) if h < 8 else (dst_banks[1], h - 8)
                            terms = fn_terms(h)
                            for ti, (lT, r_) in enumerate(terms):
                                ins = e.matmul(psq[bank][0:C, hc * 64:(hc + 1) * 64], lhsT=lT, rhs=r_, start=(ti == 0), stop=(ti == len(terms) - 1))
                        return ins
                    P.op("pe", f, reads=reads, writes=[t_psq[dst_banks[0]], t_psq[dst_banks[1]]])

                def hsl(h):
                    p, hf = h // 2, h % 2
                    return slice(64 * hf, 64 * hf + 64), p

                def evac768(dst, banks, tdst, as_f32=False):
                    copy_op("act", dst[0:C, 0:512], psq[banks[0]][0:C, 0:512], [t_psq[banks[0]]], [tdst])
                    copy_op("dve", dst[0:C, 512:768], psq[banks[1]][0:C, 0:256], [t_psq[banks[1]]], [tdst])

                seq_mm("Z", lambda h: [(AT2[:, h, 0:C], Sbf[:, h // 2, :]), (Aak[0:C, h, 0:C], Vt[0:C, head_cols(h)])],
                       [t_o["AT"], t_St, t_Aak, t_Vt], (0, 1))
                evac768(Zb, (0, 1), t_Zb)
                seq_mm("U", lambda h: [(Qf[0:C, h, 0:C], Zb[0:C, head_cols(h)])], [t_Qf, t_Zb], (2, 3))
                evac768(Ub, (2, 3), t_Ub)
                seq_mm("Y", lambda h: [(RT2[:, h, 0:C], Sbf[:, h // 2, :]), (Ark[0:C, h, 0:C], Vt[0:C, head_cols(h)]),
                                       (Arb[0:C, h, 0:C], Ub[0:C, head_cols(h)])],
                       [t_o["RT"], t_St, t_Ark, t_Vt, t_Arb, t_Ub], (0, 1))
                evac768(Yf, (0, 1), t_Yf)

                def snew(e):
                    for h in range(12):
                        psl, p = hsl(h)
                        e.matmul(ps_x[psl, p * 64:(p + 1) * 64], lhsT=Kh[0:C, head_cols(h)], rhs=Vt[0:C, head_cols(h)], start=True, stop=False)
                        ins = e.matmul(ps_x[psl, p * 64:(p + 1) * 64], lhsT=Bh[0:C, head_cols(h)], rhs=Ub[0:C, head_cols(h)], start=False, stop=True)
                    return ins
                P.op("pe", snew, reads=[t_Kh, t_Vt, t_Bh, t_Ub], writes=[t_ps_x])
                tt("dve", St[:, :, :], St[:, :, :], wc[:, 0:6].unsqueeze(2).to_broadcast([128, 6, 64]), ALU.mult, [t_St, t_wc], [t_St])
                tt("dve", St[:, :, :], St[:, :, :], ps_x[:, 0:384].rearrange("p (m v) -> p m v", m=6), ALU.add, [t_St, t_ps_x], [t_St])
                P.op("dve", lambda e: e.tensor_copy(out=Sbf[:, :, :], in_=St[:, :, :]), reads=[t_St], writes=[t_St])
                if last:
                    def trs2(e):
                        for p in range(6):
                            ins = e.transpose(out=ps_x[0:64, p * 128:(p + 1) * 128] if False else ps_mm[0][0:64, p * 64:(p + 1) * 64], in_=St[:, p, :], identity=identf[:, :])
                        return ins
                    def trs3(e):
                        for p in range(6):
                            bank = ps_mm[0] if p < 4 else ps_mm[1]
                            pc = p if p < 4 else p - 4
                            ins = e.transpose(out=bank[0:64, pc * 128:(pc + 1) * 128], in_=St[:, p, :], identity=identf[:, :])
                        return ins
                    P.op("pe", trs3, reads=[t_St, t_ident], writes=[t_ps_mm[0], t_ps_mm[1]])
                    P.op("act", lambda e: e.copy(out=svst[0:64, 0:8, :].rearrange("v h k -> v (h k)"), in_=ps_mm[0][0:64, 0:512]), reads=[t_ps_mm[0]], writes=[t_svst, t_Zb, t_Ub])
                    P.op("act", lambda e: e.copy(out=svst[0:64, 8:12, :].rearrange("v h k -> v (h k)"), in_=ps_mm[1][0:64, 0:256]), reads=[t_ps_mm[1]], writes=[t_svst, t_Zb, t_Ub])
                    dsto = (o_wkvp if mode == "p" else o_wkvs[idx]).rearrange("h v k -> v h k")
                    P.op("sp", lambda e: e.dma_start(out=dsto, in_=svst[0:64, :, :]), reads=[t_svst, t_Zb, t_Ub], dma=True, sem_tk=t_svst)
                Y3 = Yf[0:C, :].rearrange("p (h d) -> p h d", h=12)
                sqv = SF[0:C, :, :].rearrange("p m j -> p (m j)").rearrange("p (h d) -> p h d", h=12)
                P.op("dve", lambda e: e.reduce_sum(out=gsm[0:C, 0:12], in_=Y3, axis=AX.X), reads=[t_Yf], writes=[t_gsm])
                P.op("act", lambda e: e.activation(out=sqv, in_=Y3, func=AF.Square), reads=[t_Yf, t_o["RKT"]], writes=[t_S["F"]])
                P.op("dve", lambda e: e.reduce_sum(out=gsm[0:C, 12:24], in_=sqv, axis=AX.X), reads=[t_S["F"]], writes=[t_gsm])
                P.op("dve", lambda e: e.tensor_scalar(out=gsm[0:C, 24:36], in0=gsm[0:C, 0:12], scalar1=1.0 / 64, scalar2=None, op0=ALU.mult),
                     reads=[t_gsm], writes=[t_gsm])
                tt("dve", gsm[0:C, 36:48], gsm[0:C, 24:36], gsm[0:C, 24:36], ALU.mult, [t_gsm], [t_gsm])
                P.op("dve", lambda e: e.scalar_tensor_tensor(out=gsm[0:C, 48:60], in0=gsm[0:C, 12:24], scalar=1.0 / 64, in1=gsm[0:C, 36:48],
                                                             op0=ALU.mult, op1=ALU.subtract), reads=[t_gsm], writes=[t_gsm])
                P.op("dve", lambda e: e.tensor_scalar(out=gsm[0:C, 48:60], in0=gsm[0:C, 48:60], scalar1=64e-5, scalar2=None, op0=ALU.add),
                     reads=[t_gsm], writes=[t_gsm])
                P.op("act", lambda e: e.activation(out=gsm[0:C, 60:72], in_=gsm[0:C, 48:60], func=AF.Sqrt), reads=[t_gsm], writes=[t_gsm])
                P.op("dve", lambda e: e.reciprocal(out=gsm[0:C, 72:84], in_=gsm[0:C, 60:72]), reads=[t_gsm], writes=[t_gsm])
                hb3 = hbuf1[0:C, 0:768].rearrange("p (h d) -> p h d", h=12)
                tt("dve", hb3, Y3, gsm[0:C, 24:36].unsqueeze(2).to_broadcast([C, 12, 64]), ALU.subtract, [t_Yf, t_gsm], [t_hbuf1])
                tt("dve", hb3, hb3, gsm[0:C, 72:84].unsqueeze(2).to_broadcast([C, 12, 64]), ALU.mult, [t_hbuf1, t_gsm], [t_hbuf1])
                tt("dve", hbuf1[0:C, 0:768], hbuf1[0:C, 0:768], lnw_b[0:C, :], ALU.mult, [t_hbuf1, t_ln], [t_hbuf1])
                tt("dve", hbuf1[0:C, 0:768], hbuf1[0:C, 0:768], lnb_b[0:C, :], ALU.add, [t_hbuf1, t_ln], [t_hbuf1])

                def bon(e):
                    for p in range(6):
                        ins = e.matmul(ps_x[0:C, 400 + 2 * p:402 + 2 * p], lhsT=RKT[:, p, 0:C], rhs=hsel[:, 0:2], start=True, stop=True)
                    return ins
                P.op("pe", bon, reads=[t_o["RKT"], t_cst1], writes=[t_ps_x])
                P.op("act", lambda e: e.copy(out=gsm[0:C, 84:96], in_=ps_x[0:C, 400:412]), reads=[t_ps_x], writes=[t_gsm])
                tt("dve", sqv, Vt[0:C, :].rearrange("p (h d) -> p h d", h=12), gsm[0:C, 84:96].unsqueeze(2).to_broadcast([C, 12, 64]), ALU.mult,
                   [t_Vt, t_gsm, t_S["F"]], [t_S["F"]])
                tt("dve", hb3, hb3, sqv, ALU.add, [t_hbuf1, t_S["F"]], [t_hbuf1])
                io2 = nxt("o")
                if mode == "p":
                    cross_attn((lambda psl, pr, blk: KmT[psl, 1, pr, blk * 128:(blk + 1) * 128]), t_KmT[1],
                               (lambda blk, h: Vm[:, 1, blk, h, 0:65]), t_Vm[1],
                               (lambda psl, pr: QmTc[psl, pr, 0:C]), [t_QmTc], C, ps_o[io2], t_ps_o[io2])
                else:
                    barrier()
                    P.op("pool", lambda e: e.memset(cV1[:, :, :, 64:65], 1.0), writes=[t_cV1])
                    load_piece(c_mem[1, idx].rearrange("(b j) c -> j b c", b=2), 2)
                    cross_attn((lambda psl, pr, blk: cKT1[psl, 2 * blk + pr, :]), t_cKT1, (lambda blk, h: cV1[:, blk, h, 0:65]), t_cV1,
                               (lambda psl, pr: QmTc[psl, pr, 0:C]), [t_QmTc], C, ps_o[io2], t_ps_o[io2])
                normalize_o(ps_o[io2], t_ps_o[io2], C, hbuf1[0:C, 768:1024], t_hbuf1, t_rcp1, rcp1)
                col0 = 0 if mode == "p" else 4 * idx
                post_a(C, hbuf1[0:C, :], t_hbuf1, gt[0:C, :], t_gt, 8, hb1, t_hb1, hT1, t_hT1, col0)
                if mode == "p":
                    def xsrc(ys_):
                        P.op("sp", lambda e: e.dma_start(out=xres1[:, :], in_=x1_d[idx * 128:(idx + 1) * 128, :]), reads=[t_x1[idx]], writes=[t_xres1], dma=True)

                    def xdst(ys_):
                        P.op("sp", lambda e: e.dma_start(out=y_p[idx * 128:(idx + 1) * 128, :], in_=xres1[:, :]), reads=[t_xres1], dma=True)
                    post_b(G_POST[1], 128, 8, hT1, t_hT1, Wo, t_Wo, 0, xsrc, xdst)
                else:
                    barrier()

            keep1 = apos[0]
            apos[0] = outs_w0
            cst1 = carve_f(1024).rearrange("p (t n) -> p t n", t=2)
            ckb1 = carve_bf(2 * 256).rearrange("p (t n) -> p t n", t=2)
            cKT1 = carve_bf(4 * 128).rearrange("p (i n) -> p i n", i=4)
            cV1 = carve_bf(2 * 4 * 80).rearrange("p (t h d) -> p t h d", t=2, h=4)
            t_cst1b, t_ckb1, t_cKT1, t_cV1 = Tk(), Tk(), Tk(), Tk()
            apos[0] = keep1
            CTX.update(PT=PT1, t_PT=t_PT1, ytmp=[hbuf1] * 2, t_ytmp=[t_hbuf1] * 2, xres=[xres1] * 2, t_xres=[t_xres1] * 2,
                       cst=cst1, ckb=ckb1, cKT=cKT1, cV=cV1, t_cst=t_cst1b, t_ckb=t_ckb1, t_cKT=t_cKT1, t_cV=t_cV1)
            print("L1 arena words (final)", apos[0])

            def prompt_early(c):
                P.op("sp", lambda e: e.dma_start(out=xld1[:, :], in_=x1_d[c * 128:(c + 1) * 128, :]), reads=[t_x1[c]], writes=[t_xld1], dma=True)
                rmsnorm(xld1[:, :], 128, G_PRE[1], xnb1[:, :], t_xld1, t_xnb1)

                def tr(e):
                    for k in range(8):
                        ins = e.transpose(out=ps_t[:, k * 128:(k + 1) * 128], in_=xnb1[:, k * 128:(k + 1) * 128], identity=ident[:, :])
                    return ins
                P.op("pe", tr, reads=[t_xnb1, t_ident], writes=[t_ps_t])
                copy_op(ev_eng(), xnTc[:, :, :], ps_t[:].rearrange("p (k m) -> p k m", k=8), [t_ps_t], [t_xnTc])

            xn_fn = (lambda k: xnTc[:, k, :])
            n_chunks = NT if "l1few" not in DBG else 2
            prompt_early(0)
            P.op("pool", lambda e: e.memset(cols[:, :, 0:1], 0.0), writes=[t_cols])
            for c in range(n_chunks):
                if c + 1 < n_chunks and "nopipe1" not in DBG:
                    ef = (lambda c=c: prompt_early(c + 1))
                    il = [(lambda m0=m0, nm=nm: inproj_group(128, xn_fn, t_xnTc, m0, nm)) for (m0, nm) in COLS_GROUPS]
                else:
                    ef, il = None, ()
                rwkv_chunk(128, xn_fn, t_xnTc, "p", c, pre_done=(c > 0 and "nopipe1" not in DBG), early_fn=ef, interleave=il)
                if c + 1 < n_chunks and "nopipe1" in DBG:
                    prompt_early(c + 1)

            def sample_l1():
                P.op("sp", lambda e: e.dma_start(out=xld1[0:16, :], in_=x1s_d), reads=[t_x1[NT]], writes=[t_xld1], dma=True)
                rmsnorm(xld1[0:16, :], 16, G_PRE[1], xnb1[0:16, :], t_xld1, t_xnb1)

                def tr(e):
                    for k in range(8):
                        ins = e.transpose(out=ps_t[:, k * 128:k * 128 + 16], in_=xnb1[0:16, k * 128:(k + 1) * 128], identity=ident[0:16, 0:16])
                    return ins
                P.op("pe", tr, reads=[t_xnb1, t_ident], writes=[t_ps_t])
                copy_op(ev_eng(), xnTs1[:, :, :], ps_t[:].rearrange("p (k m) -> p k m", k=8)[:, :, 0:16], [t_ps_t], [t_xnTs1])
                for bb in range(4):
                    P.op("sp", (lambda e, bb=bb: e.dma_start(out=cols[:, :, 0], in_=s_shift[bb:bb + 1, :].rearrange("o (m p) -> p (o m)", p=128),
                                                             allow_slow_non_contiguous=True)), writes=[t_cols], dma=True)
                    rwkv_chunk(4, (lambda k, bb=bb: xnTs1[:, k, 4 * bb:4 * bb + 4]), t_xnTs1, "s", bb)

                def xsrc_s(ys_):
                    P.op("sp", lambda e: e.dma_start(out=xres1[0:16, :], in_=x1s_d), reads=[t_x1[NT]], writes=[t_xres1], dma=True)

                def xdst_s(ys_):
                    P.op("sp", lambda e: e.dma_start(out=y_s, in_=xres1[0:16, :]), reads=[t_xres1], dma=True)
                post_b(G_POST[1], 16, 8, hT1, t_hT1, Wo, t_Wo, 0, xsrc_s, xdst_s)
            if "nosample1" not in DBG:
                sample_l1()

        print('total_ops', len(P.ops))
        if 'dump2' in DBG:
            for _i, _o in enumerate(P.ops):
                print('OP', _i, _o.eng, _o.line, 'dma' if _o.is_dma else '')
        P.finalize_and_emit(st)
    return nc


def layer0(env):
    pass


_CACHE = {}


def kernel(x_prompt, x_sample, mem_prompt, cache_mem_kv, cache_win0, cache_win1, cache_win2, state_wkv,
           state_shift, norm_pre, norm_post, norm_mem, w_mem_kv, rel_bias, w_in_a, w_out_a, w_in_b, w_out_b,
           rwkv_mu, rwkv_w0, rwkv_w_up, rwkv_a0, rwkv_a_up, rwkv_k_k, rwkv_k_a, rwkv_r_k, rwkv_ln_w, rwkv_ln_b):
    f = lambda a: np.ascontiguousarray(np.asarray(a, dtype=np.float32))
    if "nc" not in _CACHE:
        _CACHE["nc"] = build_program()
    nc = _CACHE["nc"]
    oh = _onehot_const()
    in_maps = []
    for c in range(NCORES):
        sl = slice(4 * c, 4 * c + 4)
        in_maps.append({
            "x_p": f(x_prompt[c]),
            "x_s": f(x_sample[sl]).reshape(16, D),
            "mem_p": f(mem_prompt[c]),
            "c_mem": f(cache_mem_kv[:, sl]).reshape(2, 4, 256, 512),
            "c_w0": f(cache_win0[0, sl]).reshape(4, 128, 512),
            "c_w1": f(cache_win1[0, sl]).reshape(4, 512, 512),
            "c_w2": f(cache_win2[0, sl]).reshape(4, 2048, 512),
            "s_wkv": f(state_wkv[0, sl]),
            "s_shift": f(state_shift[0, sl]),
            "norm_pre": f(norm_pre), "norm_post": f(norm_post), "norm_mem": f(norm_mem),
            "w_mem": f(w_mem_kv), "rel_bias": f(rel_bias),
            "w_in_a": f(w_in_a[0]), "w_out_a": f(w_out_a[0]), "w_in_b": f(w_in_b[0]), "w_out_b": f(w_out_b[0]),
            "c_onehot": oh,
            "r_mu": f(rwkv_mu).reshape(1, C_SHIFT), "r_w0": f(rwkv_w0).reshape(1, 768), "r_wup": f(rwkv_w_up).reshape(64, 768),
            "r_a0": f(rwkv_a0).reshape(1, 768), "r_aup": f(rwkv_a_up).reshape(64, 768), "r_kk": f(rwkv_k_k).reshape(1, 768),
            "r_ka": f(rwkv_k_a).reshape(1, 768), "r_rk": f(rwkv_r_k).reshape(1, 768), "r_lnw": f(rwkv_ln_w).reshape(1, 768),
            "r_lnb": f(rwkv_ln_b).reshape(1, 768),
        })
    res = run_bass_kernel_spmd(nc, in_maps, core_ids=list(range(NCORES)))
    R = res.results
    cat = lambda k: np.stack([np.asarray(R[c][k]) for c in range(NCORES)])
    y_prompt = cat("y_p")
    y_sample = cat("y_s").reshape(32, 4, D)
    new_mem = cat("o_mem").transpose(1, 0, 2, 3).reshape(2, 8, 256, 2, 4, 64)
    w0p = cat("o_w0p").reshape(1, 8, 128, 2, 4, 64)
    w1p = cat("o_w1p").reshape(1, 8, 512, 2, 4, 64)
    w2p = cat("o_w2p").reshape(1, 8, 2048, 2, 4, 64)
    w0s = cat("o_w0s").reshape(1, 32, 4, 2, 4, 64)
    w1s = cat("o_w1s").reshape(1, 32, 4, 2, 4, 64)
    w2s = cat("o_w2s").reshape(1, 32, 4, 2, 4, 64)
    wkvp = cat("o_wkvp").reshape(1, 8, 12, 64, 64)
    wkvs = cat("o_wkvs").reshape(1, 32, 12, 64, 64)
    shp = cat("o_shp").reshape(1, 8, C_SHIFT)
    shs = cat("o_shs").reshape(1, 32, C_SHIFT)
    outs = (y_prompt, y_sample, new_mem, w0p, w1p, w2p, w0s, w1s, w2s, wkvp, wkvs, shp, shs)
    return tuple(np.ascontiguousarray(o.astype(np.float32)) for o in outs)
```

```python
import math
from contextlib import ExitStack
import numpy as np
import concourse.bass as bass
import concourse.mybir as mybir
from concourse.bass_utils import run_bass_kernel_spmd

F32 = mybir.dt.float32
BF16 = mybir.dt.bfloat16
AF = mybir.ActivationFunctionType
ALU = mybir.AluOpType
AX = mybir.AxisListType

ENGS = ("pe", "act", "dve", "pool", "sp")
NCORES = 8
T = 2048
D = 1024
NT = 16
NEG = -30000.0
SCALE = 0.125
DIL = (1, 4, 16)
RMS_EPS = 1e-6
C_SHIFT = 2432
A_IN = 3072
B_IN = 3712


class Tk:
    __slots__ = ("name", "lw", "rd", "dsem", "dcount", "excl")

    def __init__(self, name="", excl=False):
        self.name = name
        self.excl = excl
        self.lw = None
        self.rd = []
        self.dsem = None
        self.dcount = 0


class Op:
    __slots__ = ("eng", "fn", "deps", "is_dma", "pos", "signal", "cnt", "sem_tk", "waits", "line")


class Prog:
    def __init__(self, nc):
        self.nc = nc
        self.ops = []
        self.eng_ops = {e: [] for e in ENGS}
        self.last_dma = {}

    def op(self, eng, fn, reads=(), writes=(), dma=False, sem_tk=None, extra_deps=()):
        if len(self.ops) >= getattr(self, "maxops", 10 ** 9):
            return None
        o = Op()
        import sys as _sys
        o.line = _sys._getframe(1).f_lineno
        o.eng = eng
        o.fn = fn
        o.is_dma = dma
        o.signal = dma
        o.cnt = 0
        o.waits = []
        deps = []
        for r in reads:
            if r.lw is not None:
                deps.append((r.lw, "raw"))
            if r.excl:
                for rr in r.rd:
                    deps.append((rr, "war"))
        for w in writes:
            if w.lw is not None:
                deps.append((w.lw, "waw"))
            for rr in w.rd:
                deps.append((rr, "war"))
        for xd in extra_deps:
            deps.append((xd, "raw"))
        fdeps = []
        seen = set()
        for d, kind in deps:
            if d is o:
                continue
            if (not d.is_dma) and d.eng == eng and (not dma):
                if eng == "pe" or kind != "raw":
                    continue
            if id(d) in seen:
                continue
            seen.add(id(d))
            fdeps.append(d)
        o.deps = fdeps
        for r in (reads if fn is not None else ()):
            if r.excl:
                r.rd = [o]
            else:
                r.rd.append(o)
        for w in (writes if fn is not None else ()):
            w.lw = o
            w.rd = []
        if dma:
            if sem_tk is None:
                sem_tk = (list(writes) + list(reads))[0]
            o.sem_tk = sem_tk
            sem_tk.dcount += 1
            o.cnt = sem_tk.dcount * 16
            self.last_dma[id(sem_tk)] = o
        else:
            o.sem_tk = None
        o.pos = len(self.eng_ops[eng])
        self.eng_ops[eng].append(o)
        self.ops.append(o)
        return o

    def finalize_and_emit(self, stack):
        nc = self.nc
        waited_pos = {e: {p: -1 for p in ENGS} for e in ENGS}
        waited_dma = {e: {} for e in ENGS}
        for o in self.ops:
            e = o.eng
            for d in o.deps:
                if d.is_dma:
                    key = id(d.sem_tk)
                    if waited_dma[e].get(key, 0) >= d.cnt:
                        continue
                    waited_dma[e][key] = d.cnt
                    o.waits.append(d)
                else:
                    if waited_pos[e][d.eng] >= d.pos:
                        continue
                    waited_pos[e][d.eng] = d.pos
                    d.signal = True
                    o.waits.append(d)
        for e in ENGS:
            c = 0
            for o in self.eng_ops[e]:
                if not o.is_dma and o.signal:
                    c += 1
                    o.cnt = c
        esem = {e: stack.enter_context(nc.semaphore("es_" + e)) for e in ENGS}
        nd = 0
        for o in self.ops:
            if o.is_dma and o.sem_tk.dsem is None:
                o.sem_tk.dsem = stack.enter_context(nc.semaphore("ds%d" % nd))
                nd += 1
        self.n_dma_sems = nd
        block = stack.enter_context(nc.Block())

        def emit(engobj, elist):
            for o in elist:
                for d in o.waits:
                    if d.is_dma:
                        engobj.wait_ge(d.sem_tk.dsem, d.cnt)
                    else:
                        engobj.wait_ge(esem[d.eng], d.cnt)
                if o.fn is None:
                    continue
                ins = o.fn(engobj)
                if o.is_dma:
                    ins.then_inc(o.sem_tk.dsem, 16)
                elif o.signal:
                    ins.then_inc(esem[o.eng], 1)

        @block.tensor
        def _(pe):
            emit(pe, self.eng_ops["pe"])

        @block.scalar
        def _(act):
            emit(act, self.eng_ops["act"])

        @block.vector
        def _(dve):
            emit(dve, self.eng_ops["dve"])

        @block.gpsimd
        def _(pool):
            emit(pool, self.eng_ops["pool"])

        @block.sync
        def _(sp):
            emit(sp, self.eng_ops["sp"])
            done = set()
            for o in self.ops:
                if o.is_dma and id(o.sem_tk) not in done:
                    done.add(id(o.sem_tk))
                    sp.wait_ge(o.sem_tk.dsem, 16 * o.sem_tk.dcount)


def _t5_bucket_np(dist):
    dist = np.asarray(dist, dtype=np.int32)
    d = np.maximum(dist, 1).astype(np.float32)
    large = 16 + (np.log(d / np.float32(16.0)) / np.float32(math.log(2048 / 16)) * np.float32(16.0)).astype(np.int32)
    large = np.minimum(large, 31)
    return np.where(dist < 16, dist, large)


def _onehot_const():
    oh = np.zeros((33, 3, 384), np.float32)
    for g in range(3):
        for u in range(383):
            dist = u - 127
            if 0 <= dist <= 128:
                b = int(_t5_bucket_np(DIL[g] * dist))
                oh[b, g, u] = 1.0
            else:
                oh[32, g, u] = NEG
        oh[32, g, 383] = NEG
    return oh.reshape(33, 3 * 384)


STAGE = 2
DBG = set()


def build_program():
    nc = bass.Bass("TRN2", target_bir_lowering=False)
    try:
        nc.allow_low_precision("bf16 matmul operands with fp32 accumulation (matches problem tolerance)")
    except Exception:
        pass
    try:
        nc.allow_non_contiguous_dma("strided window / toeplitz accesses")
    except Exception:
        pass
    P = Prog(nc)
    for _d in DBG:
        if _d.startswith('maxops='):
            P.maxops = int(_d.split('=')[1])

    def din(name, shape):
        return nc.dram_tensor(name, list(shape), F32, kind="ExternalInput").ap()

    def dout(name, shape):
        return nc.dram_tensor(name, list(shape), F32, kind="ExternalOutput").ap()

    x_p = din("x_p", [T, D])
    x_s = din("x_s", [16, D])
    mem_p = din("mem_p", [256, D])
    c_mem = din("c_mem", [2, 4, 256, 512])
    c_w0 = din("c_w0", [4, 128, 512])
    c_w1 = din("c_w1", [4, 512, 512])
    c_w2 = din("c_w2", [4, 2048, 512])
    s_wkv = din("s_wkv", [4, 12, 64, 64])
    s_shift = din("s_shift", [4, C_SHIFT])
    norm_pre = din("norm_pre", [2, D])
    norm_post = din("norm_post", [2, D])
    norm_mem = din("norm_mem", [2, D])
    w_mem = din("w_mem", [2, D, 512])
    rel_bias = din("rel_bias", [32, 12])
    w_in_a = din("w_in_a", [D, A_IN])
    w_out_a = din("w_out_a", [512, D])
    w_in_b = din("w_in_b", [D, B_IN])
    w_out_b = din("w_out_b", [D, D])
    onehot = din("c_onehot", [33, 3 * 384])
    r_mu = din("r_mu", [1, C_SHIFT])
    r_w0 = din("r_w0", [1, 768])
    r_wup = din("r_wup", [64, 768])
    r_a0 = din("r_a0", [1, 768])
    r_aup = din("r_aup", [64, 768])
    r_kk = din("r_kk", [1, 768])
    r_ka = din("r_ka", [1, 768])
    r_rk = din("r_rk", [1, 768])
    r_lnw = din("r_lnw", [1, 768])
    r_lnb = din("r_lnb", [1, 768])
    y_p = dout("y_p", [T, D])
    y_s = dout("y_s", [16, D])
    o_mem = dout("o_mem", [2, 256, 512])
    o_w0p = dout("o_w0p", [128, 512])
    o_w1p = dout("o_w1p", [512, 512])
    o_w2p = dout("o_w2p", [2048, 512])
    o_w0s = dout("o_w0s", [16, 512])
    o_w1s = dout("o_w1s", [16, 512])
    o_w2s = dout("o_w2s", [16, 512])
    o_wkvp = dout("o_wkvp", [12, 64, 64])
    o_wkvs = dout("o_wkvs", [4, 12, 64, 64])
    o_shp = dout("o_shp", [1, C_SHIFT])
    o_shs = dout("o_shs", [4, C_SHIFT])
    x1_d = nc.dram_tensor("x1_d", [T, D], F32, kind="Internal").ap()
    x1s_d = nc.dram_tensor("x1s_d", [16, D], F32, kind="Internal").ap()
    e_d = nc.dram_tensor("e_d", [12, 3 * 384], F32, kind="Internal").ap()
    out_tokens = []

    with ExitStack() as st:
        def sb(name, shape, dt):
            return st.enter_context(nc.sbuf_tensor(name, list(shape), dt))

        def psum(name, shape, dt):
            return st.enter_context(nc.psum_tensor(name, list(shape), dt))

        ps_mm = [psum("ps_mm%d" % i, [128, 512], F32) for i in range(2)]
        ps_s = [psum("ps_s%d" % i, [128, 512], F32) for i in range(2)]
        ps_o = [psum("ps_o%d" % i, [128, 512], F32) for i in range(2)]
        ps_t = psum("ps_t", [128, 1024], BF16)
        ps_x = psum("ps_x", [128, 512], F32)
        t_ps_mm = [Tk("ps_mm0", True), Tk("ps_mm1", True)]
        t_ps_s = [Tk("ps_s0", True), Tk("ps_s1", True)]
        t_ps_o = [Tk("ps_o0", True), Tk("ps_o1", True)]
        t_ps_t = Tk("ps_t", True)
        t_ps_x = Tk("ps_x", True)
        rr = {"mm": 0, "s": 0, "o": 0, "ev": 0}

        def nxt(k):
            rr[k] += 1
            return rr[k] & 1

        def ev_eng():
            rr["ev"] += 1
            return "act" if (rr["ev"] & 1) else "dve"

        def copy_op(eng, out, in_, reads, writes):
            if eng == "act":
                P.op("act", lambda e: e.copy(out=out, in_=in_), reads=reads, writes=writes)
            else:
                P.op(eng, lambda e: e.tensor_copy(out=out, in_=in_), reads=reads, writes=writes)

        identf = sb("identf", [128, 128], F32)
        ident = sb("ident", [128, 128], BF16)
        t_ident = Tk()

        P.op("pool", lambda e: e.memset(identf[:], 0.0), writes=[t_ident])
        P.op("pool", lambda e: e.affine_select(out=identf[:], in_=identf[:], pattern=[[-1, 128]], compare_op=ALU.not_equal,
                                               fill=1.0, base=0, channel_multiplier=1), reads=[t_ident], writes=[t_ident])
        P.op("dve", lambda e: e.tensor_copy(out=ident[:], in_=identf[:]), reads=[t_ident], writes=[t_ident])

        ARENA_WORDS = 49000
        arena = sb("arena", [128, ARENA_WORDS], F32)
        gains4 = sb("gains", [128, 2, 1024], F32)
        gmem = arena[:, 0:2048].rearrange("p (a d) -> p a d", a=2)
        barsb = sb("barsb", [128, 16], F32)
        t_bar = {e: Tk() for e in ("pe", "act", "dve", "pool")}
        t_barinit = Tk()
        P.op("pool", lambda e: e.memset(barsb[:, :], 0.0), writes=[t_barinit])

        def barrier():
            dmas = list(P.last_dma.values())
            P.op("pe", lambda e: e.transpose(out=ps_t[0:16, 0:16], in_=ident[0:16, 0:16], identity=ident[0:16, 0:16]),
                 reads=[t_ident], writes=[t_ps_t, t_bar["pe"]], extra_deps=dmas)
            P.op("act", lambda e: e.copy(out=barsb[:, 0:1], in_=barsb[:, 8:9]), reads=[t_barinit], writes=[t_bar["act"]], extra_deps=dmas)
            P.op("dve", lambda e: e.tensor_copy(out=barsb[:, 1:2], in_=barsb[:, 9:10]), reads=[t_barinit], writes=[t_bar["dve"]], extra_deps=dmas)
            P.op("pool", lambda e: e.memset(barsb[:, 2:3], 0.0), writes=[t_bar["pool"]], extra_deps=dmas)
            allb = list(t_bar.values())
            P.op("pe", lambda e: e.transpose(out=ps_t[0:16, 0:16], in_=ident[0:16, 0:16], identity=ident[0:16, 0:16]),
                 reads=[t_ident] + allb, writes=[t_ps_t])
            P.op("act", lambda e: e.copy(out=barsb[:, 3:4], in_=barsb[:, 8:9]), reads=allb)
            P.op("dve", lambda e: e.tensor_copy(out=barsb[:, 4:5], in_=barsb[:, 9:10]), reads=allb)
            P.op("pool", lambda e: e.memset(barsb[:, 5:6], 0.0), reads=allb)
            P.op("sp", None, reads=allb)

        class _G:
            def __getitem__(self, key):
                p, gi, c = key
                return gains4[p, gi, c] if gi < 2 else gmem[p, gi - 4, c]
        gains = _G()
        t_gains = Tk()
        def load_gains(items):
            for (i, src, l) in items:
                ap_b = bass.AP(tensor=src.tensor, offset=l * D, ap=[[0, 128], [1, D]])
                P.op("sp", (lambda e, i=i, ap_b=ap_b: e.dma_start(out=gains[:, i, :], in_=ap_b)), writes=[t_gains], dma=True)
        load_gains([(0, norm_pre, 0), (1, norm_post, 0), (4, norm_mem, 0), (5, norm_mem, 1)])
        G_PRE, G_POST, G_MEM = (0, 0), (1, 1), (4, 5)

        ss = sb("ss", [128, 8], F32)
        junk = sb("junk", [128, 1024], BF16)
        t_junk = Tk()

        def rmsnorm(src_ap, np_, gi, dst_ap, t_src, t_dst, extra_reads=()):
            t_ss = Tk()
            P.op("act", lambda e: e.activation(out=junk[:np_, :], in_=src_ap, func=AF.Square),
                 reads=[t_src], writes=[t_junk])
            P.op("dve", lambda e: e.reduce_sum(out=ss[:np_, 0:1], in_=junk[:np_, :], axis=AX.X),
                 reads=[t_junk], writes=[t_ss])
            P.op("dve", lambda e: e.tensor_scalar(out=ss[:np_, 1:2], in0=ss[:np_, 0:1], scalar1=1.0 / D, scalar2=RMS_EPS,
                                                  op0=ALU.mult, op1=ALU.add), reads=[t_ss], writes=[t_ss])
            P.op("act", lambda e: e.activation(out=ss[:np_, 2:3], in_=ss[:np_, 1:2], func=AF.Sqrt), reads=[t_ss], writes=[t_ss])
            P.op("dve", lambda e: e.reciprocal(out=ss[:np_, 3:4], in_=ss[:np_, 2:3]), reads=[t_ss], writes=[t_ss])
            P.op("dve", lambda e: e.scalar_tensor_tensor(out=dst_ap, in0=src_ap, scalar=ss[:np_, 3:4], in1=gains[:np_, gi, :],
                                                         op0=ALU.mult, op1=ALU.mult),
                 reads=[t_src, t_ss, t_gains] + list(extra_reads), writes=[t_dst])

        apos = [0]
        atop = [ARENA_WORDS]

        def carve_top(nwords):
            atop[0] -= nwords
            assert atop[0] >= apos[0]
            return arena[:, atop[0]:atop[0] + nwords]

        def carve(nbytes):
            w0 = apos[0]
            nw = (nbytes + 3) // 4
            apos[0] += nw
            assert apos[0] <= atop[0], ("arena overflow", apos[0], atop[0])
            return arena[:, w0:w0 + nw]

        def carve_bf(n):
            return carve(2 * n).bitcast(BF16)

        def carve_f(n):
            return carve(4 * n)

        KmT = sb("KmT", [128, 2, 2, 256], BF16)
        Vm = sb("Vm", [128, 2, 2, 4, 80], BF16)
        t_KmT = [Tk(), Tk()]
        t_Vm = [Tk(), Tk()]
        mark = apos[0]
        carve_f(2048)
        memx = carve_f(2 * 1024).rearrange("p (b d) -> p b d", b=2)
        memn = carve_bf(2 * 1024).rearrange("p (b d) -> p b d", b=2)
        memnT = carve_bf(8 * 256).rearrange("p (k m) -> p k m", k=8)
        wm = carve_bf(8 * 512).rearrange("p (k n) -> p k n", k=8)
        kvst = carve_f(2 * 512).rearrange("p (b n) -> p b n", b=2)
        t_memx, t_memn, t_memnT, t_wm, t_kvst = Tk(), Tk(), Tk(), Tk(), [Tk(), Tk()]
        P.op("sp", lambda e: e.dma_start(out=memx, in_=mem_p.rearrange("(b p) d -> p b d", p=128)), writes=[t_memx], dma=True)
        if "novm" not in DBG:
            P.op("pool", lambda e: e.memset(Vm[:, :, :, :, 64:65], 1.0), writes=t_Vm)
        for l in range(0 if "nomem" in DBG else 2):
            P.op("pool", (lambda e, l=l: e.dma_start(out=wm, in_=w_mem[l].rearrange("(k p) n -> p k n", p=128))),
                 writes=[t_wm], dma=True)
            for b in range(2):
                rmsnorm(memx[:, b, :], 128, G_MEM[l], memn[:, b, :], t_memx, t_memn)

                def tr(e, b=b):
                    for k in range(8):
                        ins = e.transpose(out=ps_t[:, k * 128:(k + 1) * 128], in_=memn[:, b, k * 128:(k + 1) * 128], identity=ident[:])
                    return ins
                P.op("pe", tr, reads=[t_memn, t_ident], writes=[t_ps_t])
                copy_op(ev_eng(), memnT[:, :, b * 128:(b + 1) * 128], ps_t[:].rearrange("p (k m) -> p k m", k=8), [t_ps_t], [t_memnT])
            for b in range(2):
                i = nxt("mm")

                def mm(e, b=b, i=i):
                    for k in range(8):
                        ins = e.matmul(ps_mm[i][:, :], lhsT=memnT[:, k, b * 128:(b + 1) * 128], rhs=wm[:, k, :], start=(k == 0), stop=(k == 7))
                    return ins
                P.op("pe", mm, reads=[t_memnT, t_wm], writes=[t_ps_mm[i]])
                P.op("act", (lambda e, b=b, i=i: e.copy(out=kvst[:, b, :], in_=ps_mm[i][:, :])), reads=[t_ps_mm[i]], writes=[t_kvst[b]])
                P.op("dve", (lambda e, b=b, i=i, l=l: e.tensor_copy(out=Vm[:, l, b, :, 0:64],
                                                                    in_=ps_mm[i][:, 256:512].rearrange("p (h d) -> p h d", h=4))),
                     reads=[t_ps_mm[i]], writes=[t_Vm[l]])
                P.op("sp", (lambda e, b=b, l=l: e.dma_start(out=o_mem[l, b * 128:(b + 1) * 128, :], in_=kvst[:, b, :])),
                     reads=[t_kvst[b]], dma=True)
                out_tokens.append(t_kvst[b])
            for pr in range(2):
                i = nxt("mm")

                def mmk(e, pr=pr, i=i):
                    for k in range(8):
                        ins = e.matmul(ps_mm[i][:, 0:256], lhsT=wm[:, k, pr * 128:(pr + 1) * 128], rhs=memnT[:, k, :], start=(k == 0), stop=(k == 7))
                    return ins
                P.op("pe", mmk, reads=[t_memnT, t_wm], writes=[t_ps_mm[i]])
                copy_op(ev_eng(), KmT[:, l, pr, :], ps_mm[i][:, 0:256], [t_ps_mm[i]], [t_KmT[l]])


        biasT = carve_top(12 * 2 * 128).rearrange("p (a b q) -> p a b q", a=12, b=2)
        t_biasT = Tk()
        M1e = carve_top(264).bitcast(BF16)
        M1o = carve_top(264).bitcast(BF16)
        M2e = carve_top(1032).bitcast(BF16)
        M2o = carve_top(1032).bitcast(BF16)
        t_M = Tk()
        mark = apos[0]
        rb33 = carve_f(12)
        oh33 = carve_f(1152)
        esb = carve_f(1152)
        mtmp = carve_f(2064)
        t_rb, t_oh, t_esb, t_mtmp, t_ed = Tk(), Tk(), Tk(), Tk(), Tk()
        P.op("pool", lambda e: e.memset(rb33[32:33, :], 1.0), writes=[t_rb])
        P.op("sp", lambda e: e.dma_start(out=rb33[0:32, :], in_=rel_bias), writes=[t_rb], dma=True)
        P.op("sp", lambda e: e.dma_start(out=oh33[0:33, :], in_=onehot), writes=[t_oh], dma=True)
        for g in range(3):
            P.op("pe", (lambda e, g=g: e.matmul(ps_x[0:12, 0:384], lhsT=rb33[0:33, :], rhs=oh33[0:33, g * 384:(g + 1) * 384], start=True, stop=True)),
                 reads=[t_rb, t_oh], writes=[t_ps_x])
            P.op("act", (lambda e, g=g: e.copy(out=esb[0:12, g * 384:(g + 1) * 384], in_=ps_x[0:12, 0:384])), reads=[t_ps_x], writes=[t_esb])
        P.op("sp", lambda e: e.dma_start(out=e_d, in_=esb[0:12, :]), reads=[t_esb], writes=[t_ed], dma=True, sem_tk=t_esb)
        bstage = carve_f(24 * 128).rearrange("p (a q) -> p a q", a=24)
        Jrev = carve_f(128)
        t_bst, t_J = Tk(), Tk()

        P.op("pool", lambda e: e.memset(Jrev[:, :], 0.0), writes=[t_J])
        P.op("pool", lambda e: e.affine_select(out=Jrev[:, :], in_=Jrev[:, :], pattern=[[1, 128]], compare_op=ALU.not_equal, fill=1.0, base=-127,
                                               channel_multiplier=1), reads=[t_J], writes=[t_J])
        for g in range(3):
            for h in range(4):
                for blk in range(2):
                    off = (4 * g + h) * 1152 + g * 384 + (128 if blk == 0 else 0)
                    src = bass.AP(tensor=e_d.tensor, offset=off, ap=[[1, 128], [1, 128]])
                    P.op("sp", (lambda e, g=g, h=h, blk=blk, src=src: e.dma_start(out=bstage[:, (g * 4 + h) * 2 + blk, :], in_=src)),
                         reads=[t_ed], writes=[t_bst], dma=True)
        for a4 in range(6):
            i = nxt("mm")
            P.op("pe", (lambda e, a4=a4, i=i: e.matmul(ps_mm[i][:, :], lhsT=Jrev[:, :], rhs=bstage[:, 4 * a4:4 * a4 + 4, :].rearrange("p a q -> p (a q)"),
                                                         start=True, stop=True)), reads=[t_J, t_bst], writes=[t_ps_mm[i]])
            copy_op(ev_eng(), biasT.rearrange("p a b q -> p (a b q)")[:, 512 * a4:512 * (a4 + 1)], ps_mm[i][:, :], [t_ps_mm[i]], [t_biasT])
        for (Mt, ncol, base, cm) in ((M1e, 528, 16, 4), (M1o, 528, 15, 4), (M2e, 2064, 16, 16), (M2o, 2064, 15, 16)):
            P.op("pool", (lambda e, ncol=ncol: e.memset(mtmp[:, 0:ncol], 0.0)), writes=[t_mtmp])
            P.op("pool", (lambda e, ncol=ncol, base=base, cm=cm: e.affine_select(out=mtmp[:, 0:ncol], in_=mtmp[:, 0:ncol], pattern=[[-1, ncol]],
                                                                                compare_op=ALU.not_equal, fill=1.0, base=base, channel_multiplier=cm)),
                 reads=[t_mtmp], writes=[t_mtmp])
            P.op("dve", (lambda e, Mt=Mt, ncol=ncol: e.tensor_copy(out=Mt[:, :], in_=mtmp[:, 0:ncol])), reads=[t_mtmp], writes=[t_M])

        def perm_lhsT(g, r, Tq):
            if g == 1:
                Tp = Tq % 4
                if r % 2 == 0:
                    s0 = 16 + 128 * Tp - r
                    return M1e[:, s0:s0 + 128]
                s0 = 15 + 128 * Tp - r
                return M1o[:, s0:s0 + 128]
            if r % 2 == 0:
                s0 = 16 + 128 * Tq - r
                return M2e[:, s0:s0 + 128]
            s0 = 15 + 128 * Tq - r
            return M2o[:, s0:s0 + 128]

        barrier()
        apos[0] = 0
        L0 = apos[0]
        xnT = carve_bf(8 * 2048).rearrange("p (k t) -> p k t", k=8)
        xnTs = carve_bf(8 * 16).rearrange("p (k t) -> p k t", k=8)
        NW = 4
        wsl = [carve_bf(8 * 256).rearrange("p (k n) -> p k n", k=8) for _ in range(NW)]
        t_wsl = [Tk() for _ in range(NW)]
        xld = [carve_f(1024) for _ in range(2)]
        t_xld = [Tk(), Tk()]
        xnb = [carve_bf(1024) for _ in range(2)]
        t_xnb = [Tk(), Tk()]
        t_xnT = [Tk() for _ in range(NT)]
        t_xnTs = Tk()
        QT = carve_bf(2 * 2048).rearrange("p (m t) -> p m t", m=2)
        KT = carve_bf(2 * 2048).rearrange("p (m t) -> p m t", m=2)
        QmT = carve_bf(2 * 2048).rearrange("p (m t) -> p m t", m=2)
        t_QT = [[Tk() for _ in range(4)] for _ in range(2)]
        t_KT = [[Tk() for _ in range(4)] for _ in range(2)]
        t_QmT = [[Tk() for _ in range(4)] for _ in range(2)]
        QTs = carve_bf(8 * 16).rearrange("p (m t) -> p m t", m=8)
        KTs = carve_bf(6 * 16).rearrange("p (m t) -> p m t", m=6)
        t_QTs, t_KTs = Tk(), Tk()
        gate = carve_bf(16 * 512).rearrange("p (i n) -> p i n", i=16)
        t_gate = [Tk() for _ in range(NT)]
        gate_s = carve_bf(4 * 512).rearrange("p (b n) -> p b n", b=4)
        t_gate_s = Tk()
        Vg = carve_bf(16 * 4 * 80).rearrange("p (i h d) -> p i h d", i=16, h=4)
        t_Vg = [Tk() for _ in range(NT)]
        Vns = carve_bf(4 * 3 * 4 * 80).rearrange("p (b g h d) -> p b g h d", b=4, g=3, h=4)
        t_Vns = Tk()
        Og = carve_bf(48 * 260).rearrange("p (i n) -> p i n", i=48)
        t_Og = [Tk() for _ in range(48)]
        wst = [carve_f(512) for _ in range(2)]
        t_wst = [Tk(), Tk()]
        wins = carve_f(3 * 512).rearrange("p (g n) -> p g n", g=3)
        t_wins = Tk()
        sbf = [carve_f(256) for _ in range(2)]
        t_sbf = [Tk(), Tk()]
        PT = [carve_bf(256) for _ in range(2)]
        t_PT = [Tk(), Tk()]
        rr.update({"w": 0, "sb": 0, "pt": 0, "wst": 0})

        P.op("pool", lambda e: e.memset(Vg[:, :, :, 64:65], 1.0), writes=t_Vg)
        P.op("pool", lambda e: e.memset(Vns[0:4, :, :, :, 64:65], 1.0), writes=[t_Vns])

        def phaseA(src_dram, gi, lname):
            for i in range(NT + 1):
                s_ = i & 1
                np_ = 128 if i < NT else 16
                if i < NT:
                    P.op("sp", (lambda e, i=i, s_=s_: e.dma_start(out=xld[s_][:, :], in_=src_dram[0][i * 128:(i + 1) * 128, :])),
                         writes=[t_xld[s_]], dma=True)
                else:
                    P.op("sp", (lambda e, s_=s_: e.dma_start(out=xld[s_][0:16, :], in_=src_dram[1])), writes=[t_xld[s_]], dma=True)
                rmsnorm(xld[s_][0:np_, :], np_, gi, xnb[s_][0:np_, :], t_xld[s_], t_xnb[s_])

                def tr(e, s_=s_, np_=np_):
                    for k in range(8):
                        ins = e.transpose(out=ps_t[:, k * 128:k * 128 + np_], in_=xnb[s_][0:np_, k * 128:(k + 1) * 128], identity=ident[0:np_, 0:np_])
                    return ins
                P.op("pe", tr, reads=[t_xnb[s_], t_ident], writes=[t_ps_t])
                if i < NT:
                    copy_op(ev_eng(), xnT[:, :, i * 128:(i + 1) * 128], ps_t[:].rearrange("p (k m) -> p k m", k=8), [t_ps_t], [t_xnT[i]])
                else:
                    copy_op(ev_eng(), xnTs[:, :, :], ps_t[:].rearrange("p (k m) -> p k m", k=8)[:, :, 0:16], [t_ps_t], [t_xnTs])

        phaseA((x_p, x_s), G_PRE[0], "l0")

        chunk_cols = []
        for g in range(3):
            chunk_cols += [("q", g, 256 * g), ("k", g, 768 + 256 * g), ("v", g, 1536 + 256 * g)]
        chunk_cols += [("qm", 0, 2304), ("gate", 0, 2560), ("gate", 1, 2816)]
        wstate = {"loaded": 0}

        def load_w(n):
            if n >= len(chunk_cols) or n < wstate["loaded"]:
                return
            assert n == wstate["loaded"]
            wstate["loaded"] += 1
            c0 = chunk_cols[n][2]
            sl_ = n % NW
            P.op("pool", (lambda e, c0=c0, sl_=sl_: e.dma_start(out=wsl[sl_], in_=w_in_a[:, c0:c0 + 256].rearrange("(k p) n -> p k n", p=128))),
                 writes=[t_wsl[sl_]], dma=True)

        def proj_fm(sl_, dst, t_dst, sdst, t_sdst, smb0):
            for mb in range(2):
                for tg in range(4):
                    i = nxt("mm")

                    def mm(e, mb=mb, tg=tg, i=i):
                        for k in range(8):
                            ins = e.matmul(ps_mm[i][:, :], lhsT=wsl[sl_][:, k, mb * 128:(mb + 1) * 128], rhs=xnT[:, k, tg * 512:(tg + 1) * 512],
                                           start=(k == 0), stop=(k == 7))
                        return ins
                    P.op("pe", mm, reads=[t_wsl[sl_]] + t_xnT[4 * tg:4 * tg + 4], writes=[t_ps_mm[i]])
                    copy_op(ev_eng(), dst[:, mb, tg * 512:(tg + 1) * 512], ps_mm[i][:, :], [t_ps_mm[i]], [t_dst[mb][tg]])
                def mms(e, mb=mb):
                    for k in range(8):
                        ins = e.matmul(ps_x[:, 0:16], lhsT=wsl[sl_][:, k, mb * 128:(mb + 1) * 128], rhs=xnTs[:, k, :], start=(k == 0), stop=(k == 7))
                    return ins
                P.op("pe", mms, reads=[t_wsl[sl_], t_xnTs], writes=[t_ps_x])
                copy_op(ev_eng(), sdst[:, smb0 + mb, :], ps_x[:, 0:16], [t_ps_x], [t_sdst])

        def proj_tm(sl_, tok_ap_fn, reads, evac_fn):
            i = nxt("mm")

            def mm(e, i=i):
                for k in range(8):
                    ins = e.matmul(ps_mm[i][:, 0:256], lhsT=tok_ap_fn(k), rhs=wsl[sl_][:, k, :], start=(k == 0), stop=(k == 7))
                return ins
            P.op("pe", mm, reads=[t_wsl[sl_]] + list(reads), writes=[t_ps_mm[i]])
            evac_fn(ps_mm[i], t_ps_mm[i])

        def proj_tm_s(sl_, M, c0, evac_fn):
            def mm(e):
                for k in range(8):
                    ins = e.matmul(ps_x[0:M, 0:256], lhsT=xnTs[:, k, c0:c0 + M], rhs=wsl[sl_][:, k, :], start=(k == 0), stop=(k == 7))
                return ins
            P.op("pe", mm, reads=[t_wsl[sl_], t_xnTs], writes=[t_ps_x])
            evac_fn()

        def grp_tiles(g):
            d = DIL[g]
            nsb = NT // d
            return [(r, sb_) for r in range(d) for sb_ in range(nsb)]

        def tile_tok_slice(g, r, sb_):
            d = DIL[g]
            base = d * 128 * sb_ + r
            return slice(base, base + d * 127 + 1, d)

        def nat_tiles_of(g, r, sb_):
            d = DIL[g]
            return list(range(d * sb_, d * sb_ + d))

        win_out = (o_w0p, o_w1p, o_w2p)
        win_s_out = (o_w0s, o_w1s, o_w2s)
        load_w(0)
        load_w(1)
        load_w(2)
        def win_rows(g, r):
            dd = DIL[g]
            if dd == 1:
                return win_out[g]
            return win_out[g].rearrange("(j r) c -> r j c", r=dd)[r]

        def do_group(g):
            d = DIL[g]
            nsb = NT // d
            tiles = grp_tiles(g)
            n = 3 * g
            load_w(n + 3)
            proj_fm(n % NW, QT, t_QT, QTs, t_QTs, 2 * g)
            n = 3 * g + 1
            load_w(n + 3)
            proj_fm(n % NW, KT, t_KT, KTs, t_KTs, 2 * g)
            for r in range(d):
                sb_ = nsb - 1
                tsl = tile_tok_slice(g, r, sb_)

                def evk(pst, tps, r=r):
                    ws_ = nxt("wst")
                    P.op("act", lambda e: e.copy(out=wst[ws_][:, 0:256], in_=pst[:, 0:256]), reads=[tps], writes=[t_wst[ws_]])
                    rows = win_rows(g, r)
                    P.op("sp", lambda e: e.dma_start(out=rows[:, 0:256], in_=wst[ws_][:, 0:256]), reads=[t_wst[ws_]], dma=True)
                proj_tm(n % NW, (lambda k, tsl=tsl: xnT[:, k, tsl]), [t_xnT[j] for j in nat_tiles_of(g, r, sb_)], evk)

            def evks():
                P.op("act", lambda e: e.copy(out=wins[0:16, g, 0:256], in_=ps_x[0:16, 0:256]), reads=[t_ps_x], writes=[t_wins])
            proj_tm_s(n % NW, 16, 0, evks)
            n = 3 * g + 2
            load_w(n + 3)
            for gi_, (r, sb_) in enumerate(tiles):
                tsl = tile_tok_slice(g, r, sb_)
                is_win = (sb_ == nsb - 1)

                def evv(pst, tps, gi_=gi_, is_win=is_win, r=r):
                    P.op("dve", lambda e: e.tensor_copy(out=Vg[:, gi_, :, 0:64], in_=pst[:, 0:256].rearrange("p (h d) -> p h d", h=4)),
                         reads=[tps], writes=[t_Vg[gi_]])
                    if is_win:
                        ws_ = nxt("wst")
                        P.op("act", lambda e: e.copy(out=wst[ws_][:, 256:512], in_=pst[:, 0:256]), reads=[tps], writes=[t_wst[ws_]])
                        rows = win_rows(g, r)
                        P.op("sp", lambda e: e.dma_start(out=rows[:, 256:512], in_=wst[ws_][:, 256:512]), reads=[t_wst[ws_]], dma=True)
                proj_tm(n % NW, (lambda k, tsl=tsl: xnT[:, k, tsl]), [t_xnT[j] for j in nat_tiles_of(g, r, sb_)], evv)

            def evvs():
                P.op("act", lambda e: e.copy(out=wins[0:16, g, 256:512], in_=ps_x[0:16, 0:256]), reads=[t_ps_x], writes=[t_wins])
            proj_tm_s(n % NW, 16, 0, evvs)
            P.op("sp", lambda e: e.dma_start(out=win_s_out[g], in_=wins[0:16, g, :]), reads=[t_wins], dma=True)
            for bb in range(4):
                def evvn(bb=bb):
                    P.op("dve", lambda e: e.tensor_copy(out=Vns[0:4, bb, g, :, 0:64], in_=ps_x[0:4, 0:256].rearrange("p (h d) -> p h d", h=4)),
                         reads=[t_ps_x], writes=[t_Vns])
                proj_tm_s(n % NW, 4, 4 * bb, evvn)
            allqk = [x for row in t_QT for x in row] + [x for row in t_KT for x in row]
            pending = []
            for gi_, (r, sb_) in enumerate(tiles):
                qsl = tile_tok_slice(g, r, sb_)
                blocks = ([(0, (r, sb_ - 1))] if sb_ > 0 else []) + [(1, (r, sb_))]
                io = nxt("o")
                for h in range(4):
                    pr, hf = h // 2, h % 2
                    psl = slice(64 * hf, 64 * hf + 64)
                    isx = nxt("s")

                    def qk(e, isx=isx, pr=pr, psl=psl, qsl=qsl, blocks=blocks):
                        for (blk, (kr, ksb)) in blocks:
                            ksl = tile_tok_slice(g, kr, ksb)
                            ins = e.matmul(ps_s[isx][:, blk * 128:(blk + 1) * 128], lhsT=KT[psl, pr, ksl], rhs=QT[psl, pr, qsl], start=True, stop=True)
                        return ins
                    P.op("pe", qk, reads=allqk, writes=[t_ps_s[isx]])
                    b0 = blocks[0][0]
                    csl = slice(b0 * 128, 256)
                    isb = nxt("sb")
                    P.op("dve", (lambda e, isx=isx, isb=isb, csl=csl, h=h, b0=b0: e.scalar_tensor_tensor(
                        out=sbf[isb][:, csl], in0=ps_s[isx][:, csl], scalar=SCALE,
                        in1=biasT[:, g * 4 + h, b0:2, :].rearrange("p b q -> p (b q)"), op0=ALU.mult, op1=ALU.add)),
                        reads=[t_ps_s[isx], t_biasT], writes=[t_sbf[isb]])
                    ipt = nxt("pt")
                    P.op("act", (lambda e, isb=isb, ipt=ipt, csl=csl: e.activation(out=PT[ipt][:, csl], in_=sbf[isb][:, csl], func=AF.Exp)),
                         reads=[t_sbf[isb]], writes=[t_PT[ipt]])
                    if pending:
                        pending.pop(0)()

                    def later(ipt=ipt, io=io, h=h, blocks=blocks, gi_=gi_):
                        def pv(e):
                            nb = len(blocks)
                            for bi, (blk, (kr, ksb)) in enumerate(blocks):
                                kgi = kr * nsb + ksb
                                ins = e.matmul(ps_o[io][:, h * 65:(h + 1) * 65], lhsT=PT[ipt][:, blk * 128:(blk + 1) * 128], rhs=Vg[:, kgi, h, 0:65],
                                               start=(bi == 0), stop=(bi == nb - 1))
                            return ins
                        P.op("pe", pv, reads=[t_PT[ipt]] + [t_Vg[kr * nsb + ksb] for (_, (kr, ksb)) in blocks], writes=[t_ps_o[io]])
                        if h == 3:
                            copy_op(ev_eng(), Og[:, 16 * g + gi_, :], ps_o[io][:, 0:260], [t_ps_o[io]], [t_Og[16 * g + gi_]])
                    pending.append(later)
            while pending:
                pending.pop(0)()

        for g in range(3):
            do_group(g)

        n = 9
        load_w(n + 3)
        proj_fm(n % NW, QmT, t_QmT, QTs, t_QTs, 6)
        def do_gate(gc):
            n = 10 + gc
            load_w(n + 3)
            for i in range(NT):
                def evg(pst, tps, i=i, gc=gc):
                    P.op("act", lambda e: e.activation(out=gate[:, i, gc * 256:(gc + 1) * 256], in_=pst[:, 0:256], func=AF.Silu),
                         reads=[tps], writes=[t_gate[i]])
                proj_tm(n % NW, (lambda k, i=i: xnT[:, k, i * 128:(i + 1) * 128]), [t_xnT[i]], evg)
            for bb in range(4):
                def evgs(bb=bb, gc=gc):
                    P.op("act", lambda e: e.activation(out=gate_s[0:4, bb, gc * 256:(gc + 1) * 256], in_=ps_x[0:4, 0:256], func=AF.Silu),
                         reads=[t_ps_x], writes=[t_gate_s])
                proj_tm_s(n % NW, 4, 4 * bb, evgs)
        for gc in range(2):
            do_gate(gc)

        barrier()
        l0_keep = apos[0]
        apos[0] = L0
        wout = carve_bf(4 * 1024).rearrange("p (k n) -> p k n", k=4)
        t_wout = Tk()
        P.op("pool", lambda e: e.dma_start(out=wout, in_=w_out_a.rearrange("(k p) n -> p k n", p=128)), writes=[t_wout], dma=True)
        hbuf = [carve_f(512) for _ in range(2)]
        t_hbuf = [Tk(), Tk()]
        hb = [carve_bf(512) for _ in range(2)]
        t_hb = [Tk(), Tk()]
        hT = [carve_bf(512).rearrange("p (k t) -> p k t", k=4) for _ in range(2)]
        t_hT = [Tk(), Tk()]
        rcp = [carve_f(8) for _ in range(2)]
        t_rcp = [Tk(), Tk()]
        ytmp = [carve_f(1024) for _ in range(2)]
        t_ytmp = [Tk(), Tk()]
        cst = carve_f(2048).rearrange("p (t n) -> p t n", t=4)
        ckb = carve_bf(4 * 256).rearrange("p (t n) -> p t n", t=4)
        cKT = carve_bf(8 * 128).rearrange("p (i n) -> p i n", i=8)
        cV = carve_bf(4 * 4 * 80).rearrange("p (t h d) -> p t h d", t=4, h=4)
        PTz = carve_bf(16)
        PTn = carve_bf(16)
        sbn = carve_f(16)
        t_cst, t_ckb, t_cKT, t_cV, t_PTz, t_PTn, t_sbn = Tk(), Tk(), Tk(), Tk(), Tk(), Tk(), Tk()
        assert apos[0] <= L0 + 8192 + 64 + 4096, apos[0] - L0
        apos[0] = l0_keep
        xres = xld
        t_xres = t_xld
        t_x1 = [Tk() for _ in range(NT + 1)]
        rr.update({"hb": 0})
        CTX = dict(PT=PT, t_PT=t_PT, ytmp=ytmp, t_ytmp=t_ytmp, xres=xres, t_xres=t_xres,
                   cst=cst, ckb=ckb, cKT=cKT, cV=cV, t_cst=t_cst, t_ckb=t_ckb, t_cKT=t_cKT, t_cV=t_cV)
        P.op("pool", lambda e: e.memset(cV[:, :, :, 64:65], 1.0), writes=[t_cV])
        P.op("pool", lambda e: e.memset(PTz[:, :], 0.0), writes=[t_PTz])

        def cross_attn(k_ap_fn, t_k, v_ap_fn, t_v, qT_ap_fn, q_reads, nq, ps_out, t_ps_out):
            PT_, t_PT_ = CTX["PT"], CTX["t_PT"]
            for h in range(4):
                pr, hf = h // 2, h % 2
                psl = slice(64 * hf, 64 * hf + 64)
                isx = nxt("s")

                def qk(e, isx=isx, pr=pr, psl=psl):
                    for blk in range(2):
                        ins = e.matmul(ps_s[isx][:, blk * 128:blk * 128 + nq], lhsT=k_ap_fn(psl, pr, blk), rhs=qT_ap_fn(psl, pr),
                                       start=True, stop=True)
                    return ins
                P.op("pe", qk, reads=[t_k] + list(q_reads), writes=[t_ps_s[isx]])
                ipt = nxt("pt")
                P.op("act", (lambda e, isx=isx, ipt=ipt: e.activation(
                    out=PT_[ipt][:, :].rearrange("p (b q) -> p b q", b=2)[:, :, 0:nq],
                    in_=ps_s[isx][:, 0:256].rearrange("p (b q) -> p b q", b=2)[:, :, 0:nq], func=AF.Exp, scale=SCALE)),
                    reads=[t_ps_s[isx]], writes=[t_PT_[ipt]])

                def pv(e, ipt=ipt, h=h):
                    for blk in range(2):
                        ins = e.matmul(ps_out[0:nq, h * 65:(h + 1) * 65], lhsT=PT_[ipt][:, blk * 128:blk * 128 + nq], rhs=v_ap_fn(blk, h),
                                       start=(blk == 0), stop=(blk == 1))
                    return ins
                P.op("pe", pv, reads=[t_PT_[ipt], t_v], writes=[t_ps_out])

        def normalize_o(ps_in, t_ps_in, nq, dst_ap, t_dst, t_rc, rc):
            v = ps_in[0:nq, 0:260].rearrange("p (h d) -> p h d", h=4)
            P.op("dve", lambda e: e.reciprocal(out=rc[0:nq, 0:4], in_=v[:, :, 64]), reads=[t_ps_in], writes=[t_rc])
            P.op("dve", lambda e: e.tensor_tensor(out=dst_ap.rearrange("p (h d) -> p h d", h=4), in0=v[:, :, 0:64],
                                                  in1=rc[0:nq, 0:4].unsqueeze(2).to_broadcast([nq, 4, 64]), op=ALU.mult),
                 reads=[t_ps_in, t_rc], writes=[t_dst])

        def post_a(nq, hbuf_ap, t_hb_in, gate_ap, t_gate_in, nk, hb_t, t_hb_t, hT_t, t_hT_t, col0):
            P.op("dve", lambda e: e.tensor_tensor(out=hb_t[0:nq, 0:nk * 128], in0=hbuf_ap, in1=gate_ap, op=ALU.mult),
                 reads=[t_hb_in, t_gate_in], writes=[t_hb_t])

            def tr(e):
                for k in range(nk):
                    ins = e.transpose(out=ps_t[:, k * 128:k * 128 + nq], in_=hb_t[0:nq, k * 128:(k + 1) * 128], identity=ident[0:nq, 0:nq])
                return ins
            P.op("pe", tr, reads=[t_hb_t, t_ident], writes=[t_ps_t])
            copy_op(ev_eng(), hT_t[:, 0:nk, col0:col0 + nq], ps_t[:].rearrange("p (k m) -> p k m", k=8)[:, 0:nk, 0:nq], [t_ps_t], [t_hT_t])

        def post_b(gi_post, nq, nk, hT_t, t_hT_t, wout_t, t_wout_t, ys_, x_src_fn, x_dst_fn):
            ytmp_, t_ytmp_, xres_, t_xres_ = CTX["ytmp"], CTX["t_ytmp"], CTX["xres"], CTX["t_xres"]
            for nb in range(2):
                def mm(e, nb=nb):
                    for k in range(nk):
                        ins = e.matmul(ps_mm[nb][0:nq, :], lhsT=hT_t[:, k, 0:nq], rhs=wout_t[:, k, nb * 512:(nb + 1) * 512], start=(k == 0), stop=(k == nk - 1))
                    return ins
                P.op("pe", mm, reads=[t_hT_t, t_wout_t], writes=[t_ps_mm[nb]])
                copy_op("act" if nb == 0 else "dve", ytmp_[ys_][0:nq, nb * 512:(nb + 1) * 512], ps_mm[nb][0:nq, :], [t_ps_mm[nb]], [t_ytmp_[ys_]])
            x_src_fn(ys_)
            rmsnorm(ytmp_[ys_][0:nq, :], nq, gi_post, ytmp_[ys_][0:nq, :], t_ytmp_[ys_], t_ytmp_[ys_])
            P.op("dve", lambda e: e.tensor_tensor(out=xres_[ys_][0:nq, :], in0=xres_[ys_][0:nq, :], in1=ytmp_[ys_][0:nq, :], op=ALU.add),
                 reads=[t_ytmp_[ys_], t_xres_[ys_]], writes=[t_xres_[ys_]])
            x_dst_fn(ys_)

        tile_hs = {}

        def tile_X(Tq):
            hs_ = nxt("hb")
            tile_hs[Tq] = hs_
            io = nxt("o")
            srcs = [(0, 0, Tq, ident[:, :])]
            srcs += [(1, r, Tq // 4, perm_lhsT(1, r, Tq)) for r in range(4)]
            srcs += [(2, r, 0, perm_lhsT(2, r, Tq)) for r in range(16)]

            def comb(e):
                ns = len(srcs)
                for si, (g, r, sb_, lt) in enumerate(srcs):
                    gi_ = r * (NT // DIL[g]) + sb_
                    ins = e.matmul(ps_o[io][:, 0:260], lhsT=lt, rhs=Og[:, 16 * g + gi_, :], start=(si == 0), stop=(si == ns - 1))
                return ins
            P.op("pe", comb, reads=[t_ident, t_M] + [t_Og[16 * g + r * (NT // DIL[g]) + sb_] for (g, r, sb_, _) in srcs], writes=[t_ps_o[io]])
            normalize_o(ps_o[io], t_ps_o[io], 128, hbuf[hs_][:, 0:256], t_hbuf[hs_], t_rcp[hs_], rcp[hs_])
            io2 = nxt("o")
            cross_attn((lambda psl, pr, blk: KmT[psl, 0, pr, blk * 128:(blk + 1) * 128]), t_KmT[0],
                       (lambda blk, h: Vm[:, 0, blk, h, 0:65]), t_Vm[0],
                       (lambda psl, pr: QmT[psl, pr, Tq * 128:(Tq + 1) * 128]), [x for row in t_QmT for x in row],
                       128, ps_o[io2], t_ps_o[io2])
            normalize_o(ps_o[io2], t_ps_o[io2], 128, hbuf[hs_][:, 256:512], t_hbuf[hs_], t_rcp[hs_], rcp[hs_])

        def tile_Y(Tq):
            hs_ = tile_hs[Tq]
            post_a(128, hbuf[hs_][:, :], t_hbuf[hs_], gate[:, Tq, :], t_gate[Tq], 4, hb[hs_], t_hb[hs_], hT[hs_], t_hT[hs_], 0)

            def xsrc(ys_):
                P.op("sp", lambda e: e.dma_start(out=xres[ys_][:, :], in_=x_p[Tq * 128:(Tq + 1) * 128, :]), writes=[t_xres[ys_]], dma=True)

            def xdst(ys_):
                P.op("sp", lambda e: e.dma_start(out=x1_d[Tq * 128:(Tq + 1) * 128, :], in_=xres[ys_][:, :]), reads=[t_xres[ys_]], writes=[t_x1[Tq]],
                     dma=True, sem_tk=t_xres[ys_])
                if "l0out" in DBG:
                    P.op("sp", lambda e: e.dma_start(out=y_p[Tq * 128:(Tq + 1) * 128, :], in_=xres[ys_][:, :]), reads=[t_xres[ys_]], dma=True)
            post_b(G_POST[0], 128, 4, hT[hs_], t_hT[hs_], wout, t_wout, hs_, xsrc, xdst)

        tile_X(0)
        for Tq in range(1, NT):
            tile_X(Tq)
            tile_Y(Tq - 1)
        tile_Y(NT - 1)

        def load_piece(src_ap, nt_):
            cst, ckb, cKT, cV = CTX["cst"], CTX["ckb"], CTX["cKT"], CTX["cV"]
            t_cst, t_ckb, t_cKT, t_cV = CTX["t_cst"], CTX["t_ckb"], CTX["t_cKT"], CTX["t_cV"]
            P.op("sp", lambda e: e.dma_start(out=cst[:, 0:nt_, :], in_=src_ap), writes=[t_cst], dma=True)
            P.op("dve", lambda e: e.tensor_copy(out=ckb[:, 0:nt_, :], in_=cst[:, 0:nt_, 0:256]), reads=[t_cst], writes=[t_ckb])
            P.op("act", lambda e: e.copy(out=cV[:, 0:nt_, :, 0:64], in_=cst[:, 0:nt_, 256:512].rearrange("p t (h d) -> p t h d", h=4)),
                 reads=[t_cst], writes=[t_cV])

            def tr(e):
                for t_ in range(nt_):
                    for pr in range(2):
                        idx = t_ * 2 + pr
                        ins = e.transpose(out=ps_t[:, idx * 128:(idx + 1) * 128], in_=ckb[:, t_, pr * 128:(pr + 1) * 128], identity=ident[:, :])
                return ins
            P.op("pe", tr, reads=[t_ckb, t_ident], writes=[t_ps_t])
            copy_op(ev_eng(), cKT[:, 0:2 * nt_, :], ps_t[:].rearrange("p (k m) -> p k m", k=8)[:, 0:2 * nt_, :], [t_ps_t], [t_cKT])

        def sample_batch(bb):
            io = nxt("o")
            pso, tpso = ps_o[io], t_ps_o[io]
            first = [True]

            def pv_mm(e, out_ap, lhsT, rhs):
                ins = e.matmul(out_ap, lhsT=lhsT, rhs=rhs, start=first[0], stop=False, skip_group_check=True)
                first[0] = False
                return ins
            qs = slice(4 * bb, 4 * bb + 4)
            for g in range(3):
                d = DIL[g]
                if g == 0:
                    load_piece(c_w0[bb].rearrange("(j o) c -> j o c", o=1), 1)
                elif g == 1:
                    load_piece(c_w1[bb].rearrange("(j r) c -> j r c", r=4), 4)
                else:
                    load_piece(c_w2[bb].rearrange("(j r) c -> j r c", r=16)[:, 0:4, :], 4)
                for h in range(4):
                    pr, hf = h // 2, h % 2
                    psl = slice(64 * hf, 64 * hf + 64)
                    isx = nxt("s")
                    if g == 0:
                        def qk(e, isx=isx, pr=pr, psl=psl):
                            e.matmul(ps_s[isx][:, 0:4], lhsT=cKT[psl, pr, :], rhs=QTs[psl, pr, qs], start=True, stop=True)
                            return e.matmul(ps_s[isx][0:4, 8:12], lhsT=KTs[psl, pr, qs], rhs=QTs[psl, pr, qs], start=True, stop=True)
                        P.op("pe", qk, reads=[t_cKT, t_QTs, t_KTs], writes=[t_ps_s[isx]])
                        isb = nxt("sb")
                        P.op("dve", (lambda e, isx=isx, isb=isb, h=h: e.scalar_tensor_tensor(
                            out=sbf[isb][:, 0:4], in0=ps_s[isx][:, 0:4], scalar=SCALE, in1=biasT[:, h, 0, 0:4], op0=ALU.mult, op1=ALU.add)),
                            reads=[t_ps_s[isx], t_biasT], writes=[t_sbf[isb]])
                        P.op("dve", (lambda e, isx=isx, h=h: e.scalar_tensor_tensor(
                            out=sbn[0:4, 0:4], in0=ps_s[isx][0:4, 8:12], scalar=SCALE, in1=biasT[0:4, h, 1, 0:4], op0=ALU.mult, op1=ALU.add)),
                            reads=[t_ps_s[isx], t_biasT], writes=[t_sbn])
                        ipt = nxt("pt")
                        P.op("act", (lambda e, isb=isb, ipt=ipt: e.activation(out=PT[ipt][:, 0:4], in_=sbf[isb][:, 0:4], func=AF.Exp)),
                             reads=[t_sbf[isb]], writes=[t_PT[ipt]])
                        P.op("act", lambda e: e.activation(out=PTn[0:4, 0:4], in_=sbn[0:4, 0:4], func=AF.Exp), reads=[t_sbn], writes=[t_PTn])

                        def pv(e, ipt=ipt, h=h):
                            pv_mm(e, pso[0:4, h * 65:(h + 1) * 65], PT[ipt][:, 0:4], cV[:, 0, h, 0:65])
                            return pv_mm(e, pso[0:4, h * 65:(h + 1) * 65], PTn[0:4, 0:4], Vns[0:4, bb, 0, h, 0:65])
                        P.op("pe", pv, reads=[t_PT[ipt], t_PTn, t_cV, t_Vns], writes=[tpso])
                    else:
                        def qk(e, isx=isx, pr=pr, psl=psl, g=g):
                            for t_ in range(4):
                                e.matmul(ps_s[isx][:, t_:t_ + 1], lhsT=cKT[psl, 2 * t_ + pr, :], rhs=QTs[psl, 2 * g + pr, 4 * bb + t_:4 * bb + t_ + 1],
                                         start=True, stop=True)
                            return e.matmul(ps_s[isx][0:4, 8:12], lhsT=KTs[psl, 2 * g + pr, qs], rhs=QTs[psl, 2 * g + pr, qs], start=True, stop=True)
                        P.op("pe", qk, reads=[t_cKT, t_QTs, t_KTs], writes=[t_ps_s[isx]])
                        P.op("act", (lambda e, isx=isx, h=h, g=g: e.activation(out=PTz[:, 0:16:5], in_=ps_s[isx][:, 0:4], func=AF.Exp,
                                                                              bias=biasT[:, g * 4 + h, 0, 0:1], scale=SCALE)),
                             reads=[t_ps_s[isx], t_biasT], writes=[t_PTz])
                        P.op("dve", (lambda e, isx=isx, h=h, g=g: e.scalar_tensor_tensor(
                            out=sbn[0:4, 0:4], in0=ps_s[isx][0:4, 8:12], scalar=SCALE, in1=biasT[0:4, g * 4 + h, 1, 0:4], op0=ALU.mult, op1=ALU.add)),
                            reads=[t_ps_s[isx], t_biasT], writes=[t_sbn])
                        P.op("act", lambda e: e.activation(out=sbn[0:4, 4:8], in_=sbn[0:4, 0:4], func=AF.Exp), reads=[t_sbn], writes=[t_sbn])
                        P.op("dve", lambda e: e.tensor_tensor(out=PTn[0:4, 0:4], in0=sbn[0:4, 4:8], in1=identf[0:4, 0:4], op=ALU.mult),
                             reads=[t_sbn, t_ident], writes=[t_PTn])

                        def pv(e, h=h, g=g):
                            for t_ in range(4):
                                pv_mm(e, pso[0:4, h * 65:(h + 1) * 65], PTz[:, 4 * t_:4 * t_ + 4], cV[:, t_, h, 0:65])
                            return pv_mm(e, pso[0:4, h * 65:(h + 1) * 65], PTn[0:4, 0:4], Vns[0:4, bb, g, h, 0:65])
                        P.op("pe", pv, reads=[t_PTz, t_PTn, t_cV, t_Vns], writes=[tpso])
            hs_ = 0
            normalize_o(pso, tpso, 4, hbuf[hs_][0:4, 0:256], t_hbuf[hs_], t_rcp[hs_], rcp[hs_])
            load_piece(c_mem[0, bb].rearrange("(b j) c -> j b c", b=2), 2)
            io2 = nxt("o")
            cross_attn((lambda psl, pr, blk: cKT[psl, 2 * blk + pr, :]), t_cKT, (lambda blk, h: cV[:, blk, h, 0:65]), t_cV,
                       (lambda psl, pr: QTs[psl, 6 + pr, qs]), [t_QTs], 4, ps_o[io2], t_ps_o[io2])
            normalize_o(ps_o[io2], t_ps_o[io2], 4, hbuf[hs_][0:4, 256:512], t_hbuf[hs_], t_rcp[hs_], rcp[hs_])
            post_a(4, hbuf[hs_][0:4, :], t_hbuf[hs_], gate_s[0:4, bb, :], t_gate_s, 4, hb[hs_], t_hb[hs_], hT[1], t_hT[1], 4 * bb)

        for bb in range(0 if "nosample" in DBG else 4):
            sample_batch(bb)

        def xsrc_s(ys_):
            P.op("sp", lambda e: e.dma_start(out=xres[ys_][0:16, :], in_=x_s), writes=[t_xres[ys_]], dma=True)

        def xdst_s(ys_):
            P.op("sp", lambda e: e.dma_start(out=x1s_d, in_=xres[ys_][0:16, :]), reads=[t_xres[ys_]], writes=[t_x1[NT]], dma=True, sem_tk=t_xres[ys_])
            if "l0out" in DBG:
                P.op("sp", lambda e: e.dma_start(out=y_s, in_=xres[ys_][0:16, :]), reads=[t_xres[ys_]], dma=True)
        if "nosample" not in DBG:
            post_b(G_POST[0], 16, 4, hT[1], t_hT[1], wout, t_wout, 1, xsrc_s, xdst_s)
        print('n_ops', len(P.ops))
        if 'dump' in DBG:
            for _i, _o in enumerate(P.ops):
                print('OP', _i, _o.eng, _o.line, 'dma' if _o.is_dma else '')

        if STAGE >= 2:
            barrier()
            apos[0] = 0
            atop[0] = ARENA_WORDS
            load_gains([(0, norm_pre, 1), (1, norm_post, 1)])
            Wb = carve_bf(8 * B_IN).rearrange("p (k n) -> p k n", k=8)
            t_Wb = Tk()
            for c0 in range(0, B_IN, 464):
                P.op("pool", (lambda e, c0=c0: e.dma_start(out=Wb[:, :, c0:c0 + 464], in_=w_in_b[:, c0:c0 + 464].rearrange("(k p) n -> p k n", p=128))),
                     writes=[t_Wb], dma=True)
            Wo = carve_bf(8 * 1024).rearrange("p (k n) -> p k n", k=8)
            t_Wo = Tk()
            P.op("pool", lambda e: e.dma_start(out=Wo, in_=w_out_b.rearrange("(k p) n -> p k n", p=128)), writes=[t_Wo], dma=True)
            wup = carve_bf(768)
            aup = carve_bf(768)
            t_lora = Tk()
            P.op("pool", lambda e: e.dma_start(out=wup[0:64, :], in_=r_wup), writes=[t_lora], dma=True)
            P.op("pool", lambda e: e.dma_start(out=aup[64:128, :], in_=r_aup), writes=[t_lora], dma=True)
            par = carve_f(64)
            t_par = Tk()
            P_MU, P_W0, P_A0, P_KK, P_KA, P_OMKA, P_RK = 0, 19, 25, 31, 37, 43, 49
            for (src, c0, nm) in ((r_mu, P_MU, 19), (r_w0, P_W0, 6), (r_a0, P_A0, 6), (r_kk, P_KK, 6), (r_ka, P_KA, 6), (r_rk, P_RK, 6)):
                P.op("sp", (lambda e, src=src, c0=c0, nm=nm: e.dma_start(out=par[:, c0:c0 + nm], in_=src.rearrange("o (m p) -> p (o m)", p=128),
                                                                         allow_slow_non_contiguous=True)), writes=[t_par], dma=True)
            P.op("dve", lambda e: e.tensor_scalar(out=par[:, P_OMKA:P_OMKA + 6], in0=par[:, P_KA:P_KA + 6], scalar1=-1.0, scalar2=1.0, op0=ALU.mult, op1=ALU.add),
                 reads=[t_par], writes=[t_par])
            lnw_b = carve_f(768)
            lnb_b = carve_f(768)
            t_ln = Tk()
            for (dst, src) in ((lnw_b, r_lnw), (lnb_b, r_lnb)):
                apb = bass.AP(tensor=src.tensor, offset=0, ap=[[0, 128], [1, 768]])
                P.op("sp", (lambda e, dst=dst, apb=apb: e.dma_start(out=dst, in_=apb)), writes=[t_ln], dma=True)
            SA, SB, SC, SD, SE, SF, SG = [carve_f(768).rearrange("p (m j) -> p m j", m=6) for _ in range(7)]
            t_S = {k_: Tk() for k_ in "ABCDEFG"}
            t_mt1 = t_S["A"]
            mtmp1 = SA[:, :, :].rearrange("p m j -> p (m j)")
            ML = carve_bf(512).rearrange("p (h i) -> p h i", h=4)
            MU = carve_bf(512).rearrange("p (h i) -> p h i", h=4)
            MUi = carve_bf(512).rearrange("p (h i) -> p h i", h=4)
            Irep = carve_bf(512).rearrange("p (h i) -> p h i", h=4)
            rmask = carve_bf(768).rearrange("p (m j) -> p m j", m=6)
            bd = carve_f(128)
            hsel = carve_bf(2)
            t_cst1 = Tk()
            for (Mt, cm, step, cmp_) in ((ML, 1, -1, ALU.is_gt), (MU, -1, 1, ALU.is_gt), (MUi, -1, 1, ALU.is_ge), (Irep, 1, -1, ALU.is_equal)):
                P.op("pool", lambda e: e.memset(mtmp1[:, 0:512], 1.0), writes=[t_mt1])

                def mk(e, cm=cm, step=step, cmp_=cmp_):
                    v = mtmp1[:, 0:512].rearrange("p (h i) -> p h i", h=4)
                    return e.affine_select(out=v, in_=v, pattern=[[0, 4], [step, 128]], compare_op=cmp_, fill=0.0, base=0, channel_multiplier=cm)
                P.op("pool", mk, reads=[t_mt1], writes=[t_mt1])
                P.op("dve", (lambda e, Mt=Mt: e.tensor_copy(out=Mt, in_=mtmp1[:, 0:512].rearrange("p (h i) -> p h i", h=4))), reads=[t_mt1], writes=[t_cst1])

            def mk2(e):
                e.memset(rmask[:, :, :], 1.0)
                e.memset(rmask[:, :, 0:1], 0.0)
                e.memset(bd[:, :], 0.0)
                e.memset(bd[0:64, 0:64], 1.0)
                e.memset(bd[64:128, 64:128], 1.0)
                e.memset(hsel[:, :], 0.0)
                e.memset(hsel[0:64, 0:1], 1.0)
                return e.memset(hsel[64:128, 1:2], 1.0)
            P.op("pool", mk2, writes=[t_cst1])

            xl_w0 = apos[0]
            xld1 = carve_f(1024)
            xnb1 = carve_bf(1024)
            xnTc = carve_bf(8 * 128).rearrange("p (k t) -> p k t", k=8)
            xnTs1 = carve_bf(8 * 16).rearrange("p (k t) -> p k t", k=8)
            t_xld1, t_xnb1, t_xnTc, t_xnTs1 = Tk(), Tk(), Tk(), Tk()
            cols = carve_f(19 * 129).rearrange("p (m j) -> p m j", m=19)
            t_cols = Tk()
            lastc = carve_f(20)
            t_lastc_g = Tk()
            t_xs = t_cols
            QmTc = carve_bf(2 * 128).rearrange("p (m j) -> p m j", m=2)
            t_QmTc = Tk()
            gt = carve_bf(1024)
            t_gt = Tk()
            lw = carve_bf(128)
            t_lw = Tk()
            outs_w0 = apos[0]
            t_o = {k_: Tk() for k_ in ("AT", "BT", "KT", "KH", "BH", "RT", "RKT", "XV")}
            BT, KT1, KH, BH, RKT, XV = [carve_bf(768).rearrange("p (m j) -> p m j", m=6) for _ in range(6)]
            AT2 = carve_bf(12 * 128).rearrange("p (h j) -> p h j", h=12)
            RT2 = carve_bf(12 * 128).rearrange("p (h j) -> p h j", h=12)
            P.op("pool", lambda e: e.memset(AT2[:, :, :], 0.0), writes=[t_o["AT"]])
            P.op("pool", lambda e: e.memset(RT2[:, :, :], 0.0), writes=[t_o["RT"]])
            Vt = carve_bf(768)
            Kh = carve_bf(768)
            Bh = carve_bf(768)
            t_Vt, t_Kh, t_Bh = Tk(), Tk(), Tk()
            Lh = [carve_bf(512).rearrange("p (h i) -> p h i", h=4) for _ in range(3)]
            Xh = [carve_bf(512).rearrange("p (h i) -> p h i", h=4) for _ in range(3)]
            Qh = [carve_bf(512).rearrange("p (h i) -> p h i", h=4) for _ in range(3)]
            t_Lh, t_Xh, t_Qh = [Tk() for _ in range(3)], [Tk() for _ in range(3)], [Tk() for _ in range(3)]
            Qf = carve_bf(12 * 128).rearrange("p (h i) -> p h i", h=12)
            Aak = carve_bf(12 * 128).rearrange("p (h i) -> p h i", h=12)
            Ark = carve_bf(12 * 128).rearrange("p (h i) -> p h i", h=12)
            Arb = carve_bf(12 * 128).rearrange("p (h i) -> p h i", h=12)
            t_Qf, t_Aak, t_Ark, t_Arb = Tk(), Tk(), Tk(), Tk()
            zb_w0 = apos[0]
            Zb = carve_bf(768)
            Ub = carve_bf(768)
            t_Zb, t_Ub = Tk(), Tk()
            Yf = SD[:, :, :].rearrange("p m j -> p (m j)")
            t_Yf = t_S["D"]
            St = carve_f(384).rearrange("p (m v) -> p m v", m=6)
            Sbf = carve_bf(384).rearrange("p (m v) -> p m v", m=6)
            t_St = Tk()
            wc = carve_f(8)
            t_wc = Tk()
            gsm = carve_f(96)
            t_gsm = Tk()
            hbuf1 = carve_f(1024)
            hb1 = carve_bf(1024)
            hT1 = carve_bf(8 * 128).rearrange("p (k t) -> p k t", k=8)
            t_hbuf1, t_hb1, t_hT1 = Tk(), Tk(), Tk()
            rcp1 = carve_f(8)
            t_rcp1 = Tk()
            xres1 = carve_f(1024)
            t_xres1 = Tk()
            PT1 = [carve_bf(256) for _ in range(2)]
            t_PT1 = [Tk(), Tk()]
            svst = arena[:, zb_w0:zb_w0 + 768].rearrange("p (h k) -> p h k", h=12)
            t_svst = Tk()
            print("L1 arena words", apos[0])
            C0 = 0.6065306597126334
            psq = [ps_s[0], ps_s[1], ps_o[0], ps_o[1]]
            t_psq = [t_ps_s[0], t_ps_s[1], t_ps_o[0], t_ps_o[1]]
            rr.update({"q": 0})

            def nq4():
                rr["q"] += 1
                return rr["q"] % 4

            def tt(eng, out, in0, in1, op, reads, writes):
                P.op(eng, lambda e: e.tensor_tensor(out=out, in0=in0, in1=in1, op=op), reads=reads, writes=writes)

            def bc6(col0, C):
                return par[:, col0:col0 + 6].unsqueeze(2).to_broadcast([128, 6, C])

            def inproj_group(C, xn_ap, t_xn, m0, nm):
                if True:
                    i = nxt("mm")

                    def mm(e, m0=m0, nm=nm, i=i):
                        for mi in range(nm):
                            m = m0 + mi
                            for k in range(8):
                                ins = e.matmul(ps_mm[i][:, mi * 128:mi * 128 + C], lhsT=Wb[:, k, m * 128:(m + 1) * 128], rhs=xn_ap(k),
                                               start=(k == 0), stop=(k == 7))
                        return ins
                    P.op("pe", mm, reads=[t_Wb, t_xn], writes=[t_ps_mm[i]])
                    src = ps_mm[i][:, 0:nm * 128].rearrange("p (m j) -> p m j", m=nm)[:, :, 0:C]
                    if m0 < 19:
                        copy_op("act", cols[:, m0:m0 + nm, 1:1 + C], src, [t_ps_mm[i]], [t_cols])
                    else:
                        copy_op("act", QmTc[:, :, 0:C], src, [t_ps_mm[i]], [t_QmTc])

            COLS_GROUPS = [(16, 3), (0, 4), (4, 4), (8, 4), (12, 4)]

            def rwkv_chunk(C, xn_ap, t_xn, mode, idx, pre_done=False, early_fn=None, interleave=()):
                first = (idx == 0) if mode == "p" else True
                last = (idx == NT - 1) if mode == "p" else True
                interleave = list(interleave)
                if not pre_done:
                    for (m0, nm) in COLS_GROUPS:
                        inproj_group(C, xn_ap, t_xn, m0, nm)
                inproj_group(C, xn_ap, t_xn, 19, 2)
                for gc in range(2):
                    i = nxt("mm")

                    def mmg(e, gc=gc, i=i):
                        for k in range(8):
                            ins = e.matmul(ps_mm[i][0:C, :], lhsT=xn_ap(k), rhs=Wb[:, k, 2688 + gc * 512:2688 + (gc + 1) * 512], start=(k == 0), stop=(k == 7))
                        return ins
                    P.op("pe", mmg, reads=[t_Wb, t_xn], writes=[t_ps_mm[i]])
                    P.op("act", (lambda e, gc=gc, i=i: e.activation(out=gt[0:C, gc * 512:(gc + 1) * 512], in_=ps_mm[i][0:C, :], func=AF.Silu)),
                         reads=[t_ps_mm[i]], writes=[t_gt])
                t_lastc = t_lastc_g
                P.op("act", lambda e: e.copy(out=lastc[:, 0:19], in_=cols[:, :, C]), reads=[t_cols], writes=[t_lastc])
                if last:
                    dst = (o_shp if mode == "p" else o_shs[idx:idx + 1, :]).rearrange("o (m p) -> p (o m)", p=128)
                    P.op("sp", lambda e: e.dma_start(out=dst, in_=lastc[:, 0:19], allow_slow_non_contiguous=True), reads=[t_lastc], dma=True)
                for (m0, nm) in ((0, 6), (6, 6), (12, 6), (18, 1)):
                    cur = cols[:, m0:m0 + nm, 1:1 + C]
                    prv = cols[:, m0:m0 + nm, 0:C]
                    tmp = SG[:, 0:nm, 0:C]
                    tt("dve", tmp, prv, cur, ALU.subtract, [t_cols], [t_S["G"]])
                    tt("dve", tmp, tmp, par[:, P_MU + m0:P_MU + m0 + nm].unsqueeze(2).to_broadcast([128, nm, C]), ALU.mult, [t_S["G"], t_par], [t_S["G"]])
                    tt("dve", cur, tmp, cur, ALU.add, [t_S["G"], t_cols], [t_cols])
                P.op("act", lambda e: e.copy(out=cols[:, :, 0], in_=lastc[:, 0:19]), reads=[t_lastc, t_cols], writes=[t_cols])

                class _XS:
                    def __getitem__(self, key):
                        p_, m_, j_ = key
                        assert j_ == slice(0, C)
                        return cols[p_, m_, 1:1 + C]
                xs = _XS()
                xr, xk, xv_ = xs[:, 0:6, 0:C], xs[:, 6:12, 0:C], xs[:, 12:18, 0:C]
                P.op("act", lambda e: e.activation(out=lw[0:64, 0:C], in_=xs[0:64, 18, 0:C], func=AF.Tanh), reads=[t_xs], writes=[t_lw])
                P.op("dve", lambda e: e.tensor_copy(out=lw[64:128, 0:C], in_=xs[64:128, 18, 0:C]), reads=[t_xs], writes=[t_lw])
                sigw, asig = SA[:, :, 0:C], SB[:, :, 0:C]
                for (which, wt, rows, pcol, dstS, tS) in (("w", wup, slice(0, 64), P_W0, SA, "A"), ("a", aup, slice(64, 128), P_A0, SB, "B")):
                    for (p0, np_) in ((0, 4), (4, 2)):
                        q = nq4()

                        def mml(e, wt=wt, rows=rows, p0=p0, np_=np_, q=q):
                            for pi in range(np_):
                                p = p0 + pi
                                ins = e.matmul(psq[q][:, pi * 128:pi * 128 + C], lhsT=wt[rows, p * 128:(p + 1) * 128], rhs=lw[rows, 0:C], start=True, stop=True)
                            return ins
                        P.op("pe", mml, reads=[t_lora, t_lw], writes=[t_psq[q]])
                        for pi in range(np_):
                            p = p0 + pi
                            P.op("act", (lambda e, pi=pi, p=p, q=q, pcol=pcol, dstS=dstS: e.activation(
                                out=dstS[:, p, 0:C], in_=psq[q][:, pi * 128:pi * 128 + C], func=AF.Sigmoid, bias=par[:, pcol + p:pcol + p + 1], scale=1.0)),
                                reads=[t_psq[q], t_par], writes=[t_S[tS]])
                cs = SC[:, :, 0:C]
                if C == 128:
                    P.op("dve", lambda e: e.tensor_tensor_scan(out=SC[:, :, :].rearrange("p m j -> p (m j)"), data0=rmask[:, :, :].rearrange("p m j -> p (m j)"),
                                                               data1=SA[:, :, :].rearrange("p m j -> p (m j)"), initial=0.0, op0=ALU.mult, op1=ALU.add),
                         reads=[t_S["A"], t_cst1], writes=[t_S["C"]])
                else:
                    for p in range(6):
                        P.op("dve", (lambda e, p=p: e.tensor_tensor_scan(out=SC[:, p, 0:C], data0=rmask[:, p, 0:C], data1=SA[:, p, 0:C], initial=0.0,
                                                                         op0=ALU.mult, op1=ALU.add)), reads=[t_S["A"], t_cst1], writes=[t_S["C"]])
                csC = SC[:, :, C - 1:C]
                eP, eH, eA, eN = SE[:, :, 0:C], SD[:, :, 0:C], SA[:, :, 0:C], SC[:, :, 0:C]
                P.op("act", lambda e: e.activation(out=eP, in_=cs, func=AF.Exp, scale=-C0), reads=[t_S["C"]], writes=[t_S["E"]])
                tt("dve", eH, csC.to_broadcast([128, 6, C]), cs, ALU.subtract, [t_S["C"]], [t_S["D"]])
                P.op("act", lambda e: e.activation(out=eH, in_=eH, func=AF.Exp, scale=-C0), reads=[t_S["D"]], writes=[t_S["D"]])
                P.op("act", lambda e: e.activation(out=wc[:, 0:6], in_=SC[:, :, C - 1], func=AF.Exp, scale=-C0), reads=[t_S["C"]], writes=[t_wc])
                tt("dve", eA, cs, sigw, ALU.subtract, [t_S["C"], t_S["A"]], [t_S["A"]])
                P.op("act", lambda e: e.activation(out=eA, in_=eA, func=AF.Exp, scale=-C0), reads=[t_S["A"]], writes=[t_S["A"]])
                P.op("act", lambda e: e.activation(out=eN, in_=cs, func=AF.Exp, scale=C0), reads=[t_S["C"], t_S["D"], t_S["E"], t_S["A"], t_wc], writes=[t_S["C"]])
                kk, g_ = SF[:, :, 0:C], SG[:, :, 0:C]
                tt("dve", kk, xk, bc6(P_KK, C), ALU.mult, [t_xs, t_par], [t_S["F"]])
                tt("dve", g_, kk, kk, ALU.mult, [t_S["F"]], [t_S["G"]])
                qa, qb_ = nq4(), nq4()

                def mmn(e):
                    e.matmul(psq[qa][:, 0:4 * 128].rearrange("p (m j) -> p m j", m=4)[:, :, 0:C], lhsT=bd[:, :], rhs=SG[:, 0:4, 0:C], start=True, stop=True)
                    return e.matmul(psq[qb_][:, 0:2 * 128].rearrange("p (m j) -> p m j", m=2)[:, :, 0:C], lhsT=bd[:, :], rhs=SG[:, 4:6, 0:C], start=True, stop=True)
                P.op("pe", mmn, reads=[t_S["G"], t_cst1], writes=[t_psq[qa], t_psq[qb_]])
                P.op("act", lambda e: e.activation(out=SG[:, 0:4, 0:C], in_=psq[qa][:, 0:512].rearrange("p (m j) -> p m j", m=4)[:, :, 0:C], func=AF.Sqrt),
                     reads=[t_psq[qa]], writes=[t_S["G"]])
                P.op("act", lambda e: e.activation(out=SG[:, 4:6, 0:C], in_=psq[qb_][:, 0:256].rearrange("p (m j) -> p m j", m=2)[:, :, 0:C], func=AF.Sqrt),
                     reads=[t_psq[qb_]], writes=[t_S["G"]])
                P.op("dve", lambda e: e.tensor_scalar_max(out=g_, in0=g_, scalar1=1e-12), reads=[t_S["G"]], writes=[t_S["G"]])
                P.op("dve", lambda e: e.reciprocal(out=g_, in_=g_), reads=[t_S["G"]], writes=[t_S["G"]])
                tt("dve", kk, kk, g_, ALU.mult, [t_S["F"], t_S["G"]], [t_S["F"]])
                for hf_ in range(2):
                    rs_ = slice(64 * hf_, 64 * hf_ + 64)
                    P.op("dve", (lambda e, hf_=hf_, rs_=rs_: e.scalar_tensor_tensor(out=AT2[rs_, hf_:12:2, 0:C], in0=SF[rs_, :, 0:C], scalar=-1.0,
                                                                                   in1=SA[rs_, :, 0:C], op0=ALU.mult, op1=ALU.mult)),
                         reads=[t_S["F"], t_S["A"]], writes=[t_o["AT"]])
                tt("dve", g_, kk, asig, ALU.mult, [t_S["F"], t_S["B"]], [t_S["G"]])
                tt("dve", BT[:, :, 0:C], g_, eN, ALU.mult, [t_S["G"], t_S["C"]], [t_o["BT"]])
                tt("dve", BH[:, :, 0:C], g_, eH, ALU.mult, [t_S["G"], t_S["D"]], [t_o["BH"]])
                km = hbuf1[:, 0:768].rearrange("p (m j) -> p m j", m=6)[:, :, 0:C]
                tt("pool", km, asig, bc6(P_KA, C), ALU.mult, [t_S["B"], t_par], [t_hbuf1])
                tt("pool", km, km, bc6(P_OMKA, C), ALU.add, [t_hbuf1, t_par], [t_hbuf1])
                tt("pool", km, km, xk, ALU.mult, [t_hbuf1, t_xs], [t_hbuf1])
                tt("pool", KT1[:, :, 0:C], km, eN, ALU.mult, [t_hbuf1, t_S["C"]], [t_o["KT"]])
                tt("pool", KH[:, :, 0:C], km, eH, ALU.mult, [t_hbuf1, t_S["D"]], [t_o["KH"]])
                tt("pool", km, km, bc6(P_RK, C), ALU.mult, [t_hbuf1, t_par], [t_hbuf1])
                tt("pool", RKT[:, :, 0:C], km, xr, ALU.mult, [t_hbuf1, t_xs], [t_o["RKT"]])
                for hf_ in range(2):
                    rs_ = slice(64 * hf_, 64 * hf_ + 64)
                    tt("pool", RT2[rs_, hf_:12:2, 0:C], cols[rs_, 0:6, 1:1 + C], SE[rs_, :, 0:C], ALU.mult, [t_xs, t_S["E"]], [t_o["RT"]])
                P.op("act", lambda e: e.copy(out=XV[:, :, 0:C], in_=xv_), reads=[t_xs], writes=[t_o["XV"]])
                for (srcT, tsrc, dstT, tdst) in ((XV, "XV", Vt, t_Vt), (KH, "KH", Kh, t_Kh), (BH, "BH", Bh, t_Bh)):
                    def tr(e, srcT=srcT):
                        for p in range(6):
                            ins = e.transpose(out=ps_t[0:C, p * 128:(p + 1) * 128], in_=srcT[:, p, 0:C], identity=ident[:, :])
                        return ins
                    P.op("pe", tr, reads=[t_o[tsrc], t_ident], writes=[t_ps_t])
                    copy_op(ev_eng(), dstT[0:C, :], ps_t[0:C, 0:768], [t_ps_t], [tdst])
                nlev = max(1, int(math.ceil(math.log2(C))))

                def pair_mm(q, l_fn, r_fn, hg, reads):
                    def f(e):
                        for hi in range(4):
                            h = 4 * hg + hi
                            ins = e.matmul(psq[q][0:C, hi * 128:hi * 128 + C], lhsT=l_fn(h), rhs=r_fn(h), start=True, stop=True)
                        return ins
                    P.op("pe", f, reads=reads, writes=[t_psq[q]])

                A2 = lambda h: AT2[:, h, 0:C]
                R2 = lambda h: RT2[:, h, 0:C]
                Bp = lambda h: BT[:, h // 2, 0:C]
                Kp = lambda h: KT1[:, h // 2, 0:C]

                def pv4(q):
                    return psq[q][0:C, :].rearrange("p (h i) -> p h i", h=4)[:, :, 0:C]

                def sq_mm(q, lT, rT, reads, acc_ident_rhs=None):
                    def f(e):
                        for hi in range(4):
                            if acc_ident_rhs is not None:
                                e.matmul(psq[q][0:C, hi * 128:hi * 128 + C], lhsT=ident[0:C, 0:C], rhs=acc_ident_rhs[0:C, hi, 0:C], start=True, stop=False)
                            ins = e.matmul(psq[q][0:C, hi * 128:hi * 128 + C], lhsT=lT[0:C, hi, 0:C], rhs=rT[0:C, hi, 0:C],
                                           start=(acc_ident_rhs is None), stop=True)
                        return ins
                    P.op("pe", f, reads=reads + [t_ident], writes=[t_psq[q]])

                if early_fn is not None:
                    early_fn()
                for hg in range(3):
                    q = nq4()
                    pair_mm(q, A2, Bp, hg, [t_o["AT"], t_o["BT"]])
                    tt("dve", Lh[hg][0:C, :, 0:C], pv4(q), ML[0:C, :, 0:C], ALU.mult, [t_psq[q], t_cst1], [t_Lh[hg]])
                    q = nq4()
                    pair_mm(q, Bp, A2, hg, [t_o["AT"], t_o["BT"]])
                    tt("dve", Xh[hg][0:C, :, 0:C], pv4(q), MU[0:C, :, 0:C], ALU.mult, [t_psq[q], t_cst1], [t_Xh[hg]])
                    tt("pool", Qh[hg][0:C, :, 0:C], Xh[hg][0:C, :, 0:C], Irep[0:C, :, 0:C], ALU.add, [t_Xh[hg], t_cst1], [t_Qh[hg]])
                for j in range(nlev - 1):
                    need_x = (j + 1 < nlev - 1)
                    if interleave:
                        interleave.pop(0)()
                    for hg in range(3):
                        q = nq4()
                        sq_mm(q, Xh[hg], Lh[hg], [t_Xh[hg], t_Lh[hg]])
                        qx = None
                        if need_x:
                            qx = nq4()
                            sq_mm(qx, Lh[hg], Xh[hg], [t_Xh[hg], t_Lh[hg]])
                        copy_op("act", Lh[hg][0:C, :, 0:C], pv4(q), [t_psq[q]], [t_Lh[hg]])
                        if need_x:
                            copy_op("dve", Xh[hg][0:C, :, 0:C], pv4(qx), [t_psq[qx]], [t_Xh[hg]])
                    for hg in range(3):
                        q = nq4()
                        sq_mm(q, Lh[hg], Qh[hg], [t_Lh[hg], t_Qh[hg]], acc_ident_rhs=Qh[hg])
                        if j == nlev - 2:
                            copy_op("act", Qf[0:C, 4 * hg:4 * hg + 4, 0:C], pv4(q), [t_psq[q]], [t_Qf])
                        else:
                            copy_op("act", Qh[hg][0:C, :, 0:C], pv4(q), [t_psq[q]], [t_Qh[hg]])
                while interleave:
                    interleave.pop(0)()
                for hg in range(3):
                    if nlev == 1:
                        copy_op("act", Qf[0:C, 4 * hg:4 * hg + 4, 0:C], Qh[hg][0:C, :, 0:C], [t_Qh[hg]], [t_Qf])
                    q = nq4()
                    pair_mm(q, Kp, A2, hg, [t_o["KT"], t_o["AT"]])
                    tt("dve", Aak[0:C, 4 * hg:4 * hg + 4, 0:C], pv4(q), MU[0:C, :, 0:C], ALU.mult, [t_psq[q], t_cst1], [t_Aak])
                    q = nq4()
                    pair_mm(q, Kp, R2, hg, [t_o["KT"], t_o["RT"]])
                    tt("dve", Ark[0:C, 4 * hg:4 * hg + 4, 0:C], pv4(q), MUi[0:C, :, 0:C], ALU.mult, [t_psq[q], t_cst1], [t_Ark])
                    q = nq4()
                    pair_mm(q, Bp, R2, hg, [t_o["BT"], t_o["RT"]])
                    tt("dve", Arb[0:C, 4 * hg:4 * hg + 4, 0:C], pv4(q), MUi[0:C, :, 0:C], ALU.mult, [t_psq[q], t_cst1], [t_Arb])
                if first:
                    if mode == "p":
                        P.op("pool", lambda e: e.memset(St[:, :, :], 0.0), writes=[t_St])
                        P.op("pool", lambda e: e.memset(Sbf[:, :, :], 0.0), writes=[t_St])
                    else:
                        P.op("sp", lambda e: e.dma_start(out=svst[0:64, :, :], in_=s_wkv[idx].rearrange("h v k -> v h k")), writes=[t_svst, t_Zb, t_Ub], dma=True, sem_tk=t_svst)

                        def trs(e):
                            for p in range(6):
                                ins = e.transpose(out=ps_x[:, p * 64:(p + 1) * 64], in_=svst[0:64, 2 * p:2 * p + 2, :].rearrange("v h k -> v (h k)"),
                                                  identity=identf[0:64, 0:64])
                            return ins
                        P.op("pe", trs, reads=[t_svst, t_ident], writes=[t_ps_x])
                        P.op("act", lambda e: e.copy(out=St[:, :, :], in_=ps_x[:, 0:384].rearrange("p (m v) -> p m v", m=6)), reads=[t_ps_x], writes=[t_St])
                        P.op("dve", lambda e: e.tensor_copy(out=Sbf[:, :, :], in_=ps_x[:, 0:384].rearrange("p (m v) -> p m v", m=6)), reads=[t_ps_x], writes=[t_St])
                def head_cols(h):
                    return slice(h * 64, (h + 1) * 64)

                def seq_mm(name, fn_terms, reads, dst_banks):
                    def f(e):
                        for h in range(12):
                            bank, hc = (dst_banks[0], h) if h < 8 else (dst_banks[1], h - 8)
                            terms = fn_terms(h)
                            for ti, (lT, r_) in enumerate(terms):
                                ins = e.matmul(psq[bank][0:C, hc * 64:(hc + 1) * 64], lhsT=lT, rhs=r_, start=(ti == 0), stop=(ti == len(terms) - 1))
                        return ins
                    P.op("pe", f, reads=reads, writes=[t_psq[dst_banks[0]], t_psq[dst_banks[1]]])

                def hsl(h):
                    p, hf = h // 2, h % 2
                    return slice(64 * hf, 64 * hf + 64), p

                def evac768(dst, banks, tdst, as_f32=False):
                    copy_op("act", dst[0:C, 0:512], psq[banks[0]][0:C, 0:512], [t_psq[banks[0]]], [tdst])
                    copy_op("dve", dst[0:C, 512:768], psq[banks[1]][0:C, 0:256], [t_psq[banks[1]]], [tdst])

                seq_mm("Z", lambda h: [(AT2[:, h, 0:C], Sbf[:, h // 2, :]), (Aak[0:C, h, 0:C], Vt[0:C, head_cols(h)])],
                       [t_o["AT"], t_St, t_Aak, t_Vt], (0, 1))
                evac768(Zb, (0, 1), t_Zb)
                seq_mm("U", lambda h: [(Qf[0:C, h, 0:C], Zb[0:C, head_cols(h)])], [t_Qf, t_Zb], (2, 3))
                evac768(Ub, (2, 3), t_Ub)
                seq_mm("Y", lambda h: [(RT2[:, h, 0:C], Sbf[:, h // 2, :]), (Ark[0:C, h, 0:C], Vt[0:C, head_cols(h)]),
                                       (Arb[0:C, h, 0:C], Ub[0:C, head_cols(h)])],
                       [t_o["RT"], t_St, t_Ark, t_Vt, t_Arb, t_Ub], (0, 1))
                evac768(Yf, (0, 1), t_Yf)

                def snew(e):
                    for h in range(12):
                        psl, p = hsl(h)
                        e.matmul(ps_x[psl, p * 64:(p + 1) * 64], lhsT=Kh[0:C, head_cols(h)], rhs=Vt[0:C, head_cols(h)], start=True, stop=False)
                        ins = e.matmul(ps_x[psl, p * 64:(p + 1) * 64], lhsT=Bh[0:C, head_cols(h)], rhs=Ub[0:C, head_cols(h)], start=False, stop=True)
                    return ins
                P.op("pe", snew, reads=[t_Kh, t_Vt, t_Bh, t_Ub], writes=[t_ps_x])
                tt("dve", St[:, :, :], St[:, :, :], wc[:, 0:6].unsqueeze(2).to_broadcast([128, 6, 64]), ALU.mult, [t_St, t_wc], [t_St])
                tt("dve", St[:, :, :], St[:, :, :], ps_x[:, 0:384].rearrange("p (m v) -> p m v", m=6), ALU.add, [t_St, t_ps_x], [t_St])
                P.op("dve", lambda e: e.tensor_copy(out=Sbf[:, :, :], in_=St[:, :, :]), reads=[t_St], writes=[t_St])
                if last:
                    def trs2(e):
                        for p in range(6):
                            ins = e.transpose(out=ps_x[0:64, p * 128:(p + 1) * 128] if False else ps_mm[0][0:64, p * 64:(p + 1) * 64], in_=St[:, p, :], identity=identf[:, :])
                        return ins
                    def trs3(e):
                        for p in range(6):
                            bank = ps_mm[0] if p < 4 else ps_mm[1]
                            pc = p if p < 4 else p - 4
                            ins = e.transpose(out=bank[0:64, pc * 128:(pc + 1) * 128], in_=St[:, p, :], identity=identf[:, :])
                        return ins
                    P.op("pe", trs3, reads=[t_St, t_ident], writes=[t_ps_mm[0], t_ps_mm[1]])
                    P.op("act", lambda e: e.copy(out=svst[0:64, 0:8, :].rearrange("v h k -> v (h k)"), in_=ps_mm[0][0:64, 0:512]), reads=[t_ps_mm[0]], writes=[t_svst, t_Zb, t_Ub])
                    P.op("act", lambda e: e.copy(out=svst[0:64, 8:12, :].rearrange("v h k -> v (h k)"), in_=ps_mm[1][0:64, 0:256]), reads=[t_ps_mm[1]], writes=[t_svst, t_Zb, t_Ub])
                    dsto = (o_wkvp if mode == "p" else o_wkvs[idx]).rearrange("h v k -> v h k")
                    P.op("sp", lambda e: e.dma_start(out=dsto, in_=svst[0:64, :, :]), reads=[t_svst, t_Zb, t_Ub], dma=True, sem_tk=t_svst)
                Y3 = Yf[0:C, :].rearrange("p (h d) -> p h d", h=12)
                sqv = SF[0:C, :, :].rearrange("p m j -> p (m j)").rearrange("p (h d) -> p h d", h=12)
                P.op("dve", lambda e: e.reduce_sum(out=gsm[0:C, 0:12], in_=Y3, axis=AX.X), reads=[t_Yf], writes=[t_gsm])
                P.op("act", lambda e: e.activation(out=sqv, in_=Y3, func=AF.Square), reads=[t_Yf, t_o["RKT"]], writes=[t_S["F"]])
                P.op("dve", lambda e: e.reduce_sum(out=gsm[0:C, 12:24], in_=sqv, axis=AX.X), reads=[t_S["F"]], writes=[t_gsm])
                P.op("dve", lambda e: e.tensor_scalar(out=gsm[0:C, 24:36], in0=gsm[0:C, 0:12], scalar1=1.0 / 64, scalar2=None, op0=ALU.mult),
                     reads=[t_gsm], writes=[t_gsm])
                tt("dve", gsm[0:C, 36:48], gsm[0:C, 24:36], gsm[0:C, 24:36], ALU.mult, [t_gsm], [t_gsm])
                P.op("dve", lambda e: e.scalar_tensor_tensor(out=gsm[0:C, 48:60], in0=gsm[0:C, 12:24], scalar=1.0 / 64, in1=gsm[0:C, 36:48],
                                                             op0=ALU.mult, op1=ALU.subtract), reads=[t_gsm], writes=[t_gsm])
                P.op("dve", lambda e: e.tensor_scalar(out=gsm[0:C, 48:60], in0=gsm[0:C, 48:60], scalar1=64e-5, scalar2=None, op0=ALU.add),
                     reads=[t_gsm], writes=[t_gsm])
                P.op("act", lambda e: e.activation(out=gsm[0:C, 60:72], in_=gsm[0:C, 48:60], func=AF.Sqrt), reads=[t_gsm], writes=[t_gsm])
                P.op("dve", lambda e: e.reciprocal(out=gsm[0:C, 72:84], in_=gsm[0:C, 60:72]), reads=[t_gsm], writes=[t_gsm])
                hb3 = hbuf1[0:C, 0:768].rearrange("p (h d) -> p h d", h=12)
                tt("dve", hb3, Y3, gsm[0:C, 24:36].unsqueeze(2).to_broadcast([C, 12, 64]), ALU.subtract, [t_Yf, t_gsm], [t_hbuf1])
                tt("dve", hb3, hb3, gsm[0:C, 72:84].unsqueeze(2).to_broadcast([C, 12, 64]), ALU.mult, [t_hbuf1, t_gsm], [t_hbuf1])
                tt("dve", hbuf1[0:C, 0:768], hbuf1[0:C, 0:768], lnw_b[0:C, :], ALU.mult, [t_hbuf1, t_ln], [t_hbuf1])
                tt("dve", hbuf1[0:C, 0:768], hbuf1[0:C, 0:768], lnb_b[0:C, :], ALU.add, [t_hbuf1, t_ln], [t_hbuf1])

                def bon(e):
                    for p in range(6):
                        ins = e.matmul(ps_x[0:C, 400 + 2 * p:402 + 2 * p], lhsT=RKT[:, p, 0:C], rhs=hsel[:, 0:2], start=True, stop=True)
                    return ins
                P.op("pe", bon, reads=[t_o["RKT"], t_cst1], writes=[t_ps_x])
                P.op("act", lambda e: e.copy(out=gsm[0:C, 84:96], in_=ps_x[0:C, 400:412]), reads=[t_ps_x], writes=[t_gsm])
                tt("dve", sqv, Vt[0:C, :].rearrange("p (h d) -> p h d", h=12), gsm[0:C, 84:96].unsqueeze(2).to_broadcast([C, 12, 64]), ALU.mult,
                   [t_Vt, t_gsm, t_S["F"]], [t_S["F"]])
                tt("dve", hb3, hb3, sqv, ALU.add, [t_hbuf1, t_S["F"]], [t_hbuf1])
                io2 = nxt("o")
                if mode == "p":
                    cross_attn((lambda psl, pr, blk: KmT[psl, 1, pr, blk * 128:(blk + 1) * 128]), t_KmT[1],
                               (lambda blk, h: Vm[:, 1, blk, h, 0:65]), t_Vm[1],
                               (lambda psl, pr: QmTc[psl, pr, 0:C]), [t_QmTc], C, ps_o[io2], t_ps_o[io2])
                else:
                    P.op("pool", lambda e: e.memset(cV1[:, :, :, 64:65], 1.0), writes=[t_cV1])
                    load_piece(c_mem[1, idx].rearrange("(b j) c -> j b c", b=2), 2)
                    cross_attn((lambda psl, pr, blk: cKT1[psl, 2 * blk + pr, :]), t_cKT1, (lambda blk, h: cV1[:, blk, h, 0:65]), t_cV1,
                               (lambda psl, pr: QmTc[psl, pr, 0:C]), [t_QmTc], C, ps_o[io2], t_ps_o[io2])
                normalize_o(ps_o[io2], t_ps_o[io2], C, hbuf1[0:C, 768:1024], t_hbuf1, t_rcp1, rcp1)
                col0 = 0 if mode == "p" else 4 * idx
                post_a(C, hbuf1[0:C, :], t_hbuf1, gt[0:C, :], t_gt, 8, hb1, t_hb1, hT1, t_hT1, col0)
                if mode == "p":
                    def xsrc(ys_):
                        P.op("sp", lambda e: e.dma_start(out=xres1[:, :], in_=x1_d[idx * 128:(idx + 1) * 128, :]), reads=[t_x1[idx]], writes=[t_xres1], dma=True)

                    def xdst(ys_):
                        P.op("sp", lambda e: e.dma_start(out=y_p[idx * 128:(idx + 1) * 128, :], in_=xres1[:, :]), reads=[t_xres1], dma=True)
                    post_b(G_POST[1], 128, 8, hT1, t_hT1, Wo, t_Wo, 0, xsrc, xdst)

            keep1 = apos[0]
            apos[0] = xl_w0
            cst1 = carve_f(1024).rearrange("p (t n) -> p t n", t=2)
            ckb1 = carve_bf(2 * 256).rearrange("p (t n) -> p t n", t=2)
            cKT1 = carve_bf(4 * 128).rearrange("p (i n) -> p i n", i=4)
            cV1 = carve_bf(2 * 4 * 80).rearrange("p (t h d) -> p t h d", t=2, h=4)
            t_cst1b, t_ckb1, t_cKT1, t_cV1 = Tk(), Tk(), Tk(), Tk()
            assert apos[0] <= xl_w0 + 2048, apos[0] - xl_w0
            apos[0] = keep1
            CTX.update(PT=PT1, t_PT=t_PT1, ytmp=[hbuf1] * 2, t_ytmp=[t_hbuf1] * 2, xres=[xres1] * 2, t_xres=[t_xres1] * 2,
                       cst=cst1, ckb=ckb1, cKT=cKT1, cV=cV1, t_cst=t_cst1b, t_ckb=t_ckb1, t_cKT=t_cKT1, t_cV=t_cV1)
            print("L1 arena words (final)", apos[0])

            def prompt_early(c):
                P.op("sp", lambda e: e.dma_start(out=xld1[:, :], in_=x1_d[c * 128:(c + 1) * 128, :]), reads=[t_x1[c]], writes=[t_xld1], dma=True)
                rmsnorm(xld1[:, :], 128, G_PRE[1], xnb1[:, :], t_xld1, t_xnb1)

                def tr(e):
                    for k in range(8):
                        ins = e.transpose(out=ps_t[:, k * 128:(k + 1) * 128], in_=xnb1[:, k * 128:(k + 1) * 128], identity=ident[:, :])
                    return ins
                P.op("pe", tr, reads=[t_xnb1, t_ident], writes=[t_ps_t])
                copy_op(ev_eng(), xnTc[:, :, :], ps_t[:].rearrange("p (k m) -> p k m", k=8), [t_ps_t], [t_xnTc])

            xn_fn = (lambda k: xnTc[:, k, :])
            n_chunks = NT if "l1few" not in DBG else 2
            prompt_early(0)
            P.op("pool", lambda e: e.memset(cols[:, :, 0:1], 0.0), writes=[t_cols])
            for c in range(n_chunks):
                if c + 1 < n_chunks and "nopipe1" not in DBG:
                    ef = (lambda c=c: prompt_early(c + 1))
                    il = [(lambda m0=m0, nm=nm: inproj_group(128, xn_fn, t_xnTc, m0, nm)) for (m0, nm) in COLS_GROUPS]
                else:
                    ef, il = None, ()
                rwkv_chunk(128, xn_fn, t_xnTc, "p", c, pre_done=(c > 0 and "nopipe1" not in DBG), early_fn=ef, interleave=il)
                if c + 1 < n_chunks and "nopipe1" in DBG:
                    prompt_early(c + 1)

            def sample_l1():
                P.op("sp", lambda e: e.dma_start(out=xld1[0:16, :], in_=x1s_d), reads=[t_x1[NT]], writes=[t_xld1], dma=True)
                rmsnorm(xld1[0:16, :], 16, G_PRE[1], xnb1[0:16, :], t_xld1, t_xnb1)

                def tr(e):
                    for k in range(8):
                        ins = e.transpose(out=ps_t[:, k * 128:k * 128 + 16], in_=xnb1[0:16, k * 128:(k + 1) * 128], identity=ident[0:16, 0:16])
                    return ins
                P.op("pe", tr, reads=[t_xnb1, t_ident], writes=[t_ps_t])
                copy_op(ev_eng(), xnTs1[:, :, :], ps_t[:].rearrange("p (k m) -> p k m", k=8)[:, :, 0:16], [t_ps_t], [t_xnTs1])
                barrier()
                for bb in range(4):
                    P.op("sp", (lambda e, bb=bb: e.dma_start(out=cols[:, :, 0], in_=s_shift[bb:bb + 1, :].rearrange("o (m p) -> p (o m)", p=128),
                                                             allow_slow_non_contiguous=True)), writes=[t_cols], dma=True)
                    rwkv_chunk(4, (lambda k, bb=bb: xnTs1[:, k, 4 * bb:4 * bb + 4]), t_xnTs1, "s", bb)

                def xsrc_s(ys_):
                    P.op("sp", lambda e: e.dma_start(out=xres1[0:16, :], in_=x1s_d), reads=[t_x1[NT]], writes=[t_xres1], dma=True)

                def xdst_s(ys_):
                    P.op("sp", lambda e: e.dma_start(out=y_s, in_=xres1[0:16, :]), reads=[t_xres1], dma=True)
                post_b(G_POST[1], 16, 8, hT1, t_hT1, Wo, t_Wo, 0, xsrc_s, xdst_s)
            if "nosample1" not in DBG:
                sample_l1()

        print('total_ops', len(P.ops))
        if 'dump2' in DBG:
            for _i, _o in enumerate(P.ops):
                print('OP', _i, _o.eng, _o.line, 'dma' if _o.is_dma else '')
        P.finalize_and_emit(st)
    return nc


def layer0(env):
    pass


_CACHE = {}


def kernel(x_prompt, x_sample, mem_prompt, cache_mem_kv, cache_win0, cache_win1, cache_win2, state_wkv,
           state_shift, norm_pre, norm_post, norm_mem, w_mem_kv, rel_bias, w_in_a, w_out_a, w_in_b, w_out_b,
           rwkv_mu, rwkv_w0, rwkv_w_up, rwkv_a0, rwkv_a_up, rwkv_k_k, rwkv_k_a, rwkv_r_k, rwkv_ln_w, rwkv_ln_b):
    f = lambda a: np.ascontiguousarray(np.asarray(a, dtype=np.float32))
    if "nc" not in _CACHE:
        _CACHE["nc"] = build_program()
    nc = _CACHE["nc"]
    oh = _onehot_const()
    in_maps = []
    for c in range(NCORES):
        sl = slice(4 * c, 4 * c + 4)
        in_maps.append({
            "x_p": f(x_prompt[c]),
            "x_s": f(x_sample[sl]).reshape(16, D),
            "mem_p": f(mem_prompt[c]),
            "c_mem": f(cache_mem_kv[:, sl]).reshape(2, 4, 256, 512),
            "c_w0": f(cache_win0[0, sl]).reshape(4, 128, 512),
            "c_w1": f(cache_win1[0, sl]).reshape(4, 512, 512),
            "c_w2": f(cache_win2[0, sl]).reshape(4, 2048, 512),
            "s_wkv": f(state_wkv[0, sl]),
            "s_shift": f(state_shift[0, sl]),
            "norm_pre": f(norm_pre), "norm_post": f(norm_post), "norm_mem": f(norm_mem),
            "w_mem": f(w_mem_kv), "rel_bias": f(rel_bias),
            "w_in_a": f(w_in_a[0]), "w_out_a": f(w_out_a[0]), "w_in_b": f(w_in_b[0]), "w_out_b": f(w_out_b[0]),
            "c_onehot": oh,
            "r_mu": f(rwkv_mu).reshape(1, C_SHIFT), "r_w0": f(rwkv_w0).reshape(1, 768), "r_wup": f(rwkv_w_up).reshape(64, 768),
            "r_a0": f(rwkv_a0).reshape(1, 768), "r_aup": f(rwkv_a_up).reshape(64, 768), "r_kk": f(rwkv_k_k).reshape(1, 768),
            "r_ka": f(rwkv_k_a).reshape(1, 768), "r_rk": f(rwkv_r_k).reshape(1, 768), "r_lnw": f(rwkv_ln_w).reshape(1, 768),
            "r_lnb": f(rwkv_ln_b).reshape(1, 768),
        })
    res = run_bass_kernel_spmd(nc, in_maps, core_ids=list(range(NCORES)))
    R = res.results
    cat = lambda k: np.stack([np.asarray(R[c][k]) for c in range(NCORES)])
    y_prompt = cat("y_p")
    y_sample = cat("y_s").reshape(32, 4, D)
    new_mem = cat("o_mem").transpose(1, 0, 2, 3).reshape(2, 8, 256, 2, 4, 64)
    w0p = cat("o_w0p").reshape(1, 8, 128, 2, 4, 64)
    w1p = cat("o_w1p").reshape(1, 8, 512, 2, 4, 64)
    w2p = cat("o_w2p").reshape(1, 8, 2048, 2, 4, 64)
    w0s = cat("o_w0s").reshape(1, 32, 4, 2, 4, 64)
    w1s = cat("o_w1s").reshape(1, 32, 4, 2, 4, 64)
    w2s = cat("o_w2s").reshape(1, 32, 4, 2, 4, 64)
    wkvp = cat("o_wkvp").reshape(1, 8, 12, 64, 64)
    wkvs = cat("o_wkvs").reshape(1, 32, 12, 64, 64)
    shp = cat("o_shp").reshape(1, 8, C_SHIFT)
    shs = cat("o_shs").reshape(1, 32, C_SHIFT)
    outs = (y_prompt, y_sample, new_mem, w0p, w1p, w2p, w0s, w1s, w2s, wkvp, wkvs, shp, shs)
    return tuple(np.ascontiguousarray(o.astype(np.float32)) for o in outs)
```
